# Optimizing a Trainium2 kernel written in Bass

```python
import math
import jax
import jax.numpy as jnp
from jax import lax

D_MODEL = 1024
BATCH = 16
SEQ = 256
DEPTH = 2
DEC_BATCH = 2
DEC_SEQ = 2048
PAST_LEN = 256

F32 = jnp.float32
GRID_W = 64
BRANCH = D_MODEL // 4
MIX_WIDTH = 4 * BRANCH
EPS = 1e-6
ROPE_BASE = 10000.0
Q_BLOCK = 128
DA_HEADS = 4
DA_V = BRANCH // DA_HEADS
DA_QK = DA_V // 2
S5_CH = 16
S5_GROUPS = BRANCH // S5_CH
S5_STATE = 64
S5_DT_MIN = 1e-3
S5_DT_MAX = 1e-1
HG_HEADS = 4
HG_DK = BRANCH // HG_HEADS
HG_DV = BRANCH // HG_HEADS
HG_CHUNK = 64
MLA_HEADS = 4
MLA_NOPE = 64
MLA_ROPE = 32
MLA_V = BRANCH // MLA_HEADS
MLA_Q_RANK = 192
MLA_KV_RANK = 128

IN_SIZES = (
    DA_HEADS * 2 * DA_QK,
    DA_HEADS * 2 * DA_QK,
    DA_HEADS * DA_V,
    BRANCH,
    BRANCH,
    BRANCH,
    HG_HEADS * HG_DK,
    HG_HEADS * HG_DK,
    HG_HEADS * HG_DK,
    HG_HEADS * HG_DV,
    BRANCH,
    MLA_Q_RANK,
    MLA_KV_RANK,
    MLA_ROPE,
    BRANCH,
)
IN_WIDTH = sum(IN_SIZES)

kernel_name = "hybrid_diff_s5_hgrn2_mla_prefix_step"


def rms_norm(x):
    xf = x.astype(F32)
    return xf * lax.rsqrt(jnp.mean(xf * xf, axis=-1, keepdims=True) + EPS)


def split_columns(z):
    out, start = [], 0
    for n in IN_SIZES:
        out.append(z[..., start:start + n])
        start += n
    return out


def rope_2d(x, row, col):
    half = x.shape[-1] // 2
    nf = half // 2
    inv_freq = ROPE_BASE ** (-jnp.arange(nf, dtype=F32) / nf)
    bshape = (1, x.shape[1]) + (1,) * (x.ndim - 3) + (nf,)
    xf = x.astype(F32)

    def rot(xp, pos):
        ang = (pos.astype(F32)[:, None] * inv_freq).reshape(bshape)
        cos, sin = jnp.cos(ang), jnp.sin(ang)
        x1, x2 = xp[..., :nf], xp[..., nf:]
        return jnp.concatenate([x1 * cos - x2 * sin, x1 * sin + x2 * cos], axis=-1)

    return jnp.concatenate([rot(xf[..., :half], row), rot(xf[..., half:], col)], axis=-1).astype(x.dtype)


def over_query_blocks(fn, q):
    B, L = q.shape[:2]
    nb = L // Q_BLOCK
    qb = jnp.moveaxis(q.reshape((B, nb, Q_BLOCK) + q.shape[2:]), 1, 0)
    o = jnp.moveaxis(lax.map(fn, qb), 0, 1)
    return o.reshape((B, L) + o.shape[3:])


def blocked_attention(q, k, v, scale):
    def one(qb):
        s = jnp.einsum("bqhd,bkhd->bhqk", qb, k).astype(F32) * scale
        p = jax.nn.softmax(s, axis=-1).astype(v.dtype)
        return jnp.einsum("bhqk,bkhd->bqhd", p, v)
    return over_query_blocks(one, q)


def diff_attn_branch(q_in, k_in, v_in, gate, lam_vec, norm_g, lam_init, pos, ctx):
    B, L, _ = q_in.shape
    dt = q_in.dtype
    q = q_in.reshape(B, L, DA_HEADS, 2, DA_QK)
    k = k_in.reshape(B, L, DA_HEADS, 2, DA_QK)
    v = v_in.reshape(B, L, DA_HEADS, DA_V)
    if ctx is None:
        q_r, k_all, v_all = q, k, v
    else:
        ck, cv = ctx
        q_r = rope_2d(q, *pos)
        ck = ck.astype(dt).reshape(ck.shape[:3] + (2, DA_QK))
        k_all = jnp.concatenate([rope_2d(k, *pos), ck], axis=1)
        v_all = jnp.concatenate([v, cv.astype(dt)], axis=1)
    lv = lam_vec.astype(F32)
    lam = jnp.exp(jnp.sum(lv[0] * lv[1])) - jnp.exp(jnp.sum(lv[2] * lv[3])) + lam_init
    sm_scale = DA_QK ** -0.5

    def block(qb):
        s = jnp.einsum("bqhcd,bkhcd->bchqk", qb, k_all).astype(F32) * sm_scale
        p = jax.nn.softmax(s, axis=-1)
        a = (p[:, 0] - lam * p[:, 1]).astype(v_all.dtype)
        return jnp.einsum("bhqk,bkhd->bqhd", a, v_all)

    o = over_query_blocks(block, q_r)
    o = rms_norm(o) * norm_g.astype(F32) * (1.0 - lam_init)
    out = o.reshape(B, L, BRANCH) * jax.nn.silu(gate.astype(F32))
    return out.astype(dt), (k.reshape(B, L, DA_HEADS, 2 * DA_QK), v)


def s5_discretize(a_re, a_im, log_dt, b_re, b_im):
    a_re, a_im = a_re.astype(F32), a_im.astype(F32)
    step = jnp.exp(log_dt.astype(F32))[:, None]
    mag = jnp.exp(a_re * step)
    ab_re, ab_im = mag * jnp.cos(a_im * step), mag * jnp.sin(a_im * step)
    den = a_re * a_re + a_im * a_im
    f_re = ((ab_re - 1.0) * a_re + ab_im * a_im) / den
    f_im = (ab_im * a_re - (ab_re - 1.0) * a_im) / den
    b_re, b_im = b_re.astype(F32), b_im.astype(F32)
    bb_re = f_re[..., None] * b_re - f_im[..., None] * b_im
    bb_im = f_re[..., None] * b_im + f_im[..., None] * b_re
    return ab_re, ab_im, bb_re, bb_im


def s5_scan(u, ab_re, ab_im, bb_re, bb_im, h0, reverse):
    bu_re = jnp.einsum("gph,blgh->blgp", bb_re, u)
    bu_im = jnp.einsum("gph,blgh->blgp", bb_im, u)
    a_re = jnp.broadcast_to(ab_re, bu_re.shape)
    a_im = jnp.broadcast_to(ab_im, bu_re.shape)

    def combine(e1, e2):
        a1r, a1i, b1r, b1i = e1
        a2r, a2i, b2r, b2i = e2
        return (a2r * a1r - a2i * a1i, a2r * a1i + a2i * a1r,
                a2r * b1r - a2i * b1i + b2r, a2r * b1i + a2i * b1r + b2i)

    ar, ai, hr, hi = lax.associative_scan(combine, (a_re, a_im, bu_re, bu_im), axis=1, reverse=reverse)
    if h0 is not None:
        h0r, h0i = h0[0][:, None], h0[1][:, None]
        hr, hi = hr + ar * h0r - ai * h0i, hi + ar * h0i + ai * h0r
    return hr, hi


def s5_branch(u, gate, P, l, h0):
    B, L, _ = u.shape
    uf = u.astype(F32)
    ug = uf.reshape(B, L, S5_GROUPS, S5_CH)
    y = uf * P["s5_d"][l].astype(F32)
    finals = []
    for d in range(2):
        ab_re, ab_im, bb_re, bb_im = s5_discretize(P["s5_a_re"][l, d], P["s5_a_im"][l, d], P["s5_log_dt"][l, d],
                                                   P["s5_b_re"][l, d], P["s5_b_im"][l, d])
        init = None if h0 is None else (h0[:, d, ..., 0].astype(F32), h0[:, d, ..., 1].astype(F32))
        hr, hi = s5_scan(ug, ab_re, ab_im, bb_re, bb_im, init, reverse=(d == 1))
        c_re, c_im = P["s5_c_re"][l, d].astype(F32), P["s5_c_im"][l, d].astype(F32)
        y = y + (jnp.einsum("ghp,blgp->blgh", c_re, hr)
                 - jnp.einsum("ghp,blgp->blgh", c_im, hi)).reshape(B, L, BRANCH)
        if h0 is None:
            t = L - 1 if d == 0 else 0
            finals.append(jnp.stack([hr[:, t], hi[:, t]], axis=-1))
    glu = jax.nn.gelu(y) @ P["s5_w_glu"][l].astype(F32)
    out = glu[..., :BRANCH] * jax.nn.sigmoid(glu[..., BRANCH:]) * jax.nn.silu(gate.astype(F32))
    state = jnp.stack(finals, axis=1) if h0 is None else None
    return out.astype(u.dtype), state


def hgrn_chunkwise(q, k, v, g, s0):
    B, L, H, _ = q.shape
    n = L // HG_CHUNK

    def chunks(t):
        return jnp.moveaxis(t.reshape(B, n, HG_CHUNK, H, t.shape[-1]), 1, 0)

    causal = jnp.tril(jnp.ones((HG_CHUNK, HG_CHUNK), dtype=bool))[None, :, :, None, None]

    def step(S, inp):
        qc, kc, vc, gc = inp
        b = jnp.cumsum(gc, axis=1)
        decay = jnp.exp(jnp.where(causal, b[:, :, None] - b[:, None, :], -jnp.inf))
        scores = jnp.einsum("bthd,btshd,bshd->bhts", qc, decay, kc)
        o = (jnp.einsum("bhts,bshv->bthv", scores, vc)
             + jnp.einsum("bthd,bhdv->bthv", qc * jnp.exp(b), S))
        b_last = b[:, -1]
        S = S * jnp.exp(b_last)[..., None] + jnp.einsum("bshd,bshv->bhdv", kc * jnp.exp(b_last[:, None] - b), vc)
        return S, o

    S, o = lax.scan(step, s0, (chunks(q), chunks(k), chunks(v), chunks(g)))
    return jnp.moveaxis(o, 0, 1).reshape(B, L, H, v.shape[-1]), S


def hgrn_branch(q, ff, fb, iv, gate, lb_l, norm_g, s0):
    B, L, _ = q.shape
    shp = (B, L, HG_HEADS, HG_DK)
    qf = q.astype(F32).reshape(shp)
    vf = iv.astype(F32).reshape(B, L, HG_HEADS, HG_DV)
    o, finals = None, []
    for d, zf in enumerate((ff, fb)):
        lb = lb_l[d].reshape(HG_HEADS, HG_DK)
        z = zf.astype(F32).reshape(shp)
        log_f = jnp.log(lb + (1.0 - lb) * jax.nn.sigmoid(z))
        k = (1.0 - lb) * jax.nn.sigmoid(-z)
        init = jnp.zeros((B, HG_HEADS, HG_DK, HG_DV), F32) if s0 is None else s0[:, d].astype(F32)
        if d == 0:
            od, Sd = hgrn_chunkwise(qf, k, vf, log_f, init)
        else:
            od, Sd = hgrn_chunkwise(jnp.flip(qf, 1), jnp.flip(k, 1), jnp.flip(vf, 1), jnp.flip(log_f, 1), init)
            od = jnp.flip(od, 1)
        o = od if o is None else o + od
        finals.append(Sd)
    o = rms_norm(o) * norm_g.astype(F32)
    out = o.reshape(B, L, BRANCH) * jax.nn.silu(gate.astype(F32))
    state = jnp.stack(finals, axis=1) if s0 is None else None
    return out.astype(q.dtype), state


def mla_branch(cq, ckv, kr, gate, P, l, pos, ctx):
    B, L, _ = cq.shape
    dt = cq.dtype
    q = (rms_norm(cq) * P["mla_q_norm"][l].astype(F32)).astype(dt) @ P["mla_w_uq"][l]
    q = q.reshape(B, L, MLA_HEADS, MLA_NOPE + MLA_ROPE)
    ckv_n = (rms_norm(ckv) * P["mla_kv_norm"][l].astype(F32)).astype(dt)
    if ctx is None:
        ckv_all, kr_all = ckv_n, kr
    else:
        q = jnp.concatenate([q[..., :MLA_NOPE], rope_2d(q[..., MLA_NOPE:], *pos)], axis=-1)
        kr_lat = rope_2d(kr[:, :, None, :], *pos)[:, :, 0]
        ckv_all = jnp.concatenate([ckv_n, ctx[0].astype(dt)], axis=1)
        kr_all = jnp.concatenate([kr_lat, ctx[1].astype(dt)], axis=1)
    Lk = ckv_all.shape[1]
    kv = (ckv_all @ P["mla_w_ukv"][l]).reshape(B, Lk, MLA_HEADS, MLA_NOPE + MLA_V)
    k = jnp.concatenate([kv[..., :MLA_NOPE],
                         jnp.broadcast_to(kr_all[:, :, None, :], (B, Lk, MLA_HEADS, MLA_ROPE))], axis=-1)
    o = blocked_attention(q, k, kv[..., MLA_NOPE:], (MLA_NOPE + MLA_ROPE) ** -0.5)
    out = o.reshape(B, L, BRANCH).astype(F32) * jax.nn.silu(gate.astype(F32))
    return out.astype(dt), (ckv_n, kr)


def trunk_layer(x, mod, P, l, pos, ctx):
    dt = x.dtype
    shift, scale, gate = jnp.split(mod.astype(F32), 3, axis=-1)
    h = (rms_norm(x) * (1.0 + scale) + shift).astype(dt)
    (da_q, da_k, da_v, da_g, s5_u, s5_g, hg_q, hg_ff, hg_fb, hg_i, hg_g,
     mla_cq, mla_ckv, mla_kr, mla_g) = split_columns(h @ P["w_in"][l])
    latent = ctx is not None
    lam_init = 0.8 - 0.6 * math.exp(-0.3 * l)
    a_out, a_ctx = diff_attn_branch(da_q, da_k, da_v, da_g, P["da_lambda"][l], P["da_norm"][l], lam_init,
                                    pos, ctx[0:2] if latent else None)
    b_out, b_ctx = s5_branch(s5_u, s5_g, P, l, ctx[2] if latent else None)
    c_out, c_ctx = hgrn_branch(hg_q, hg_ff, hg_fb, hg_i, hg_g, P["hg_lb"][l], P["hg_norm"][l],
                               ctx[3] if latent else None)
    d_out, d_ctx = mla_branch(mla_cq, mla_ckv, mla_kr, mla_g, P, l, pos, ctx[4:6] if latent else None)
    mixed = jnp.concatenate([a_out, b_out, c_out, d_out], axis=-1)
    x = (x.astype(F32) + gate * (mixed @ P["w_out"][l]).astype(F32)).astype(dt)
    if latent:
        return x, None
    return x, (a_ctx[0], a_ctx[1], b_ctx, c_ctx, d_ctx[0], d_ctx[1])


def setup_inputs(seed: int = 0) -> dict:
    key = jax.random.key(seed)
    keys = iter(jax.random.split(key, 48))

    def nrm(shape, s=1.0):
        return jax.random.normal(next(keys), shape, F32) * s

    def gain(shape):
        return 1.0 + nrm(shape, 0.02)

    L, G, N, H = DEPTH, S5_GROUPS, S5_STATE, S5_CH
    s5_n = jnp.arange(N, dtype=F32)
    return {
        "x_prompt": nrm((BATCH, SEQ, D_MODEL)),
        "x_sample": nrm((DEC_BATCH, DEC_SEQ, D_MODEL)),
        "cache_diff_k": nrm((DEC_BATCH, DEPTH, PAST_LEN, DA_HEADS, 2 * DA_QK)),
        "cache_diff_v": nrm((DEC_BATCH, DEPTH, PAST_LEN, DA_HEADS, DA_V)),
        "state_s5": nrm((DEC_BATCH, DEPTH, 2, G, N, 2), 0.3),
        "state_hgrn": nrm((DEC_BATCH, DEPTH, 2, HG_HEADS, HG_DK, HG_DV), 0.5),
        "cache_mla_ckv": nrm((DEC_BATCH, DEPTH, PAST_LEN, MLA_KV_RANK)),
        "cache_mla_krope": nrm((DEC_BATCH, DEPTH, PAST_LEN, MLA_ROPE)),
        "c": nrm((DEC_BATCH, D_MODEL)),
        "c_ctx": nrm((D_MODEL,)),
        "w_mod": nrm((L, D_MODEL, 3 * D_MODEL), 0.5 * D_MODEL ** -0.5),
        "b_mod": nrm((L, 3 * D_MODEL), 0.02),
        "w_in": nrm((L, D_MODEL, IN_WIDTH), D_MODEL ** -0.5),
        "w_out": nrm((L, MIX_WIDTH, D_MODEL), MIX_WIDTH ** -0.5),
        "da_lambda": nrm((L, 4, DA_QK), 0.1),
        "da_norm": gain((L, DA_V)),
        "s5_a_re": -0.5 + nrm((L, 2, G, N), 0.01),
        "s5_a_im": math.pi * s5_n + nrm((L, 2, G, N), 0.01),
        "s5_log_dt": jax.random.uniform(next(keys), (L, 2, G), F32, math.log(S5_DT_MIN), math.log(S5_DT_MAX)),
        "s5_b_re": nrm((L, 2, G, N, H), H ** -0.5),
        "s5_b_im": nrm((L, 2, G, N, H), H ** -0.5),
        "s5_c_re": nrm((L, 2, G, H, N), N ** -0.5),
        "s5_c_im": nrm((L, 2, G, H, N), N ** -0.5),
        "s5_d": nrm((L, BRANCH)),
        "s5_w_glu": nrm((L, BRANCH, 2 * BRANCH), BRANCH ** -0.5),
        "hg_lb": nrm((L, 2, HG_HEADS * HG_DK), 0.1),
        "hg_norm": gain((L, HG_DV)),
        "mla_q_norm": gain((L, MLA_Q_RANK)),
        "mla_w_uq": nrm((L, MLA_Q_RANK, MLA_HEADS * (MLA_NOPE + MLA_ROPE)), MLA_Q_RANK ** -0.5),
        "mla_kv_norm": gain((L, MLA_KV_RANK)),
        "mla_w_ukv": nrm((L, MLA_KV_RANK, MLA_HEADS * (MLA_NOPE + MLA_V)), MLA_KV_RANK ** -0.5),
        "final_norm": gain((D_MODEL,)),
    }


def reference(x_prompt, x_sample, cache_diff_k, cache_diff_v, state_s5, state_hgrn, cache_mla_ckv,
              cache_mla_krope, c, c_ctx, w_mod, b_mod, w_in, w_out, da_lambda, da_norm, s5_a_re, s5_a_im,
              s5_log_dt, s5_b_re, s5_b_im, s5_c_re, s5_c_im, s5_d, s5_w_glu, hg_lb, hg_norm, mla_q_norm,
              mla_w_uq, mla_kv_norm, mla_w_ukv, final_norm):
    lb_w = jax.nn.softmax(hg_lb.astype(F32), axis=0)
    lb_all = jnp.cumsum(lb_w, axis=0) - lb_w[0:1]
    P = {
        "w_in": w_in, "w_out": w_out, "da_lambda": da_lambda, "da_norm": da_norm,
        "s5_a_re": s5_a_re, "s5_a_im": s5_a_im, "s5_log_dt": s5_log_dt, "s5_b_re": s5_b_re,
        "s5_b_im": s5_b_im, "s5_c_re": s5_c_re, "s5_c_im": s5_c_im, "s5_d": s5_d, "s5_w_glu": s5_w_glu,
        "hg_lb": lb_all, "hg_norm": hg_norm, "mla_q_norm": mla_q_norm, "mla_w_uq": mla_w_uq,
        "mla_kv_norm": mla_kv_norm, "mla_w_ukv": mla_w_ukv,
    }

    x = x_prompt
    ctx_layers = []
    for l in range(DEPTH):
        mod = (jax.nn.silu(c_ctx.astype(F32)) @ w_mod[l].astype(F32) + b_mod[l].astype(F32))[None, None]
        x, st = trunk_layer(x, mod, P, l, None, None)
        ctx_layers.append(st)
    y_prompt = (rms_norm(x) * final_norm.astype(F32)).astype(x_prompt.dtype)
    new_diff_k = jnp.stack([s[0] for s in ctx_layers], axis=1)
    new_diff_v = jnp.stack([s[1] for s in ctx_layers], axis=1)
    new_s5 = jnp.stack([s[2] for s in ctx_layers], axis=1)
    new_hgrn = jnp.stack([s[3] for s in ctx_layers], axis=1)
    new_mla_ckv = jnp.stack([s[4] for s in ctx_layers], axis=1)
    new_mla_krope = jnp.stack([s[5] for s in ctx_layers], axis=1)

    n_rows = x_sample.shape[1] // GRID_W
    row = jnp.repeat(jnp.arange(n_rows, dtype=jnp.int32), GRID_W)
    col = jnp.tile(jnp.arange(GRID_W, dtype=jnp.int32), n_rows)
    x = x_sample
    for l in range(DEPTH):
        mod = (jax.nn.silu(c.astype(F32)) @ w_mod[l].astype(F32) + b_mod[l].astype(F32))[:, None]
        ctx = (cache_diff_k[:, l], cache_diff_v[:, l], state_s5[:, l], state_hgrn[:, l],
               cache_mla_ckv[:, l], cache_mla_krope[:, l])
        x, _ = trunk_layer(x, mod, P, l, (row, col), ctx)
    y_sample = (rms_norm(x) * final_norm.astype(F32)).astype(x_sample.dtype)

    return (y_prompt, y_sample, new_diff_k, new_diff_v, new_s5, new_hgrn, new_mla_ckv, new_mla_krope)
```

```python
import numpy as np
import concourse.bass as bass
import concourse.mybir as mybir
from concourse.bass_utils import run_bass_kernel_spmd

F32 = mybir.dt.float32
BF16 = mybir.dt.bfloat16
I32 = mybir.dt.int32
AF = mybir.ActivationFunctionType
ALU = mybir.AluOpType
AX = mybir.AxisListType

SAME_ENGINE_SYNC = True
CSTOP = 99
N_DMA_SEMS = 24


class Buf:
    def __init__(self, name, ap, parent=None):
        self.name = name
        self.ap = ap
        self.parent = parent
        self.children = []
        self.lastw = None
        self.readers = []
        if parent is not None:
            parent.children.append(self)

    def view(self, ap, name=None):
        return Buf(name or self.name + ".v", ap, parent=self)

    def __getitem__(self, idx):
        return Ref(self, self.ap[idx])

    @property
    def buf(self):
        return self

    def re(self, pat_, **kw):
        return Ref(self, self.ap.rearrange(pat_, **kw))

    def bc(self, shape):
        return Ref(self, self.ap.broadcast_to(list(shape)))

    def _up(self):
        b = self.parent
        while b is not None:
            yield b
            b = b.parent

    def _down(self):
        for c in self.children:
            yield c
            yield from c._down()


class Ref:
    __slots__ = ("buf", "ap")

    def __init__(self, buf, ap):
        self.buf = buf
        self.ap = ap

    def __getitem__(self, idx):
        return Ref(self.buf, self.ap[idx])

    def re(self, pat_, **kw):
        return Ref(self.buf, self.ap.rearrange(pat_, **kw))

    def bc(self, shape):
        return Ref(self.buf, self.ap.broadcast_to(list(shape)))


class Op:
    __slots__ = ("eng", "fn", "deps", "idx", "is_dma", "ticket", "waits", "signal", "out", "clk", "inc")

    def __init__(self, eng, fn, is_dma=False, out=False):
        self.inc = 16
        self.eng = eng
        self.fn = fn
        self.deps = set()
        self.is_dma = is_dma
        self.ticket = None
        self.waits = []
        self.signal = False
        self.out = out
        self.clk = None


class Prog:
    def __init__(self, nc):
        self.nc = nc
        self.ops = []
        self._ctx = []
        self.nsb = 0
        self.scopes = []
        self.scope_pending = []
        self.allbufs = []

    def sbuf(self, name, shape, dtype):
        self.nsb += 1
        cm = self.nc.sbuf_tensor("%s_%d" % (name, self.nsb), list(shape), dtype)
        t = cm.__enter__()
        self._ctx.append(cm)
        b = Buf(name, t.ap() if hasattr(t, "ap") and callable(getattr(t, "ap")) else t[:])
        b.init_deps = list(self.scope_pending)
        self.allbufs.append(b)
        return b

    def push_scope(self):
        self.scopes.append((len(self._ctx), len(self.allbufs)))

    def pop_scope(self):
        nctx, nb = self.scopes.pop()
        dead = self.allbufs[nb:]
        del self.allbufs[nb:]
        pend = set(self.scope_pending)
        for b in dead:
            for x in (b, *b._down()):
                if x.lastw is not None:
                    pend.add(x.lastw)
                pend.update(x.readers)
        last = {}
        keep = []
        for o in pend:
            if o.is_dma:
                keep.append(o)
            elif o.eng not in last or last[o.eng].idx < o.idx:
                last[o.eng] = o
        self.scope_pending = keep + list(last.values())
        while len(self._ctx) > nctx:
            self._ctx.pop().__exit__(None, None, None)

    def psum(self, name, shape, dtype):
        cm = self.nc.psum_tensor(name, list(shape), dtype)
        t = cm.__enter__()
        self._ctx.append(cm)
        return Buf(name, t.ap() if hasattr(t, "ap") and callable(getattr(t, "ap")) else t[:])

    def _track(self, op, reads, writes):
        for b in (*reads, *writes):
            r = b
            while r.parent is not None:
                r = r.parent
            idp = getattr(r, "init_deps", None)
            if idp:
                op.deps.update(idp)
        for b in reads:
            for x in (b, *b._up(), *b._down()):
                if x.lastw is not None:
                    op.deps.add(x.lastw)
        for b in writes:
            for x in (b, *b._up(), *b._down()):
                if x.lastw is not None:
                    op.deps.add(x.lastw)
                for r in x.readers:
                    op.deps.add(r)
        for b in reads:
            b.readers.append(op)
        for b in writes:
            b.lastw = op
            b.readers = []
            for x in b._down():
                x.lastw = None
                x.readers = []
        op.deps.discard(op)

    def op(self, eng, fn, reads=(), writes=()):
        o = Op(eng, fn)
        o.idx = len(self.ops)
        self._track(o, reads, writes)
        self.ops.append(o)
        return o

    def dma(self, out_ap, in_ap, reads=(), writes=(), eng="sync", out=False, **kw):
        o = Op(eng, lambda e: e.dma_start(out=out_ap, in_=in_ap, **kw), is_dma=True, out=out)
        o.idx = len(self.ops)
        self._track(o, reads, writes)
        self.ops.append(o)
        return o

    def collective(self, fn, reads=(), writes=()):
        o = Op("gpsimd", fn, is_dma=True)
        o.inc = 1
        o.idx = len(self.ops)
        self._track(o, reads, writes)
        self.ops.append(o)
        return o

    def emit(self):
        nc = self.nc
        engines = ["sync", "tensor", "vector", "scalar", "gpsimd"]
        for o in self.ops:
            for d in o.deps:
                if d.eng == o.eng and not d.is_dma and not o.is_dma:
                    if o.eng == "tensor" or not SAME_ENGINE_SYNC:
                        continue
                d.signal = True
        for o in self.ops:
            if o.is_dma:
                o.signal = True
        sems = {}
        ctxs = []

        def mksem(name):
            cm = nc.semaphore(name)
            s = cm.__enter__()
            ctxs.append(cm)
            return s

        for e in engines:
            sems[e] = mksem("s_" + e)
        dma_sems = {e: [mksem("d_%s_%d" % (e, i)) for i in range(N_DMA_SEMS)] for e in ("sync", "scalar", "gpsimd")}
        dma_cnt = {e: [0] * N_DMA_SEMS for e in dma_sems}
        dma_last = {e: [None] * N_DMA_SEMS for e in dma_sems}
        dma_rr = {e: 0 for e in dma_sems}
        cnt = {e: 0 for e in engines}
        cc_sems = []
        clock = {e: {} for e in engines}
        final_waits = []
        for o in self.ops:
            E = o.eng
            deps = set(o.deps)
            if o.is_dma and o.inc == 1:
                pass
            elif o.is_dma:
                k = dma_rr[E]
                dma_rr[E] = (k + 1) % N_DMA_SEMS
                if dma_last[E][k] is not None:
                    deps.add(dma_last[E][k])
                dma_last[E][k] = o
            ck = clock[E]
            for d in sorted(deps, key=lambda z: z.idx):
                if d.ticket is None:
                    continue
                if d.eng == E and not d.is_dma and not o.is_dma and (E == "tensor" or not SAME_ENGINE_SYNC):
                    continue
                skey, val = d.ticket
                if ck.get(skey, 0) >= val:
                    continue
                o.waits.append((skey, val))
                ck[skey] = val
                for k2, v2 in d.clk.items():
                    if ck.get(k2, 0) < v2:
                        ck[k2] = v2
            if o.is_dma and o.inc == 1:
                cc_sems.append(mksem("cc%d" % len(cc_sems)))
                o.ticket = (("c", len(cc_sems) - 1), 1)
            elif o.is_dma:
                dma_cnt[E][k] += o.inc
                o.ticket = (("d", E, k), dma_cnt[E][k])
                if o.out:
                    final_waits.append(o.ticket)
            elif o.signal:
                cnt[E] += 1
                o.ticket = (("e", E), cnt[E])
            o.clk = dict(ck)
            if o.ticket is not None and not o.is_dma:
                o.clk[o.ticket[0]] = o.ticket[1]

        def semof(skey):
            if skey[0] == "e":
                return sems[skey[1]]
            if skey[0] == "c":
                return cc_sems[skey[1]]
            return dma_sems[skey[1]][skey[2]]

        ops = self.ops
        with nc.Block() as block:
            def run(engname):
                def body(eng):
                    for o in ops:
                        if o.eng != engname:
                            continue
                        for skey, val in o.waits:
                            eng.wait_ge(semof(skey), val)
                        ins = o.fn(eng)
                        if o.ticket is not None:
                            if o.is_dma:
                                ins.then_inc(semof(o.ticket[0]), o.inc)
                            else:
                                ins.then_inc(semof(o.ticket[0]), 1)
                    if engname == "sync":
                        for skey, val in final_waits:
                            eng.wait_ge(semof(skey), val)
                return body
            block.sync(run("sync"))
            block.tensor(run("tensor"))
            block.vector(run("vector"))
            block.scalar(run("scalar"))
            block.gpsimd(run("gpsimd"))
        for cm in reversed(ctxs):
            cm.__exit__(None, None, None)
        for cm in reversed(self._ctx):
            cm.__exit__(None, None, None)


D = 1024
NL = 2
LC = 512
LL = 2048
PAST = 256
EPS = 1e-6
TB = 512
S5T = 256
HGC = 32
GRID_W = 64

OFF = dict(da_q=0, da_k=256, da_v=512, da_g=768, s5_u=1024, s5_g=1280, hg_q=1536, hg_ff=1792, hg_fb=2048,
           hg_i=2304, hg_g=2560, mla_cq=2816, mla_ckv=3008, mla_kr=3136, mla_g=3168)


def _rope_perm32():
    p = np.zeros(32, np.int64)
    for i in range(32):
        p[i] = i + 8 if (i % 16) < 8 else i - 8
    return p


def _groups():
    cols = []
    table = {}

    def add(name, src):
        src = list(src)
        assert len(src) <= 128
        src = src + [-1] * (128 - len(src))
        table[name] = len(cols)
        cols.extend(src)

    perm = _rope_perm32()
    for nm, off in (("Aq", OFF["da_q"]), ("Ak", OFF["da_k"])):
        for h in range(4):
            c1 = [off + h * 64 + d for d in range(32)]
            c2 = [off + h * 64 + 32 + d for d in range(32)]
            add("%s%d" % (nm, h), c1 + [-1] * 32 + c2 + [-1] * 32)
            p1 = [off + h * 64 + perm[d] for d in range(32)]
            p2 = [off + h * 64 + 32 + perm[d] for d in range(32)]
            add("%sp%d" % (nm, h), p1 + [-1] * 32 + p2 + [-1] * 32)
    for i in range(2):
        add("Av%d" % i, range(OFF["da_v"] + 128 * i, OFF["da_v"] + 128 * i + 128))
        add("Akt%d" % i, range(OFF["da_k"] + 128 * i, OFF["da_k"] + 128 * i + 128))
        add("Ag%d" % i, range(OFF["da_g"] + 128 * i, OFF["da_g"] + 128 * i + 128))
        add("Bu%d" % i, range(OFF["s5_u"] + 128 * i, OFF["s5_u"] + 128 * i + 128))
        add("Bg%d" % i, range(OFF["s5_g"] + 128 * i, OFF["s5_g"] + 128 * i + 128))
        add("Cq%d" % i, range(OFF["hg_q"] + 128 * i, OFF["hg_q"] + 128 * i + 128))
        add("Cf0%d" % i, range(OFF["hg_ff"] + 128 * i, OFF["hg_ff"] + 128 * i + 128))
        add("Cf1%d" % i, range(OFF["hg_fb"] + 128 * i, OFF["hg_fb"] + 128 * i + 128))
        add("Cv%d" % i, range(OFF["hg_i"] + 128 * i, OFF["hg_i"] + 128 * i + 128))
        add("Cg%d" % i, range(OFF["hg_g"] + 128 * i, OFF["hg_g"] + 128 * i + 128))
        add("Dg%d" % i, range(OFF["mla_g"] + 128 * i, OFF["mla_g"] + 128 * i + 128))
    add("Dcq0", range(OFF["mla_cq"], OFF["mla_cq"] + 128))
    add("Dcq1", range(OFF["mla_cq"] + 128, OFF["mla_cq"] + 192))
    add("Dckv", range(OFF["mla_ckv"], OFF["mla_ckv"] + 128))
    kr = [OFF["mla_kr"] + d for d in range(32)]
    add("Dkr", [-1] * 64 + kr)
    add("Dkrp", [-1] * 64 + [OFF["mla_kr"] + perm[d] for d in range(32)])
    add("Dkrt", kr)
    return table, np.array(cols, np.int64)


GT, GCOLS = _groups()
NCOLS = len(GCOLS)


def _groups_lat(r):
    cols = []
    table = {}

    def add(name, src):
        src = list(src)
        src = src + [-1] * (128 - len(src))
        table[name] = len(cols)
        cols.extend(src)

    perm = _rope_perm32()
    for nm, off in (("Aq", OFF["da_q"]), ("Ak", OFF["da_k"])):
        c1 = [off + r * 64 + d for d in range(32)]
        c2 = [off + r * 64 + 32 + d for d in range(32)]
        add(nm, c1 + [-1] * 32 + c2 + [-1] * 32)
        p1 = [off + r * 64 + perm[d] for d in range(32)]
        p2 = [off + r * 64 + 32 + perm[d] for d in range(32)]
        add(nm + "p", p1 + [-1] * 32 + p2 + [-1] * 32)
    for nm, key in (("Av", "da_v"), ("Ag", "da_g"), ("Bu", "s5_u"), ("Cq", "hg_q"), ("Cf0", "hg_ff"), ("Cf1", "hg_fb"),
                    ("Cv", "hg_i"), ("Cg", "hg_g"), ("Dg", "mla_g")):
        add(nm, range(OFF[key] + 64 * r, OFF[key] + 64 * r + 64))
    for i in range(2):
        add("Bg%d" % i, range(OFF["s5_g"] + 128 * i, OFF["s5_g"] + 128 * i + 128))
    add("Dcq0", range(OFF["mla_cq"], OFF["mla_cq"] + 128))
    add("Dcq1", range(OFF["mla_cq"] + 128, OFF["mla_cq"] + 192))
    add("Dckv", range(OFF["mla_ckv"], OFF["mla_ckv"] + 128))
    kr = [OFF["mla_kr"] + d for d in range(32)]
    add("Dkr", [-1] * 64 + kr)
    add("Dkrp", [-1] * 64 + [OFF["mla_kr"] + perm[d] for d in range(32)])
    return table, np.array(cols, np.int64)


LGT = _groups_lat(0)[0]
NCOLS_L = len(_groups_lat(0)[1])


def _rope_tables():
    t = np.arange(LL)
    row = (t // GRID_W).astype(np.float32)
    col = (t % GRID_W).astype(np.float32)
    inv = (10000.0 ** (-np.arange(8, dtype=np.float32) / 8)).astype(np.float32)
    C = np.ones((128, LL), np.float32)
    S = np.zeros((128, LL), np.float32)
    for base in (0, 64):
        for i in range(32):
            pos = row if i < 16 else col
            ang = pos * inv[i % 8]
            C[base + i] = np.cos(ang)
            S[base + i] = (-1.0 if (i % 16) < 8 else 1.0) * np.sin(ang)
    return C, S


def _consts():
    c = {}
    c["ident"] = np.eye(128, dtype=np.float32)
    s = np.arange(128)[:, None]
    t = np.arange(128)[None, :]
    same = (s // HGC) == (t // HGC)
    c["hmask0"] = (same & (s <= t)).astype(np.float32)
    c["hmask1"] = (same & (s >= t)).astype(np.float32)
    m4 = np.zeros((128, 4), np.float32)
    m4[np.arange(128), np.arange(128) // HGC] = 1.0
    c["m4"] = m4
    m01 = np.ones((128, TB), np.float32)
    m01[:, ::HGC] = 0.0
    c["m01"] = m01
    c["iota"] = np.broadcast_to(np.arange(1, S5T + 1, dtype=np.float32)[None, :], (128, S5T)).copy()
    bd = np.zeros((128, 128), np.float32)
    bd[:64, :64] = 1.0
    bd[64:, 64:] = 1.0
    c["bdones"] = bd
    C, S = _rope_tables()
    c["ropeC"] = C
    c["ropeS"] = S
    sel = np.zeros((2, 2, 128), np.float32)
    sel[0, 0] = 1.0
    sel[1, 1] = 1.0
    c["selc"] = sel
    return c


CONST_SHAPES = dict(ident=[128, 128], hmask0=[128, 128], hmask1=[128, 128], m4=[128, 4], m01=[128, TB],
                    iota=[128, S5T], bdones=[128, 128], ropeC=[128, LL], ropeS=[128, LL], selc=[2, 2, 128])


IN_SHAPES = dict(
    xc=[4, 128, D], xl=[4, 128, D], cvec=[128, 8, 2],
    wmod=[NL, 24, 128, 8, 128], bmodT=[NL, 128, 24], bmodg=[NL, 2, D],
    win=[NL, NCOLS // 128, 128, 8, 128], wout=[NL, 128, 8, D], wglu=[NL, 128, 2, 512],
    wuq=[NL, 128, 2, 768], mqn=[NL, 128, 2], wukv=[NL, 128, 512], mkvn=[NL, 128, 128], fnorm=[128, D],
    s5B=[NL, 2, 8, 2, 128, 128], s5C=[NL, 2, 8, 2, 128, 128], s5p=[NL, 128, 16, 3], s5d=[NL, 128, 2],
    hglb=[NL, 128, 4], hgn=[NL, 128, 1], dalam=[NL, 1, 128], dan=[NL, 128, 1],
    cdk=[NL, 2, 128, 256], cdv=[NL, 2, 128, 256], cs5=[NL, 128, 16, 2], chg=[NL, 2, 4, 64, 64],
    cckv=[NL, 2, 128, 128], ckr=[NL, 2, 128, 32],
)
IN_SHAPES.update(dict(
    winL=[NL, NCOLS_L // 128, 128, 8, 128], woutL=[NL, 128, 10, D], wgluL=[NL, 64, 4, 512], wuqL=[NL, 128, 2, 192], wukvL=[NL, 128, 128],
    s5BL=[NL, 2, 2, 2, 64, 128], s5CL=[NL, 2, 2, 2, 128, 64], s5pL=[NL, 128, 4, 3], s5dL=[NL, 64, 1], hglbL=[NL, 64, 2],
    cdkL=[NL, 2, 128, 64], cdvL=[NL, 2, 128, 64], cs5L=[NL, 128, 4, 2], chgL=[NL, 2, 64, 64], oh=[128, 4],
))
IN_SHAPES.update(CONST_SHAPES)
OUT_SHAPES = dict(
    y_c=[4, 128, D], y_l=[4, 128, D], o_dk=[2, NL, 256, 256], o_dv=[2, NL, 256, 256],
    o_s5=[2, NL, 2, 16, 64, 2], o_hg=[2, NL, 2, 4, 64, 64], o_ckv=[2, NL, 256, 128], o_kr=[2, NL, 256, 32],
)


def _shared_inputs(inp):
    f = lambda a: np.ascontiguousarray(np.asarray(a, dtype=np.float32))
    sh = {}
    w_in = f(inp["w_in"])
    wpad = np.concatenate([w_in, np.zeros((NL, D, 1), np.float32)], axis=2)
    win = wpad[:, :, GCOLS]
    sh["win"] = f(win.reshape(NL, 8, 128, NCOLS // 128, 128).transpose(0, 3, 2, 1, 4))
    sh["wmod"] = f(f(inp["w_mod"]).reshape(NL, 8, 128, 24, 128).transpose(0, 3, 2, 1, 4))
    bm = f(inp["b_mod"])
    sh["bmodT"] = f(bm.reshape(NL, 24, 128).transpose(0, 2, 1))
    sh["bmodg"] = f(np.broadcast_to(bm[:, None, 2 * D:3 * D], (NL, 2, D)))
    sh["wout"] = f(f(inp["w_out"]).reshape(NL, 8, 128, D).transpose(0, 2, 1, 3))
    sh["wglu"] = f(f(inp["s5_w_glu"]).reshape(NL, 2, 128, 512).transpose(0, 2, 1, 3))
    perm = _rope_perm32()
    wuq = f(inp["mla_w_uq"])
    cols_n = np.arange(384)
    cols_p = np.array([h * 96 + (j if j < 64 else 64 + perm[j - 64]) for h in range(4) for j in range(96)])
    wq = np.concatenate([wuq[:, :, cols_n], wuq[:, :, cols_p]], axis=2)
    wq = np.concatenate([wq, np.zeros((NL, 64, 768), np.float32)], axis=1)
    sh["wuq"] = f(wq.reshape(NL, 2, 128, 768).transpose(0, 2, 1, 3))
    qn = np.concatenate([f(inp["mla_q_norm"]), np.zeros((NL, 64), np.float32)], axis=1)
    sh["mqn"] = f(qn.reshape(NL, 2, 128).transpose(0, 2, 1))
    sh["wukv"] = f(inp["mla_w_ukv"])
    sh["mkvn"] = f(np.broadcast_to(f(inp["mla_kv_norm"])[:, None, :], (NL, 128, 128)))
    sh["fnorm"] = f(np.broadcast_to(f(inp["final_norm"])[None, :], (128, D)))
    bre, bim = f(inp["s5_b_re"]), f(inp["s5_b_im"])
    cre, cim = f(inp["s5_c_re"]), f(inp["s5_c_im"])
    sB = np.zeros((NL, 2, 8, 2, 128, 128), np.float32)
    sC = np.zeros((NL, 2, 8, 2, 128, 128), np.float32)
    for st in range(8):
        for gi in range(2):
            g = 2 * st + gi
            r0 = 16 * (g % 8)
            for ri, (bb, cc) in enumerate(((bre, cre), (bim, cim))):
                sB[:, :, st, ri, r0:r0 + 16, 64 * gi:64 * gi + 64] = bb[:, :, g].transpose(0, 1, 3, 2)
                sC[:, :, st, ri, 64 * gi:64 * gi + 64, r0:r0 + 16] = cc[:, :, g].transpose(0, 1, 3, 2)
    sh["s5B"], sh["s5C"] = sB, sC
    are, aim, ldt = f(inp["s5_a_re"]), f(inp["s5_a_im"]), f(inp["s5_log_dt"])
    sp = np.zeros((NL, 128, 16, 3), np.float32)
    for d in range(2):
        for st in range(8):
            for gi in range(2):
                g = 2 * st + gi
                sp[:, 64 * gi:64 * gi + 64, d * 8 + st, 0] = are[:, d, g]
                sp[:, 64 * gi:64 * gi + 64, d * 8 + st, 1] = aim[:, d, g]
                sp[:, 64 * gi:64 * gi + 64, d * 8 + st, 2] = ldt[:, d, g][:, None]
    sh["s5p"] = sp
    sh["s5d"] = f(f(inp["s5_d"]).reshape(NL, 2, 128).transpose(0, 2, 1))
    lb = f(inp["hg_lb"])
    sh["hglb"] = f(lb.reshape(NL, 2, 2, 128).transpose(0, 3, 1, 2).reshape(NL, 128, 4))
    sh["hgn"] = f(np.tile(f(inp["hg_norm"]), (1, 2))[:, :, None])
    sh["dalam"] = f(f(inp["da_lambda"]).reshape(NL, 1, 128))
    sh["dan"] = f(np.tile(f(inp["da_norm"]), (1, 2))[:, :, None])
    sh.update(_consts())
    return sh


def _core_inputs(inp, sh, core):
    f = lambda a: np.ascontiguousarray(np.asarray(a, dtype=np.float32))
    m = dict(sh)
    s = core // 4
    m["xc"] = f(np.asarray(inp["x_prompt"])[2 * core:2 * core + 2].reshape(4, 128, D))
    m["xl"] = f(np.asarray(inp["x_sample"])[s].reshape(4, 4, 128, D)[core % 4])
    cv = np.stack([f(inp["c_ctx"]), f(inp["c"])[s]], axis=-1)
    m["cvec"] = f(cv.reshape(8, 128, 2).transpose(1, 0, 2))
    m["cdk"] = f(np.asarray(inp["cache_diff_k"])[s].reshape(NL, 2, 128, 256))
    m["cdv"] = f(np.asarray(inp["cache_diff_v"])[s].reshape(NL, 2, 128, 256))
    st5 = f(np.asarray(inp["state_s5"])[s])
    m["cs5"] = f(st5.reshape(NL, 2, 8, 2, 64, 2).transpose(0, 3, 4, 1, 2, 5).reshape(NL, 128, 16, 2))
    m["chg"] = f(np.asarray(inp["state_hgrn"])[s])
    m["cckv"] = f(np.asarray(inp["cache_mla_ckv"])[s].reshape(NL, 2, 128, 128))
    m["ckr"] = f(np.asarray(inp["cache_mla_krope"])[s].reshape(NL, 2, 128, 32))
    r = core % 4
    w_in = f(inp["w_in"])
    wpad = np.concatenate([w_in, np.zeros((NL, D, 1), np.float32)], axis=2)
    lcols = _groups_lat(r)[1]
    m["winL"] = f(wpad[:, :, lcols].reshape(NL, 8, 128, NCOLS_L // 128, 128).transpose(0, 3, 2, 1, 4))
    wo = f(inp["w_out"])
    chunks = []
    z64 = np.zeros((NL, 64, D), np.float32)
    for rr in range(4):
        chunks.append(np.concatenate([wo[:, 64 * rr:64 * rr + 64], wo[:, 512 + 64 * rr:512 + 64 * rr + 64]], axis=1))
        chunks.append(np.concatenate([wo[:, 768 + 64 * rr:768 + 64 * rr + 64], z64], axis=1))
    chunks.append(wo[:, 256:384])
    chunks.append(wo[:, 384:512])
    m["woutL"] = f(np.stack(chunks, axis=2))
    m["wgluL"] = f(f(inp["s5_w_glu"]).reshape(NL, 4, 64, 512).transpose(0, 2, 1, 3))
    perm = _rope_perm32()
    wuq = f(inp["mla_w_uq"])
    cn = np.array([r * 96 + j for j in range(96)])
    cp_ = np.array([r * 96 + (j if j < 64 else 64 + perm[j - 64]) for j in range(96)])
    wq = np.concatenate([wuq[:, :, cn], wuq[:, :, cp_]], axis=2)
    wq = np.concatenate([wq, np.zeros((NL, 64, 192), np.float32)], axis=1)
    m["wuqL"] = f(wq.reshape(NL, 2, 128, 192).transpose(0, 2, 1, 3))
    m["wukvL"] = f(f(inp["mla_w_ukv"])[:, :, r * 128:(r + 1) * 128])
    bre, bim = f(inp["s5_b_re"]), f(inp["s5_b_im"])
    cre, cim = f(inp["s5_c_re"]), f(inp["s5_c_im"])
    sB = np.zeros((NL, 2, 2, 2, 64, 128), np.float32)
    sC = np.zeros((NL, 2, 2, 2, 128, 64), np.float32)
    are, aim, ldt = f(inp["s5_a_re"]), f(inp["s5_a_im"]), f(inp["s5_log_dt"])
    sp = np.zeros((NL, 128, 4, 3), np.float32)
    st5 = f(np.asarray(inp["state_s5"])[s])
    c5 = np.zeros((NL, 128, 4, 2), np.float32)
    for st in range(2):
        for gi in range(2):
            g = 4 * r + 2 * st + gi
            r0 = 16 * (g % 4)
            for ri, (bb, cc) in enumerate(((bre, cre), (bim, cim))):
                sB[:, :, st, ri, r0:r0 + 16, 64 * gi:64 * gi + 64] = bb[:, :, g].transpose(0, 1, 3, 2)
                sC[:, :, st, ri, 64 * gi:64 * gi + 64, r0:r0 + 16] = cc[:, :, g].transpose(0, 1, 3, 2)
            for d in range(2):
                sp[:, 64 * gi:64 * gi + 64, d * 2 + st, 0] = are[:, d, g]
                sp[:, 64 * gi:64 * gi + 64, d * 2 + st, 1] = aim[:, d, g]
                sp[:, 64 * gi:64 * gi + 64, d * 2 + st, 2] = ldt[:, d, g][:, None]
                c5[:, 64 * gi:64 * gi + 64, d * 2 + st, :] = st5[:, d, g]
    m["s5BL"], m["s5CL"], m["s5pL"], m["cs5L"] = sB, sC, sp, c5
    m["s5dL"] = f(f(inp["s5_d"])[:, 64 * r:64 * r + 64, None])
    m["hglbL"] = f(f(inp["hg_lb"])[:, :, 64 * r:64 * r + 64].transpose(0, 2, 1))
    m["cdkL"] = f(np.asarray(inp["cache_diff_k"])[s][:, :, r].reshape(NL, 2, 128, 64))
    m["cdvL"] = f(np.asarray(inp["cache_diff_v"])[s][:, :, r].reshape(NL, 2, 128, 64))
    m["chgL"] = f(np.asarray(inp["state_hgrn"])[s][:, :, r])
    oh = np.zeros((128, 4), np.float32)
    oh[:, r] = 1.0
    m["oh"] = oh
    for k, shp in IN_SHAPES.items():
        assert list(m[k].shape) == list(shp), (k, m[k].shape, shp)
    return {k: m[k] for k in IN_SHAPES}


class DR:
    buf = None

    def __init__(self, ap):
        self.ap = ap

    def __getitem__(self, idx):
        return DR(self.ap[idx])

    def re(self, pat_, **kw):
        return DR(self.ap.rearrange(pat_, **kw))


class Ring:
    def __init__(self, bufs):
        self.bufs = bufs
        self.i = 0

    def next(self):
        b = self.bufs[self.i % len(self.bufs)]
        self.i += 1
        return b


def _b(xs):
    return [x.buf for x in xs if x is not None and not isinstance(x, (int, float)) and x.buf is not None]


class KB:
    def __init__(self, P):
        self.P = P

    def tt(self, eng, o, a, b, op):
        self.P.op(eng, lambda e: e.tensor_tensor(o.ap, a.ap, b.ap, op=op), _b([a, b]), _b([o]))

    def ts(self, eng, o, a, s1, op0, s2=None, op1=None):
        v1 = s1.ap if hasattr(s1, "ap") else s1
        v2 = s2.ap if hasattr(s2, "ap") else s2
        if op1 is None:
            fn = lambda e: e.tensor_scalar(o.ap, a.ap, v1, None, op0=op0)
        else:
            fn = lambda e: e.tensor_scalar(o.ap, a.ap, v1, v2, op0=op0, op1=op1)
        self.P.op(eng, fn, _b([a, s1, s2]), _b([o]))

    def stt(self, eng, o, a, s, b, op0, op1):
        v = s.ap if hasattr(s, "ap") else s
        self.P.op(eng, lambda e: e.scalar_tensor_tensor(o.ap, a.ap, v, b.ap, op0=op0, op1=op1),
                  _b([a, s, b]), _b([o]))

    def act(self, o, a, func, bias=None, scale=None, accum=None):
        kw = {}
        if bias is not None:
            kw["bias"] = bias.ap if hasattr(bias, "ap") else bias
        if scale is not None:
            kw["scale"] = scale.ap if hasattr(scale, "ap") else scale
        if accum is not None:
            kw["accum_out"] = accum.ap
        self.P.op("scalar", lambda e: e.activation(o.ap, a.ap, func, **kw), _b([a, bias, scale]), _b([o, accum]))

    def cp(self, eng, o, a):
        if eng == "scalar":
            self.P.op(eng, lambda e: e.activation(o.ap, a.ap, AF.Copy), _b([a]), _b([o]))
        else:
            self.P.op(eng, lambda e: e.tensor_copy(o.ap, a.ap), _b([a]), _b([o]))

    def recip(self, o, a):
        self.P.op("vector", lambda e: e.reciprocal(o.ap, a.ap), _b([a]), _b([o]))

    def memset(self, eng, o, val):
        self.P.op(eng, lambda e: e.memset(o.ap, val), [], _b([o]))

    def mm(self, o, lhsT, rhs, start=True, stop=True):
        self.P.op("tensor", lambda e: e.matmul(o.ap, lhsT.ap, rhs.ap, start=start, stop=stop),
                  _b([lhsT, rhs]), _b([o]))

    def scan(self, o, d0, d1, init, op0=ALU.mult, op1=ALU.add):
        iv = init.ap if hasattr(init, "ap") else init
        self.P.op("vector", lambda e: e.tensor_tensor_scan(o.ap, d0.ap, d1.ap, iv, op0=op0, op1=op1),
                  _b([d0, d1, init]), _b([o]))

    def dma(self, o, a, out=False, eng="sync"):
        self.P.dma(o.ap, a.ap, reads=_b([a]), writes=_b([o]), out=out, eng=eng)


def lam_init(l):
    import math
    return 0.8 - 0.6 * math.exp(-0.3 * l)


def build(jobs=("ctx", "lat"), nlayers=NL, branches="ABCD", dbg=(), nwf=12):
    nc = bass.Bass("TRN2", target_bir_lowering=False)
    P = Prog(nc)
    K = KB(P)
    din = {k: DR(nc.dram_tensor(k, shp, F32, kind="ExternalInput").ap()) for k, shp in IN_SHAPES.items()}
    dout = {k: DR(nc.dram_tensor(k, shp, F32, kind="ExternalOutput").ap()) for k, shp in OUT_SHAPES.items()}
    dbg_shapes = {}

    def dump(name, ref, shape):
        if name in dbg:
            shape = list(shape)
            dbg_shapes[name] = shape
            dd = DR(nc.dram_tensor("dbg_" + name, shape, F32, kind="ExternalOutput").ap())
            t = P.sbuf("dbgt_" + name, shape, F32)
            K.cp("vector", t, ref)
            K.dma(dd, t, out=True, eng="gpsimd")

    PI = float(np.pi)
    ADD, SUB, MUL, MAX, MIN = ALU.add, ALU.subtract, ALU.mult, ALU.max, ALU.min
    psr = Ring([P.psum("ps%d" % i, [128, 512], F32) for i in range(6)])
    accs = Ring([P.psum("acc%d" % i, [128, 512], F32) for i in range(2)])
    wst = Ring([P.sbuf("wst%d" % i, [128, 8, 128], F32) for i in range(5)])
    wbf = Ring([P.sbuf("wbf%d" % i, [128, 8, 128], BF16) for i in range(5)])
    wf = Ring([P.sbuf("wf%d" % i, [128, 512], F32) for i in range(nwf)])
    wb = Ring([P.sbuf("wb%d" % i, [128, 512], BF16) for i in range(10)])
    xnr = Ring([P.sbuf("xn%d" % i, [128, D], BF16) for i in range(2)])
    sm = Ring([P.sbuf("sm%d" % i, [128, 16], F32) for i in range(16)])
    junk = P.sbuf("junk", [128, D], BF16)

    def cbf(name, shape, src):
        t = P.sbuf(name, shape, BF16)
        n = shape[1]
        for c0 in range(0, n, 512):
            w = min(512, n - c0)
            s = wf.next()
            K.dma(s[:, 0:w], src[:, c0:c0 + w])
            K.cp("vector", t[:, c0:c0 + w], s[:, 0:w])
        return t

    ident_b = cbf("ident_b", [128, 128], din["ident"])
    hmask = [cbf("hmask%d" % i, [128, 128], din["hmask%d" % i]) for i in range(2)]
    bdones = cbf("bdones", [128, 128], din["bdones"])
    ropeC = cbf("ropeC", [128, LL], din["ropeC"])
    ropeS = cbf("ropeS", [128, LL], din["ropeS"])
    ones_b = P.sbuf("ones_b", [128, 128], BF16)
    K.memset("vector", ones_b, 1.0)
    ones_f = P.sbuf("ones_f", [128, 128], F32)
    K.memset("vector", ones_f, 1.0)
    epsc = P.sbuf("epsc", [128, 1], F32)
    K.memset("vector", epsc, EPS)
    zpad = P.sbuf("zpad", [128, 128], BF16)
    K.memset("vector", zpad, 0.0)
    m4 = P.sbuf("m4", [128, 4], F32)
    K.dma(m4, din["m4"])
    ohs = P.sbuf("ohs", [128, 4], F32)
    K.dma(ohs, din["oh"])
    m01 = P.sbuf("m01", [128, TB], F32)
    K.dma(m01, din["m01"])
    iota = P.sbuf("iota", [128, S5T], F32)
    K.dma(iota, din["iota"])
    modT = P.sbuf("modT", [128, NL, 2, 16], F32)
    gsc = DR(nc.dram_tensor("gsc", [NL, 2, D], F32).ap())
    gsc_tok = Buf("gsc_tok", None)
    bmT = P.sbuf("bmT", [128, NL, 24], F32)
    cs = P.sbuf("cs", [128, 8, 2], F32)
    for l in range(NL):
        K.dma(bmT[:, l, :], din["bmodT"][l])
    K.dma(cs, din["cvec"])
    K.act(cs, cs, AF.Silu)
    for l in range(nlayers):
        for j in range(16):
            s = wst.next()
            K.dma(s, din["wmod"][l][j])
            ps = psr.next()
            for k in range(8):
                K.mm(ps[:, 0:2], s[:, k, :], cs[:, k, :], start=(k == 0), stop=(k == 7))
            K.ts("vector", modT[:, l, :, j], ps[:, 0:2], bmT[:, l, j:j + 1], ADD, 1.0 if j >= 8 else 0.0, ADD)
        for n in range(8):
            s = wst.next()
            K.dma(s, din["wmod"][l][16 + n])
            ps = psr.next()
            for k in range(8):
                K.mm(ps[0:2, 0:128], cs[:, k, :], s[:, k, :], start=(k == 0), stop=(k == 7))
            bt = wf.next()
            K.dma(bt[0:2, 0:128], din["bmodg"][l][:, n * 128:(n + 1) * 128])
            K.tt("vector", bt[0:2, 128:256], ps[0:2, 0:128], bt[0:2, 0:128], ADD)
            P.dma(gsc[l][:, n * 128:(n + 1) * 128].ap, bt[0:2, 128:256].ap, reads=[bt], writes=[gsc_tok])
    lamr = P.sbuf("lamr", [1, NL, 128], F32)
    lamv = P.sbuf("lamv", [1, 8], F32)
    neglam = P.sbuf("neglam", [128, NL], F32)
    dan_s = P.sbuf("dan_s", [128, NL], F32)
    for l in range(NL):
        K.dma(lamr[:, l, :], din["dalam"][l])
        K.dma(dan_s[:, l:l + 1], din["dan"][l])
        K.ts("vector", dan_s[:, l:l + 1], dan_s[:, l:l + 1], 1.0 - lam_init(l), MUL)
        t = sm.next()
        e = sm.next()
        for c in range(2):
            K.tt("vector", lamr[:, l, 64 * c:64 * c + 32], lamr[:, l, 64 * c:64 * c + 32],
                 lamr[:, l, 64 * c + 32:64 * c + 64], MUL)
            P.op("vector", lambda e_, o=t[0:1, c:c + 1], a=lamr[:, l, 64 * c:64 * c + 32]: e_.reduce_sum(o.ap, a.ap, axis=AX.X),
                 _b([lamr]), _b([t]))
        K.act(e[0:1, 0:2], t[0:1, 0:2], AF.Exp)
        K.tt("vector", lamv[:, l:l + 1], e[0:1, 0:1], e[0:1, 1:2], SUB)
        K.ts("vector", lamv[:, l:l + 1], lamv[:, l:l + 1], -1.0, MUL, -lam_init(l), ADD)
    ps = psr.next()
    K.mm(ps[:, 0:NL], ones_f[0:1, :], lamv[0:1, 0:NL])
    K.cp("vector", neglam, ps[:, 0:NL])
    hgl = P.sbuf("hgl", [128, NL, 4], F32)
    lb = P.sbuf("lb", [128, NL, 4], F32)
    oml = P.sbuf("oml", [128, NL, 4], F32)
    noml = P.sbuf("noml", [128, NL, 4], F32)
    hgn = P.sbuf("hgn", [128, NL], F32)
    for l in range(NL):
        K.dma(hgl[:, l, :], din["hglb"][l])
        K.dma(hgn[:, l:l + 1], din["hgn"][l])
    K.memset("vector", lb, 0.0)
    K.tt("vector", lb[:, 1, :], hgl[:, 1, :], hgl[:, 0, :], SUB)
    K.act(lb[:, 1, :], lb[:, 1, :], AF.Sigmoid)
    K.ts("vector", oml, lb, -1.0, MUL, 1.0, ADD)
    K.ts("vector", noml, oml, -1.0, MUL)
    hglL = P.sbuf("hglL", [128, NL, 2], F32)
    lbL = P.sbuf("lbL", [128, NL, 2], F32)
    omlL = P.sbuf("omlL", [128, NL, 2], F32)
    nomlL = P.sbuf("nomlL", [128, NL, 2], F32)
    s5dl = P.sbuf("s5dl", [128, NL], F32)
    K.memset("vector", hglL, 0.0)
    K.memset("vector", s5dl, 0.0)
    for l in range(NL):
        K.dma(hglL[0:64, l, :], din["hglbL"][l])
        K.dma(s5dl[0:64, l:l + 1], din["s5dL"][l])
    K.memset("vector", lbL, 0.0)
    K.tt("vector", lbL[:, 1, :], hglL[:, 1, :], hglL[:, 0, :], SUB)
    K.act(lbL[:, 1, :], lbL[:, 1, :], AF.Sigmoid)
    K.ts("vector", omlL, lbL, -1.0, MUL, 1.0, ADD)
    K.ts("vector", nomlL, omlL, -1.0, MUL)
    mqn = P.sbuf("mqn", [128, NL, 2], F32)
    mkvn = P.sbuf("mkvn", [128, NL, 128], F32)
    s5d = P.sbuf("s5d", [128, NL, 2], F32)
    for l in range(NL):
        K.dma(mqn[:, l, :], din["mqn"][l])
        K.dma(mkvn[:, l, :], din["mkvn"][l])
        K.dma(s5d[:, l, :], din["s5d"][l])

    def run_job(lat):
        P.push_scope()
        nt = 16 if lat else 4
        L = nt * 128
        v = 1 if lat else 0
        seqs = [(0, L)] if lat else [(0, 256), (256, 256)]
        BS = 512 if lat else 256
        xsrc = din["xl"] if lat else din["xc"]
        ydst = dout["y_l"] if lat else dout["y_c"]
        ntx = 4 if lat else nt
        x = P.sbuf("x", [128, ntx, D], F32)
        xv = [x.view(x.ap[:, i, :], "x%d" % i) for i in range(ntx)]
        hT = P.sbuf("hT", [128, 8, L], BF16)
        hTo = P.sbuf("hTo", [128, 8, 512], BF16) if lat else hT
        mixed = P.sbuf("mixed", [128, 2, L], BF16) if not lat else None
        mixh = P.sbuf("mixh", [64, 4, L], BF16) if lat else None
        NH = 1 if lat else 4
        for i in range(ntx):
            K.dma(xv[i], xsrc[i])

        def rstd_of(xt, n, ss_scale):
            s = sm.next()
            K.act(junk[:, 0:n], xt, AF.Square, accum=s[:, 0:1])
            K.act(s[:, 1:2], s[:, 0:1], AF.Sqrt, bias=epsc, scale=ss_scale)
            K.recip(s[:, 2:3], s[:, 1:2])
            return s[:, 2:3]

        def load_group(l, name):
            if lat:
                if name.startswith("Cf"):
                    name = name[:3]
                elif name[:2] not in ("Bg", "Dc"):
                    name = name.rstrip("0123456789")
            off = (LGT if lat else GT)[name]
            s = wst.next()
            K.dma(s, din["winL" if lat else "win"][l][off // 128])
            w = wbf.next()
            K.cp("scalar", w, s)
            return w

        def fm_cols(w, c0, n, M=128):
            ps = psr.next()
            for k in range(8):
                K.mm(ps[0:M, 0:n], w[:, k, 0:M], hT[:, k, c0:c0 + n], start=(k == 0), stop=(k == 7))
            return ps

        def fm_block(w, tb, M=128):
            return fm_cols(w, tb * TB, TB, M)

        def tm_tile(w, i, N=128):
            ps = psr.next()
            for k in range(8):
                K.mm(ps[:, 0:N], hT[:, k, i * 128:(i + 1) * 128], w[:, k, 0:N], start=(k == 0), stop=(k == 7))
            return ps

        def blk(tb):
            return slice(tb * TB, (tb + 1) * TB)

        def tile_seq_rows(i):
            return i // 2, (i % 2) * 128

        def norm_mod(l):
            for i in range(ntx):
                r = rstd_of(xv[i], D, 1.0 / D)
                xn = xnr.next()
                K.ts("vector", xn, xv[i], r, MUL)
                for half in range(2):
                    ps = psr.next()
                    for kk in range(4):
                        k = half * 4 + kk
                        K.mm(ps[:, kk * 128:(kk + 1) * 128], xn[:, k * 128:(k + 1) * 128], ident_b)
                    for kk in range(4):
                        k = half * 4 + kk
                        eng = "vector" if kk % 2 == 0 else "gpsimd"
                        if eng == "gpsimd":
                            K.act(hTo[:, k, i * 128:(i + 1) * 128], ps[:, kk * 128:(kk + 1) * 128], AF.Identity,
                                  bias=modT[:, l, v, k:k + 1], scale=modT[:, l, v, 8 + k:9 + k])
                        else:
                            K.ts("vector", hTo[:, k, i * 128:(i + 1) * 128], ps[:, kk * 128:(kk + 1) * 128],
                                 modT[:, l, v, 8 + k:9 + k], MUL, modT[:, l, v, k:k + 1], ADD)
            if lat:
                ahin = DR(nc.dram_tensor("ahin%d" % l, [1024, 512], BF16).ap())
                ahout = DR(nc.dram_tensor("ahout%d" % l, [4096, 512], BF16).ap())
                ahin_tok = Buf("ahin_tok%d" % l, None)
                ahout_tok = Buf("ahout_tok%d" % l, None)
                P.dma(ahin.ap.rearrange("(k p) t -> p k t", p=128), hTo.ap, reads=[hTo], writes=[ahin_tok], eng="gpsimd")
                P.collective(lambda e: e.collective_compute("AllGather", ALU.bypass, replica_groups=[[0, 1, 2, 3], [4, 5, 6, 7]],
                                                            ins=[ahin.ap.opt()], outs=[ahout.ap.opt()]),
                             reads=[ahin_tok], writes=[ahout_tok])
                HTL["ahout"], HTL["tok"] = ahout, ahout_tok

        HTL = {}

        def hT_load():
            ahout, ahout_tok = HTL["ahout"], HTL["tok"]
            for r_ in range(4):
                P.dma(hT.ap[:, :, r_ * 512:(r_ + 1) * 512], ahout.ap[r_ * 1024:(r_ + 1) * 1024, :].rearrange("(k p) t -> p k t", p=128),
                      reads=[ahout_tok], writes=[hT])

        def out_proj(l, bi):
            wo = [wbf.next(), wbf.next()]
            for kc in range(2):
                s = wst.next()
                sv = s.re("p a b -> p (a b)")
                K.dma(sv, din["wout"][l][:, 2 * bi + kc, :])
                wov = wo[kc].re("p a b -> p (a b)")
                for n in range(2):
                    gb = wf.next()
                    P.dma(gb.ap, gsc[l][v:v + 1, n * 512:(n + 1) * 512].ap.partition_broadcast(128), reads=[gsc_tok], writes=[gb])
                    K.tt("vector", wov[:, n * 512:(n + 1) * 512], gb, sv[:, n * 512:(n + 1) * 512], MUL)
            for i in range(nt):
                for n in range(2):
                    ps = psr.next()
                    for kc in range(2):
                        K.mm(ps, mixed[:, kc, i * 128:(i + 1) * 128], wo[kc].re("p a b -> p (a b)")[:, n * 512:(n + 1) * 512],
                             start=(kc == 0), stop=(kc == 1))
                    K.tt("vector", xv[i][:, n * 512:(n + 1) * 512], ps, xv[i][:, n * 512:(n + 1) * 512], ADD)

        def attention(qT, kT, Vaug_of, Kdim, bases, scale, h, epilogue):
            pb = 64 * (h % 2)
            dn = 64 - pb
            QB = BS
            steps = []
            for (s0, sl) in seqs:
                if lat:
                    ktiles = list(range((L + PAST) // 128))
                else:
                    ktiles = list(range(s0 // 128, (s0 + sl) // 128))
                for qb in range(sl // QB):
                    q0 = s0 + qb * QB
                    for bi_, base in enumerate(bases):
                        for idx, kt in enumerate(ktiles):
                            steps.append((q0, bi_, base, idx, kt, len(ktiles)))
            pend = []
            state = {}

            def do_pv(item):
                (q0, bi_, base, idx, kt, nk), pt = item
                if idx == 0:
                    state["acc"] = accs.next()
                    if bi_ == 0:
                        state["ocs"] = []
                acc = state["acc"]
                K.mm(acc[:, 0:QB], Vaug_of(kt), pt[:, 0:QB], start=(idx == 0), stop=(idx == nk - 1))
                if idx == nk - 1:
                    rec = wf.next()
                    K.act(rec[dn:dn + 64, 0:QB], acc[dn:dn + 64, 0:QB], AF.Ln)
                    K.act(rec[dn:dn + 64, 0:QB], rec[dn:dn + 64, 0:QB], AF.Exp, scale=-1.0)
                    oc = wf.next()
                    K.tt("vector", oc[pb:pb + 64, 0:QB], acc[pb:pb + 64, 0:QB], rec[dn:dn + 64, 0:QB], MUL)
                    state["ocs"].append(oc)
                    if bi_ == len(bases) - 1:
                        epilogue(state["ocs"], q0, QB, pb)

            for stp in steps:
                (q0, bi_, base, idx, kt, nk) = stp
                pss = psr.next()
                K.mm(pss[:, 0:QB], kT[base:base + Kdim, kt * 128:(kt + 1) * 128], qT[base:base + Kdim, q0:q0 + QB])
                pt = wb.next()
                K.act(pt[:, 0:QB], pss[:, 0:QB], AF.Exp, scale=scale)
                pend.append((stp, pt))
                if len(pend) > 3:
                    do_pv(pend.pop(0))
            while pend:
                do_pv(pend.pop(0))

        def branch_A(l):
            P.push_scope()
            Lk = L + PAST if lat else L
            nkt = Lk // 128
            if not lat:
                for i2 in range(2):
                    for nm, dst in (("Av", "o_dv"), ("Akt", "o_dk")):
                        w = load_group(l, "%s%d" % (nm, i2))
                        for i in range(nt):
                            ps = tm_tile(w, i)
                            o = wf.next()
                            K.cp("scalar" if i % 2 else "vector", o[:, 0:128], ps[:, 0:128])
                            sq, r0 = tile_seq_rows(i)
                            K.dma(dout[dst][sq, l, r0:r0 + 128, i2 * 128:(i2 + 1) * 128], o[:, 0:128], out=True, eng="gpsimd")
            else:
                ckf = P.sbuf("ckf", [128, 2, 64], F32)
                cvf = P.sbuf("cvf", [128, 2, 64], F32)
                for j in range(2):
                    K.dma(cvf[:, j, :], din["cdvL"][l][j])
                    K.dma(ckf[:, j, :], din["cdkL"][l][j])
            sgA = P.sbuf("sgA", [128, L], BF16)
            Vh = P.sbuf("VhA", [128, nkt, 128], BF16)
            qT = P.sbuf("qT", [128, L], BF16)
            kT = P.sbuf("kT", [128, Lk], BF16)
            kpad = P.sbuf("kpad", [128, 128], BF16)
            K.memset("vector", kpad, 0.0)
            for h in range(NH):
                pbh = 64 * (h % 2)
                if h % 2 == 0:
                    w = load_group(l, "Ag%d" % (h // 2))
                    for tb in range(L // TB):
                        ps = fm_block(w, tb)
                        K.act(sgA[:, blk(tb)], ps, AF.Silu)
                w = load_group(l, "Av%d" % (h // 2))
                K.memset("gpsimd", Vh[:, :, 64 - pbh:128 - pbh], 1.0)
                for i in range(nt):
                    ps = psr.next()
                    for k in range(8):
                        K.mm(ps[:, 0:64], hT[:, k, i * 128:(i + 1) * 128], w[:, k, pbh:pbh + 64], start=(k == 0), stop=(k == 7))
                    K.cp("scalar" if i % 2 else "vector", Vh[:, i, pbh:pbh + 64], ps[:, 0:64])
                if lat:
                    for j in range(2):
                        K.cp("gpsimd", Vh[:, nt + j, pbh:pbh + 64], cvf[:, j, h * 64:(h + 1) * 64])
                for nm, dst in (("Aq", qT), ("Ak", kT)):
                    w = load_group(l, "%s%d" % (nm, h))
                    if lat:
                        wp = load_group(l, "%sp%d" % (nm, h))
                    for tb in range(L // TB):
                        ps = fm_block(w, tb)
                        if lat:
                            psp = fm_block(wp, tb)
                            t1 = wf.next()
                            t2 = wf.next()
                            K.tt("vector", t1, ps, ropeC[:, blk(tb)], MUL)
                            K.tt("vector", t2, psp, ropeS[:, blk(tb)], MUL)
                            K.tt("vector", dst[:, blk(tb)], t1, t2, ADD)
                        else:
                            K.cp("scalar", dst[:, blk(tb)], ps)
                if lat:
                    for j in range(2):
                        K.cp("vector", kpad[:, 0:32], ckf[:, j, h * 64:h * 64 + 32])
                        K.cp("vector", kpad[:, 64:96], ckf[:, j, h * 64 + 32:h * 64 + 64])
                        ps = psr.next()
                        K.mm(ps[:, 0:128], kpad, ident_b)
                        K.cp("vector", kT[:, L + j * 128:L + (j + 1) * 128], ps[:, 0:128])

                def epi(ocs, q0, QB, pb, h=h):
                    o = wf.next()
                    K.stt("vector", o[pb:pb + 64, 0:QB], ocs[1][pb:pb + 64, 0:QB], neglam[pb:pb + 64, l:l + 1],
                          ocs[0][pb:pb + 64, 0:QB], MUL, ADD)
                    sq = wb.next()
                    K.tt("vector", sq[pb:pb + 64, 0:QB], o[pb:pb + 64, 0:QB], o[pb:pb + 64, 0:QB], MUL)
                    pss = psr.next()
                    K.mm(pss[:, 0:QB], ones_b[pb:pb + 64, :], sq[pb:pb + 64, 0:QB])
                    rs = wf.next()
                    K.act(rs[pb:pb + 64, 0:QB], pss[pb:pb + 64, 0:QB], AF.Ln, bias=epsc[pb:pb + 64, :], scale=1.0 / 64)
                    K.act(rs[pb:pb + 64, 0:QB], rs[pb:pb + 64, 0:QB], AF.Exp, scale=-0.5)
                    K.tt("vector", o[pb:pb + 64, 0:QB], o[pb:pb + 64, 0:QB], rs[pb:pb + 64, 0:QB], MUL)
                    dst = mixh[0:64, 0, q0:q0 + QB] if lat else mixed[pb:pb + 64, h // 2, q0:q0 + QB]
                    K.stt("vector", dst, o[pb:pb + 64, 0:QB], dan_s[pb:pb + 64, l:l + 1],
                          sgA[pb:pb + 64, q0:q0 + QB], MUL, MUL)

                attention(qT, kT, lambda kt: Vh[:, kt, :], 64, (0, 64), 32 ** -0.5, h, epi)
            P.pop_scope()

        def branch_D(l):
            P.push_scope()
            Lk = L + PAST if lat else L
            nkt = Lk // 128
            wuq_b = P.sbuf("wuq_b", [128, 2, 192 if lat else 768], BF16)
            wukv_b = P.sbuf("wukv_b", [128, 128 if lat else 512], BF16)
            P.push_scope()
            wuq_f = P.sbuf("wuq_f", [128, 2, 768], F32)
            NQ = 192 if lat else 768
            K.dma(wuq_f[:, :, 0:NQ], din["wuqL" if lat else "wuq"][l])
            for kc in range(2):
                K.ts("vector", wuq_b[:, kc, 0:NQ], wuq_f[:, kc, 0:NQ], mqn[:, l, kc:kc + 1], MUL)
            s = wf.next()
            NKV = 128 if lat else 512
            K.dma(s[:, 0:NKV], din["wukvL" if lat else "wukv"][l])
            K.cp("vector", wukv_b[:, 0:NKV], s[:, 0:NKV])
            QPO = 96 if lat else 384
            P.pop_scope()
            cqb = P.sbuf("cqb", [128, 2, L], BF16)
            for kc in range(2):
                w = load_group(l, "Dcq%d" % kc)
                for tb in range(L // TB):
                    ps = fm_block(w, tb)
                    K.cp("scalar", cqb[:, kc, blk(tb)], ps)
            ckvT = P.sbuf("ckvT", [128, Lk], BF16)
            krT = P.sbuf("krT", [128, Lk], BF16)
            w = load_group(l, "Dckv")

            def ckv_tile(src_ps, col0, raw_is_psum=True, out_to=None):
                cnb = wb.next()
                if out_to is None:
                    K.cp("vector", cnb[:, 0:128], src_ps)
                else:
                    r = rstd_of(src_ps, 128, 1.0 / 128)
                    cn = wf.next()
                    K.stt("vector", cn[:, 0:128], src_ps, r, mkvn[:, l, :], MUL, MUL)
                    if out_to is not False:
                        K.dma(out_to, cn[:, 0:128], out=True, eng="gpsimd")
                    K.cp("gpsimd", cnb[:, 0:128], cn[:, 0:128])
                pst = psr.next()
                K.mm(pst[:, 0:128], cnb[:, 0:128], ident_b)
                K.cp("scalar", ckvT[:, col0:col0 + 128], pst[:, 0:128])

            for i in range(nt):
                ps = tm_tile(w, i)
                if lat:
                    ckv_tile(ps[:, 0:128], i * 128, out_to=False)
                else:
                    sq, r0 = tile_seq_rows(i)
                    ckv_tile(ps[:, 0:128], i * 128, out_to=dout["o_ckv"][sq, l, r0:r0 + 128, :])
            if lat:
                for j in range(2):
                    s = wf.next()
                    K.dma(s[:, 0:128], din["cckv"][l][j])
                    ckv_tile(s[:, 0:128], L + j * 128)
            w = load_group(l, "Dkr")
            if lat:
                wp = load_group(l, "Dkrp")
            for tb in range(L // TB):
                ps = fm_block(w, tb, M=96)
                if lat:
                    psp = fm_block(wp, tb, M=96)
                    t1 = wf.next()
                    t2 = wf.next()
                    K.tt("vector", t1[64:96, :], ps[64:96, :], ropeC[64:96, blk(tb)], MUL)
                    K.tt("vector", t2[64:96, :], psp[64:96, :], ropeS[64:96, blk(tb)], MUL)
                    K.tt("gpsimd", krT[64:96, blk(tb)], t1[64:96, :], t2[64:96, :], ADD)
                else:
                    K.cp("scalar", krT[64:96, blk(tb)], ps[64:96, :])
            if lat:
                kpad = P.sbuf("kpadD", [128, 128], BF16)
                K.memset("vector", kpad, 0.0)
                for j in range(2):
                    s = wf.next()
                    K.dma(s[:, 0:32], din["ckr"][l][j])
                    K.cp("vector", kpad[:, 64:96], s[:, 0:32])
                    pst = psr.next()
                    K.mm(pst[0:96, 0:128], kpad[:, 0:96], ident_b)
                    K.cp("vector", krT[64:96, L + j * 128:L + (j + 1) * 128], pst[64:96, 0:128])
            else:
                w = load_group(l, "Dkrt")
                for i in range(nt):
                    ps = tm_tile(w, i, N=32)
                    o = wf.next()
                    K.cp("vector", o[:, 0:32], ps[:, 0:32])
                    sq, r0 = tile_seq_rows(i)
                    K.dma(dout["o_kr"][sq, l, r0:r0 + 128, :], o[:, 0:32], out=True, eng="gpsimd")
            qT = P.sbuf("qTD", [128, L], BF16)
            kT = P.sbuf("kTD", [128, Lk], BF16)
            Vh = P.sbuf("VhD", [128, nkt, 128], BF16)
            sgD = P.sbuf("sgD", [128, L], BF16)
            for h in range(NH):
                pb = 64 * (h % 2)
                if h % 2 == 0:
                    w = load_group(l, "Dg%d" % (h // 2))
                    for tb in range(L // TB):
                        ps = fm_block(w, tb)
                        K.act(sgD[:, blk(tb)], ps, AF.Silu)
                for tb in range(L // TB):
                    sq0 = wb.next()
                    sq1 = wb.next()
                    K.act(sq0, cqb[:, 0, blk(tb)], AF.Square)
                    K.act(sq1, cqb[:, 1, blk(tb)], AF.Square)
                    pss = psr.next()
                    K.mm(pss[0:96, :], ones_b[:, 0:96], sq0, start=True, stop=False)
                    K.mm(pss[0:96, :], ones_b[:, 0:96], sq1, start=False, stop=True)
                    rq = wf.next()
                    K.act(rq[0:96, :], pss[0:96, :], AF.Ln, bias=epsc[0:96, :], scale=1.0 / 192)
                    K.act(rq[0:96, :], rq[0:96, :], AF.Exp, scale=-0.5)
                    z = psr.next()
                    for kc in range(2):
                        K.mm(z[0:96, :], wuq_b[:, kc, h * 96:(h + 1) * 96], cqb[:, kc, blk(tb)], start=(kc == 0), stop=(kc == 1))
                    K.tt("vector", qT[0:64, blk(tb)], z[0:64, :], rq[0:64, :], MUL)
                    if lat:
                        zp = psr.next()
                        for kc in range(2):
                            K.mm(zp[0:96, :], wuq_b[:, kc, QPO + h * 96:QPO + (h + 1) * 96], cqb[:, kc, blk(tb)],
                                 start=(kc == 0), stop=(kc == 1))
                        t1 = wf.next()
                        t2 = wf.next()
                        K.tt("vector", t1[64:96, :], z[64:96, :], ropeC[64:96, blk(tb)], MUL)
                        K.tt("vector", t2[64:96, :], zp[64:96, :], ropeS[64:96, blk(tb)], MUL)
                        K.tt("vector", t1[64:96, :], t1[64:96, :], t2[64:96, :], ADD)
                        K.tt("vector", qT[64:96, blk(tb)], t1[64:96, :], rq[64:96, :], MUL)
                    else:
                        K.tt("vector", qT[64:96, blk(tb)], z[64:96, :], rq[64:96, :], MUL)
                for c0 in range(0, Lk, TB):
                    n = min(TB, Lk - c0)
                    ps = psr.next()
                    K.mm(ps[0:64, 0:n], wukv_b[:, h * 128:h * 128 + 64], ckvT[:, c0:c0 + n])
                    K.cp("scalar", kT[0:64, c0:c0 + n], ps[0:64, 0:n])
                K.cp("vector", kT[64:96, :], krT[64:96, :])
                K.memset("gpsimd", Vh[:, :, 64 - pb:128 - pb], 1.0)
                for kt in range(nkt):
                    ps = psr.next()
                    K.mm(ps[:, 0:64], ckvT[:, kt * 128:(kt + 1) * 128], wukv_b[:, h * 128 + 64:h * 128 + 128])
                    K.cp("vector" if kt % 2 else "scalar", Vh[:, kt, pb:pb + 64], ps[:, 0:64])

                def epi(ocs, q0, QB, pb, h=h):
                    dst = mixh[0:64, 2, q0:q0 + QB] if lat else mixed[pb:pb + 64, h // 2, q0:q0 + QB]
                    K.tt("vector", dst, ocs[0][pb:pb + 64, 0:QB], sgD[pb:pb + 64, q0:q0 + QB], MUL)

                attention(qT, kT, lambda kt: Vh[:, kt, :], 96, (0,), 96 ** -0.5, h, epi)
            P.pop_scope()

        def branch_C(l):
            P.push_scope()
            qb_ = P.sbuf("hq", [128, L], BF16)
            vtok = P.sbuf("hv", [128, nt, 128], BF16)
            obuf = P.sbuf("ho", [128, L], F32)
            if lat:
                K.memset("gpsimd", obuf, 0.0)
            Sfr = Ring([P.sbuf("hSf%d" % i, [128, 128], F32) for i in range(3)])
            sbring = Ring([P.sbuf("hSb%d" % i, [128, 128], BF16) for i in range(6)])
            hsg, hf, hg_, hkk, hb, heb = [P.sbuf("h_t%d" % i, [128, 512], F32) for i in range(6)]
            hqt, hkt, hkh = [P.sbuf("h_b%d" % i, [128, 512], BF16) for i in range(3)]
            HH = 1 if lat else 2
            for hp in range(1 if lat else 2):
                w = load_group(l, "Cq%d" % hp)
                for tb in range(L // TB):
                    ps = fm_block(w, tb)
                    K.cp("scalar", qb_[:, blk(tb)], ps)
                w = load_group(l, "Cv%d" % hp)
                for i in range(nt):
                    ps = tm_tile(w, i)
                    K.cp("vector", vtok[:, i, :], ps[:, 0:128])
                for d in range(2):
                    w = load_group(l, "Cf%d%d" % (d, hp))
                    ci = 2 * d + hp
                    if lat:
                        lbv, omv, nomv = lbL[:, l, d:d + 1], omlL[:, l, d:d + 1], nomlL[:, l, d:d + 1]
                    else:
                        lbv, omv, nomv = lb[:, l, ci:ci + 1], oml[:, l, ci:ci + 1], noml[:, l, ci:ci + 1]
                    for si, (s0, sl) in enumerate(seqs):
                        Sf = Sfr.next()
                        K.memset("vector", Sf, 0.0)
                        if lat:
                            K.dma(Sf[0:64, 0:64], din["chgL"][l][d])
                        cur = {"sb": sbring.next(), "sf": Sf}
                        K.cp("scalar", cur["sb"], Sf)
                        nb = sl // BS
                        for bi in (range(nb) if d == 0 else range(nb - 1, -1, -1)):
                            c0 = s0 + bi * BS
                            ncks = BS // HGC
                            if CSTOP < 1:
                                continue
                            ps = fm_cols(w, c0, BS)
                            sg = hsg
                            K.act(sg[:, 0:BS], ps[:, 0:BS], AF.Sigmoid)
                            f = hf
                            K.ts("vector", f[:, 0:BS], sg[:, 0:BS], omv, MUL, lbv, ADD)
                            g = hg_
                            K.act(g[:, 0:BS], f[:, 0:BS], AF.Ln)
                            kk = hkk
                            K.ts("vector", kk[:, 0:BS], sg[:, 0:BS], nomv, MUL, omv, ADD)
                            b = hb
                            if d == 0:
                                K.scan(b[:, 0:BS], m01[:, 0:BS], g[:, 0:BS], 0.0)
                            else:
                                K.scan(b[:, BS - 1::-1] if False else b[:, 0:BS][:, ::-1], m01[:, 0:BS], g[:, 0:BS][:, ::-1], 0.0)
                            if CSTOP < 2:
                                continue
                            eb = heb
                            K.act(eb[:, 0:BS], b[:, 0:BS], AF.Exp)
                            enb = g
                            K.act(enb[:, 0:BS], b[:, 0:BS], AF.Exp, scale=-1.0)
                            if CSTOP < 2.1:
                                continue
                            qt = hqt
                            K.tt("vector", qt[:, 0:BS], qb_[:, c0:c0 + BS], eb[:, 0:BS], MUL)
                            kt_ = hkt
                            K.tt("gpsimd", kt_[:, 0:BS], kk[:, 0:BS], enb[:, 0:BS], MUL)
                            if CSTOP < 2.2:
                                continue
                            b3 = b[:, 0:BS].re("p (c j) -> p c j", j=HGC)
                            e = HGC - 1 if d == 0 else 0
                            d2 = f
                            K.tt("vector", d2[:, 0:BS].re("p (c j) -> p c j", j=HGC), b3[:, :, e:e + 1].bc([128, ncks, HGC]), b3, SUB)
                            K.act(d2[:, 0:BS], d2[:, 0:BS], AF.Exp)
                            if CSTOP < 2.3:
                                continue
                            kh = hkh
                            K.tt("gpsimd", kh[:, 0:BS], kk[:, 0:BS], d2[:, 0:BS], MUL)
                            ntile = BS // 128
                            if CSTOP < 3:
                                continue
                            for ti in (range(ntile) if d == 0 else range(ntile - 1, -1, -1)):
                                lo = ti * 128
                                gi = (c0 + lo) // 128
                                pss2 = [psr.next(), psr.next()]
                                for hh in range(HH):
                                    K.mm(pss2[hh][:, 0:128], kt_[64 * hh:64 * hh + 64, lo:lo + 128],
                                         qt[64 * hh:64 * hh + 64, lo:lo + 128])
                                if CSTOP < 3.1:
                                    continue
                                sc = wb.next()
                                for hh in range(HH):
                                    K.tt("vector", sc[:, hh * 128:(hh + 1) * 128], pss2[hh][:, 0:128], hmask[d], MUL)
                                if CSTOP < 4:
                                    continue
                                pst = psr.next()
                                K.mm(pst[:, 0:128], kh[:, lo:lo + 128], ident_b)
                                kexp = wb.next()
                                K.tt("vector", kexp[:, :].re("p (c k) -> p c k", c=4),
                                     pst[:, 0:128].re("p (o k) -> p o k", o=1).bc([128, 4, 128]),
                                     m4[:, :].re("p (c o) -> p c o", o=1).bc([128, 4, 128]), MUL)
                                if CSTOP < 5:
                                    continue
                                psu = psr.next()
                                for c in range(4):
                                    K.mm(psu[:, c * 128:(c + 1) * 128], kexp[:, c * 128:(c + 1) * 128], vtok[:, gi, :])
                                if CSTOP < 6:
                                    continue
                                corder = list(range(4)) if d == 0 else [3, 2, 1, 0]
                                Sbs = []
                                for cn, c in enumerate(corder):
                                    Sbs.append(cur["sb"])
                                    ce = lo + c * HGC + e
                                    Sfn = Sfr.next()
                                    K.stt("vector", Sfn, cur["sf"], eb[:, ce:ce + 1], psu[:, c * 128:(c + 1) * 128], MUL, ADD)
                                    cur["sf"] = Sfn
                                    nsb = sbring.next()
                                    K.cp("scalar", nsb, Sfn)
                                    cur["sb"] = nsb
                                psos = [psr.next(), psr.next()]
                                for cn, c in enumerate(corder):
                                    for hh in range(HH):
                                        pso = psos[hh]
                                        K.mm(pso[:, c * HGC:(c + 1) * HGC], vtok[:, gi, :],
                                             sc[:, hh * 128 + c * HGC:hh * 128 + (c + 1) * HGC], start=True, stop=False)
                                        K.mm(pso[:, c * HGC:(c + 1) * HGC], Sbs[cn][64 * hh:64 * hh + 64, :],
                                             qt[64 * hh:64 * hh + 64, lo + c * HGC:lo + (c + 1) * HGC], start=False, stop=True)
                                t0 = c0 + lo
                                for hh in range(HH):
                                    pbh = 64 * hh
                                    pso = psos[hh]
                                    if d == 0:
                                        K.cp("vector", obuf[pbh:pbh + 64, t0:t0 + 128], pso[pbh:pbh + 64, 0:128])
                                    else:
                                        K.tt("vector", obuf[pbh:pbh + 64, t0:t0 + 128], pso[pbh:pbh + 64, 0:128],
                                             obuf[pbh:pbh + 64, t0:t0 + 128], ADD)
                        if not lat:
                            seqi = si
                            for hh in range(2):
                                K.dma(dout["o_hg"][seqi, l, d, 2 * hp + hh], cur["sf"][64 * hh:64 * hh + 64, 64 * hh:64 * hh + 64],
                                      out=True, eng="gpsimd")
                w = load_group(l, "Cg%d" % hp)
                for tb in range(L // TB):
                    ps = fm_block(w, tb)
                    sg = wb.next()
                    K.act(sg, ps, AF.Silu)
                    sq = wb.next()
                    K.tt("gpsimd", sq, obuf[:, blk(tb)], obuf[:, blk(tb)], MUL)
                    pss = psr.next()
                    K.mm(pss, bdones, sq)
                    rs = wf.next()
                    K.act(rs, pss, AF.Ln, bias=epsc, scale=1.0 / 64)
                    K.act(rs, rs, AF.Exp, scale=-0.5)
                    t = wf.next()
                    K.tt("vector", t, obuf[:, blk(tb)], rs, MUL)
                    if lat:
                        K.stt("vector", mixh[0:64, 1, blk(tb)], t[0:64, :], hgn[0:64, l:l + 1], sg[0:64, :], MUL, MUL)
                    else:
                        K.stt("vector", mixed[:, hp, blk(tb)], t, hgn[:, l:l + 1], sg, MUL, MUL)
            P.pop_scope()

        BP = {}

        def b_prep(l):
            P.push_scope()
            C1 = 6.28125
            C2 = 2.0 * PI - C1
            NST = 2 if lat else 8
            NCC = 1 if lat else 2
            KR = 64 if lat else 128
            sp = P.sbuf("s5sp", [128, 16, 3], F32)
            K.dma(sp[:, 0:2 * NST, :], din["s5pL" if lat else "s5p"][l])
            step = P.sbuf("s5step", [128, 16], F32)
            th = P.sbuf("s5th", [128, 16], F32)
            rmag = P.sbuf("s5mag", [128, 16], F32)
            N2 = 2 * NST
            K.act(step[:, 0:N2], sp[:, 0:N2, 2], AF.Exp)
            K.tt("vector", th[:, 0:N2], sp[:, 0:N2, 1], step[:, 0:N2], MUL)
            K.tt("vector", rmag[:, 0:N2], sp[:, 0:N2, 0], step[:, 0:N2], MUL)
            K.act(rmag[:, 0:N2], rmag[:, 0:N2], AF.Exp)
            EC2 = P.sbuf("s5EC", [128, 2, NST, S5T], F32)
            ES2 = P.sbuf("s5ES", [128, 2, NST, S5T], F32)
            Bm2 = P.sbuf("s5Bm", [128, 2, NST, 2, 128], BF16)
            Cm2 = P.sbuf("s5Cm", [128, 2, NST, 3, 128], BF16)
            fz2 = P.sbuf("s5f", [128, 2, 8, 4], F32)
            cst = P.sbuf("s5cst", [128, 8, 2], F32)
            h0 = P.sbuf("s5h0", [128, 16, 2], F32)
            if lat:
                K.dma(h0[:, 0:4, :], din["cs5L"][l])
            else:
                wglu_b = P.sbuf("wglu_b", [128, 2, 512], BF16)
                s = wst.next()
                sv = s.re("p a b -> p (a b)")
                K.dma(sv, din["wglu"][l].re("p a b -> p (a b)"))
                K.cp("vector", wglu_b[:, :, :].re("p a b -> p (a b)"), sv)
            for d in range(2):
                EC, ES, Bm, Cm, fz = EC2[:, d], ES2[:, d], Bm2[:, d], Cm2[:, d], fz2[:, d]
                P.push_scope()
                kint = P.sbuf("s5ki", [128, 512], I32)
                SH = min(512 // S5T, NST)
                for half in range(NST // SH):
                    ang = wf.next()
                    a3 = ang[:, 0:SH * S5T].re("p (s t) -> p s t", s=SH)
                    K.tt("vector", a3, iota[:, :].re("p (o t) -> p o t", o=1).bc([128, SH, S5T]),
                         th[:, d * NST + half * SH:d * NST + half * SH + SH].re("p (s o) -> p s o", o=1).bc([128, SH, S5T]), MUL)
                    W_ = SH * S5T
                    for tab, shift in ((ES, 0.0), (EC, PI / 2)):
                        xx = wf.next()
                        K.ts("vector", xx[:, 0:W_], ang[:, 0:W_], shift, ADD)
                        kf = wf.next()
                        K.ts("vector", kf[:, 0:W_], xx[:, 0:W_], 1.0 / (2 * PI), MUL)
                        K.cp("vector", kint[:, 0:W_], kf[:, 0:W_])
                        K.cp("vector", kf[:, 0:W_], kint[:, 0:W_])
                        K.stt("vector", xx[:, 0:W_], kf[:, 0:W_], -C1, xx[:, 0:W_], MUL, ADD)
                        K.stt("vector", xx[:, 0:W_], kf[:, 0:W_], -C2, xx[:, 0:W_], MUL, ADD)
                        K.ts("vector", xx[:, 0:W_], xx[:, 0:W_], PI, MIN, -PI, MAX)
                        K.act(tab[:, half * SH:half * SH + SH, :].re("p s t -> p (s t)"), xx[:, 0:W_], AF.Sin)
                P.pop_scope()
                are = sp[:, d * NST:d * NST + NST, 0]
                aim = sp[:, d * NST:d * NST + NST, 1]
                mg = rmag[:, d * NST:d * NST + NST]
                t = sm.next()
                abr, abi, den, t1, t2 = t[:, 0:NST], t[:, 8:8 + NST], None, None, None
                u = sm.next()
                den, t1 = u[:, 0:NST], u[:, 8:8 + NST]
                u2 = sm.next()
                t2, t3 = u2[:, 0:NST], u2[:, 8:8 + NST]
                fzv = fz[:, 0:NST, :]
                K.tt("vector", abr, mg, EC[:, 0:NST, 0], MUL)
                K.tt("vector", abi, mg, ES[:, 0:NST, 0], MUL)
                K.ts("vector", abr, abr, -1.0, ADD)
                K.tt("vector", den, are, are, MUL)
                K.tt("vector", t1, aim, aim, MUL)
                K.tt("vector", den, den, t1, ADD)
                K.recip(den, den)
                K.tt("vector", t1, abr, are, MUL)
                K.tt("vector", t2, abi, aim, MUL)
                K.tt("vector", t1, t1, t2, ADD)
                K.tt("vector", fzv[:, :, 0], t1, den, MUL)
                K.tt("vector", t1, abi, are, MUL)
                K.tt("vector", t2, abr, aim, MUL)
                K.tt("vector", t1, t1, t2, SUB)
                K.tt("vector", fzv[:, :, 1], t1, den, MUL)
                K.ts("vector", fzv[:, :, 2], fzv[:, :, 1], -1.0, MUL)
                for st in range(NST):
                    s = wst.next()
                    sv = s.re("p a b -> p (a b)")
                    if lat:
                        K.dma(sv[0:64, 0:256].re("p (r c) -> p r c", r=2), din["s5BL"][l][d][st].re("r p c -> p r c"))
                        K.cp("gpsimd", Bm[0:64, st, :, :], sv[0:64, 0:256].re("p (r c) -> p r c", r=2))
                        K.memset("vector", sv[:, 256:512], 0.0)
                        K.dma(sv[:, 256:512].re("p (r c) -> p r c", r=2)[:, :, 0:64], din["s5CL"][l][d][st].re("r p c -> p r c"))
                    else:
                        K.dma(sv[:, 0:256].re("p (r c) -> p r c", r=2), din["s5B"][l][d][st].re("r p c -> p r c"))
                        K.cp("gpsimd", Bm[:, st, :, :], sv[:, 0:256].re("p (r c) -> p r c", r=2))
                        K.dma(sv[:, 256:512].re("p (r c) -> p r c", r=2), din["s5C"][l][d][st].re("r p c -> p r c"))
                    cre, cim = sv[:, 256:384], sv[:, 384:512]
                    tw = wf.next()
                    K.ts("vector", tw[:, 0:128], cre, fz[:, st, 0:1], MUL)
                    K.stt("vector", tw[:, 0:128], cim, fz[:, st, 2:3], tw[:, 0:128], MUL, ADD)
                    K.cp("vector", Cm[:, st, 0, :], tw[:, 0:128])
                    K.ts("vector", Cm[:, st, 1, :], tw[:, 0:128], -1.0, MUL)
                    K.ts("vector", tw[:, 128:256], cre, fz[:, st, 1:2], MUL)
                    K.stt("vector", tw[:, 128:256], cim, fz[:, st, 0:1], tw[:, 128:256], MUL, ADD)
                    K.ts("vector", Cm[:, st, 2, :], tw[:, 128:256], -1.0, MUL)
            BP.update(dict(sp=sp, rmag=rmag, EC2=EC2, ES2=ES2, Bm2=Bm2, Cm2=Cm2, fz2=fz2, cst=cst, h0=h0, NST=NST, NCC=NCC, KR=KR))
            if not lat:
                BP["wglu_b"] = wglu_b

        def branch_B(l):
            P.push_scope()
            sp, rmag, EC2, ES2, Bm2, Cm2, fz2, cst, h0 = (BP[k_] for k_ in ("sp", "rmag", "EC2", "ES2", "Bm2", "Cm2", "fz2", "cst", "h0"))
            NST, NCC, KR = BP["NST"], BP["NCC"], BP["KR"]
            if not lat:
                wglu_b = BP["wglu_b"]
            ytot_t = [P.sbuf("s5yt%d" % i, [128, 512], F32) for i in range(NCC)]
            NG4 = 2 if lat else 4
            WGr = Ring([P.sbuf("s5wg%d" % i, [128, NG4, 4 * S5T], F32) for i in range(2)])
            uT = P.sbuf("s5u", [128, NCC, L], BF16)
            yf = mixed if not lat else P.sbuf("s5yfL", [128, 1, L], BF16)
            for cc in range(NCC):
                w = load_group(l, "Bu%d" % cc)
                for tb in range(L // TB):
                    ps = fm_block(w, tb)
                    K.cp("scalar", uT[:, cc, blk(tb)], ps)
            for d in range(2):
                EC, ES, Bm, Cm, fz = EC2[:, d], ES2[:, d], Bm2[:, d], Cm2[:, d], fz2[:, d]
                fzv = fz[:, 0:NST, :]
                for si, (s0, sl) in enumerate(seqs):
                    if lat:
                        hr, hi = h0[:, d * NST:d * NST + NST, 0], h0[:, d * NST:d * NST + NST, 1]
                        t = sm.next()
                        n2, ta = t[:, 0:NST], t[:, 8:8 + NST]
                        u = sm.next()
                        tb_, tc = u[:, 0:NST], u[:, 8:8 + NST]
                        cstv = cst[:, 0:NST, :]
                        K.tt("vector", n2, fzv[:, :, 0], fzv[:, :, 0], MUL)
                        K.tt("vector", ta, fzv[:, :, 1], fzv[:, :, 1], MUL)
                        K.tt("vector", n2, n2, ta, ADD)
                        K.recip(n2, n2)
                        K.tt("vector", ta, hr, fzv[:, :, 0], MUL)
                        K.tt("vector", tb_, hi, fzv[:, :, 1], MUL)
                        K.tt("vector", ta, ta, tb_, ADD)
                        K.tt("vector", cstv[:, :, 0], ta, n2, MUL)
                        K.tt("vector", ta, hi, fzv[:, :, 0], MUL)
                        K.tt("vector", tb_, hr, fzv[:, :, 1], MUL)
                        K.tt("vector", ta, ta, tb_, SUB)
                        K.tt("vector", cstv[:, :, 1], ta, n2, MUL)
                    else:
                        K.memset("vector", cst, 0.0)
                    nch = sl // S5T
                    ytot = None
                    for c in (range(nch) if d == 0 else range(nch - 1, -1, -1)):
                        cols = slice(s0 + c * S5T, s0 + (c + 1) * S5T)
                        yps = [accs.next() for _ in range(NCC)]
                        for grp in range(NCC):
                            cc = grp
                            sts = list(range(4 * grp, 4 * grp + 4)) if not lat else [0, 1]
                            psbs, p12s, wgs, qs = {}, {}, {}, {}
                            tabs = {}
                            T_ = S5T
                            for st in sts:
                                psb = psr.next()
                                K.mm(psb[:, 0:T_], Bm[0:KR, st, 0, :], uT[0:KR, cc, cols])
                                K.mm(psb[:, T_:2 * T_], Bm[0:KR, st, 1, :], uT[0:KR, cc, cols])
                                psbs[st] = psb
                                tabs[st] = (EC[:, st, :].re("p (o t) -> p o t", o=1).bc([128, 2, T_]),
                                            ES[:, st, :].re("p (o t) -> p o t", o=1).bc([128, 2, T_]))
                            for st in sts:
                                bu3 = psbs[st][:, 0:2 * T_].re("p (r t) -> p r t", r=2)
                                if d == 1:
                                    bu3 = bu3[:, :, ::-1]
                                ecb, esb = tabs[st]
                                p1t = wf.next()
                                p2t = wf.next()
                                K.tt("vector", p1t[:, 0:2 * T_].re("p (r t) -> p r t", r=2), bu3, ecb, MUL)
                                K.tt("vector", p2t[:, 0:2 * T_].re("p (r t) -> p r t", r=2), bu3[:, ::-1, :], esb, MUL)
                                p12s[st] = (p1t, p2t)
                            WG = WGr.next()
                            for si_, st in enumerate(sts):
                                p1t, p2t = p12s[st]
                                wg = WG[:, si_, :]
                                K.tt("gpsimd", wg[:, 0:T_], p1t[:, 0:T_], p2t[:, 0:T_], ADD)
                                K.tt("gpsimd", wg[:, T_:2 * T_], p1t[:, T_:2 * T_], p2t[:, T_:2 * T_], SUB)
                                wgs[st] = wg
                            for st in sts:
                                wg = wgs[st]
                                rb = rmag[:, d * NST + st:d * NST + st + 1].bc([128, T_])
                                K.scan(wg[:, 2 * T_:3 * T_], rb, wg[:, 0:T_], cst[:, st, 0:1])
                                K.scan(wg[:, 3 * T_:4 * T_], rb, wg[:, T_:2 * T_], cst[:, st, 1:2])
                            for st in sts:
                                ecb, esb = tabs[st]
                                gg3 = wgs[st][:, 2 * T_:4 * T_].re("p (r t) -> p r t", r=2)
                                q1t = wb.next()
                                q2t = wb.next()
                                K.tt("gpsimd", q1t[:, 0:2 * T_].re("p (r t) -> p r t", r=2), gg3, ecb, MUL)
                                K.tt("vector", q2t[:, 0:2 * T_].re("p (r t) -> p r t", r=2), gg3, esb, MUL)
                                qs[st] = (q1t, q2t)
                            ns_ = len(sts)
                            s0_, s1_ = sts[0], sts[-1] + 1
                            gl = WG[:, :, 2 * T_:4 * T_].re("p s (r t) -> p s r t", r=2)[:, :, :, T_ - 1]
                            a = sm.next()
                            a1 = a[:, 0:2 * ns_].re("p (s r) -> p s r", r=2)
                            a2 = a[:, 8:8 + 2 * ns_].re("p (s r) -> p s r", r=2)
                            K.tt("vector", a1, gl, EC[:, s0_:s1_, T_ - 1:T_].bc([128, ns_, 2]), MUL)
                            K.tt("vector", a2, gl, ES[:, s0_:s1_, T_ - 1:T_].bc([128, ns_, 2]), MUL)
                            K.tt("vector", cst[:, s0_:s1_, 0], a1[:, :, 0], a2[:, :, 1], SUB)
                            K.tt("vector", cst[:, s0_:s1_, 1], a2[:, :, 0], a1[:, :, 1], ADD)
                            for st in sts:
                                q1t, q2t = qs[st]
                                K.mm(yps[cc][:, 0:T_], Cm[:, st, 0, :], q1t[:, 0:T_], start=(st == sts[0]), stop=False)
                                K.mm(yps[cc][:, 0:T_], Cm[:, st, 1, :], q2t[:, T_:2 * T_], start=False, stop=False)
                                K.mm(yps[cc][:, 0:T_], Cm[:, st, 2, :], q2t[:, 0:T_], start=False, stop=False)
                                K.mm(yps[cc][:, 0:T_], Cm[:, st, 2, :], q1t[:, T_:2 * T_], start=False, stop=(st == sts[-1]))
                        if d == 0:
                            for cc in range(NCC):
                                dsc = s5dl[:, l:l + 1] if lat else s5d[:, l, cc:cc + 1]
                                K.stt("vector", yf[:, cc, cols], uT[:, cc, cols], dsc, yps[cc][:, 0:S5T], MUL, ADD)
                        else:
                            lc = ((s0 + c * S5T) % BS)
                            if ytot is None:
                                ytot = ytot_t
                            for cc in range(NCC):
                                K.tt("vector", ytot[cc][:, lc:lc + S5T], yps[cc][:, 0:S5T][:, ::-1], yf[:, cc, cols], ADD)
                            if lc == 0:
                                b0 = s0 + c * S5T
                                gy = []
                                for cc in range(NCC):
                                    xx = ytot[cc]
                                    t = wf.next()
                                    K.tt("gpsimd", t[:, 0:BS], xx[:, 0:BS], xx[:, 0:BS], MUL)
                                    K.ts("vector", t[:, 0:BS], t[:, 0:BS], 0.044715, MUL, 1.0, ADD)
                                    K.tt("gpsimd", t[:, 0:BS], t[:, 0:BS], xx[:, 0:BS], MUL)
                                    K.act(t[:, 0:BS], t[:, 0:BS], AF.Sigmoid, scale=1.5957691216057308)
                                    if lat:
                                        K.tt("vector", mixh[0:64, 3, b0:b0 + BS], xx[0:64, 0:BS], t[0:64, 0:BS], MUL)
                                        continue
                                    gb = wb.next()
                                    K.tt("vector", gb[:, 0:BS], xx[:, 0:BS], t[:, 0:BS], MUL)
                                    gy.append(gb)
                                ytot = None
                                if lat:
                                    continue
                                pg = []
                                for fo in range(4):
                                    ps = psr.next()
                                    for cc in range(2):
                                        K.mm(ps[:, 0:BS], wglu_b[:, cc, fo * 128:(fo + 1) * 128], gy[cc][:, 0:BS],
                                             start=(cc == 0), stop=(cc == 1))
                                    pg.append(ps)
                                for ch in range(2):
                                    sgm = wf.next()
                                    K.act(sgm[:, 0:BS], pg[2 + ch][:, 0:BS], AF.Sigmoid)
                                    t = wf.next()
                                    K.tt("vector", t[:, 0:BS], pg[ch][:, 0:BS], sgm[:, 0:BS], MUL)
                                    w = load_group(l, "Bg%d" % ch)
                                    psg = fm_cols(w, b0, BS)
                                    sgB = wb.next()
                                    K.act(sgB[:, 0:BS], psg[:, 0:BS], AF.Silu)
                                    K.tt("vector", mixed[:, ch, b0:b0 + BS], t[:, 0:BS], sgB[:, 0:BS], MUL)
                    if d == 0 and l == 0 and si == len(seqs) - 1:
                        dump("s5yf", yf[:, :, 0:512], [128, 2, 512])
                    if not lat:
                        o = sm.next()
                        o3 = o[:, :].re("p (s r) -> p s r", r=2)
                        t = sm.next()
                        K.tt("vector", t[:, 0:8], cst[:, :, 0], fz[:, :, 0], MUL)
                        K.tt("vector", t[:, 8:16], cst[:, :, 1], fz[:, :, 1], MUL)
                        K.tt("vector", o3[:, :, 0], t[:, 0:8], t[:, 8:16], SUB)
                        K.tt("vector", t[:, 0:8], cst[:, :, 0], fz[:, :, 1], MUL)
                        K.tt("vector", t[:, 8:16], cst[:, :, 1], fz[:, :, 0], MUL)
                        K.tt("vector", o3[:, :, 1], t[:, 0:8], t[:, 8:16], ADD)
                        P.dma(dout["o_s5"][si, l, d].ap.rearrange("(s g) p r -> (g p) s r", g=2), o3.ap,
                              reads=[o], writes=[], out=True, eng="gpsimd", allow_slow_non_contiguous=True)
            P.pop_scope()

        def post_gather(l):
            agin = DR(nc.dram_tensor("agin%d" % l, [256, LL], BF16).ap())
            agout = DR(nc.dram_tensor("agout%d" % l, [1024, LL], BF16).ap())
            agin_tok = Buf("agin_tok%d" % l, None)
            agout_tok = Buf("agout_tok%d" % l, None)
            P.dma(agin.ap.rearrange("(s p) t -> p s t", p=64), mixh.ap, reads=[mixh], writes=[agin_tok], eng="gpsimd")
            P.collective(lambda e: e.collective_compute("AllGather", ALU.bypass, replica_groups=[[0, 1, 2, 3], [4, 5, 6, 7]],
                                                        ins=[agin.ap.opt()], outs=[agout.ap.opt()]),
                         reads=[agin_tok], writes=[agout_tok])
            P.push_scope()
            woL = P.sbuf("woL", [128, 10, D], BF16)
            wgl = P.sbuf("wgl", [64, 4, 512], BF16)
            PB = 256
            gyb = P.sbuf("gyb", [64, 4, PB], BF16)
            mB = P.sbuf("mB", [128, 2, PB], BF16)
            mcb = P.sbuf("mcb", [128, 8, PB], BF16)
            mcr = Ring([P.sbuf("mcj%d" % i, [128, 8, PB], BF16) for i in range(2)])
            gyr = Ring([P.sbuf("gyj%d" % i, [64, 4, PB], BF16) for i in range(2)])
            for chunk in range(10):
                s = wst.next()
                sv = s.re("p a b -> p (a b)")
                K.dma(sv, din["woutL"][l][:, chunk, :])
                for n in range(2):
                    gb = wf.next()
                    P.dma(gb.ap, gsc[l][v:v + 1, n * 512:(n + 1) * 512].ap.partition_broadcast(128), reads=[gsc_tok], writes=[gb])
                    K.tt("vector" if n == 0 else "gpsimd", woL[:, chunk, n * 512:(n + 1) * 512], gb, sv[:, n * 512:(n + 1) * 512], MUL)
            for r in range(4):
                s = wf.next()
                K.dma(s[0:64, :], din["wgluL"][l][:, r, :])
                K.cp("vector", wgl[:, r, :], s[0:64, :])
            g4 = agout.ap.rearrange("(r s p) t -> p r s t", r=4, s=4)
            c8 = agout.ap.rearrange("(c p) t -> p c t", p=128)
            for tb in range(512 // PB):
                sgBs = []
                for ch in range(2):
                    w = load_group(l, "Bg%d" % ch)
                    psg = psr.next()
                    for k in range(8):
                        K.mm(psg[:, 0:PB], w[:, k, :], hTo[:, k, tb * PB:(tb + 1) * PB], start=(k == 0), stop=(k == 7))
                    sgB = wb.next()
                    K.act(sgB[:, 0:PB], psg[:, 0:PB], AF.Silu)
                    sgBs.append(sgB)
                for j_ in range(4):
                    cj = j_ * 512 + tb * PB
                    gj = gyr.next()
                    P.dma(gj.ap, g4[:, :, 3, cj:cj + PB], reads=[agout_tok], writes=[gj])
                    mj = mcr.next()
                    P.dma(mj.ap, c8[:, :, cj:cj + PB], reads=[agout_tok], writes=[mj])
                    if j_ == 0:
                        K.ts("vector", gyb, gj, ohs[0:64, 0:1], MUL)
                        K.ts("vector", mcb, mj, ohs[:, 0:1], MUL)
                    else:
                        K.stt("gpsimd" if False else "vector", gyb, gj, ohs[0:64, j_:j_ + 1], gyb, MUL, ADD)
                        K.stt("vector", mcb, mj, ohs[:, j_:j_ + 1], mcb, MUL, ADD)
                pg = []
                for fo in range(4):
                    ps = psr.next()
                    for r in range(4):
                        K.mm(ps[:, 0:PB], wgl[:, r, fo * 128:(fo + 1) * 128], gyb[:, r, :], start=(r == 0), stop=(r == 3))
                    pg.append(ps)
                for ch in range(2):
                    sgm = wf.next()
                    K.act(sgm[:, 0:PB], pg[2 + ch][:, 0:PB], AF.Sigmoid)
                    t = wf.next()
                    K.tt("vector", t[:, 0:PB], pg[ch][:, 0:PB], sgm[:, 0:PB], MUL)
                    K.tt("gpsimd", mB[:, ch, :], t[:, 0:PB], sgBs[ch][:, 0:PB], MUL)
                for ii in range(PB // 128):
                    i = tb * (PB // 128) + ii
                    for n in range(2):
                        ps = psr.next()
                        for chunk in range(10):
                            lhs = mcb[:, chunk, ii * 128:(ii + 1) * 128] if chunk < 8 else mB[:, chunk - 8, ii * 128:(ii + 1) * 128]
                            K.mm(ps, lhs, woL[:, chunk, n * 512:(n + 1) * 512], start=(chunk == 0), stop=(chunk == 9))
                        K.tt("vector", xv[i][:, n * 512:(n + 1) * 512], ps, xv[i][:, n * 512:(n + 1) * 512], ADD)
            P.pop_scope()

        fns = dict(A=(branch_A, 0), B=(branch_B, 1), C=(branch_C, 2), D=(branch_D, 3))
        for l in range(nlayers):
            norm_mod(l)
            dump("hT_%s_%d" % ("lat" if lat else "ctx", l), hT[:, :, 0:512], [128, 8, 512])
            if "B" in branches and lat:
                b_prep(l)
            if lat:
                hT_load()
            for bn in branches:
                fn, bi = fns[bn]
                if bn == "B" and not lat:
                    b_prep(l)
                fn(l)
                if bn == "B":
                    P.pop_scope()
                if lat:
                    continue
                dump("mix%s_%s_%d" % (bn, "lat" if lat else "ctx", l), mixed[:, :, 0:512], [128, 2, 512])
                out_proj(l, bi)
            if lat:
                dump("mixh_%d" % l, mixh[:, :, 0:512], [64, 4, 512])
                post_gather(l)
        for i in range(ntx):
            r = rstd_of(xv[i], D, 1.0 / D)
            for n in range(2):
                o = wf.next()
                fn_ = wf.next()
                K.dma(fn_, din["fnorm"][:, n * 512:(n + 1) * 512])
                K.stt("vector", o, xv[i][:, n * 512:(n + 1) * 512], r, fn_, MUL, MUL)
                K.dma(ydst[i][:, n * 512:(n + 1) * 512], o, out=True, eng="gpsimd")
        P.pop_scope()

    for j in jobs:
        run_job(j == "lat")
    P.emit()
    return nc, dbg_shapes


_NC_CACHE = {}


def kernel(**inputs):
    if "nc" not in _NC_CACHE:
        _NC_CACHE["nc"] = build()[0]
    nc = _NC_CACHE["nc"]
    sh = _shared_inputs(inputs)
    in_maps = [_core_inputs(inputs, sh, c) for c in range(8)]
    res = run_bass_kernel_spmd(nc, in_maps, core_ids=list(range(8)))
    return assemble([r for r in res.results])


def assemble(rs):
    B, SEQ = 16, 256
    y_prompt = np.concatenate([r["y_c"].reshape(2, SEQ, D) for r in rs], axis=0)
    y_sample = np.stack([np.concatenate([rs[4 * s_ + r_]["y_l"].reshape(512, D) for r_ in range(4)], axis=0) for s_ in range(2)], axis=0)
    dk = np.concatenate([r["o_dk"] for r in rs], axis=0).reshape(B, NL, SEQ, 4, 64)
    dv = np.concatenate([r["o_dv"] for r in rs], axis=0).reshape(B, NL, SEQ, 4, 64)
    s5 = np.concatenate([r["o_s5"] for r in rs], axis=0)
    hg = np.concatenate([r["o_hg"] for r in rs], axis=0)
    ckv = np.concatenate([r["o_ckv"] for r in rs], axis=0)
    kr = np.concatenate([r["o_kr"] for r in rs], axis=0)
    f = lambda a: np.ascontiguousarray(a, dtype=np.float32)
    return tuple(f(a) for a in (y_prompt, y_sample, dk, dv, s5, hg, ckv, kr))
```

```python
import numpy as np
import concourse.bass as bass
import concourse.mybir as mybir
from concourse.bass_utils import run_bass_kernel_spmd

F32 = mybir.dt.float32
BF16 = mybir.dt.bfloat16
I32 = mybir.dt.int32
AF = mybir.ActivationFunctionType
ALU = mybir.AluOpType
AX = mybir.AxisListType

SAME_ENGINE_SYNC = True
CSTOP = 99
N_DMA_SEMS = 24


class Buf:
    def __init__(self, name, ap, parent=None):
        self.name = name
        self.ap = ap
        self.parent = parent
        self.children = []
        self.lastw = None
        self.readers = []
        if parent is not None:
            parent.children.append(self)

    def view(self, ap, name=None):
        return Buf(name or self.name + ".v", ap, parent=self)

    def __getitem__(self, idx):
        return Ref(self, self.ap[idx])

    @property
    def buf(self):
        return self

    def re(self, pat_, **kw):
        return Ref(self, self.ap.rearrange(pat_, **kw))

    def bc(self, shape):
        return Ref(self, self.ap.broadcast_to(list(shape)))

    def _up(self):
        b = self.parent
        while b is not None:
            yield b
            b = b.parent

    def _down(self):
        for c in self.children:
            yield c
            yield from c._down()


class Ref:
    __slots__ = ("buf", "ap")

    def __init__(self, buf, ap):
        self.buf = buf
        self.ap = ap

    def __getitem__(self, idx):
        return Ref(self.buf, self.ap[idx])

    def re(self, pat_, **kw):
        return Ref(self.buf, self.ap.rearrange(pat_, **kw))

    def bc(self, shape):
        return Ref(self.buf, self.ap.broadcast_to(list(shape)))


class Op:
    __slots__ = ("eng", "fn", "deps", "idx", "is_dma", "ticket", "waits", "signal", "out", "clk", "inc")

    def __init__(self, eng, fn, is_dma=False, out=False):
        self.inc = 16
        self.eng = eng
        self.fn = fn
        self.deps = set()
        self.is_dma = is_dma
        self.ticket = None
        self.waits = []
        self.signal = False
        self.out = out
        self.clk = None


class Prog:
    def __init__(self, nc):
        self.nc = nc
        self.ops = []
        self._ctx = []
        self.nsb = 0
        self.scopes = []
        self.scope_pending = []
        self.allbufs = []

    def sbuf(self, name, shape, dtype):
        self.nsb += 1
        cm = self.nc.sbuf_tensor("%s_%d" % (name, self.nsb), list(shape), dtype)
        t = cm.__enter__()
        self._ctx.append(cm)
        b = Buf(name, t.ap() if hasattr(t, "ap") and callable(getattr(t, "ap")) else t[:])
        b.init_deps = list(self.scope_pending)
        self.allbufs.append(b)
        return b

    def push_scope(self):
        self.scopes.append((len(self._ctx), len(self.allbufs)))

    def pop_scope(self):
        nctx, nb = self.scopes.pop()
        dead = self.allbufs[nb:]
        del self.allbufs[nb:]
        pend = set(self.scope_pending)
        for b in dead:
            for x in (b, *b._down()):
                if x.lastw is not None:
                    pend.add(x.lastw)
                pend.update(x.readers)
        last = {}
        keep = []
        for o in pend:
            if o.is_dma:
                keep.append(o)
            elif o.eng not in last or last[o.eng].idx < o.idx:
                last[o.eng] = o
        self.scope_pending = keep + list(last.values())
        while len(self._ctx) > nctx:
            self._ctx.pop().__exit__(None, None, None)

    def psum(self, name, shape, dtype):
        cm = self.nc.psum_tensor(name, list(shape), dtype)
        t = cm.__enter__()
        self._ctx.append(cm)
        return Buf(name, t.ap() if hasattr(t, "ap") and callable(getattr(t, "ap")) else t[:])

    def _track(self, op, reads, writes):
        for b in (*reads, *writes):
            r = b
            while r.parent is not None:
                r = r.parent
            idp = getattr(r, "init_deps", None)
            if idp:
                op.deps.update(idp)
        for b in reads:
            for x in (b, *b._up(), *b._down()):
                if x.lastw is not None:
                    op.deps.add(x.lastw)
        for b in writes:
            for x in (b, *b._up(), *b._down()):
                if x.lastw is not None:
                    op.deps.add(x.lastw)
                for r in x.readers:
                    op.deps.add(r)
        for b in reads:
            b.readers.append(op)
        for b in writes:
            b.lastw = op
            b.readers = []
            for x in b._down():
                x.lastw = None
                x.readers = []
        op.deps.discard(op)

    def op(self, eng, fn, reads=(), writes=()):
        o = Op(eng, fn)
        o.idx = len(self.ops)
        self._track(o, reads, writes)
        self.ops.append(o)
        return o

    def dma(self, out_ap, in_ap, reads=(), writes=(), eng="sync", out=False, **kw):
        o = Op(eng, lambda e: e.dma_start(out=out_ap, in_=in_ap, **kw), is_dma=True, out=out)
        o.idx = len(self.ops)
        self._track(o, reads, writes)
        self.ops.append(o)
        return o

    def collective(self, fn, reads=(), writes=()):
        o = Op("gpsimd", fn, is_dma=True)
        o.inc = 1
        o.idx = len(self.ops)
        self._track(o, reads, writes)
        self.ops.append(o)
        return o

    def emit(self):
        nc = self.nc
        engines = ["sync", "tensor", "vector", "scalar", "gpsimd"]
        for o in self.ops:
            for d in o.deps:
                if d.eng == o.eng and not d.is_dma and not o.is_dma:
                    if o.eng == "tensor" or not SAME_ENGINE_SYNC:
                        continue
                d.signal = True
        for o in self.ops:
            if o.is_dma:
                o.signal = True
        sems = {}
        ctxs = []

        def mksem(name):
            cm = nc.semaphore(name)
            s = cm.__enter__()
            ctxs.append(cm)
            return s

        for e in engines:
            sems[e] = mksem("s_" + e)
        dma_sems = {e: [mksem("d_%s_%d" % (e, i)) for i in range(N_DMA_SEMS)] for e in ("sync", "scalar", "gpsimd")}
        dma_cnt = {e: [0] * N_DMA_SEMS for e in dma_sems}
        dma_last = {e: [None] * N_DMA_SEMS for e in dma_sems}
        dma_rr = {e: 0 for e in dma_sems}
        cnt = {e: 0 for e in engines}
        cc_sems = []
        clock = {e: {} for e in engines}
        final_waits = []
        for o in self.ops:
            E = o.eng
            deps = set(o.deps)
            if o.is_dma and o.inc == 1:
                pass
            elif o.is_dma:
                k = dma_rr[E]
                dma_rr[E] = (k + 1) % N_DMA_SEMS
                if dma_last[E][k] is not None:
                    deps.add(dma_last[E][k])
                dma_last[E][k] = o
            ck = clock[E]
            for d in sorted(deps, key=lambda z: z.idx):
                if d.ticket is None:
                    continue
                if d.eng == E and not d.is_dma and not o.is_dma and (E == "tensor" or not SAME_ENGINE_SYNC):
                    continue
                skey, val = d.ticket
                if ck.get(skey, 0) >= val:
                    continue
                o.waits.append((skey, val))
                ck[skey] = val
                for k2, v2 in d.clk.items():
                    if ck.get(k2, 0) < v2:
                        ck[k2] = v2
            if o.is_dma and o.inc == 1:
                cc_sems.append(mksem("cc%d" % len(cc_sems)))
                o.ticket = (("c", len(cc_sems) - 1), 1)
            elif o.is_dma:
                dma_cnt[E][k] += o.inc
                o.ticket = (("d", E, k), dma_cnt[E][k])
                if o.out:
                    final_waits.append(o.ticket)
            elif o.signal:
                cnt[E] += 1
                o.ticket = (("e", E), cnt[E])
            o.clk = dict(ck)
            if o.ticket is not None and not o.is_dma:
                o.clk[o.ticket[0]] = o.ticket[1]

        def semof(skey):
            if skey[0] == "e":
                return sems[skey[1]]
            if skey[0] == "c":
                return cc_sems[skey[1]]
            return dma_sems[skey[1]][skey[2]]

        ops = self.ops
        with nc.Block() as block:
            def run(engname):
                def body(eng):
                    for o in ops:
                        if o.eng != engname:
                            continue
                        for skey, val in o.waits:
                            eng.wait_ge(semof(skey), val)
                        ins = o.fn(eng)
                        if o.ticket is not None:
                            if o.is_dma:
                                ins.then_inc(semof(o.ticket[0]), o.inc)
                            else:
                                ins.then_inc(semof(o.ticket[0]), 1)
                    if engname == "sync":
                        for skey, val in final_waits:
                            eng.wait_ge(semof(skey), val)
                return body
            block.sync(run("sync"))
            block.tensor(run("tensor"))
            block.vector(run("vector"))
            block.scalar(run("scalar"))
            block.gpsimd(run("gpsimd"))
        for cm in reversed(ctxs):
            cm.__exit__(None, None, None)
        for cm in reversed(self._ctx):
            cm.__exit__(None, None, None)


D = 1024
NL = 2
LC = 512
LL = 2048
PAST = 256
EPS = 1e-6
TB = 512
S5T = 256
HGC = 32
GRID_W = 64

OFF = dict(da_q=0, da_k=256, da_v=512, da_g=768, s5_u=1024, s5_g=1280, hg_q=1536, hg_ff=1792, hg_fb=2048,
           hg_i=2304, hg_g=2560, mla_cq=2816, mla_ckv=3008, mla_kr=3136, mla_g=3168)


def _rope_perm32():
    p = np.zeros(32, np.int64)
    for i in range(32):
        p[i] = i + 8 if (i % 16) < 8 else i - 8
    return p


def _groups():
    cols = []
    table = {}

    def add(name, src):
        src = list(src)
        assert len(src) <= 128
        src = src + [-1] * (128 - len(src))
        table[name] = len(cols)
        cols.extend(src)

    perm = _rope_perm32()
    for nm, off in (("Aq", OFF["da_q"]), ("Ak", OFF["da_k"])):
        for h in range(4):
            c1 = [off + h * 64 + d for d in range(32)]
            c2 = [off + h * 64 + 32 + d for d in range(32)]
            add("%s%d" % (nm, h), c1 + [-1] * 32 + c2 + [-1] * 32)
            p1 = [off + h * 64 + perm[d] for d in range(32)]
            p2 = [off + h * 64 + 32 + perm[d] for d in range(32)]
            add("%sp%d" % (nm, h), p1 + [-1] * 32 + p2 + [-1] * 32)
    for i in range(2):
        add("Av%d" % i, range(OFF["da_v"] + 128 * i, OFF["da_v"] + 128 * i + 128))
        add("Akt%d" % i, range(OFF["da_k"] + 128 * i, OFF["da_k"] + 128 * i + 128))
        add("Ag%d" % i, range(OFF["da_g"] + 128 * i, OFF["da_g"] + 128 * i + 128))
        add("Bu%d" % i, range(OFF["s5_u"] + 128 * i, OFF["s5_u"] + 128 * i + 128))
        add("Bg%d" % i, range(OFF["s5_g"] + 128 * i, OFF["s5_g"] + 128 * i + 128))
        add("Cq%d" % i, range(OFF["hg_q"] + 128 * i, OFF["hg_q"] + 128 * i + 128))
        add("Cf0%d" % i, range(OFF["hg_ff"] + 128 * i, OFF["hg_ff"] + 128 * i + 128))
        add("Cf1%d" % i, range(OFF["hg_fb"] + 128 * i, OFF["hg_fb"] + 128 * i + 128))
        add("Cv%d" % i, range(OFF["hg_i"] + 128 * i, OFF["hg_i"] + 128 * i + 128))
        add("Cg%d" % i, range(OFF["hg_g"] + 128 * i, OFF["hg_g"] + 128 * i + 128))
        add("Dg%d" % i, range(OFF["mla_g"] + 128 * i, OFF["mla_g"] + 128 * i + 128))
    add("Dcq0", range(OFF["mla_cq"], OFF["mla_cq"] + 128))
    add("Dcq1", range(OFF["mla_cq"] + 128, OFF["mla_cq"] + 192))
    add("Dckv", range(OFF["mla_ckv"], OFF["mla_ckv"] + 128))
    kr = [OFF["mla_kr"] + d for d in range(32)]
    add("Dkr", [-1] * 64 + kr)
    add("Dkrp", [-1] * 64 + [OFF["mla_kr"] + perm[d] for d in range(32)])
    add("Dkrt", kr)
    return table, np.array(cols, np.int64)


GT, GCOLS = _groups()
NCOLS = len(GCOLS)


def _groups_lat(r):
    cols = []
    table = {}

    def add(name, src):
        src = list(src)
        src = src + [-1] * (128 - len(src))
        table[name] = len(cols)
        cols.extend(src)

    perm = _rope_perm32()
    for nm, off in (("Aq", OFF["da_q"]), ("Ak", OFF["da_k"])):
        c1 = [off + r * 64 + d for d in range(32)]
        c2 = [off + r * 64 + 32 + d for d in range(32)]
        add(nm, c1 + [-1] * 32 + c2 + [-1] * 32)
        p1 = [off + r * 64 + perm[d] for d in range(32)]
        p2 = [off + r * 64 + 32 + perm[d] for d in range(32)]
        add(nm + "p", p1 + [-1] * 32 + p2 + [-1] * 32)
    for nm, key in (("Av", "da_v"), ("Ag", "da_g"), ("Bu", "s5_u"), ("Cq", "hg_q"), ("Cf0", "hg_ff"), ("Cf1", "hg_fb"),
                    ("Cv", "hg_i"), ("Cg", "hg_g"), ("Dg", "mla_g")):
        add(nm, range(OFF[key] + 64 * r, OFF[key] + 64 * r + 64))
    for i in range(2):
        add("Bg%d" % i, range(OFF["s5_g"] + 128 * i, OFF["s5_g"] + 128 * i + 128))
    add("Dcq0", range(OFF["mla_cq"], OFF["mla_cq"] + 128))
    add("Dcq1", range(OFF["mla_cq"] + 128, OFF["mla_cq"] + 192))
    add("Dckv", range(OFF["mla_ckv"], OFF["mla_ckv"] + 128))
    kr = [OFF["mla_kr"] + d for d in range(32)]
    add("Dkr", [-1] * 64 + kr)
    add("Dkrp", [-1] * 64 + [OFF["mla_kr"] + perm[d] for d in range(32)])
    return table, np.array(cols, np.int64)


LGT = _groups_lat(0)[0]
NCOLS_L = len(_groups_lat(0)[1])


def _rope_tables():
    t = np.arange(LL)
    row = (t // GRID_W).astype(np.float32)
    col = (t % GRID_W).astype(np.float32)
    inv = (10000.0 ** (-np.arange(8, dtype=np.float32) / 8)).astype(np.float32)
    C = np.ones((128, LL), np.float32)
    S = np.zeros((128, LL), np.float32)
    for base in (0, 64):
        for i in range(32):
            pos = row if i < 16 else col
            ang = pos * inv[i % 8]
            C[base + i] = np.cos(ang)
            S[base + i] = (-1.0 if (i % 16) < 8 else 1.0) * np.sin(ang)
    return C, S


def _consts():
    c = {}
    c["ident"] = np.eye(128, dtype=np.float32)
    s = np.arange(128)[:, None]
    t = np.arange(128)[None, :]
    same = (s // HGC) == (t // HGC)
    c["hmask0"] = (same & (s <= t)).astype(np.float32)
    c["hmask1"] = (same & (s >= t)).astype(np.float32)
    m4 = np.zeros((128, 4), np.float32)
    m4[np.arange(128), np.arange(128) // HGC] = 1.0
    c["m4"] = m4
    m01 = np.ones((128, TB), np.float32)
    m01[:, ::HGC] = 0.0
    c["m01"] = m01
    c["iota"] = np.broadcast_to(np.arange(1, S5T + 1, dtype=np.float32)[None, :], (128, S5T)).copy()
    bd = np.zeros((128, 128), np.float32)
    bd[:64, :64] = 1.0
    bd[64:, 64:] = 1.0
    c["bdones"] = bd
    C, S = _rope_tables()
    c["ropeC"] = C
    c["ropeS"] = S
    sel = np.zeros((2, 2, 128), np.float32)
    sel[0, 0] = 1.0
    sel[1, 1] = 1.0
    c["selc"] = sel
    return c


CONST_SHAPES = dict(ident=[128, 128], hmask0=[128, 128], hmask1=[128, 128], m4=[128, 4], m01=[128, TB],
                    iota=[128, S5T], bdones=[128, 128], ropeC=[128, LL], ropeS=[128, LL], selc=[2, 2, 128])


IN_SHAPES = dict(
    xc=[4, 128, D], xl=[4, 128, D], cvec=[128, 8, 2],
    wmod=[NL, 24, 128, 8, 128], bmodT=[NL, 128, 24], bmodg=[NL, 2, D],
    win=[NL, NCOLS // 128, 128, 8, 128], wout=[NL, 128, 8, D], wglu=[NL, 128, 2, 512],
    wuq=[NL, 128, 2, 768], mqn=[NL, 128, 2], wukv=[NL, 128, 512], mkvn=[NL, 128, 128], fnorm=[128, D],
    s5B=[NL, 2, 8, 2, 128, 128], s5C=[NL, 2, 8, 2, 128, 128], s5p=[NL, 128, 16, 3], s5d=[NL, 128, 2],
    hglb=[NL, 128, 4], hgn=[NL, 128, 1], dalam=[NL, 1, 128], dan=[NL, 128, 1],
    cdk=[NL, 2, 128, 256], cdv=[NL, 2, 128, 256], cs5=[NL, 128, 16, 2], chg=[NL, 2, 4, 64, 64],
    cckv=[NL, 2, 128, 128], ckr=[NL, 2, 128, 32],
)
IN_SHAPES.update(dict(
    winL=[NL, NCOLS_L // 128, 128, 8, 128], woutL=[NL, 128, 10, D], wgluL=[NL, 64, 4, 512], wuqL=[NL, 128, 2, 192], wukvL=[NL, 128, 128],
    s5BL=[NL, 2, 2, 2, 64, 128], s5CL=[NL, 2, 2, 2, 128, 64], s5pL=[NL, 128, 4, 3], s5dL=[NL, 64, 1], hglbL=[NL, 64, 2],
    cdkL=[NL, 2, 128, 64], cdvL=[NL, 2, 128, 64], cs5L=[NL, 128, 4, 2], chgL=[NL, 2, 64, 64], oh=[128, 4],
))
IN_SHAPES.update(CONST_SHAPES)
OUT_SHAPES = dict(
    y_c=[4, 128, D], y_l=[4, 128, D], o_dk=[2, NL, 256, 256], o_dv=[2, NL, 256, 256],
    o_s5=[2, NL, 2, 16, 64, 2], o_hg=[2, NL, 2, 4, 64, 64], o_ckv=[2, NL, 256, 128], o_kr=[2, NL, 256, 32],
)


def _shared_inputs(inp):
    f = lambda a: np.ascontiguousarray(np.asarray(a, dtype=np.float32))
    sh = {}
    w_in = f(inp["w_in"])
    wpad = np.concatenate([w_in, np.zeros((NL, D, 1), np.float32)], axis=2)
    win = wpad[:, :, GCOLS]
    sh["win"] = f(win.reshape(NL, 8, 128, NCOLS // 128, 128).transpose(0, 3, 2, 1, 4))
    sh["wmod"] = f(f(inp["w_mod"]).reshape(NL, 8, 128, 24, 128).transpose(0, 3, 2, 1, 4))
    bm = f(inp["b_mod"])
    sh["bmodT"] = f(bm.reshape(NL, 24, 128).transpose(0, 2, 1))
    sh["bmodg"] = f(np.broadcast_to(bm[:, None, 2 * D:3 * D], (NL, 2, D)))
    sh["wout"] = f(f(inp["w_out"]).reshape(NL, 8, 128, D).transpose(0, 2, 1, 3))
    sh["wglu"] = f(f(inp["s5_w_glu"]).reshape(NL, 2, 128, 512).transpose(0, 2, 1, 3))
    perm = _rope_perm32()
    wuq = f(inp["mla_w_uq"])
    cols_n = np.arange(384)
    cols_p = np.array([h * 96 + (j if j < 64 else 64 + perm[j - 64]) for h in range(4) for j in range(96)])
    wq = np.concatenate([wuq[:, :, cols_n], wuq[:, :, cols_p]], axis=2)
    wq = np.concatenate([wq, np.zeros((NL, 64, 768), np.float32)], axis=1)
    sh["wuq"] = f(wq.reshape(NL, 2, 128, 768).transpose(0, 2, 1, 3))
    qn = np.concatenate([f(inp["mla_q_norm"]), np.zeros((NL, 64), np.float32)], axis=1)
    sh["mqn"] = f(qn.reshape(NL, 2, 128).transpose(0, 2, 1))
    sh["wukv"] = f(inp["mla_w_ukv"])
    sh["mkvn"] = f(np.broadcast_to(f(inp["mla_kv_norm"])[:, None, :], (NL, 128, 128)))
    sh["fnorm"] = f(np.broadcast_to(f(inp["final_norm"])[None, :], (128, D)))
    bre, bim = f(inp["s5_b_re"]), f(inp["s5_b_im"])
    cre, cim = f(inp["s5_c_re"]), f(inp["s5_c_im"])
    sB = np.zeros((NL, 2, 8, 2, 128, 128), np.float32)
    sC = np.zeros((NL, 2, 8, 2, 128, 128), np.float32)
    for st in range(8):
        for gi in range(2):
            g = 2 * st + gi
            r0 = 16 * (g % 8)
            for ri, (bb, cc) in enumerate(((bre, cre), (bim, cim))):
                sB[:, :, st, ri, r0:r0 + 16, 64 * gi:64 * gi + 64] = bb[:, :, g].transpose(0, 1, 3, 2)
                sC[:, :, st, ri, 64 * gi:64 * gi + 64, r0:r0 + 16] = cc[:, :, g].transpose(0, 1, 3, 2)
    sh["s5B"], sh["s5C"] = sB, sC
    are, aim, ldt = f(inp["s5_a_re"]), f(inp["s5_a_im"]), f(inp["s5_log_dt"])
    sp = np.zeros((NL, 128, 16, 3), np.float32)
    for d in range(2):
        for st in range(8):
            for gi in range(2):
                g = 2 * st + gi
                sp[:, 64 * gi:64 * gi + 64, d * 8 + st, 0] = are[:, d, g]
                sp[:, 64 * gi:64 * gi + 64, d * 8 + st, 1] = aim[:, d, g]
                sp[:, 64 * gi:64 * gi + 64, d * 8 + st, 2] = ldt[:, d, g][:, None]
    sh["s5p"] = sp
    sh["s5d"] = f(f(inp["s5_d"]).reshape(NL, 2, 128).transpose(0, 2, 1))
    lb = f(inp["hg_lb"])
    sh["hglb"] = f(lb.reshape(NL, 2, 2, 128).transpose(0, 3, 1, 2).reshape(NL, 128, 4))
    sh["hgn"] = f(np.tile(f(inp["hg_norm"]), (1, 2))[:, :, None])
    sh["dalam"] = f(f(inp["da_lambda"]).reshape(NL, 1, 128))
    sh["dan"] = f(np.tile(f(inp["da_norm"]), (1, 2))[:, :, None])
    sh.update(_consts())
    return sh


def _core_inputs(inp, sh, core):
    f = lambda a: np.ascontiguousarray(np.asarray(a, dtype=np.float32))
    m = dict(sh)
    s = core // 4
    m["xc"] = f(np.asarray(inp["x_prompt"])[2 * core:2 * core + 2].reshape(4, 128, D))
    m["xl"] = f(np.asarray(inp["x_sample"])[s].reshape(4, 4, 128, D)[core % 4])
    cv = np.stack([f(inp["c_ctx"]), f(inp["c"])[s]], axis=-1)
    m["cvec"] = f(cv.reshape(8, 128, 2).transpose(1, 0, 2))
    m["cdk"] = f(np.asarray(inp["cache_diff_k"])[s].reshape(NL, 2, 128, 256))
    m["cdv"] = f(np.asarray(inp["cache_diff_v"])[s].reshape(NL, 2, 128, 256))
    st5 = f(np.asarray(inp["state_s5"])[s])
    m["cs5"] = f(st5.reshape(NL, 2, 8, 2, 64, 2).transpose(0, 3, 4, 1, 2, 5).reshape(NL, 128, 16, 2))
    m["chg"] = f(np.asarray(inp["state_hgrn"])[s])
    m["cckv"] = f(np.asarray(inp["cache_mla_ckv"])[s].reshape(NL, 2, 128, 128))
    m["ckr"] = f(np.asarray(inp["cache_mla_krope"])[s].reshape(NL, 2, 128, 32))
    r = core % 4
    w_in = f(inp["w_in"])
    wpad = np.concatenate([w_in, np.zeros((NL, D, 1), np.float32)], axis=2)
    lcols = _groups_lat(r)[1]
    m["winL"] = f(wpad[:, :, lcols].reshape(NL, 8, 128, NCOLS_L // 128, 128).transpose(0, 3, 2, 1, 4))
    wo = f(inp["w_out"])
    chunks = []
    z64 = np.zeros((NL, 64, D), np.float32)
    for rr in range(4):
        chunks.append(np.concatenate([wo[:, 64 * rr:64 * rr + 64], wo[:, 512 + 64 * rr:512 + 64 * rr + 64]], axis=1))
        chunks.append(np.concatenate([wo[:, 768 + 64 * rr:768 + 64 * rr + 64], z64], axis=1))
    chunks.append(wo[:, 256:384])
    chunks.append(wo[:, 384:512])
    m["woutL"] = f(np.stack(chunks, axis=2))
    m["wgluL"] = f(f(inp["s5_w_glu"]).reshape(NL, 4, 64, 512).transpose(0, 2, 1, 3))
    perm = _rope_perm32()
    wuq = f(inp["mla_w_uq"])
    cn = np.array([r * 96 + j for j in range(96)])
    cp_ = np.array([r * 96 + (j if j < 64 else 64 + perm[j - 64]) for j in range(96)])
    wq = np.concatenate([wuq[:, :, cn], wuq[:, :, cp_]], axis=2)
    wq = np.concatenate([wq, np.zeros((NL, 64, 192), np.float32)], axis=1)
    m["wuqL"] = f(wq.reshape(NL, 2, 128, 192).transpose(0, 2, 1, 3))
    m["wukvL"] = f(f(inp["mla_w_ukv"])[:, :, r * 128:(r + 1) * 128])
    bre, bim = f(inp["s5_b_re"]), f(inp["s5_b_im"])
    cre, cim = f(inp["s5_c_re"]), f(inp["s5_c_im"])
    sB = np.zeros((NL, 2, 2, 2, 64, 128), np.float32)
    sC = np.zeros((NL, 2, 2, 2, 128, 64), np.float32)
    are, aim, ldt = f(inp["s5_a_re"]), f(inp["s5_a_im"]), f(inp["s5_log_dt"])
    sp = np.zeros((NL, 128, 4, 3), np.float32)
    st5 = f(np.asarray(inp["state_s5"])[s])
    c5 = np.zeros((NL, 128, 4, 2), np.float32)
    for st in range(2):
        for gi in range(2):
            g = 4 * r + 2 * st + gi
            r0 = 16 * (g % 4)
            for ri, (bb, cc) in enumerate(((bre, cre), (bim, cim))):
                sB[:, :, st, ri, r0:r0 + 16, 64 * gi:64 * gi + 64] = bb[:, :, g].transpose(0, 1, 3, 2)
                sC[:, :, st, ri, 64 * gi:64 * gi + 64, r0:r0 + 16] = cc[:, :, g].transpose(0, 1, 3, 2)
            for d in range(2):
                sp[:, 64 * gi:64 * gi + 64, d * 2 + st, 0] = are[:, d, g]
                sp[:, 64 * gi:64 * gi + 64, d * 2 + st, 1] = aim[:, d, g]
                sp[:, 64 * gi:64 * gi + 64, d * 2 + st, 2] = ldt[:, d, g][:, None]
                c5[:, 64 * gi:64 * gi + 64, d * 2 + st, :] = st5[:, d, g]
    m["s5BL"], m["s5CL"], m["s5pL"], m["cs5L"] = sB, sC, sp, c5
    m["s5dL"] = f(f(inp["s5_d"])[:, 64 * r:64 * r + 64, None])
    m["hglbL"] = f(f(inp["hg_lb"])[:, :, 64 * r:64 * r + 64].transpose(0, 2, 1))
    m["cdkL"] = f(np.asarray(inp["cache_diff_k"])[s][:, :, r].reshape(NL, 2, 128, 64))
    m["cdvL"] = f(np.asarray(inp["cache_diff_v"])[s][:, :, r].reshape(NL, 2, 128, 64))
    m["chgL"] = f(np.asarray(inp["state_hgrn"])[s][:, :, r])
    oh = np.zeros((128, 4), np.float32)
    oh[:, r] = 1.0
    m["oh"] = oh
    for k, shp in IN_SHAPES.items():
        assert list(m[k].shape) == list(shp), (k, m[k].shape, shp)
    return {k: m[k] for k in IN_SHAPES}


class DR:
    buf = None

    def __init__(self, ap):
        self.ap = ap

    def __getitem__(self, idx):
        return DR(self.ap[idx])

    def re(self, pat_, **kw):
        return DR(self.ap.rearrange(pat_, **kw))


class Ring:
    def __init__(self, bufs):
        self.bufs = bufs
        self.i = 0

    def next(self):
        b = self.bufs[self.i % len(self.bufs)]
        self.i += 1
        return b


def _b(xs):
    return [x.buf for x in xs if x is not None and not isinstance(x, (int, float)) and x.buf is not None]


class KB:
    def __init__(self, P):
        self.P = P

    def tt(self, eng, o, a, b, op):
        self.P.op(eng, lambda e: e.tensor_tensor(o.ap, a.ap, b.ap, op=op), _b([a, b]), _b([o]))

    def ts(self, eng, o, a, s1, op0, s2=None, op1=None):
        v1 = s1.ap if hasattr(s1, "ap") else s1
        v2 = s2.ap if hasattr(s2, "ap") else s2
        if op1 is None:
            fn = lambda e: e.tensor_scalar(o.ap, a.ap, v1, None, op0=op0)
        else:
            fn = lambda e: e.tensor_scalar(o.ap, a.ap, v1, v2, op0=op0, op1=op1)
        self.P.op(eng, fn, _b([a, s1, s2]), _b([o]))

    def stt(self, eng, o, a, s, b, op0, op1):
        v = s.ap if hasattr(s, "ap") else s
        self.P.op(eng, lambda e: e.scalar_tensor_tensor(o.ap, a.ap, v, b.ap, op0=op0, op1=op1),
                  _b([a, s, b]), _b([o]))

    def act(self, o, a, func, bias=None, scale=None, accum=None):
        kw = {}
        if bias is not None:
            kw["bias"] = bias.ap if hasattr(bias, "ap") else bias
        if scale is not None:
            kw["scale"] = scale.ap if hasattr(scale, "ap") else scale
        if accum is not None:
            kw["accum_out"] = accum.ap
        self.P.op("scalar", lambda e: e.activation(o.ap, a.ap, func, **kw), _b([a, bias, scale]), _b([o, accum]))

    def cp(self, eng, o, a):
        if eng == "scalar":
            self.P.op(eng, lambda e: e.activation(o.ap, a.ap, AF.Copy), _b([a]), _b([o]))
        else:
            self.P.op(eng, lambda e: e.tensor_copy(o.ap, a.ap), _b([a]), _b([o]))

    def recip(self, o, a):
        self.P.op("vector", lambda e: e.reciprocal(o.ap, a.ap), _b([a]), _b([o]))

    def memset(self, eng, o, val):
        self.P.op(eng, lambda e: e.memset(o.ap, val), [], _b([o]))

    def mm(self, o, lhsT, rhs, start=True, stop=True):
        self.P.op("tensor", lambda e: e.matmul(o.ap, lhsT.ap, rhs.ap, start=start, stop=stop),
                  _b([lhsT, rhs]), _b([o]))

    def scan(self, o, d0, d1, init, op0=ALU.mult, op1=ALU.add):
        iv = init.ap if hasattr(init, "ap") else init
        self.P.op("vector", lambda e: e.tensor_tensor_scan(o.ap, d0.ap, d1.ap, iv, op0=op0, op1=op1),
                  _b([d0, d1, init]), _b([o]))

    def dma(self, o, a, out=False, eng="sync"):
        self.P.dma(o.ap, a.ap, reads=_b([a]), writes=_b([o]), out=out, eng=eng)


def lam_init(l):
    import math
    return 0.8 - 0.6 * math.exp(-0.3 * l)


def build(jobs=("ctx", "lat"), nlayers=NL, branches="ABCD", dbg=(), nwf=12):
    nc = bass.Bass("TRN2", target_bir_lowering=False)
    P = Prog(nc)
    K = KB(P)
    din = {k: DR(nc.dram_tensor(k, shp, F32, kind="ExternalInput").ap()) for k, shp in IN_SHAPES.items()}
    dout = {k: DR(nc.dram_tensor(k, shp, F32, kind="ExternalOutput").ap()) for k, shp in OUT_SHAPES.items()}
    dbg_shapes = {}

    def dump(name, ref, shape):
        if name in dbg:
            shape = list(shape)
            dbg_shapes[name] = shape
            dd = DR(nc.dram_tensor("dbg_" + name, shape, F32, kind="ExternalOutput").ap())
            t = P.sbuf("dbgt_" + name, shape, F32)
            K.cp("vector", t, ref)
            K.dma(dd, t, out=True, eng="gpsimd")

    PI = float(np.pi)
    ADD, SUB, MUL, MAX, MIN = ALU.add, ALU.subtract, ALU.mult, ALU.max, ALU.min
    psr = Ring([P.psum("ps%d" % i, [128, 512], F32) for i in range(6)])
    accs = Ring([P.psum("acc%d" % i, [128, 512], F32) for i in range(2)])
    wst = Ring([P.sbuf("wst%d" % i, [128, 8, 128], F32) for i in range(5)])
    wbf = Ring([P.sbuf("wbf%d" % i, [128, 8, 128], BF16) for i in range(5)])
    wf = Ring([P.sbuf("wf%d" % i, [128, 512], F32) for i in range(nwf)])
    wb = Ring([P.sbuf("wb%d" % i, [128, 512], BF16) for i in range(10)])
    xnr = Ring([P.sbuf("xn%d" % i, [128, D], BF16) for i in range(2)])
    sm = Ring([P.sbuf("sm%d" % i, [128, 16], F32) for i in range(16)])
    junk = P.sbuf("junk", [128, D], BF16)

    def cbf(name, shape, src):
        t = P.sbuf(name, shape, BF16)
        n = shape[1]
        for c0 in range(0, n, 512):
            w = min(512, n - c0)
            s = wf.next()
            K.dma(s[:, 0:w], src[:, c0:c0 + w])
            K.cp("vector", t[:, c0:c0 + w], s[:, 0:w])
        return t

    ident_b = cbf("ident_b", [128, 128], din["ident"])
    hmask = [cbf("hmask%d" % i, [128, 128], din["hmask%d" % i]) for i in range(2)]
    bdones = cbf("bdones", [128, 128], din["bdones"])
    ropeC = cbf("ropeC", [128, LL], din["ropeC"])
    ropeS = cbf("ropeS", [128, LL], din["ropeS"])
    ones_b = P.sbuf("ones_b", [128, 128], BF16)
    K.memset("vector", ones_b, 1.0)
    ones_f = P.sbuf("ones_f", [128, 128], F32)
    K.memset("vector", ones_f, 1.0)
    epsc = P.sbuf("epsc", [128, 1], F32)
    K.memset("vector", epsc, EPS)
    halfpi = P.sbuf("halfpi", [128, 1], F32)
    K.memset("vector", halfpi, float(np.pi / 2))
    zpad = P.sbuf("zpad", [128, 128], BF16)
    K.memset("vector", zpad, 0.0)
    m4 = P.sbuf("m4", [128, 4], F32)
    K.dma(m4, din["m4"])
    ohs = P.sbuf("ohs", [128, 4], F32)
    K.dma(ohs, din["oh"])
    m01 = P.sbuf("m01", [128, TB], F32)
    K.dma(m01, din["m01"])
    iota = P.sbuf("iota", [128, S5T], F32)
    K.dma(iota, din["iota"])
    modT = P.sbuf("modT", [128, NL, 2, 16], F32)
    gsc = DR(nc.dram_tensor("gsc", [NL, 2, D], F32).ap())
    gsc_tok = Buf("gsc_tok", None)
    bmT = P.sbuf("bmT", [128, NL, 24], F32)
    cs = P.sbuf("cs", [128, 8, 2], F32)
    for l in range(NL):
        K.dma(bmT[:, l, :], din["bmodT"][l])
    K.dma(cs, din["cvec"])
    K.act(cs, cs, AF.Silu)
    for l in range(nlayers):
        for j in range(16):
            s = wst.next()
            K.dma(s, din["wmod"][l][j])
            ps = psr.next()
            for k in range(8):
                K.mm(ps[:, 0:2], s[:, k, :], cs[:, k, :], start=(k == 0), stop=(k == 7))
            K.ts("vector", modT[:, l, :, j], ps[:, 0:2], bmT[:, l, j:j + 1], ADD, 1.0 if j >= 8 else 0.0, ADD)
        for n in range(8):
            s = wst.next()
            K.dma(s, din["wmod"][l][16 + n])
            ps = psr.next()
            for k in range(8):
                K.mm(ps[0:2, 0:128], cs[:, k, :], s[:, k, :], start=(k == 0), stop=(k == 7))
            bt = wf.next()
            K.dma(bt[0:2, 0:128], din["bmodg"][l][:, n * 128:(n + 1) * 128])
            K.tt("vector", bt[0:2, 128:256], ps[0:2, 0:128], bt[0:2, 0:128], ADD)
            P.dma(gsc[l][:, n * 128:(n + 1) * 128].ap, bt[0:2, 128:256].ap, reads=[bt], writes=[gsc_tok])
    lamr = P.sbuf("lamr", [1, NL, 128], F32)
    lamv = P.sbuf("lamv", [1, 8], F32)
    neglam = P.sbuf("neglam", [128, NL], F32)
    dan_s = P.sbuf("dan_s", [128, NL], F32)
    for l in range(NL):
        K.dma(lamr[:, l, :], din["dalam"][l])
        K.dma(dan_s[:, l:l + 1], din["dan"][l])
        K.ts("vector", dan_s[:, l:l + 1], dan_s[:, l:l + 1], 1.0 - lam_init(l), MUL)
        t = sm.next()
        e = sm.next()
        for c in range(2):
            K.tt("vector", lamr[:, l, 64 * c:64 * c + 32], lamr[:, l, 64 * c:64 * c + 32],
                 lamr[:, l, 64 * c + 32:64 * c + 64], MUL)
            P.op("vector", lambda e_, o=t[0:1, c:c + 1], a=lamr[:, l, 64 * c:64 * c + 32]: e_.reduce_sum(o.ap, a.ap, axis=AX.X),
                 _b([lamr]), _b([t]))
        K.act(e[0:1, 0:2], t[0:1, 0:2], AF.Exp)
        K.tt("vector", lamv[:, l:l + 1], e[0:1, 0:1], e[0:1, 1:2], SUB)
        K.ts("vector", lamv[:, l:l + 1], lamv[:, l:l + 1], -1.0, MUL, -lam_init(l), ADD)
    ps = psr.next()
    K.mm(ps[:, 0:NL], ones_f[0:1, :], lamv[0:1, 0:NL])
    K.cp("vector", neglam, ps[:, 0:NL])
    hgl = P.sbuf("hgl", [128, NL, 4], F32)
    lb = P.sbuf("lb", [128, NL, 4], F32)
    oml = P.sbuf("oml", [128, NL, 4], F32)
    noml = P.sbuf("noml", [128, NL, 4], F32)
    hgn = P.sbuf("hgn", [128, NL], F32)
    for l in range(NL):
        K.dma(hgl[:, l, :], din["hglb"][l])
        K.dma(hgn[:, l:l + 1], din["hgn"][l])
    K.memset("vector", lb, 0.0)
    K.tt("vector", lb[:, 1, :], hgl[:, 1, :], hgl[:, 0, :], SUB)
    K.act(lb[:, 1, :], lb[:, 1, :], AF.Sigmoid)
    K.ts("vector", oml, lb, -1.0, MUL, 1.0, ADD)
    K.ts("vector", noml, oml, -1.0, MUL)
    hglL = P.sbuf("hglL", [128, NL, 2], F32)
    lbL = P.sbuf("lbL", [128, NL, 2], F32)
    omlL = P.sbuf("omlL", [128, NL, 2], F32)
    nomlL = P.sbuf("nomlL", [128, NL, 2], F32)
    s5dl = P.sbuf("s5dl", [128, NL], F32)
    K.memset("vector", hglL, 0.0)
    K.memset("vector", s5dl, 0.0)
    for l in range(NL):
        K.dma(hglL[0:64, l, :], din["hglbL"][l])
        K.dma(s5dl[0:64, l:l + 1], din["s5dL"][l])
    K.memset("vector", lbL, 0.0)
    K.tt("vector", lbL[:, 1, :], hglL[:, 1, :], hglL[:, 0, :], SUB)
    K.act(lbL[:, 1, :], lbL[:, 1, :], AF.Sigmoid)
    K.ts("vector", omlL, lbL, -1.0, MUL, 1.0, ADD)
    K.ts("vector", nomlL, omlL, -1.0, MUL)
    mqn = P.sbuf("mqn", [128, NL, 2], F32)
    mkvn = P.sbuf("mkvn", [128, NL, 128], F32)
    s5d = P.sbuf("s5d", [128, NL, 2], F32)
    for l in range(NL):
        K.dma(mqn[:, l, :], din["mqn"][l])
        K.dma(mkvn[:, l, :], din["mkvn"][l])
        K.dma(s5d[:, l, :], din["s5d"][l])

    def run_job(lat):
        P.push_scope()
        nt = 16 if lat else 4
        L = nt * 128
        v = 1 if lat else 0
        seqs = [(0, L)] if lat else [(0, 256), (256, 256)]
        BS = 512 if lat else 256
        xsrc = din["xl"] if lat else din["xc"]
        ydst = dout["y_l"] if lat else dout["y_c"]
        ntx = 4 if lat else nt
        x = P.sbuf("x", [128, ntx, D], F32)
        xv = [x.view(x.ap[:, i, :], "x%d" % i) for i in range(ntx)]
        hT = P.sbuf("hT", [128, 8, L], BF16)
        hTo = P.sbuf("hTo", [128, 8, 512], BF16) if lat else hT
        mixed = P.sbuf("mixed", [128, 2, L], BF16) if not lat else None
        mixh = P.sbuf("mixh", [64, 4, L], BF16) if lat else None
        NH = 1 if lat else 4
        for i in range(ntx):
            K.dma(xv[i], xsrc[i])

        def rstd_of(xt, n, ss_scale):
            s = sm.next()
            K.act(junk[:, 0:n], xt, AF.Square, accum=s[:, 0:1])
            K.act(s[:, 1:2], s[:, 0:1], AF.Sqrt, bias=epsc, scale=ss_scale)
            K.recip(s[:, 2:3], s[:, 1:2])
            return s[:, 2:3]

        def load_group(l, name):
            if lat:
                if name.startswith("Cf"):
                    name = name[:3]
                elif name[:2] not in ("Bg", "Dc"):
                    name = name.rstrip("0123456789")
            off = (LGT if lat else GT)[name]
            s = wst.next()
            K.dma(s, din["winL" if lat else "win"][l][off // 128])
            w = wbf.next()
            K.cp("scalar", w, s)
            return w

        def fm_cols(w, c0, n, M=128):
            ps = psr.next()
            for k in range(8):
                K.mm(ps[0:M, 0:n], w[:, k, 0:M], hT[:, k, c0:c0 + n], start=(k == 0), stop=(k == 7))
            return ps

        def fm_block(w, tb, M=128):
            return fm_cols(w, tb * TB, TB, M)

        def tm_tile(w, i, N=128):
            ps = psr.next()
            for k in range(8):
                K.mm(ps[:, 0:N], hT[:, k, i * 128:(i + 1) * 128], w[:, k, 0:N], start=(k == 0), stop=(k == 7))
            return ps

        def blk(tb):
            return slice(tb * TB, (tb + 1) * TB)

        def tile_seq_rows(i):
            return i // 2, (i % 2) * 128

        def norm_mod(l):
            for i in range(ntx):
                r = rstd_of(xv[i], D, 1.0 / D)
                xn = xnr.next()
                K.ts("vector", xn, xv[i], r, MUL)
                for half in range(2):
                    ps = psr.next()
                    for kk in range(4):
                        k = half * 4 + kk
                        K.mm(ps[:, kk * 128:(kk + 1) * 128], xn[:, k * 128:(k + 1) * 128], ident_b)
                    for kk in range(4):
                        k = half * 4 + kk
                        eng = "vector" if kk % 2 == 0 else "gpsimd"
                        if eng == "gpsimd":
                            K.act(hTo[:, k, i * 128:(i + 1) * 128], ps[:, kk * 128:(kk + 1) * 128], AF.Identity,
                                  bias=modT[:, l, v, k:k + 1], scale=modT[:, l, v, 8 + k:9 + k])
                        else:
                            K.ts("vector", hTo[:, k, i * 128:(i + 1) * 128], ps[:, kk * 128:(kk + 1) * 128],
                                 modT[:, l, v, 8 + k:9 + k], MUL, modT[:, l, v, k:k + 1], ADD)
            if lat:
                ahin = DR(nc.dram_tensor("ahin%d" % l, [1024, 512], BF16).ap())
                ahout = DR(nc.dram_tensor("ahout%d" % l, [4096, 512], BF16).ap())
                ahin_tok = Buf("ahin_tok%d" % l, None)
                ahout_tok = Buf("ahout_tok%d" % l, None)
                P.dma(ahin.ap.rearrange("(k p) t -> p k t", p=128), hTo.ap, reads=[hTo], writes=[ahin_tok], eng="gpsimd")
                P.collective(lambda e: e.collective_compute("AllGather", ALU.bypass, replica_groups=[[0, 1, 2, 3], [4, 5, 6, 7]],
                                                            ins=[ahin.ap.opt()], outs=[ahout.ap.opt()]),
                             reads=[ahin_tok], writes=[ahout_tok])
                HTL["ahout"], HTL["tok"] = ahout, ahout_tok

        HTL = {}

        def hT_load():
            ahout, ahout_tok = HTL["ahout"], HTL["tok"]
            for r_ in range(4):
                P.dma(hT.ap[:, :, r_ * 512:(r_ + 1) * 512], ahout.ap[r_ * 1024:(r_ + 1) * 1024, :].rearrange("(k p) t -> p k t", p=128),
                      reads=[ahout_tok], writes=[hT])

        def out_proj(l, bi):
            wo = [wbf.next(), wbf.next()]
            for kc in range(2):
                s = wst.next()
                sv = s.re("p a b -> p (a b)")
                K.dma(sv, din["wout"][l][:, 2 * bi + kc, :])
                wov = wo[kc].re("p a b -> p (a b)")
                for n in range(2):
                    gb = wf.next()
                    P.dma(gb.ap, gsc[l][v:v + 1, n * 512:(n + 1) * 512].ap.partition_broadcast(128), reads=[gsc_tok], writes=[gb])
                    K.tt("vector", wov[:, n * 512:(n + 1) * 512], gb, sv[:, n * 512:(n + 1) * 512], MUL)
            for i in range(nt):
                for n in range(2):
                    ps = psr.next()
                    for kc in range(2):
                        K.mm(ps, mixed[:, kc, i * 128:(i + 1) * 128], wo[kc].re("p a b -> p (a b)")[:, n * 512:(n + 1) * 512],
                             start=(kc == 0), stop=(kc == 1))
                    K.tt("vector", xv[i][:, n * 512:(n + 1) * 512], ps, xv[i][:, n * 512:(n + 1) * 512], ADD)

        def attention(qT, kT, Vaug_of, Kdim, bases, scale, h, epilogue):
            pb = 64 * (h % 2)
            dn = 64 - pb
            QB = BS
            steps = []
            for (s0, sl) in seqs:
                if lat:
                    ktiles = list(range((L + PAST) // 128))
                else:
                    ktiles = list(range(s0 // 128, (s0 + sl) // 128))
                for qb in range(sl // QB):
                    q0 = s0 + qb * QB
                    for bi_, base in enumerate(bases):
                        for idx, kt in enumerate(ktiles):
                            steps.append((q0, bi_, base, idx, kt, len(ktiles)))
            pend = []
            state = {}

            def do_pv(item):
                (q0, bi_, base, idx, kt, nk), pt = item
                if idx == 0:
                    state["acc"] = accs.next()
                    if bi_ == 0:
                        state["ocs"] = []
                acc = state["acc"]
                K.mm(acc[:, 0:QB], Vaug_of(kt), pt[:, 0:QB], start=(idx == 0), stop=(idx == nk - 1))
                if idx == nk - 1:
                    rec = wf.next()
                    K.act(rec[dn:dn + 64, 0:QB], acc[dn:dn + 64, 0:QB], AF.Ln)
                    K.act(rec[dn:dn + 64, 0:QB], rec[dn:dn + 64, 0:QB], AF.Exp, scale=-1.0)
                    oc = wf.next()
                    K.tt("vector", oc[pb:pb + 64, 0:QB], acc[pb:pb + 64, 0:QB], rec[dn:dn + 64, 0:QB], MUL)
                    state["ocs"].append(oc)
                    if bi_ == len(bases) - 1:
                        epilogue(state["ocs"], q0, QB, pb)

            for stp in steps:
                (q0, bi_, base, idx, kt, nk) = stp
                pss = psr.next()
                K.mm(pss[:, 0:QB], kT[base:base + Kdim, kt * 128:(kt + 1) * 128], qT[base:base + Kdim, q0:q0 + QB])
                pt = wb.next()
                K.act(pt[:, 0:QB], pss[:, 0:QB], AF.Exp, scale=scale)
                pend.append((stp, pt))
                if len(pend) > 3:
                    do_pv(pend.pop(0))
            while pend:
                do_pv(pend.pop(0))

        def branch_A(l):
            P.push_scope()
            Lk = L + PAST if lat else L
            nkt = Lk // 128
            if not lat:
                for i2 in range(2):
                    for nm, dst in (("Av", "o_dv"), ("Akt", "o_dk")):
                        w = load_group(l, "%s%d" % (nm, i2))
                        for i in range(nt):
                            ps = tm_tile(w, i)
                            o = wf.next()
                            K.cp("scalar" if i % 2 else "vector", o[:, 0:128], ps[:, 0:128])
                            sq, r0 = tile_seq_rows(i)
                            K.dma(dout[dst][sq, l, r0:r0 + 128, i2 * 128:(i2 + 1) * 128], o[:, 0:128], out=True, eng="gpsimd")
            else:
                ckf = P.sbuf("ckf", [128, 2, 64], F32)
                cvf = P.sbuf("cvf", [128, 2, 64], F32)
                for j in range(2):
                    K.dma(cvf[:, j, :], din["cdvL"][l][j])
                    K.dma(ckf[:, j, :], din["cdkL"][l][j])
            sgA = P.sbuf("sgA", [128, L], BF16)
            Vh = P.sbuf("VhA", [128, nkt, 128], BF16)
            qT = P.sbuf("qT", [128, L], BF16)
            kT = P.sbuf("kT", [128, Lk], BF16)
            kpad = P.sbuf("kpad", [128, 128], BF16)
            K.memset("vector", kpad, 0.0)
            for h in range(NH):
                pbh = 64 * (h % 2)
                if h % 2 == 0:
                    w = load_group(l, "Ag%d" % (h // 2))
                    for tb in range(L // TB):
                        ps = fm_block(w, tb)
                        K.act(sgA[:, blk(tb)], ps, AF.Silu)
                w = load_group(l, "Av%d" % (h // 2))
                K.memset("gpsimd", Vh[:, :, 64 - pbh:128 - pbh], 1.0)
                for i in range(nt):
                    ps = psr.next()
                    for k in range(8):
                        K.mm(ps[:, 0:64], hT[:, k, i * 128:(i + 1) * 128], w[:, k, pbh:pbh + 64], start=(k == 0), stop=(k == 7))
                    K.cp("scalar" if i % 2 else "vector", Vh[:, i, pbh:pbh + 64], ps[:, 0:64])
                if lat:
                    for j in range(2):
                        K.cp("gpsimd", Vh[:, nt + j, pbh:pbh + 64], cvf[:, j, h * 64:(h + 1) * 64])
                for nm, dst in (("Aq", qT), ("Ak", kT)):
                    w = load_group(l, "%s%d" % (nm, h))
                    if lat:
                        wp = load_group(l, "%sp%d" % (nm, h))
                    for tb in range(L // TB):
                        ps = fm_block(w, tb)
                        if lat:
                            psp = fm_block(wp, tb)
                            t1 = wf.next()
                            t2 = wf.next()
                            K.tt("vector", t1, ps, ropeC[:, blk(tb)], MUL)
                            K.tt("vector", t2, psp, ropeS[:, blk(tb)], MUL)
                            K.tt("vector", dst[:, blk(tb)], t1, t2, ADD)
                        else:
                            K.cp("scalar", dst[:, blk(tb)], ps)
                if lat:
                    for j in range(2):
                        K.cp("vector", kpad[:, 0:32], ckf[:, j, h * 64:h * 64 + 32])
                        K.cp("vector", kpad[:, 64:96], ckf[:, j, h * 64 + 32:h * 64 + 64])
                        ps = psr.next()
                        K.mm(ps[:, 0:128], kpad, ident_b)
                        K.cp("vector", kT[:, L + j * 128:L + (j + 1) * 128], ps[:, 0:128])

                def epi(ocs, q0, QB, pb, h=h):
                    o = wf.next()
                    K.stt("vector", o[pb:pb + 64, 0:QB], ocs[1][pb:pb + 64, 0:QB], neglam[pb:pb + 64, l:l + 1],
                          ocs[0][pb:pb + 64, 0:QB], MUL, ADD)
                    sq = wb.next()
                    K.tt("vector", sq[pb:pb + 64, 0:QB], o[pb:pb + 64, 0:QB], o[pb:pb + 64, 0:QB], MUL)
                    pss = psr.next()
                    K.mm(pss[:, 0:QB], ones_b[pb:pb + 64, :], sq[pb:pb + 64, 0:QB])
                    rs = wf.next()
                    K.act(rs[pb:pb + 64, 0:QB], pss[pb:pb + 64, 0:QB], AF.Ln, bias=epsc[pb:pb + 64, :], scale=1.0 / 64)
                    K.act(rs[pb:pb + 64, 0:QB], rs[pb:pb + 64, 0:QB], AF.Exp, scale=-0.5)
                    K.tt("vector", o[pb:pb + 64, 0:QB], o[pb:pb + 64, 0:QB], rs[pb:pb + 64, 0:QB], MUL)
                    dst = mixh[0:64, 0, q0:q0 + QB] if lat else mixed[pb:pb + 64, h // 2, q0:q0 + QB]
                    K.stt("vector", dst, o[pb:pb + 64, 0:QB], dan_s[pb:pb + 64, l:l + 1],
                          sgA[pb:pb + 64, q0:q0 + QB], MUL, MUL)

                attention(qT, kT, lambda kt: Vh[:, kt, :], 64, (0, 64), 32 ** -0.5, h, epi)
            P.pop_scope()

        def branch_D(l):
            P.push_scope()
            Lk = L + PAST if lat else L
            nkt = Lk // 128
            wuq_b = P.sbuf("wuq_b", [128, 2, 192 if lat else 768], BF16)
            wukv_b = P.sbuf("wukv_b", [128, 128 if lat else 512], BF16)
            P.push_scope()
            wuq_f = P.sbuf("wuq_f", [128, 2, 768], F32)
            NQ = 192 if lat else 768
            K.dma(wuq_f[:, :, 0:NQ], din["wuqL" if lat else "wuq"][l])
            for kc in range(2):
                K.ts("vector", wuq_b[:, kc, 0:NQ], wuq_f[:, kc, 0:NQ], mqn[:, l, kc:kc + 1], MUL)
            s = wf.next()
            NKV = 128 if lat else 512
            K.dma(s[:, 0:NKV], din["wukvL" if lat else "wukv"][l])
            K.cp("vector", wukv_b[:, 0:NKV], s[:, 0:NKV])
            QPO = 96 if lat else 384
            P.pop_scope()
            cqb = P.sbuf("cqb", [128, 2, L], BF16)
            for kc in range(2):
                w = load_group(l, "Dcq%d" % kc)
                for tb in range(L // TB):
                    ps = fm_block(w, tb)
                    K.cp("scalar", cqb[:, kc, blk(tb)], ps)
            ckvT = P.sbuf("ckvT", [128, Lk], BF16)
            krT = P.sbuf("krT", [128, Lk], BF16)
            w = load_group(l, "Dckv")

            def ckv_tile(src_ps, col0, raw_is_psum=True, out_to=None):
                cnb = wb.next()
                if out_to is None:
                    K.cp("vector", cnb[:, 0:128], src_ps)
                else:
                    r = rstd_of(src_ps, 128, 1.0 / 128)
                    cn = wf.next()
                    K.stt("vector", cn[:, 0:128], src_ps, r, mkvn[:, l, :], MUL, MUL)
                    if out_to is not False:
                        K.dma(out_to, cn[:, 0:128], out=True, eng="gpsimd")
                    K.cp("gpsimd", cnb[:, 0:128], cn[:, 0:128])
                pst = psr.next()
                K.mm(pst[:, 0:128], cnb[:, 0:128], ident_b)
                K.cp("scalar", ckvT[:, col0:col0 + 128], pst[:, 0:128])

            for i in range(nt):
                ps = tm_tile(w, i)
                if lat:
                    ckv_tile(ps[:, 0:128], i * 128, out_to=False)
                else:
                    sq, r0 = tile_seq_rows(i)
                    ckv_tile(ps[:, 0:128], i * 128, out_to=dout["o_ckv"][sq, l, r0:r0 + 128, :])
            if lat:
                for j in range(2):
                    s = wf.next()
                    K.dma(s[:, 0:128], din["cckv"][l][j])
                    ckv_tile(s[:, 0:128], L + j * 128)
            w = load_group(l, "Dkr")
            if lat:
                wp = load_group(l, "Dkrp")
            for tb in range(L // TB):
                ps = fm_block(w, tb, M=96)
                if lat:
                    psp = fm_block(wp, tb, M=96)
                    t1 = wf.next()
                    t2 = wf.next()
                    K.tt("vector", t1[64:96, :], ps[64:96, :], ropeC[64:96, blk(tb)], MUL)
                    K.tt("vector", t2[64:96, :], psp[64:96, :], ropeS[64:96, blk(tb)], MUL)
                    K.tt("gpsimd", krT[64:96, blk(tb)], t1[64:96, :], t2[64:96, :], ADD)
                else:
                    K.cp("scalar", krT[64:96, blk(tb)], ps[64:96, :])
            if lat:
                kpad = P.sbuf("kpadD", [128, 128], BF16)
                K.memset("vector", kpad, 0.0)
                for j in range(2):
                    s = wf.next()
                    K.dma(s[:, 0:32], din["ckr"][l][j])
                    K.cp("vector", kpad[:, 64:96], s[:, 0:32])
                    pst = psr.next()
                    K.mm(pst[0:96, 0:128], kpad[:, 0:96], ident_b)
                    K.cp("vector", krT[64:96, L + j * 128:L + (j + 1) * 128], pst[64:96, 0:128])
            else:
                w = load_group(l, "Dkrt")
                for i in range(nt):
                    ps = tm_tile(w, i, N=32)
                    o = wf.next()
                    K.cp("vector", o[:, 0:32], ps[:, 0:32])
                    sq, r0 = tile_seq_rows(i)
                    K.dma(dout["o_kr"][sq, l, r0:r0 + 128, :], o[:, 0:32], out=True, eng="gpsimd")
            qT = P.sbuf("qTD", [128, L], BF16)
            kT = P.sbuf("kTD", [128, Lk], BF16)
            Vh = P.sbuf("VhD", [128, nkt, 128], BF16)
            sgD = P.sbuf("sgD", [128, L], BF16)
            for h in range(NH):
                pb = 64 * (h % 2)
                if h % 2 == 0:
                    w = load_group(l, "Dg%d" % (h // 2))
                    for tb in range(L // TB):
                        ps = fm_block(w, tb)
                        K.act(sgD[:, blk(tb)], ps, AF.Silu)
                for tb in range(L // TB):
                    sq0 = wb.next()
                    sq1 = wb.next()
                    K.act(sq0, cqb[:, 0, blk(tb)], AF.Square)
                    K.act(sq1, cqb[:, 1, blk(tb)], AF.Square)
                    pss = psr.next()
                    K.mm(pss[0:96, :], ones_b[:, 0:96], sq0, start=True, stop=False)
                    K.mm(pss[0:96, :], ones_b[:, 0:96], sq1, start=False, stop=True)
                    rq = wf.next()
                    K.act(rq[0:96, :], pss[0:96, :], AF.Ln, bias=epsc[0:96, :], scale=1.0 / 192)
                    K.act(rq[0:96, :], rq[0:96, :], AF.Exp, scale=-0.5)
                    z = psr.next()
                    for kc in range(2):
                        K.mm(z[0:96, :], wuq_b[:, kc, h * 96:(h + 1) * 96], cqb[:, kc, blk(tb)], start=(kc == 0), stop=(kc == 1))
                    K.tt("vector", qT[0:64, blk(tb)], z[0:64, :], rq[0:64, :], MUL)
                    if lat:
                        zp = psr.next()
                        for kc in range(2):
                            K.mm(zp[0:96, :], wuq_b[:, kc, QPO + h * 96:QPO + (h + 1) * 96], cqb[:, kc, blk(tb)],
                                 start=(kc == 0), stop=(kc == 1))
                        t1 = wf.next()
                        t2 = wf.next()
                        K.tt("vector", t1[64:96, :], z[64:96, :], ropeC[64:96, blk(tb)], MUL)
                        K.tt("vector", t2[64:96, :], zp[64:96, :], ropeS[64:96, blk(tb)], MUL)
                        K.tt("vector", t1[64:96, :], t1[64:96, :], t2[64:96, :], ADD)
                        K.tt("vector", qT[64:96, blk(tb)], t1[64:96, :], rq[64:96, :], MUL)
                    else:
                        K.tt("vector", qT[64:96, blk(tb)], z[64:96, :], rq[64:96, :], MUL)
                for c0 in range(0, Lk, TB):
                    n = min(TB, Lk - c0)
                    ps = psr.next()
                    K.mm(ps[0:64, 0:n], wukv_b[:, h * 128:h * 128 + 64], ckvT[:, c0:c0 + n])
                    K.cp("scalar", kT[0:64, c0:c0 + n], ps[0:64, 0:n])
                K.cp("vector", kT[64:96, :], krT[64:96, :])
                K.memset("gpsimd", Vh[:, :, 64 - pb:128 - pb], 1.0)
                for kt in range(nkt):
                    ps = psr.next()
                    K.mm(ps[:, 0:64], ckvT[:, kt * 128:(kt + 1) * 128], wukv_b[:, h * 128 + 64:h * 128 + 128])
                    K.cp("vector" if kt % 2 else "scalar", Vh[:, kt, pb:pb + 64], ps[:, 0:64])

                def epi(ocs, q0, QB, pb, h=h):
                    dst = mixh[0:64, 2, q0:q0 + QB] if lat else mixed[pb:pb + 64, h // 2, q0:q0 + QB]
                    K.tt("vector", dst, ocs[0][pb:pb + 64, 0:QB], sgD[pb:pb + 64, q0:q0 + QB], MUL)

                attention(qT, kT, lambda kt: Vh[:, kt, :], 96, (0,), 96 ** -0.5, h, epi)
            P.pop_scope()

        def branch_C(l):
            P.push_scope()
            qb_ = P.sbuf("hq", [128, L], BF16)
            vtok = P.sbuf("hv", [128, nt, 128], BF16)
            obuf = P.sbuf("ho", [128, L], F32)
            if lat:
                K.memset("gpsimd", obuf, 0.0)
            Sfr = Ring([P.sbuf("hSf%d" % i, [128, 128], F32) for i in range(3)])
            sbring = Ring([P.sbuf("hSb%d" % i, [128, 128], BF16) for i in range(6)])
            hsg, hf, hg_, hkk, hb, heb = [P.sbuf("h_t%d" % i, [128, 512], F32) for i in range(6)]
            hqt, hkt, hkh = [P.sbuf("h_b%d" % i, [128, 512], BF16) for i in range(3)]
            HH = 1 if lat else 2
            for hp in range(1 if lat else 2):
                w = load_group(l, "Cq%d" % hp)
                for tb in range(L // TB):
                    ps = fm_block(w, tb)
                    K.cp("scalar", qb_[:, blk(tb)], ps)
                w = load_group(l, "Cv%d" % hp)
                for i in range(nt):
                    ps = tm_tile(w, i)
                    K.cp("vector", vtok[:, i, :], ps[:, 0:128])
                for d in range(2):
                    w = load_group(l, "Cf%d%d" % (d, hp))
                    ci = 2 * d + hp
                    if lat:
                        lbv, omv, nomv = lbL[:, l, d:d + 1], omlL[:, l, d:d + 1], nomlL[:, l, d:d + 1]
                    else:
                        lbv, omv, nomv = lb[:, l, ci:ci + 1], oml[:, l, ci:ci + 1], noml[:, l, ci:ci + 1]
                    for si, (s0, sl) in enumerate(seqs):
                        Sf = Sfr.next()
                        K.memset("vector", Sf, 0.0)
                        if lat:
                            K.dma(Sf[0:64, 0:64], din["chgL"][l][d])
                        cur = {"sb": sbring.next(), "sf": Sf}
                        K.cp("scalar", cur["sb"], Sf)
                        nb = sl // BS
                        for bi in (range(nb) if d == 0 else range(nb - 1, -1, -1)):
                            c0 = s0 + bi * BS
                            ncks = BS // HGC
                            if CSTOP < 1:
                                continue
                            ps = fm_cols(w, c0, BS)
                            sg = hsg
                            K.act(sg[:, 0:BS], ps[:, 0:BS], AF.Sigmoid)
                            f = hf
                            K.ts("vector", f[:, 0:BS], sg[:, 0:BS], omv, MUL, lbv, ADD)
                            g = hg_
                            K.act(g[:, 0:BS], f[:, 0:BS], AF.Ln)
                            kk = hkk
                            K.ts("vector", kk[:, 0:BS], sg[:, 0:BS], nomv, MUL, omv, ADD)
                            b = hb
                            if d == 0:
                                K.scan(b[:, 0:BS], m01[:, 0:BS], g[:, 0:BS], 0.0)
                            else:
                                K.scan(b[:, BS - 1::-1] if False else b[:, 0:BS][:, ::-1], m01[:, 0:BS], g[:, 0:BS][:, ::-1], 0.0)
                            if CSTOP < 2:
                                continue
                            eb = heb
                            K.act(eb[:, 0:BS], b[:, 0:BS], AF.Exp)
                            enb = g
                            K.act(enb[:, 0:BS], b[:, 0:BS], AF.Exp, scale=-1.0)
                            if CSTOP < 2.1:
                                continue
                            qt = hqt
                            K.tt("vector", qt[:, 0:BS], qb_[:, c0:c0 + BS], eb[:, 0:BS], MUL)
                            kt_ = hkt
                            K.tt("gpsimd", kt_[:, 0:BS], kk[:, 0:BS], enb[:, 0:BS], MUL)
                            if CSTOP < 2.2:
                                continue
                            b3 = b[:, 0:BS].re("p (c j) -> p c j", j=HGC)
                            e = HGC - 1 if d == 0 else 0
                            d2 = f
                            K.tt("vector", d2[:, 0:BS].re("p (c j) -> p c j", j=HGC), b3[:, :, e:e + 1].bc([128, ncks, HGC]), b3, SUB)
                            K.act(d2[:, 0:BS], d2[:, 0:BS], AF.Exp)
                            if CSTOP < 2.3:
                                continue
                            kh = hkh
                            K.tt("gpsimd", kh[:, 0:BS], kk[:, 0:BS], d2[:, 0:BS], MUL)
                            ntile = BS // 128
                            if CSTOP < 3:
                                continue
                            for ti in (range(ntile) if d == 0 else range(ntile - 1, -1, -1)):
                                lo = ti * 128
                                gi = (c0 + lo) // 128
                                pss2 = [psr.next(), psr.next()]
                                for hh in range(HH):
                                    K.mm(pss2[hh][:, 0:128], kt_[64 * hh:64 * hh + 64, lo:lo + 128],
                                         qt[64 * hh:64 * hh + 64, lo:lo + 128])
                                if CSTOP < 3.1:
                                    continue
                                sc = wb.next()
                                for hh in range(HH):
                                    K.tt("vector", sc[:, hh * 128:(hh + 1) * 128], pss2[hh][:, 0:128], hmask[d], MUL)
                                if CSTOP < 4:
                                    continue
                                pst = psr.next()
                                K.mm(pst[:, 0:128], kh[:, lo:lo + 128], ident_b)
                                kexp = wb.next()
                                K.tt("vector", kexp[:, :].re("p (c k) -> p c k", c=4),
                                     pst[:, 0:128].re("p (o k) -> p o k", o=1).bc([128, 4, 128]),
                                     m4[:, :].re("p (c o) -> p c o", o=1).bc([128, 4, 128]), MUL)
                                if CSTOP < 5:
                                    continue
                                psu = psr.next()
                                for c in range(4):
                                    K.mm(psu[:, c * 128:(c + 1) * 128], kexp[:, c * 128:(c + 1) * 128], vtok[:, gi, :])
                                if CSTOP < 6:
                                    continue
                                corder = list(range(4)) if d == 0 else [3, 2, 1, 0]
                                Sbs = []
                                for cn, c in enumerate(corder):
                                    Sbs.append(cur["sb"])
                                    ce = lo + c * HGC + e
                                    Sfn = Sfr.next()
                                    K.stt("vector", Sfn, cur["sf"], eb[:, ce:ce + 1], psu[:, c * 128:(c + 1) * 128], MUL, ADD)
                                    cur["sf"] = Sfn
                                    nsb = sbring.next()
                                    K.cp("scalar", nsb, Sfn)
                                    cur["sb"] = nsb
                                psos = [psr.next(), psr.next()]
                                for cn, c in enumerate(corder):
                                    for hh in range(HH):
                                        pso = psos[hh]
                                        K.mm(pso[:, c * HGC:(c + 1) * HGC], vtok[:, gi, :],
                                             sc[:, hh * 128 + c * HGC:hh * 128 + (c + 1) * HGC], start=True, stop=False)
                                        K.mm(pso[:, c * HGC:(c + 1) * HGC], Sbs[cn][64 * hh:64 * hh + 64, :],
                                             qt[64 * hh:64 * hh + 64, lo + c * HGC:lo + (c + 1) * HGC], start=False, stop=True)
                                t0 = c0 + lo
                                for hh in range(HH):
                                    pbh = 64 * hh
                                    pso = psos[hh]
                                    if d == 0:
                                        K.cp("vector", obuf[pbh:pbh + 64, t0:t0 + 128], pso[pbh:pbh + 64, 0:128])
                                    else:
                                        K.tt("vector", obuf[pbh:pbh + 64, t0:t0 + 128], pso[pbh:pbh + 64, 0:128],
                                             obuf[pbh:pbh + 64, t0:t0 + 128], ADD)
                        if not lat:
                            seqi = si
                            for hh in range(2):
                                K.dma(dout["o_hg"][seqi, l, d, 2 * hp + hh], cur["sf"][64 * hh:64 * hh + 64, 64 * hh:64 * hh + 64],
                                      out=True, eng="gpsimd")
                w = load_group(l, "Cg%d" % hp)
                for tb in range(L // TB):
                    ps = fm_block(w, tb)
                    sg = wb.next()
                    K.act(sg, ps, AF.Silu)
                    sq = wb.next()
                    K.tt("gpsimd", sq, obuf[:, blk(tb)], obuf[:, blk(tb)], MUL)
                    pss = psr.next()
                    K.mm(pss, bdones, sq)
                    rs = wf.next()
                    K.act(rs, pss, AF.Ln, bias=epsc, scale=1.0 / 64)
                    K.act(rs, rs, AF.Exp, scale=-0.5)
                    t = wf.next()
                    K.tt("vector", t, obuf[:, blk(tb)], rs, MUL)
                    if lat:
                        K.stt("vector", mixh[0:64, 1, blk(tb)], t[0:64, :], hgn[0:64, l:l + 1], sg[0:64, :], MUL, MUL)
                    else:
                        K.stt("vector", mixed[:, hp, blk(tb)], t, hgn[:, l:l + 1], sg, MUL, MUL)
            P.pop_scope()

        BP = {}

        def b_prep(l):
            P.push_scope()
            C1 = 6.28125
            C2 = 2.0 * PI - C1
            NST = 2 if lat else 8
            NCC = 1 if lat else 2
            KR = 64 if lat else 128
            sp = P.sbuf("s5sp", [128, 16, 3], F32)
            K.dma(sp[:, 0:2 * NST, :], din["s5pL" if lat else "s5p"][l])
            step = P.sbuf("s5step", [128, 16], F32)
            th = P.sbuf("s5th", [128, 16], F32)
            rmag = P.sbuf("s5mag", [128, 16], F32)
            N2 = 2 * NST
            K.act(step[:, 0:N2], sp[:, 0:N2, 2], AF.Exp)
            K.tt("vector", th[:, 0:N2], sp[:, 0:N2, 1], step[:, 0:N2], MUL)
            K.tt("vector", rmag[:, 0:N2], sp[:, 0:N2, 0], step[:, 0:N2], MUL)
            K.act(rmag[:, 0:N2], rmag[:, 0:N2], AF.Exp)
            EC2 = P.sbuf("s5EC", [128, 2, NST, S5T], F32)
            ES2 = P.sbuf("s5ES", [128, 2, NST, S5T], F32)
            Bm2 = P.sbuf("s5Bm", [128, 2, NST, 2, 128], BF16)
            Cm2 = P.sbuf("s5Cm", [128, 2, NST, 3, 128], BF16)
            fz2 = P.sbuf("s5f", [128, 2, 8, 4], F32)
            cst = P.sbuf("s5cst", [128, 8, 2], F32)
            h0 = P.sbuf("s5h0", [128, 16, 2], F32)
            if lat:
                K.dma(h0[:, 0:4, :], din["cs5L"][l])
            else:
                wglu_b = P.sbuf("wglu_b", [128, 2, 512], BF16)
                s = wst.next()
                sv = s.re("p a b -> p (a b)")
                K.dma(sv, din["wglu"][l].re("p a b -> p (a b)"))
                K.cp("vector", wglu_b[:, :, :].re("p a b -> p (a b)"), sv)
            for d in range(2):
                EC, ES, Bm, Cm, fz = EC2[:, d], ES2[:, d], Bm2[:, d], Cm2[:, d], fz2[:, d]
                P.push_scope()
                kint = P.sbuf("s5ki", [128, 512], I32)
                SH = min(512 // S5T, NST)
                for half in range(NST // SH):
                    ang = wf.next()
                    a3 = ang[:, 0:SH * S5T].re("p (s t) -> p s t", s=SH)
                    K.tt("vector", a3, iota[:, :].re("p (o t) -> p o t", o=1).bc([128, SH, S5T]),
                         th[:, d * NST + half * SH:d * NST + half * SH + SH].re("p (s o) -> p s o", o=1).bc([128, SH, S5T]), MUL)
                    W_ = SH * S5T
                    kf = wf.next()
                    K.ts("vector", kf[:, 0:W_], ang[:, 0:W_], 1.0 / (2 * PI), MUL)
                    K.cp("vector", kint[:, 0:W_], kf[:, 0:W_])
                    K.cp("vector", kf[:, 0:W_], kint[:, 0:W_])
                    xx = wf.next()
                    K.stt("vector", xx[:, 0:W_], kf[:, 0:W_], -C1, ang[:, 0:W_], MUL, ADD)
                    K.stt("vector", xx[:, 0:W_], kf[:, 0:W_], -C2, xx[:, 0:W_], MUL, ADD)
                    K.ts("vector", xx[:, 0:W_], xx[:, 0:W_], PI, MIN, -PI, MAX)
                    K.act(ES[:, half * SH:half * SH + SH, :].re("p s t -> p (s t)"), xx[:, 0:W_], AF.Sin)
                    K.act(kf[:, 0:W_], xx[:, 0:W_], AF.Sin, scale=0.5)
                    K.tt("vector", kf[:, 0:W_], kf[:, 0:W_], kf[:, 0:W_], MUL)
                    K.ts("vector", EC[:, half * SH:half * SH + SH, :].re("p s t -> p (s t)"), kf[:, 0:W_], -2.0, MUL, 1.0, ADD)
                P.pop_scope()
                are = sp[:, d * NST:d * NST + NST, 0]
                aim = sp[:, d * NST:d * NST + NST, 1]
                mg = rmag[:, d * NST:d * NST + NST]
                t = sm.next()
                abr, abi, den, t1, t2 = t[:, 0:NST], t[:, 8:8 + NST], None, None, None
                u = sm.next()
                den, t1 = u[:, 0:NST], u[:, 8:8 + NST]
                u2 = sm.next()
                t2, t3 = u2[:, 0:NST], u2[:, 8:8 + NST]
                fzv = fz[:, 0:NST, :]
                K.tt("vector", abr, mg, EC[:, 0:NST, 0], MUL)
                K.tt("vector", abi, mg, ES[:, 0:NST, 0], MUL)
                K.ts("vector", abr, abr, -1.0, ADD)
                K.tt("vector", den, are, are, MUL)
                K.tt("vector", t1, aim, aim, MUL)
                K.tt("vector", den, den, t1, ADD)
                K.recip(den, den)
                K.tt("vector", t1, abr, are, MUL)
                K.tt("vector", t2, abi, aim, MUL)
                K.tt("vector", t1, t1, t2, ADD)
                K.tt("vector", fzv[:, :, 0], t1, den, MUL)
                K.tt("vector", t1, abi, are, MUL)
                K.tt("vector", t2, abr, aim, MUL)
                K.tt("vector", t1, t1, t2, SUB)
                K.tt("vector", fzv[:, :, 1], t1, den, MUL)
                K.ts("vector", fzv[:, :, 2], fzv[:, :, 1], -1.0, MUL)
                for st in range(NST):
                    s = wst.next()
                    sv = s.re("p a b -> p (a b)")
                    if lat:
                        K.dma(sv[0:64, 0:256].re("p (r c) -> p r c", r=2), din["s5BL"][l][d][st].re("r p c -> p r c"))
                        K.cp("gpsimd", Bm[0:64, st, :, :], sv[0:64, 0:256].re("p (r c) -> p r c", r=2))
                        K.memset("vector", sv[:, 256:512], 0.0)
                        K.dma(sv[:, 256:512].re("p (r c) -> p r c", r=2)[:, :, 0:64], din["s5CL"][l][d][st].re("r p c -> p r c"))
                    else:
                        K.dma(sv[:, 0:256].re("p (r c) -> p r c", r=2), din["s5B"][l][d][st].re("r p c -> p r c"))
                        K.cp("gpsimd", Bm[:, st, :, :], sv[:, 0:256].re("p (r c) -> p r c", r=2))
                        K.dma(sv[:, 256:512].re("p (r c) -> p r c", r=2), din["s5C"][l][d][st].re("r p c -> p r c"))
                    cre, cim = sv[:, 256:384], sv[:, 384:512]
                    tw = wf.next()
                    K.ts("vector", tw[:, 0:128], cre, fz[:, st, 0:1], MUL)
                    K.stt("vector", tw[:, 0:128], cim, fz[:, st, 2:3], tw[:, 0:128], MUL, ADD)
                    K.cp("vector", Cm[:, st, 0, :], tw[:, 0:128])
                    K.ts("vector", Cm[:, st, 1, :], tw[:, 0:128], -1.0, MUL)
                    K.ts("vector", tw[:, 128:256], cre, fz[:, st, 1:2], MUL)
                    K.stt("vector", tw[:, 128:256], cim, fz[:, st, 0:1], tw[:, 128:256], MUL, ADD)
                    K.ts("vector", Cm[:, st, 2, :], tw[:, 128:256], -1.0, MUL)
            BP.update(dict(sp=sp, rmag=rmag, EC2=EC2, ES2=ES2, Bm2=Bm2, Cm2=Cm2, fz2=fz2, cst=cst, h0=h0, NST=NST, NCC=NCC, KR=KR))
            if not lat:
                BP["wglu_b"] = wglu_b

        def branch_B(l):
            P.push_scope()
            sp, rmag, EC2, ES2, Bm2, Cm2, fz2, cst, h0 = (BP[k_] for k_ in ("sp", "rmag", "EC2", "ES2", "Bm2", "Cm2", "fz2", "cst", "h0"))
            NST, NCC, KR = BP["NST"], BP["NCC"], BP["KR"]
            if not lat:
                wglu_b = BP["wglu_b"]
            ytot_t = [P.sbuf("s5yt%d" % i, [128, 512], F32) for i in range(NCC)]
            NG4 = 2 if lat else 4
            WGr = Ring([P.sbuf("s5wg%d" % i, [128, NG4, 4 * S5T], F32) for i in range(2)])
            uT = P.sbuf("s5u", [128, NCC, L], BF16)
            yf = mixed if not lat else P.sbuf("s5yfL", [128, 1, L], BF16)
            for cc in range(NCC):
                w = load_group(l, "Bu%d" % cc)
                for tb in range(L // TB):
                    ps = fm_block(w, tb)
                    K.cp("scalar", uT[:, cc, blk(tb)], ps)
            for d in range(2):
                EC, ES, Bm, Cm, fz = EC2[:, d], ES2[:, d], Bm2[:, d], Cm2[:, d], fz2[:, d]
                fzv = fz[:, 0:NST, :]
                for si, (s0, sl) in enumerate(seqs):
                    if lat:
                        hr, hi = h0[:, d * NST:d * NST + NST, 0], h0[:, d * NST:d * NST + NST, 1]
                        t = sm.next()
                        n2, ta = t[:, 0:NST], t[:, 8:8 + NST]
                        u = sm.next()
                        tb_, tc = u[:, 0:NST], u[:, 8:8 + NST]
                        cstv = cst[:, 0:NST, :]
                        K.tt("vector", n2, fzv[:, :, 0], fzv[:, :, 0], MUL)
                        K.tt("vector", ta, fzv[:, :, 1], fzv[:, :, 1], MUL)
                        K.tt("vector", n2, n2, ta, ADD)
                        K.recip(n2, n2)
                        K.tt("vector", ta, hr, fzv[:, :, 0], MUL)
                        K.tt("vector", tb_, hi, fzv[:, :, 1], MUL)
                        K.tt("vector", ta, ta, tb_, ADD)
                        K.tt("vector", cstv[:, :, 0], ta, n2, MUL)
                        K.tt("vector", ta, hi, fzv[:, :, 0], MUL)
                        K.tt("vector", tb_, hr, fzv[:, :, 1], MUL)
                        K.tt("vector", ta, ta, tb_, SUB)
                        K.tt("vector", cstv[:, :, 1], ta, n2, MUL)
                    else:
                        K.memset("vector", cst, 0.0)
                    nch = sl // S5T
                    ytot = None
                    for c in (range(nch) if d == 0 else range(nch - 1, -1, -1)):
                        cols = slice(s0 + c * S5T, s0 + (c + 1) * S5T)
                        yps = [accs.next() for _ in range(NCC)]
                        for grp in range(NCC):
                            cc = grp
                            sts = list(range(4 * grp, 4 * grp + 4)) if not lat else [0, 1]
                            psbs, p12s, wgs, qs = {}, {}, {}, {}
                            tabs = {}
                            T_ = S5T
                            for st in sts:
                                psb = psr.next()
                                K.mm(psb[:, 0:T_], Bm[0:KR, st, 0, :], uT[0:KR, cc, cols])
                                K.mm(psb[:, T_:2 * T_], Bm[0:KR, st, 1, :], uT[0:KR, cc, cols])
                                psbs[st] = psb
                                tabs[st] = (EC[:, st, :].re("p (o t) -> p o t", o=1).bc([128, 2, T_]),
                                            ES[:, st, :].re("p (o t) -> p o t", o=1).bc([128, 2, T_]))
                            for st in sts:
                                bu3 = psbs[st][:, 0:2 * T_].re("p (r t) -> p r t", r=2)
                                if d == 1:
                                    bu3 = bu3[:, :, ::-1]
                                ecb, esb = tabs[st]
                                p1t = wf.next()
                                p2t = wf.next()
                                K.tt("vector", p1t[:, 0:2 * T_].re("p (r t) -> p r t", r=2), bu3, ecb, MUL)
                                K.tt("vector", p2t[:, 0:2 * T_].re("p (r t) -> p r t", r=2), bu3[:, ::-1, :], esb, MUL)
                                p12s[st] = (p1t, p2t)
                            WG = WGr.next()
                            for si_, st in enumerate(sts):
                                p1t, p2t = p12s[st]
                                wg = WG[:, si_, :]
                                K.tt("gpsimd", wg[:, 0:T_], p1t[:, 0:T_], p2t[:, 0:T_], ADD)
                                K.tt("gpsimd", wg[:, T_:2 * T_], p1t[:, T_:2 * T_], p2t[:, T_:2 * T_], SUB)
                                wgs[st] = wg
                            for st in sts:
                                wg = wgs[st]
                                rb = rmag[:, d * NST + st:d * NST + st + 1].bc([128, T_])
                                K.scan(wg[:, 2 * T_:3 * T_], rb, wg[:, 0:T_], cst[:, st, 0:1])
                                K.scan(wg[:, 3 * T_:4 * T_], rb, wg[:, T_:2 * T_], cst[:, st, 1:2])
                            for st in sts:
                                ecb, esb = tabs[st]
                                gg3 = wgs[st][:, 2 * T_:4 * T_].re("p (r t) -> p r t", r=2)
                                q1t = wb.next()
                                q2t = wb.next()
                                K.tt("gpsimd", q1t[:, 0:2 * T_].re("p (r t) -> p r t", r=2), gg3, ecb, MUL)
                                K.tt("vector", q2t[:, 0:2 * T_].re("p (r t) -> p r t", r=2), gg3, esb, MUL)
                                qs[st] = (q1t, q2t)
                            ns_ = len(sts)
                            s0_, s1_ = sts[0], sts[-1] + 1
                            gl = WG[:, :, 2 * T_:4 * T_].re("p s (r t) -> p s r t", r=2)[:, :, :, T_ - 1]
                            a = sm.next()
                            a1 = a[:, 0:2 * ns_].re("p (s r) -> p s r", r=2)
                            a2 = a[:, 8:8 + 2 * ns_].re("p (s r) -> p s r", r=2)
                            K.tt("vector", a1, gl, EC[:, s0_:s1_, T_ - 1:T_].bc([128, ns_, 2]), MUL)
                            K.tt("vector", a2, gl, ES[:, s0_:s1_, T_ - 1:T_].bc([128, ns_, 2]), MUL)
                            K.tt("vector", cst[:, s0_:s1_, 0], a1[:, :, 0], a2[:, :, 1], SUB)
                            K.tt("vector", cst[:, s0_:s1_, 1], a2[:, :, 0], a1[:, :, 1], ADD)
                            for st in sts:
                                q1t, q2t = qs[st]
                                K.mm(yps[cc][:, 0:T_], Cm[:, st, 0, :], q1t[:, 0:T_], start=(st == sts[0]), stop=False)
                                K.mm(yps[cc][:, 0:T_], Cm[:, st, 1, :], q2t[:, T_:2 * T_], start=False, stop=False)
                                K.mm(yps[cc][:, 0:T_], Cm[:, st, 2, :], q2t[:, 0:T_], start=False, stop=False)
                                K.mm(yps[cc][:, 0:T_], Cm[:, st, 2, :], q1t[:, T_:2 * T_], start=False, stop=(st == sts[-1]))
                        if d == 0:
                            for cc in range(NCC):
                                dsc = s5dl[:, l:l + 1] if lat else s5d[:, l, cc:cc + 1]
                                K.stt("vector", yf[:, cc, cols], uT[:, cc, cols], dsc, yps[cc][:, 0:S5T], MUL, ADD)
                        else:
                            lc = ((s0 + c * S5T) % BS)
                            if ytot is None:
                                ytot = ytot_t
                            for cc in range(NCC):
                                K.tt("vector", ytot[cc][:, lc:lc + S5T], yps[cc][:, 0:S5T][:, ::-1], yf[:, cc, cols], ADD)
                            if lc == 0:
                                b0 = s0 + c * S5T
                                gy = []
                                for cc in range(NCC):
                                    xx = ytot[cc]
                                    t = wf.next()
                                    K.tt("gpsimd", t[:, 0:BS], xx[:, 0:BS], xx[:, 0:BS], MUL)
                                    K.ts("vector", t[:, 0:BS], t[:, 0:BS], 0.044715, MUL, 1.0, ADD)
                                    K.tt("gpsimd", t[:, 0:BS], t[:, 0:BS], xx[:, 0:BS], MUL)
                                    K.act(t[:, 0:BS], t[:, 0:BS], AF.Sigmoid, scale=1.5957691216057308)
                                    if lat:
                                        K.tt("vector", mixh[0:64, 3, b0:b0 + BS], xx[0:64, 0:BS], t[0:64, 0:BS], MUL)
                                        continue
                                    gb = wb.next()
                                    K.tt("vector", gb[:, 0:BS], xx[:, 0:BS], t[:, 0:BS], MUL)
                                    gy.append(gb)
                                ytot = None
                                if lat:
                                    continue
                                pg = []
                                for fo in range(4):
                                    ps = psr.next()
                                    for cc in range(2):
                                        K.mm(ps[:, 0:BS], wglu_b[:, cc, fo * 128:(fo + 1) * 128], gy[cc][:, 0:BS],
                                             start=(cc == 0), stop=(cc == 1))
                                    pg.append(ps)
                                for ch in range(2):
                                    sgm = wf.next()
                                    K.act(sgm[:, 0:BS], pg[2 + ch][:, 0:BS], AF.Sigmoid)
                                    t = wf.next()
                                    K.tt("vector", t[:, 0:BS], pg[ch][:, 0:BS], sgm[:, 0:BS], MUL)
                                    w = load_group(l, "Bg%d" % ch)
                                    psg = fm_cols(w, b0, BS)
                                    sgB = wb.next()
                                    K.act(sgB[:, 0:BS], psg[:, 0:BS], AF.Silu)
                                    K.tt("vector", mixed[:, ch, b0:b0 + BS], t[:, 0:BS], sgB[:, 0:BS], MUL)
                    if d == 0 and l == 0 and si == len(seqs) - 1:
                        dump("s5yf", yf[:, :, 0:512], [128, 2, 512])
                    if not lat:
                        o = sm.next()
                        o3 = o[:, :].re("p (s r) -> p s r", r=2)
                        t = sm.next()
                        K.tt("vector", t[:, 0:8], cst[:, :, 0], fz[:, :, 0], MUL)
                        K.tt("vector", t[:, 8:16], cst[:, :, 1], fz[:, :, 1], MUL)
                        K.tt("vector", o3[:, :, 0], t[:, 0:8], t[:, 8:16], SUB)
                        K.tt("vector", t[:, 0:8], cst[:, :, 0], fz[:, :, 1], MUL)
                        K.tt("vector", t[:, 8:16], cst[:, :, 1], fz[:, :, 0], MUL)
                        K.tt("vector", o3[:, :, 1], t[:, 0:8], t[:, 8:16], ADD)
                        P.dma(dout["o_s5"][si, l, d].ap.rearrange("(s g) p r -> (g p) s r", g=2), o3.ap,
                              reads=[o], writes=[], out=True, eng="gpsimd", allow_slow_non_contiguous=True)
            P.pop_scope()

        def post_gather(l):
            agin = DR(nc.dram_tensor("agin%d" % l, [256, LL], BF16).ap())
            agout = DR(nc.dram_tensor("agout%d" % l, [1024, LL], BF16).ap())
            agin_tok = Buf("agin_tok%d" % l, None)
            agout_tok = Buf("agout_tok%d" % l, None)
            P.dma(agin.ap.rearrange("(s p) t -> p s t", p=64), mixh.ap, reads=[mixh], writes=[agin_tok], eng="gpsimd")
            P.collective(lambda e: e.collective_compute("AllGather", ALU.bypass, replica_groups=[[0, 1, 2, 3], [4, 5, 6, 7]],
                                                        ins=[agin.ap.opt()], outs=[agout.ap.opt()]),
                         reads=[agin_tok], writes=[agout_tok])
            P.push_scope()
            woL = P.sbuf("woL", [128, 10, D], BF16)
            wgl = P.sbuf("wgl", [64, 4, 512], BF16)
            PB = 256
            gyb = P.sbuf("gyb", [64, 4, PB], BF16)
            mB = P.sbuf("mB", [128, 2, PB], BF16)
            mcb = P.sbuf("mcb", [128, 8, PB], BF16)
            mcr = Ring([P.sbuf("mcj%d" % i, [128, 8, PB], BF16) for i in range(2)])
            gyr = Ring([P.sbuf("gyj%d" % i, [64, 4, PB], BF16) for i in range(2)])
            for chunk in range(10):
                s = wst.next()
                sv = s.re("p a b -> p (a b)")
                K.dma(sv, din["woutL"][l][:, chunk, :])
                for n in range(2):
                    gb = wf.next()
                    P.dma(gb.ap, gsc[l][v:v + 1, n * 512:(n + 1) * 512].ap.partition_broadcast(128), reads=[gsc_tok], writes=[gb])
                    K.tt("vector" if n == 0 else "gpsimd", woL[:, chunk, n * 512:(n + 1) * 512], gb, sv[:, n * 512:(n + 1) * 512], MUL)
            for r in range(4):
                s = wf.next()
                K.dma(s[0:64, :], din["wgluL"][l][:, r, :])
                K.cp("vector", wgl[:, r, :], s[0:64, :])
            g4 = agout.ap.rearrange("(r s p) t -> p r s t", r=4, s=4)
            c8 = agout.ap.rearrange("(c p) t -> p c t", p=128)
            for tb in range(512 // PB):
                sgBs = []
                for ch in range(2):
                    w = load_group(l, "Bg%d" % ch)
                    psg = psr.next()
                    for k in range(8):
                        K.mm(psg[:, 0:PB], w[:, k, :], hTo[:, k, tb * PB:(tb + 1) * PB], start=(k == 0), stop=(k == 7))
                    sgB = wb.next()
                    K.act(sgB[:, 0:PB], psg[:, 0:PB], AF.Silu)
                    sgBs.append(sgB)
                for j_ in range(4):
                    cj = j_ * 512 + tb * PB
                    gj = gyr.next()
                    P.dma(gj.ap, g4[:, :, 3, cj:cj + PB], reads=[agout_tok], writes=[gj])
                    mj = mcr.next()
                    P.dma(mj.ap, c8[:, :, cj:cj + PB], reads=[agout_tok], writes=[mj])
                    if j_ == 0:
                        K.ts("vector", gyb, gj, ohs[0:64, 0:1], MUL)
                        K.ts("vector", mcb, mj, ohs[:, 0:1], MUL)
                    else:
                        K.stt("gpsimd" if False else "vector", gyb, gj, ohs[0:64, j_:j_ + 1], gyb, MUL, ADD)
                        K.stt("vector", mcb, mj, ohs[:, j_:j_ + 1], mcb, MUL, ADD)
                pg = []
                for fo in range(4):
                    ps = psr.next()
                    for r in range(4):
                        K.mm(ps[:, 0:PB], wgl[:, r, fo * 128:(fo + 1) * 128], gyb[:, r, :], start=(r == 0), stop=(r == 3))
                    pg.append(ps)
                for ch in range(2):
                    sgm = wf.next()
                    K.act(sgm[:, 0:PB], pg[2 + ch][:, 0:PB], AF.Sigmoid)
                    t = wf.next()
                    K.tt("vector", t[:, 0:PB], pg[ch][:, 0:PB], sgm[:, 0:PB], MUL)
                    K.tt("gpsimd", mB[:, ch, :], t[:, 0:PB], sgBs[ch][:, 0:PB], MUL)
                for ii in range(PB // 128):
                    i = tb * (PB // 128) + ii
                    for n in range(2):
                        ps = psr.next()
                        for chunk in range(10):
                            lhs = mcb[:, chunk, ii * 128:(ii + 1) * 128] if chunk < 8 else mB[:, chunk - 8, ii * 128:(ii + 1) * 128]
                            K.mm(ps, lhs, woL[:, chunk, n * 512:(n + 1) * 512], start=(chunk == 0), stop=(chunk == 9))
                        K.tt("vector", xv[i][:, n * 512:(n + 1) * 512], ps, xv[i][:, n * 512:(n + 1) * 512], ADD)
            P.pop_scope()

        fns = dict(A=(branch_A, 0), B=(branch_B, 1), C=(branch_C, 2), D=(branch_D, 3))
        for l in range(nlayers):
            norm_mod(l)
            dump("hT_%s_%d" % ("lat" if lat else "ctx", l), hT[:, :, 0:512], [128, 8, 512])
            if "B" in branches and lat:
                b_prep(l)
            if lat:
                hT_load()
            for bn in branches:
                fn, bi = fns[bn]
                if bn == "B" and not lat:
                    b_prep(l)
                fn(l)
                if bn == "B":
                    P.pop_scope()
                if lat:
                    continue
                dump("mix%s_%s_%d" % (bn, "lat" if lat else "ctx", l), mixed[:, :, 0:512], [128, 2, 512])
                out_proj(l, bi)
            if lat:
                dump("mixh_%d" % l, mixh[:, :, 0:512], [64, 4, 512])
                post_gather(l)
        for i in range(ntx):
            r = rstd_of(xv[i], D, 1.0 / D)
            for n in range(2):
                o = wf.next()
                fn_ = wf.next()
                K.dma(fn_, din["fnorm"][:, n * 512:(n + 1) * 512])
                K.stt("vector", o, xv[i][:, n * 512:(n + 1) * 512], r, fn_, MUL, MUL)
                K.dma(ydst[i][:, n * 512:(n + 1) * 512], o, out=True, eng="gpsimd")
        P.pop_scope()

    for j in jobs:
        run_job(j == "lat")
    P.emit()
    return nc, dbg_shapes


_NC_CACHE = {}


def kernel(**inputs):
    if "nc" not in _NC_CACHE:
        _NC_CACHE["nc"] = build()[0]
    nc = _NC_CACHE["nc"]
    sh = _shared_inputs(inputs)
    in_maps = [_core_inputs(inputs, sh, c) for c in range(8)]
    res = run_bass_kernel_spmd(nc, in_maps, core_ids=list(range(8)))
    return assemble([r for r in res.results])


def assemble(rs):
    B, SEQ = 16, 256
    y_prompt = np.concatenate([r["y_c"].reshape(2, SEQ, D) for r in rs], axis=0)
    y_sample = np.stack([np.concatenate([rs[4 * s_ + r_]["y_l"].reshape(512, D) for r_ in range(4)], axis=0) for s_ in range(2)], axis=0)
    dk = np.concatenate([r["o_dk"] for r in rs], axis=0).reshape(B, NL, SEQ, 4, 64)
    dv = np.concatenate([r["o_dv"] for r in rs], axis=0).reshape(B, NL, SEQ, 4, 64)
    s5 = np.concatenate([r["o_s5"] for r in rs], axis=0)
    hg = np.concatenate([r["o_hg"] for r in rs], axis=0)
    ckv = np.concatenate([r["o_ckv"] for r in rs], axis=0)
    kr = np.concatenate([r["o_kr"] for r in rs], axis=0)
    f = lambda a: np.ascontiguousarray(a, dtype=np.float32)
    return tuple(f(a) for a in (y_prompt, y_sample, dk, dv, s5, hg, ckv, kr))
```

```python
import numpy as np
import concourse.bass as bass
import concourse.mybir as mybir
from concourse.bass_utils import run_bass_kernel_spmd

F32 = mybir.dt.float32
BF16 = mybir.dt.bfloat16
I32 = mybir.dt.int32
AF = mybir.ActivationFunctionType
ALU = mybir.AluOpType
AX = mybir.AxisListType

SAME_ENGINE_SYNC = True
ATTACH_WAIT = True
CSTOP = 99
N_DMA_SEMS = 24


class Buf:
    def __init__(self, name, ap, parent=None):
        self.name = name
        self.ap = ap
        self.parent = parent
        self.children = []
        self.lastw = None
        self.readers = []
        if parent is not None:
            parent.children.append(self)

    def view(self, ap, name=None):
        return Buf(name or self.name + ".v", ap, parent=self)

    def __getitem__(self, idx):
        return Ref(self, self.ap[idx])

    @property
    def buf(self):
        return self

    def re(self, pat_, **kw):
        return Ref(self, self.ap.rearrange(pat_, **kw))

    def bc(self, shape):
        return Ref(self, self.ap.broadcast_to(list(shape)))

    def _up(self):
        b = self.parent
        while b is not None:
            yield b
            b = b.parent

    def _down(self):
        for c in self.children:
            yield c
            yield from c._down()


class Ref:
    __slots__ = ("buf", "ap")

    def __init__(self, buf, ap):
        self.buf = buf
        self.ap = ap

    def __getitem__(self, idx):
        return Ref(self.buf, self.ap[idx])

    def re(self, pat_, **kw):
        return Ref(self.buf, self.ap.rearrange(pat_, **kw))

    def bc(self, shape):
        return Ref(self.buf, self.ap.broadcast_to(list(shape)))


class Op:
    __slots__ = ("eng", "fn", "deps", "idx", "is_dma", "ticket", "waits", "signal", "out", "clk", "inc")

    def __init__(self, eng, fn, is_dma=False, out=False):
        self.inc = 16
        self.eng = eng
        self.fn = fn
        self.deps = set()
        self.is_dma = is_dma
        self.ticket = None
        self.waits = []
        self.signal = False
        self.out = out
        self.clk = None


class Prog:
    def __init__(self, nc):
        self.nc = nc
        self.ops = []
        self._ctx = []
        self.nsb = 0
        self.scopes = []
        self.scope_pending = []
        self.allbufs = []

    def sbuf(self, name, shape, dtype):
        self.nsb += 1
        cm = self.nc.sbuf_tensor("%s_%d" % (name, self.nsb), list(shape), dtype)
        t = cm.__enter__()
        self._ctx.append(cm)
        b = Buf(name, t.ap() if hasattr(t, "ap") and callable(getattr(t, "ap")) else t[:])
        b.init_deps = list(self.scope_pending)
        self.allbufs.append(b)
        return b

    def push_scope(self):
        self.scopes.append((len(self._ctx), len(self.allbufs)))

    def pop_scope(self):
        nctx, nb = self.scopes.pop()
        dead = self.allbufs[nb:]
        del self.allbufs[nb:]
        pend = set(self.scope_pending)
        for b in dead:
            for x in (b, *b._down()):
                if x.lastw is not None:
                    pend.add(x.lastw)
                pend.update(x.readers)
        last = {}
        keep = []
        for o in pend:
            if o.is_dma:
                keep.append(o)
            elif o.eng not in last or last[o.eng].idx < o.idx:
                last[o.eng] = o
        self.scope_pending = keep + list(last.values())
        while len(self._ctx) > nctx:
            self._ctx.pop().__exit__(None, None, None)

    def psum(self, name, shape, dtype):
        cm = self.nc.psum_tensor(name, list(shape), dtype)
        t = cm.__enter__()
        self._ctx.append(cm)
        return Buf(name, t.ap() if hasattr(t, "ap") and callable(getattr(t, "ap")) else t[:])

    def _track(self, op, reads, writes):
        for b in (*reads, *writes):
            r = b
            while r.parent is not None:
                r = r.parent
            idp = getattr(r, "init_deps", None)
            if idp:
                op.deps.update(idp)
        for b in reads:
            for x in (b, *b._up(), *b._down()):
                if x.lastw is not None:
                    op.deps.add(x.lastw)
        for b in writes:
            for x in (b, *b._up(), *b._down()):
                if x.lastw is not None:
                    op.deps.add(x.lastw)
                for r in x.readers:
                    op.deps.add(r)
        for b in reads:
            b.readers.append(op)
        for b in writes:
            b.lastw = op
            b.readers = []
            for x in b._down():
                x.lastw = None
                x.readers = []
        op.deps.discard(op)

    def op(self, eng, fn, reads=(), writes=()):
        o = Op(eng, fn)
        o.idx = len(self.ops)
        self._track(o, reads, writes)
        self.ops.append(o)
        return o

    def dma(self, out_ap, in_ap, reads=(), writes=(), eng="sync", out=False, **kw):
        o = Op(eng, lambda e: e.dma_start(out=out_ap, in_=in_ap, **kw), is_dma=True, out=out)
        o.idx = len(self.ops)
        self._track(o, reads, writes)
        self.ops.append(o)
        return o

    def collective(self, fn, reads=(), writes=()):
        o = Op("gpsimd", fn, is_dma=True)
        o.inc = 1
        o.idx = len(self.ops)
        self._track(o, reads, writes)
        self.ops.append(o)
        return o

    def emit(self):
        nc = self.nc
        engines = ["sync", "tensor", "vector", "scalar", "gpsimd"]
        for o in self.ops:
            for d in o.deps:
                if d.eng == o.eng and not d.is_dma and not o.is_dma:
                    if o.eng == "tensor" or not SAME_ENGINE_SYNC:
                        continue
                d.signal = True
        for o in self.ops:
            if o.is_dma:
                o.signal = True
        sems = {}
        ctxs = []

        def mksem(name):
            cm = nc.semaphore(name)
            s = cm.__enter__()
            ctxs.append(cm)
            return s

        for e in engines:
            sems[e] = mksem("s_" + e)
        dma_sems = {e: [mksem("d_%s_%d" % (e, i)) for i in range(N_DMA_SEMS)] for e in ("sync", "scalar", "gpsimd")}
        dma_cnt = {e: [0] * N_DMA_SEMS for e in dma_sems}
        dma_last = {e: [None] * N_DMA_SEMS for e in dma_sems}
        dma_rr = {e: 0 for e in dma_sems}
        cnt = {e: 0 for e in engines}
        cc_sems = []
        clock = {e: {} for e in engines}
        final_waits = []
        for o in self.ops:
            E = o.eng
            deps = set(o.deps)
            if o.is_dma and o.inc == 1:
                pass
            elif o.is_dma:
                k = dma_rr[E]
                dma_rr[E] = (k + 1) % N_DMA_SEMS
                if dma_last[E][k] is not None:
                    deps.add(dma_last[E][k])
                dma_last[E][k] = o
            ck = clock[E]
            for d in sorted(deps, key=lambda z: z.idx):
                if d.ticket is None:
                    continue
                if d.eng == E and not d.is_dma and not o.is_dma and (E == "tensor" or not SAME_ENGINE_SYNC):
                    continue
                skey, val = d.ticket
                if ck.get(skey, 0) >= val:
                    continue
                o.waits.append((skey, val))
                ck[skey] = val
                for k2, v2 in d.clk.items():
                    if ck.get(k2, 0) < v2:
                        ck[k2] = v2
            if o.is_dma and o.inc == 1:
                cc_sems.append(mksem("cc%d" % len(cc_sems)))
                o.ticket = (("c", len(cc_sems) - 1), 1)
            elif o.is_dma:
                dma_cnt[E][k] += o.inc
                o.ticket = (("d", E, k), dma_cnt[E][k])
                if o.out:
                    final_waits.append(o.ticket)
            elif o.signal:
                cnt[E] += 1
                o.ticket = (("e", E), cnt[E])
            o.clk = dict(ck)
            if o.ticket is not None and not o.is_dma:
                o.clk[o.ticket[0]] = o.ticket[1]

        def semof(skey):
            if skey[0] == "e":
                return sems[skey[1]]
            if skey[0] == "c":
                return cc_sems[skey[1]]
            return dma_sems[skey[1]][skey[2]]

        ops = self.ops
        with nc.Block() as block:
            def run(engname):
                def body(eng):
                    for o in ops:
                        if o.eng != engname:
                            continue
                        ws = list(o.waits)
                        att = None
                        if ATTACH_WAIT and ws and not o.is_dma:
                            att = ws.pop()
                        for skey, val in ws:
                            eng.wait_ge(semof(skey), val)
                        ins = o.fn(eng)
                        if att is not None:
                            ins._wait_ge(semof(att[0]), att[1])
                        if o.ticket is not None:
                            if o.is_dma:
                                ins.then_inc(semof(o.ticket[0]), o.inc)
                            else:
                                ins.then_inc(semof(o.ticket[0]), 1)
                    if engname == "sync":
                        for skey, val in final_waits:
                            eng.wait_ge(semof(skey), val)
                return body
            block.sync(run("sync"))
            block.tensor(run("tensor"))
            block.vector(run("vector"))
            block.scalar(run("scalar"))
            block.gpsimd(run("gpsimd"))
        for cm in reversed(ctxs):
            cm.__exit__(None, None, None)
        for cm in reversed(self._ctx):
            cm.__exit__(None, None, None)


D = 1024
NL = 2
LC = 512
LL = 2048
PAST = 256
EPS = 1e-6
TB = 512
S5T = 256
HGC = 32
GRID_W = 64

OFF = dict(da_q=0, da_k=256, da_v=512, da_g=768, s5_u=1024, s5_g=1280, hg_q=1536, hg_ff=1792, hg_fb=2048,
           hg_i=2304, hg_g=2560, mla_cq=2816, mla_ckv=3008, mla_kr=3136, mla_g=3168)


def _rope_perm32():
    p = np.zeros(32, np.int64)
    for i in range(32):
        p[i] = i + 8 if (i % 16) < 8 else i - 8
    return p


def _groups():
    cols = []
    table = {}

    def add(name, src):
        src = list(src)
        assert len(src) <= 128
        src = src + [-1] * (128 - len(src))
        table[name] = len(cols)
        cols.extend(src)

    perm = _rope_perm32()
    for nm, off in (("Aq", OFF["da_q"]), ("Ak", OFF["da_k"])):
        for h in range(4):
            c1 = [off + h * 64 + d for d in range(32)]
            c2 = [off + h * 64 + 32 + d for d in range(32)]
            add("%s%d" % (nm, h), c1 + [-1] * 32 + c2 + [-1] * 32)
            p1 = [off + h * 64 + perm[d] for d in range(32)]
            p2 = [off + h * 64 + 32 + perm[d] for d in range(32)]
            add("%sp%d" % (nm, h), p1 + [-1] * 32 + p2 + [-1] * 32)
    for i in range(2):
        add("Av%d" % i, range(OFF["da_v"] + 128 * i, OFF["da_v"] + 128 * i + 128))
        add("Akt%d" % i, range(OFF["da_k"] + 128 * i, OFF["da_k"] + 128 * i + 128))
        add("Ag%d" % i, range(OFF["da_g"] + 128 * i, OFF["da_g"] + 128 * i + 128))
        add("Bu%d" % i, range(OFF["s5_u"] + 128 * i, OFF["s5_u"] + 128 * i + 128))
        add("Bg%d" % i, range(OFF["s5_g"] + 128 * i, OFF["s5_g"] + 128 * i + 128))
        add("Cq%d" % i, range(OFF["hg_q"] + 128 * i, OFF["hg_q"] + 128 * i + 128))
        add("Cf0%d" % i, range(OFF["hg_ff"] + 128 * i, OFF["hg_ff"] + 128 * i + 128))
        add("Cf1%d" % i, range(OFF["hg_fb"] + 128 * i, OFF["hg_fb"] + 128 * i + 128))
        add("Cv%d" % i, range(OFF["hg_i"] + 128 * i, OFF["hg_i"] + 128 * i + 128))
        add("Cg%d" % i, range(OFF["hg_g"] + 128 * i, OFF["hg_g"] + 128 * i + 128))
        add("Dg%d" % i, range(OFF["mla_g"] + 128 * i, OFF["mla_g"] + 128 * i + 128))
    add("Dcq0", range(OFF["mla_cq"], OFF["mla_cq"] + 128))
    add("Dcq1", range(OFF["mla_cq"] + 128, OFF["mla_cq"] + 192))
    add("Dckv", range(OFF["mla_ckv"], OFF["mla_ckv"] + 128))
    kr = [OFF["mla_kr"] + d for d in range(32)]
    add("Dkr", [-1] * 64 + kr)
    add("Dkrp", [-1] * 64 + [OFF["mla_kr"] + perm[d] for d in range(32)])
    add("Dkrt", kr)
    return table, np.array(cols, np.int64)


GT, GCOLS = _groups()
NCOLS = len(GCOLS)


def _groups_lat(r):
    cols = []
    table = {}

    def add(name, src):
        src = list(src)
        src = src + [-1] * (128 - len(src))
        table[name] = len(cols)
        cols.extend(src)

    perm = _rope_perm32()
    for nm, off in (("Aq", OFF["da_q"]), ("Ak", OFF["da_k"])):
        c1 = [off + r * 64 + d for d in range(32)]
        c2 = [off + r * 64 + 32 + d for d in range(32)]
        add(nm, c1 + [-1] * 32 + c2 + [-1] * 32)
        p1 = [off + r * 64 + perm[d] for d in range(32)]
        p2 = [off + r * 64 + 32 + perm[d] for d in range(32)]
        add(nm + "p", p1 + [-1] * 32 + p2 + [-1] * 32)
    for nm, key in (("Av", "da_v"), ("Ag", "da_g"), ("Bu", "s5_u"), ("Cq", "hg_q"), ("Cf0", "hg_ff"), ("Cf1", "hg_fb"),
                    ("Cv", "hg_i"), ("Cg", "hg_g"), ("Dg", "mla_g")):
        add(nm, range(OFF[key] + 64 * r, OFF[key] + 64 * r + 64))
    for i in range(2):
        add("Bg%d" % i, range(OFF["s5_g"] + 128 * i, OFF["s5_g"] + 128 * i + 128))
    add("Dcq0", range(OFF["mla_cq"], OFF["mla_cq"] + 128))
    add("Dcq1", range(OFF["mla_cq"] + 128, OFF["mla_cq"] + 192))
    add("Dckv", range(OFF["mla_ckv"], OFF["mla_ckv"] + 128))
    kr = [OFF["mla_kr"] + d for d in range(32)]
    add("Dkr", [-1] * 64 + kr)
    add("Dkrp", [-1] * 64 + [OFF["mla_kr"] + perm[d] for d in range(32)])
    return table, np.array(cols, np.int64)


LGT = _groups_lat(0)[0]
NCOLS_L = len(_groups_lat(0)[1])


def _rope_tables():
    t = np.arange(LL)
    row = (t // GRID_W).astype(np.float32)
    col = (t % GRID_W).astype(np.float32)
    inv = (10000.0 ** (-np.arange(8, dtype=np.float32) / 8)).astype(np.float32)
    C = np.ones((128, LL), np.float32)
    S = np.zeros((128, LL), np.float32)
    for base in (0, 64):
        for i in range(32):
            pos = row if i < 16 else col
            ang = pos * inv[i % 8]
            C[base + i] = np.cos(ang)
            S[base + i] = (-1.0 if (i % 16) < 8 else 1.0) * np.sin(ang)
    return C, S


def _consts():
    c = {}
    c["ident"] = np.eye(128, dtype=np.float32)
    s = np.arange(128)[:, None]
    t = np.arange(128)[None, :]
    same = (s // HGC) == (t // HGC)
    c["hmask0"] = (same & (s <= t)).astype(np.float32)
    c["hmask1"] = (same & (s >= t)).astype(np.float32)
    m4 = np.zeros((128, 4), np.float32)
    m4[np.arange(128), np.arange(128) // HGC] = 1.0
    c["m4"] = m4
    m01 = np.ones((128, TB), np.float32)
    m01[:, ::HGC] = 0.0
    c["m01"] = m01
    c["iota"] = np.broadcast_to(np.arange(1, S5T + 1, dtype=np.float32)[None, :], (128, S5T)).copy()
    bd = np.zeros((128, 128), np.float32)
    bd[:64, :64] = 1.0
    bd[64:, 64:] = 1.0
    c["bdones"] = bd
    C, S = _rope_tables()
    c["ropeC"] = C
    c["ropeS"] = S
    sel = np.zeros((2, 2, 128), np.float32)
    sel[0, 0] = 1.0
    sel[1, 1] = 1.0
    c["selc"] = sel
    return c


CONST_SHAPES = dict(ident=[128, 128], hmask0=[128, 128], hmask1=[128, 128], m4=[128, 4], m01=[128, TB],
                    iota=[128, S5T], bdones=[128, 128], ropeC=[128, LL], ropeS=[128, LL], selc=[2, 2, 128])


IN_SHAPES = dict(
    xc=[4, 128, D], xl=[4, 128, D], cvec=[128, 8, 2],
    wmod=[NL, 24, 128, 8, 128], bmodT=[NL, 128, 24], bmodg=[NL, 2, D],
    win=[NL, NCOLS // 128, 128, 8, 128], wout=[NL, 128, 8, D], wglu=[NL, 128, 2, 512],
    wuq=[NL, 128, 2, 768], mqn=[NL, 128, 2], wukv=[NL, 128, 512], mkvn=[NL, 128, 128], fnorm=[128, D],
    s5B=[NL, 2, 8, 2, 128, 128], s5C=[NL, 2, 8, 2, 128, 128], s5p=[NL, 128, 16, 3], s5d=[NL, 128, 2],
    hglb=[NL, 128, 4], hgn=[NL, 128, 1], dalam=[NL, 1, 128], dan=[NL, 128, 1],
    cdk=[NL, 2, 128, 256], cdv=[NL, 2, 128, 256], cs5=[NL, 128, 16, 2], chg=[NL, 2, 4, 64, 64],
    cckv=[NL, 2, 128, 128], ckr=[NL, 2, 128, 32],
)
IN_SHAPES.update(dict(
    winL=[NL, NCOLS_L // 128, 128, 8, 128], woutL=[NL, 128, 10, D], wgluL=[NL, 64, 4, 512], wuqL=[NL, 128, 2, 192], wukvL=[NL, 128, 128],
    s5BL=[NL, 2, 2, 2, 64, 128], s5CL=[NL, 2, 2, 2, 128, 64], s5pL=[NL, 128, 4, 3], s5dL=[NL, 64, 1], hglbL=[NL, 64, 2],
    cdkL=[NL, 2, 128, 64], cdvL=[NL, 2, 128, 64], cs5L=[NL, 128, 4, 2], chgL=[NL, 2, 64, 64], oh=[128, 4],
))
IN_SHAPES.update(CONST_SHAPES)
OUT_SHAPES = dict(
    y_c=[4, 128, D], y_l=[4, 128, D], o_dk=[2, NL, 256, 256], o_dv=[2, NL, 256, 256],
    o_s5=[2, NL, 2, 16, 64, 2], o_hg=[2, NL, 2, 4, 64, 64], o_ckv=[2, NL, 256, 128], o_kr=[2, NL, 256, 32],
)


def _shared_inputs(inp):
    f = lambda a: np.ascontiguousarray(np.asarray(a, dtype=np.float32))
    sh = {}
    w_in = f(inp["w_in"])
    wpad = np.concatenate([w_in, np.zeros((NL, D, 1), np.float32)], axis=2)
    win = wpad[:, :, GCOLS]
    sh["win"] = f(win.reshape(NL, 8, 128, NCOLS // 128, 128).transpose(0, 3, 2, 1, 4))
    sh["wmod"] = f(f(inp["w_mod"]).reshape(NL, 8, 128, 24, 128).transpose(0, 3, 2, 1, 4))
    bm = f(inp["b_mod"])
    sh["bmodT"] = f(bm.reshape(NL, 24, 128).transpose(0, 2, 1))
    sh["bmodg"] = f(np.broadcast_to(bm[:, None, 2 * D:3 * D], (NL, 2, D)))
    sh["wout"] = f(f(inp["w_out"]).reshape(NL, 8, 128, D).transpose(0, 2, 1, 3))
    sh["wglu"] = f(f(inp["s5_w_glu"]).reshape(NL, 2, 128, 512).transpose(0, 2, 1, 3))
    perm = _rope_perm32()
    wuq = f(inp["mla_w_uq"])
    cols_n = np.arange(384)
    cols_p = np.array([h * 96 + (j if j < 64 else 64 + perm[j - 64]) for h in range(4) for j in range(96)])
    wq = np.concatenate([wuq[:, :, cols_n], wuq[:, :, cols_p]], axis=2)
    wq = np.concatenate([wq, np.zeros((NL, 64, 768), np.float32)], axis=1)
    sh["wuq"] = f(wq.reshape(NL, 2, 128, 768).transpose(0, 2, 1, 3))
    qn = np.concatenate([f(inp["mla_q_norm"]), np.zeros((NL, 64), np.float32)], axis=1)
    sh["mqn"] = f(qn.reshape(NL, 2, 128).transpose(0, 2, 1))
    sh["wukv"] = f(inp["mla_w_ukv"])
    sh["mkvn"] = f(np.broadcast_to(f(inp["mla_kv_norm"])[:, None, :], (NL, 128, 128)))
    sh["fnorm"] = f(np.broadcast_to(f(inp["final_norm"])[None, :], (128, D)))
    bre, bim = f(inp["s5_b_re"]), f(inp["s5_b_im"])
    cre, cim = f(inp["s5_c_re"]), f(inp["s5_c_im"])
    sB = np.zeros((NL, 2, 8, 2, 128, 128), np.float32)
    sC = np.zeros((NL, 2, 8, 2, 128, 128), np.float32)
    for st in range(8):
        for gi in range(2):
            g = 2 * st + gi
            r0 = 16 * (g % 8)
            for ri, (bb, cc) in enumerate(((bre, cre), (bim, cim))):
                sB[:, :, st, ri, r0:r0 + 16, 64 * gi:64 * gi + 64] = bb[:, :, g].transpose(0, 1, 3, 2)
                sC[:, :, st, ri, 64 * gi:64 * gi + 64, r0:r0 + 16] = cc[:, :, g].transpose(0, 1, 3, 2)
    sh["s5B"], sh["s5C"] = sB, sC
    are, aim, ldt = f(inp["s5_a_re"]), f(inp["s5_a_im"]), f(inp["s5_log_dt"])
    sp = np.zeros((NL, 128, 16, 3), np.float32)
    for d in range(2):
        for st in range(8):
            for gi in range(2):
                g = 2 * st + gi
                sp[:, 64 * gi:64 * gi + 64, d * 8 + st, 0] = are[:, d, g]
                sp[:, 64 * gi:64 * gi + 64, d * 8 + st, 1] = aim[:, d, g]
                sp[:, 64 * gi:64 * gi + 64, d * 8 + st, 2] = ldt[:, d, g][:, None]
    sh["s5p"] = sp
    sh["s5d"] = f(f(inp["s5_d"]).reshape(NL, 2, 128).transpose(0, 2, 1))
    lb = f(inp["hg_lb"])
    sh["hglb"] = f(lb.reshape(NL, 2, 2, 128).transpose(0, 3, 1, 2).reshape(NL, 128, 4))
    sh["hgn"] = f(np.tile(f(inp["hg_norm"]), (1, 2))[:, :, None])
    sh["dalam"] = f(f(inp["da_lambda"]).reshape(NL, 1, 128))
    sh["dan"] = f(np.tile(f(inp["da_norm"]), (1, 2))[:, :, None])
    sh.update(_consts())
    return sh


def _core_inputs(inp, sh, core):
    f = lambda a: np.ascontiguousarray(np.asarray(a, dtype=np.float32))
    m = dict(sh)
    s = core // 4
    m["xc"] = f(np.asarray(inp["x_prompt"])[2 * core:2 * core + 2].reshape(4, 128, D))
    m["xl"] = f(np.asarray(inp["x_sample"])[s].reshape(4, 4, 128, D)[core % 4])
    cv = np.stack([f(inp["c_ctx"]), f(inp["c"])[s]], axis=-1)
    m["cvec"] = f(cv.reshape(8, 128, 2).transpose(1, 0, 2))
    m["cdk"] = f(np.asarray(inp["cache_diff_k"])[s].reshape(NL, 2, 128, 256))
    m["cdv"] = f(np.asarray(inp["cache_diff_v"])[s].reshape(NL, 2, 128, 256))
    st5 = f(np.asarray(inp["state_s5"])[s])
    m["cs5"] = f(st5.reshape(NL, 2, 8, 2, 64, 2).transpose(0, 3, 4, 1, 2, 5).reshape(NL, 128, 16, 2))
    m["chg"] = f(np.asarray(inp["state_hgrn"])[s])
    m["cckv"] = f(np.asarray(inp["cache_mla_ckv"])[s].reshape(NL, 2, 128, 128))
    m["ckr"] = f(np.asarray(inp["cache_mla_krope"])[s].reshape(NL, 2, 128, 32))
    r = core % 4
    w_in = f(inp["w_in"])
    wpad = np.concatenate([w_in, np.zeros((NL, D, 1), np.float32)], axis=2)
    lcols = _groups_lat(r)[1]
    m["winL"] = f(wpad[:, :, lcols].reshape(NL, 8, 128, NCOLS_L // 128, 128).transpose(0, 3, 2, 1, 4))
    wo = f(inp["w_out"])
    chunks = []
    z64 = np.zeros((NL, 64, D), np.float32)
    for rr in range(4):
        chunks.append(np.concatenate([wo[:, 64 * rr:64 * rr + 64], wo[:, 512 + 64 * rr:512 + 64 * rr + 64]], axis=1))
        chunks.append(np.concatenate([wo[:, 768 + 64 * rr:768 + 64 * rr + 64], z64], axis=1))
    chunks.append(wo[:, 256:384])
    chunks.append(wo[:, 384:512])
    m["woutL"] = f(np.stack(chunks, axis=2))
    m["wgluL"] = f(f(inp["s5_w_glu"]).reshape(NL, 4, 64, 512).transpose(0, 2, 1, 3))
    perm = _rope_perm32()
    wuq = f(inp["mla_w_uq"])
    cn = np.array([r * 96 + j for j in range(96)])
    cp_ = np.array([r * 96 + (j if j < 64 else 64 + perm[j - 64]) for j in range(96)])
    wq = np.concatenate([wuq[:, :, cn], wuq[:, :, cp_]], axis=2)
    wq = np.concatenate([wq, np.zeros((NL, 64, 192), np.float32)], axis=1)
    m["wuqL"] = f(wq.reshape(NL, 2, 128, 192).transpose(0, 2, 1, 3))
    m["wukvL"] = f(f(inp["mla_w_ukv"])[:, :, r * 128:(r + 1) * 128])
    bre, bim = f(inp["s5_b_re"]), f(inp["s5_b_im"])
    cre, cim = f(inp["s5_c_re"]), f(inp["s5_c_im"])
    sB = np.zeros((NL, 2, 2, 2, 64, 128), np.float32)
    sC = np.zeros((NL, 2, 2, 2, 128, 64), np.float32)
    are, aim, ldt = f(inp["s5_a_re"]), f(inp["s5_a_im"]), f(inp["s5_log_dt"])
    sp = np.zeros((NL, 128, 4, 3), np.float32)
    st5 = f(np.asarray(inp["state_s5"])[s])
    c5 = np.zeros((NL, 128, 4, 2), np.float32)
    for st in range(2):
        for gi in range(2):
            g = 4 * r + 2 * st + gi
            r0 = 16 * (g % 4)
            for ri, (bb, cc) in enumerate(((bre, cre), (bim, cim))):
                sB[:, :, st, ri, r0:r0 + 16, 64 * gi:64 * gi + 64] = bb[:, :, g].transpose(0, 1, 3, 2)
                sC[:, :, st, ri, 64 * gi:64 * gi + 64, r0:r0 + 16] = cc[:, :, g].transpose(0, 1, 3, 2)
            for d in range(2):
                sp[:, 64 * gi:64 * gi + 64, d * 2 + st, 0] = are[:, d, g]
                sp[:, 64 * gi:64 * gi + 64, d * 2 + st, 1] = aim[:, d, g]
                sp[:, 64 * gi:64 * gi + 64, d * 2 + st, 2] = ldt[:, d, g][:, None]
                c5[:, 64 * gi:64 * gi + 64, d * 2 + st, :] = st5[:, d, g]
    m["s5BL"], m["s5CL"], m["s5pL"], m["cs5L"] = sB, sC, sp, c5
    m["s5dL"] = f(f(inp["s5_d"])[:, 64 * r:64 * r + 64, None])
    m["hglbL"] = f(f(inp["hg_lb"])[:, :, 64 * r:64 * r + 64].transpose(0, 2, 1))
    m["cdkL"] = f(np.asarray(inp["cache_diff_k"])[s][:, :, r].reshape(NL, 2, 128, 64))
    m["cdvL"] = f(np.asarray(inp["cache_diff_v"])[s][:, :, r].reshape(NL, 2, 128, 64))
    m["chgL"] = f(np.asarray(inp["state_hgrn"])[s][:, :, r])
    oh = np.zeros((128, 4), np.float32)
    oh[:, r] = 1.0
    m["oh"] = oh
    for k, shp in IN_SHAPES.items():
        assert list(m[k].shape) == list(shp), (k, m[k].shape, shp)
    return {k: m[k] for k in IN_SHAPES}


class DR:
    buf = None

    def __init__(self, ap):
        self.ap = ap

    def __getitem__(self, idx):
        return DR(self.ap[idx])

    def re(self, pat_, **kw):
        return DR(self.ap.rearrange(pat_, **kw))


class Ring:
    def __init__(self, bufs):
        self.bufs = bufs
        self.i = 0

    def next(self):
        b = self.bufs[self.i % len(self.bufs)]
        self.i += 1
        return b


def _b(xs):
    return [x.buf for x in xs if x is not None and not isinstance(x, (int, float)) and x.buf is not None]


class KB:
    def __init__(self, P):
        self.P = P

    def tt(self, eng, o, a, b, op):
        self.P.op(eng, lambda e: e.tensor_tensor(o.ap, a.ap, b.ap, op=op), _b([a, b]), _b([o]))

    def ts(self, eng, o, a, s1, op0, s2=None, op1=None):
        v1 = s1.ap if hasattr(s1, "ap") else s1
        v2 = s2.ap if hasattr(s2, "ap") else s2
        if op1 is None:
            fn = lambda e: e.tensor_scalar(o.ap, a.ap, v1, None, op0=op0)
        else:
            fn = lambda e: e.tensor_scalar(o.ap, a.ap, v1, v2, op0=op0, op1=op1)
        self.P.op(eng, fn, _b([a, s1, s2]), _b([o]))

    def stt(self, eng, o, a, s, b, op0, op1):
        v = s.ap if hasattr(s, "ap") else s
        self.P.op(eng, lambda e: e.scalar_tensor_tensor(o.ap, a.ap, v, b.ap, op0=op0, op1=op1),
                  _b([a, s, b]), _b([o]))

    def act(self, o, a, func, bias=None, scale=None, accum=None):
        kw = {}
        if bias is not None:
            kw["bias"] = bias.ap if hasattr(bias, "ap") else bias
        if scale is not None:
            kw["scale"] = scale.ap if hasattr(scale, "ap") else scale
        if accum is not None:
            kw["accum_out"] = accum.ap
        self.P.op("scalar", lambda e: e.activation(o.ap, a.ap, func, **kw), _b([a, bias, scale]), _b([o, accum]))

    def cp(self, eng, o, a):
        if eng == "scalar":
            self.P.op(eng, lambda e: e.activation(o.ap, a.ap, AF.Copy), _b([a]), _b([o]))
        else:
            self.P.op(eng, lambda e: e.tensor_copy(o.ap, a.ap), _b([a]), _b([o]))

    def recip(self, o, a):
        self.P.op("vector", lambda e: e.reciprocal(o.ap, a.ap), _b([a]), _b([o]))

    def memset(self, eng, o, val):
        self.P.op(eng, lambda e: e.memset(o.ap, val), [], _b([o]))

    def mm(self, o, lhsT, rhs, start=True, stop=True):
        self.P.op("tensor", lambda e: e.matmul(o.ap, lhsT.ap, rhs.ap, start=start, stop=stop),
                  _b([lhsT, rhs]), _b([o]))

    def scan(self, o, d0, d1, init, op0=ALU.mult, op1=ALU.add):
        iv = init.ap if hasattr(init, "ap") else init
        self.P.op("vector", lambda e: e.tensor_tensor_scan(o.ap, d0.ap, d1.ap, iv, op0=op0, op1=op1),
                  _b([d0, d1, init]), _b([o]))

    def dma(self, o, a, out=False, eng="sync"):
        self.P.dma(o.ap, a.ap, reads=_b([a]), writes=_b([o]), out=out, eng=eng)


def lam_init(l):
    import math
    return 0.8 - 0.6 * math.exp(-0.3 * l)


def build(jobs=("ctx", "lat"), nlayers=NL, branches="ABCD", dbg=(), nwf=12):
    nc = bass.Bass("TRN2", target_bir_lowering=False)
    P = Prog(nc)
    K = KB(P)
    din = {k: DR(nc.dram_tensor(k, shp, F32, kind="ExternalInput").ap()) for k, shp in IN_SHAPES.items()}
    dout = {k: DR(nc.dram_tensor(k, shp, F32, kind="ExternalOutput").ap()) for k, shp in OUT_SHAPES.items()}
    dbg_shapes = {}

    def dump(name, ref, shape):
        if name in dbg:
            shape = list(shape)
            dbg_shapes[name] = shape
            dd = DR(nc.dram_tensor("dbg_" + name, shape, F32, kind="ExternalOutput").ap())
            t = P.sbuf("dbgt_" + name, shape, F32)
            K.cp("vector", t, ref)
            K.dma(dd, t, out=True, eng="gpsimd")

    PI = float(np.pi)
    ADD, SUB, MUL, MAX, MIN = ALU.add, ALU.subtract, ALU.mult, ALU.max, ALU.min
    psr = Ring([P.psum("ps%d" % i, [128, 512], F32) for i in range(6)])
    accs = Ring([P.psum("acc%d" % i, [128, 512], F32) for i in range(2)])
    wst = Ring([P.sbuf("wst%d" % i, [128, 8, 128], F32) for i in range(5)])
    wbf = Ring([P.sbuf("wbf%d" % i, [128, 8, 128], BF16) for i in range(5)])
    wf = Ring([P.sbuf("wf%d" % i, [128, 512], F32) for i in range(nwf)])
    wb = Ring([P.sbuf("wb%d" % i, [128, 512], BF16) for i in range(10)])
    xnr = Ring([P.sbuf("xn%d" % i, [128, D], BF16) for i in range(2)])
    sm = Ring([P.sbuf("sm%d" % i, [128, 16], F32) for i in range(16)])
    junk = P.sbuf("junk", [128, D], BF16)

    def cbf(name, shape, src):
        t = P.sbuf(name, shape, BF16)
        n = shape[1]
        for c0 in range(0, n, 512):
            w = min(512, n - c0)
            s = wf.next()
            K.dma(s[:, 0:w], src[:, c0:c0 + w])
            K.cp("vector", t[:, c0:c0 + w], s[:, 0:w])
        return t

    ident_b = cbf("ident_b", [128, 128], din["ident"])
    hmask = [cbf("hmask%d" % i, [128, 128], din["hmask%d" % i]) for i in range(2)]
    bdones = cbf("bdones", [128, 128], din["bdones"])
    ropeC = cbf("ropeC", [128, LL], din["ropeC"])
    ropeS = cbf("ropeS", [128, LL], din["ropeS"])
    ones_b = P.sbuf("ones_b", [128, 128], BF16)
    K.memset("vector", ones_b, 1.0)
    ones_f = P.sbuf("ones_f", [128, 128], F32)
    K.memset("vector", ones_f, 1.0)
    epsc = P.sbuf("epsc", [128, 1], F32)
    K.memset("vector", epsc, EPS)
    halfpi = P.sbuf("halfpi", [128, 1], F32)
    K.memset("vector", halfpi, float(np.pi / 2))
    zpad = P.sbuf("zpad", [128, 128], BF16)
    K.memset("vector", zpad, 0.0)
    m4 = P.sbuf("m4", [128, 4], F32)
    K.dma(m4, din["m4"])
    ohs = P.sbuf("ohs", [128, 4], F32)
    K.dma(ohs, din["oh"])
    m01 = P.sbuf("m01", [128, TB], F32)
    K.dma(m01, din["m01"])
    iota = P.sbuf("iota", [128, S5T], F32)
    K.dma(iota, din["iota"])
    modT = P.sbuf("modT", [128, NL, 2, 16], F32)
    gsc = DR(nc.dram_tensor("gsc", [NL, 2, D], F32).ap())
    gsc_tok = Buf("gsc_tok", None)
    bmT = P.sbuf("bmT", [128, NL, 24], F32)
    cs = P.sbuf("cs", [128, 8, 2], F32)
    for l in range(NL):
        K.dma(bmT[:, l, :], din["bmodT"][l])
    K.dma(cs, din["cvec"])
    K.act(cs, cs, AF.Silu)
    for l in range(nlayers):
        for j in range(16):
            s = wst.next()
            K.dma(s, din["wmod"][l][j])
            ps = psr.next()
            for k in range(8):
                K.mm(ps[:, 0:2], s[:, k, :], cs[:, k, :], start=(k == 0), stop=(k == 7))
            K.ts("vector", modT[:, l, :, j], ps[:, 0:2], bmT[:, l, j:j + 1], ADD, 1.0 if j >= 8 else 0.0, ADD)
        for n in range(8):
            s = wst.next()
            K.dma(s, din["wmod"][l][16 + n])
            ps = psr.next()
            for k in range(8):
                K.mm(ps[0:2, 0:128], cs[:, k, :], s[:, k, :], start=(k == 0), stop=(k == 7))
            bt = wf.next()
            K.dma(bt[0:2, 0:128], din["bmodg"][l][:, n * 128:(n + 1) * 128])
            K.tt("vector", bt[0:2, 128:256], ps[0:2, 0:128], bt[0:2, 0:128], ADD)
            P.dma(gsc[l][:, n * 128:(n + 1) * 128].ap, bt[0:2, 128:256].ap, reads=[bt], writes=[gsc_tok])
    lamr = P.sbuf("lamr", [1, NL, 128], F32)
    lamv = P.sbuf("lamv", [1, 8], F32)
    neglam = P.sbuf("neglam", [128, NL], F32)
    dan_s = P.sbuf("dan_s", [128, NL], F32)
    for l in range(NL):
        K.dma(lamr[:, l, :], din["dalam"][l])
        K.dma(dan_s[:, l:l + 1], din["dan"][l])
        K.ts("vector", dan_s[:, l:l + 1], dan_s[:, l:l + 1], 1.0 - lam_init(l), MUL)
        t = sm.next()
        e = sm.next()
        for c in range(2):
            K.tt("vector", lamr[:, l, 64 * c:64 * c + 32], lamr[:, l, 64 * c:64 * c + 32],
                 lamr[:, l, 64 * c + 32:64 * c + 64], MUL)
            P.op("vector", lambda e_, o=t[0:1, c:c + 1], a=lamr[:, l, 64 * c:64 * c + 32]: e_.reduce_sum(o.ap, a.ap, axis=AX.X),
                 _b([lamr]), _b([t]))
        K.act(e[0:1, 0:2], t[0:1, 0:2], AF.Exp)
        K.tt("vector", lamv[:, l:l + 1], e[0:1, 0:1], e[0:1, 1:2], SUB)
        K.ts("vector", lamv[:, l:l + 1], lamv[:, l:l + 1], -1.0, MUL, -lam_init(l), ADD)
    ps = psr.next()
    K.mm(ps[:, 0:NL], ones_f[0:1, :], lamv[0:1, 0:NL])
    K.cp("vector", neglam, ps[:, 0:NL])
    hgl = P.sbuf("hgl", [128, NL, 4], F32)
    lb = P.sbuf("lb", [128, NL, 4], F32)
    oml = P.sbuf("oml", [128, NL, 4], F32)
    noml = P.sbuf("noml", [128, NL, 4], F32)
    hgn = P.sbuf("hgn", [128, NL], F32)
    for l in range(NL):
        K.dma(hgl[:, l, :], din["hglb"][l])
        K.dma(hgn[:, l:l + 1], din["hgn"][l])
    K.memset("vector", lb, 0.0)
    K.tt("vector", lb[:, 1, :], hgl[:, 1, :], hgl[:, 0, :], SUB)
    K.act(lb[:, 1, :], lb[:, 1, :], AF.Sigmoid)
    K.ts("vector", oml, lb, -1.0, MUL, 1.0, ADD)
    K.ts("vector", noml, oml, -1.0, MUL)
    hglL = P.sbuf("hglL", [128, NL, 2], F32)
    lbL = P.sbuf("lbL", [128, NL, 2], F32)
    omlL = P.sbuf("omlL", [128, NL, 2], F32)
    nomlL = P.sbuf("nomlL", [128, NL, 2], F32)
    s5dl = P.sbuf("s5dl", [128, NL], F32)
    K.memset("vector", hglL, 0.0)
    K.memset("vector", s5dl, 0.0)
    for l in range(NL):
        K.dma(hglL[0:64, l, :], din["hglbL"][l])
        K.dma(s5dl[0:64, l:l + 1], din["s5dL"][l])
    K.memset("vector", lbL, 0.0)
    K.tt("vector", lbL[:, 1, :], hglL[:, 1, :], hglL[:, 0, :], SUB)
    K.act(lbL[:, 1, :], lbL[:, 1, :], AF.Sigmoid)
    K.ts("vector", omlL, lbL, -1.0, MUL, 1.0, ADD)
    K.ts("vector", nomlL, omlL, -1.0, MUL)
    mqn = P.sbuf("mqn", [128, NL, 2], F32)
    mkvn = P.sbuf("mkvn", [128, NL, 128], F32)
    s5d = P.sbuf("s5d", [128, NL, 2], F32)
    for l in range(NL):
        K.dma(mqn[:, l, :], din["mqn"][l])
        K.dma(mkvn[:, l, :], din["mkvn"][l])
        K.dma(s5d[:, l, :], din["s5d"][l])

    def run_job(lat):
        P.push_scope()
        nt = 16 if lat else 4
        L = nt * 128
        v = 1 if lat else 0
        seqs = [(0, L)] if lat else [(0, 256), (256, 256)]
        BS = 512 if lat else 256
        xsrc = din["xl"] if lat else din["xc"]
        ydst = dout["y_l"] if lat else dout["y_c"]
        ntx = 4 if lat else nt
        x = P.sbuf("x", [128, ntx, D], F32)
        xv = [x.view(x.ap[:, i, :], "x%d" % i) for i in range(ntx)]
        hT = P.sbuf("hT", [128, 8, L], BF16)
        hTo = P.sbuf("hTo", [128, 8, 512], BF16) if lat else hT
        mixed = P.sbuf("mixed", [128, 2, L], BF16) if not lat else None
        mixh = P.sbuf("mixh", [64, 4, L], BF16) if lat else None
        NH = 1 if lat else 4
        for i in range(ntx):
            K.dma(xv[i], xsrc[i])

        def rstd_of(xt, n, ss_scale):
            s = sm.next()
            K.act(junk[:, 0:n], xt, AF.Square, accum=s[:, 0:1])
            K.act(s[:, 1:2], s[:, 0:1], AF.Sqrt, bias=epsc, scale=ss_scale)
            K.recip(s[:, 2:3], s[:, 1:2])
            return s[:, 2:3]

        def load_group(l, name):
            if lat:
                if name.startswith("Cf"):
                    name = name[:3]
                elif name[:2] not in ("Bg", "Dc"):
                    name = name.rstrip("0123456789")
            off = (LGT if lat else GT)[name]
            s = wst.next()
            K.dma(s, din["winL" if lat else "win"][l][off // 128])
            w = wbf.next()
            K.cp("scalar", w, s)
            return w

        def fm_cols(w, c0, n, M=128):
            ps = psr.next()
            for k in range(8):
                K.mm(ps[0:M, 0:n], w[:, k, 0:M], hT[:, k, c0:c0 + n], start=(k == 0), stop=(k == 7))
            return ps

        def fm_block(w, tb, M=128):
            return fm_cols(w, tb * TB, TB, M)

        def tm_tile(w, i, N=128):
            ps = psr.next()
            for k in range(8):
                K.mm(ps[:, 0:N], hT[:, k, i * 128:(i + 1) * 128], w[:, k, 0:N], start=(k == 0), stop=(k == 7))
            return ps

        def blk(tb):
            return slice(tb * TB, (tb + 1) * TB)

        def tile_seq_rows(i):
            return i // 2, (i % 2) * 128

        def norm_mod(l):
            for i in range(ntx):
                r = rstd_of(xv[i], D, 1.0 / D)
                xn = xnr.next()
                K.ts("vector", xn, xv[i], r, MUL)
                for half in range(2):
                    ps = psr.next()
                    for kk in range(4):
                        k = half * 4 + kk
                        K.mm(ps[:, kk * 128:(kk + 1) * 128], xn[:, k * 128:(k + 1) * 128], ident_b)
                    for kk in range(4):
                        k = half * 4 + kk
                        eng = "vector" if kk % 2 == 0 else "gpsimd"
                        if eng == "gpsimd":
                            K.act(hTo[:, k, i * 128:(i + 1) * 128], ps[:, kk * 128:(kk + 1) * 128], AF.Identity,
                                  bias=modT[:, l, v, k:k + 1], scale=modT[:, l, v, 8 + k:9 + k])
                        else:
                            K.ts("vector", hTo[:, k, i * 128:(i + 1) * 128], ps[:, kk * 128:(kk + 1) * 128],
                                 modT[:, l, v, 8 + k:9 + k], MUL, modT[:, l, v, k:k + 1], ADD)
            if lat:
                ahin = DR(nc.dram_tensor("ahin%d" % l, [1024, 512], BF16).ap())
                ahout = DR(nc.dram_tensor("ahout%d" % l, [4096, 512], BF16).ap())
                ahin_tok = Buf("ahin_tok%d" % l, None)
                ahout_tok = Buf("ahout_tok%d" % l, None)
                P.dma(ahin.ap.rearrange("(k p) t -> p k t", p=128), hTo.ap, reads=[hTo], writes=[ahin_tok], eng="gpsimd")
                P.collective(lambda e: e.collective_compute("AllGather", ALU.bypass, replica_groups=[[0, 1, 2, 3], [4, 5, 6, 7]],
                                                            ins=[ahin.ap.opt()], outs=[ahout.ap.opt()]),
                             reads=[ahin_tok], writes=[ahout_tok])
                HTL["ahout"], HTL["tok"] = ahout, ahout_tok

        HTL = {}

        def hT_load():
            ahout, ahout_tok = HTL["ahout"], HTL["tok"]
            for r_ in range(4):
                P.dma(hT.ap[:, :, r_ * 512:(r_ + 1) * 512], ahout.ap[r_ * 1024:(r_ + 1) * 1024, :].rearrange("(k p) t -> p k t", p=128),
                      reads=[ahout_tok], writes=[hT])

        def out_proj(l, bi):
            wo = [wbf.next(), wbf.next()]
            for kc in range(2):
                s = wst.next()
                sv = s.re("p a b -> p (a b)")
                K.dma(sv, din["wout"][l][:, 2 * bi + kc, :])
                wov = wo[kc].re("p a b -> p (a b)")
                for n in range(2):
                    gb = wf.next()
                    P.dma(gb.ap, gsc[l][v:v + 1, n * 512:(n + 1) * 512].ap.partition_broadcast(128), reads=[gsc_tok], writes=[gb])
                    K.tt("vector", wov[:, n * 512:(n + 1) * 512], gb, sv[:, n * 512:(n + 1) * 512], MUL)
            for i in range(nt):
                for n in range(2):
                    ps = psr.next()
                    for kc in range(2):
                        K.mm(ps, mixed[:, kc, i * 128:(i + 1) * 128], wo[kc].re("p a b -> p (a b)")[:, n * 512:(n + 1) * 512],
                             start=(kc == 0), stop=(kc == 1))
                    K.tt("vector", xv[i][:, n * 512:(n + 1) * 512], ps, xv[i][:, n * 512:(n + 1) * 512], ADD)

        def attention(qT, kT, Vaug_of, Kdim, bases, scale, h, epilogue):
            pb = 64 * (h % 2)
            dn = 64 - pb
            QB = BS
            steps = []
            for (s0, sl) in seqs:
                if lat:
                    ktiles = list(range((L + PAST) // 128))
                else:
                    ktiles = list(range(s0 // 128, (s0 + sl) // 128))
                for qb in range(sl // QB):
                    q0 = s0 + qb * QB
                    for bi_, base in enumerate(bases):
                        for idx, kt in enumerate(ktiles):
                            steps.append((q0, bi_, base, idx, kt, len(ktiles)))
            pend = []
            state = {}

            def do_pv(item):
                (q0, bi_, base, idx, kt, nk), pt = item
                if idx == 0:
                    state["acc"] = accs.next()
                    if bi_ == 0:
                        state["ocs"] = []
                acc = state["acc"]
                K.mm(acc[:, 0:QB], Vaug_of(kt), pt[:, 0:QB], start=(idx == 0), stop=(idx == nk - 1))
                if idx == nk - 1:
                    rec = wf.next()
                    K.act(rec[dn:dn + 64, 0:QB], acc[dn:dn + 64, 0:QB], AF.Ln)
                    K.act(rec[dn:dn + 64, 0:QB], rec[dn:dn + 64, 0:QB], AF.Exp, scale=-1.0)
                    oc = wf.next()
                    K.tt("vector", oc[pb:pb + 64, 0:QB], acc[pb:pb + 64, 0:QB], rec[dn:dn + 64, 0:QB], MUL)
                    state["ocs"].append(oc)
                    if bi_ == len(bases) - 1:
                        epilogue(state["ocs"], q0, QB, pb)

            for stp in steps:
                (q0, bi_, base, idx, kt, nk) = stp
                pss = psr.next()
                K.mm(pss[:, 0:QB], kT[base:base + Kdim, kt * 128:(kt + 1) * 128], qT[base:base + Kdim, q0:q0 + QB])
                pt = wb.next()
                K.act(pt[:, 0:QB], pss[:, 0:QB], AF.Exp, scale=scale)
                pend.append((stp, pt))
                if len(pend) > 3:
                    do_pv(pend.pop(0))
            while pend:
                do_pv(pend.pop(0))

        def branch_A(l):
            P.push_scope()
            Lk = L + PAST if lat else L
            nkt = Lk // 128
            if not lat:
                for i2 in range(2):
                    for nm, dst in (("Av", "o_dv"), ("Akt", "o_dk")):
                        w = load_group(l, "%s%d" % (nm, i2))
                        for i in range(nt):
                            ps = tm_tile(w, i)
                            o = wf.next()
                            K.cp("scalar" if i % 2 else "vector", o[:, 0:128], ps[:, 0:128])
                            sq, r0 = tile_seq_rows(i)
                            K.dma(dout[dst][sq, l, r0:r0 + 128, i2 * 128:(i2 + 1) * 128], o[:, 0:128], out=True, eng="gpsimd")
            else:
                ckf = P.sbuf("ckf", [128, 2, 64], F32)
                cvf = P.sbuf("cvf", [128, 2, 64], F32)
                for j in range(2):
                    K.dma(cvf[:, j, :], din["cdvL"][l][j])
                    K.dma(ckf[:, j, :], din["cdkL"][l][j])
            sgA = P.sbuf("sgA", [128, L], BF16)
            Vh = P.sbuf("VhA", [128, nkt, 128], BF16)
            qT = P.sbuf("qT", [128, L], BF16)
            kT = P.sbuf("kT", [128, Lk], BF16)
            kpad = P.sbuf("kpad", [128, 128], BF16)
            K.memset("vector", kpad, 0.0)
            for h in range(NH):
                pbh = 64 * (h % 2)
                if h % 2 == 0:
                    w = load_group(l, "Ag%d" % (h // 2))
                    for tb in range(L // TB):
                        ps = fm_block(w, tb)
                        K.act(sgA[:, blk(tb)], ps, AF.Silu)
                w = load_group(l, "Av%d" % (h // 2))
                K.memset("gpsimd", Vh[:, :, 64 - pbh:128 - pbh], 1.0)
                for i in range(nt):
                    ps = psr.next()
                    for k in range(8):
                        K.mm(ps[:, 0:64], hT[:, k, i * 128:(i + 1) * 128], w[:, k, pbh:pbh + 64], start=(k == 0), stop=(k == 7))
                    K.cp("scalar" if i % 2 else "vector", Vh[:, i, pbh:pbh + 64], ps[:, 0:64])
                if lat:
                    for j in range(2):
                        K.cp("gpsimd", Vh[:, nt + j, pbh:pbh + 64], cvf[:, j, h * 64:(h + 1) * 64])
                for nm, dst in (("Aq", qT), ("Ak", kT)):
                    w = load_group(l, "%s%d" % (nm, h))
                    if lat:
                        wp = load_group(l, "%sp%d" % (nm, h))
                    for tb in range(L // TB):
                        ps = fm_block(w, tb)
                        if lat:
                            psp = fm_block(wp, tb)
                            t1 = wf.next()
                            t2 = wf.next()
                            K.tt("vector", t1, ps, ropeC[:, blk(tb)], MUL)
                            K.tt("vector", t2, psp, ropeS[:, blk(tb)], MUL)
                            K.tt("vector", dst[:, blk(tb)], t1, t2, ADD)
                        else:
                            K.cp("scalar", dst[:, blk(tb)], ps)
                if lat:
                    for j in range(2):
                        K.cp("vector", kpad[:, 0:32], ckf[:, j, h * 64:h * 64 + 32])
                        K.cp("vector", kpad[:, 64:96], ckf[:, j, h * 64 + 32:h * 64 + 64])
                        ps = psr.next()
                        K.mm(ps[:, 0:128], kpad, ident_b)
                        K.cp("vector", kT[:, L + j * 128:L + (j + 1) * 128], ps[:, 0:128])

                def epi(ocs, q0, QB, pb, h=h):
                    o = wf.next()
                    K.stt("vector", o[pb:pb + 64, 0:QB], ocs[1][pb:pb + 64, 0:QB], neglam[pb:pb + 64, l:l + 1],
                          ocs[0][pb:pb + 64, 0:QB], MUL, ADD)
                    sq = wb.next()
                    K.tt("vector", sq[pb:pb + 64, 0:QB], o[pb:pb + 64, 0:QB], o[pb:pb + 64, 0:QB], MUL)
                    pss = psr.next()
                    K.mm(pss[:, 0:QB], ones_b[pb:pb + 64, :], sq[pb:pb + 64, 0:QB])
                    rs = wf.next()
                    K.act(rs[pb:pb + 64, 0:QB], pss[pb:pb + 64, 0:QB], AF.Ln, bias=epsc[pb:pb + 64, :], scale=1.0 / 64)
                    K.act(rs[pb:pb + 64, 0:QB], rs[pb:pb + 64, 0:QB], AF.Exp, scale=-0.5)
                    K.tt("vector", o[pb:pb + 64, 0:QB], o[pb:pb + 64, 0:QB], rs[pb:pb + 64, 0:QB], MUL)
                    dst = mixh[0:64, 0, q0:q0 + QB] if lat else mixed[pb:pb + 64, h // 2, q0:q0 + QB]
                    K.stt("vector", dst, o[pb:pb + 64, 0:QB], dan_s[pb:pb + 64, l:l + 1],
                          sgA[pb:pb + 64, q0:q0 + QB], MUL, MUL)

                attention(qT, kT, lambda kt: Vh[:, kt, :], 64, (0, 64), 32 ** -0.5, h, epi)
            P.pop_scope()

        def branch_D(l):
            P.push_scope()
            Lk = L + PAST if lat else L
            nkt = Lk // 128
            wuq_b = P.sbuf("wuq_b", [128, 2, 192 if lat else 768], BF16)
            wukv_b = P.sbuf("wukv_b", [128, 128 if lat else 512], BF16)
            P.push_scope()
            wuq_f = P.sbuf("wuq_f", [128, 2, 768], F32)
            NQ = 192 if lat else 768
            K.dma(wuq_f[:, :, 0:NQ], din["wuqL" if lat else "wuq"][l])
            for kc in range(2):
                K.ts("vector", wuq_b[:, kc, 0:NQ], wuq_f[:, kc, 0:NQ], mqn[:, l, kc:kc + 1], MUL)
            s = wf.next()
            NKV = 128 if lat else 512
            K.dma(s[:, 0:NKV], din["wukvL" if lat else "wukv"][l])
            K.cp("vector", wukv_b[:, 0:NKV], s[:, 0:NKV])
            QPO = 96 if lat else 384
            P.pop_scope()
            cqb = P.sbuf("cqb", [128, 2, L], BF16)
            for kc in range(2):
                w = load_group(l, "Dcq%d" % kc)
                for tb in range(L // TB):
                    ps = fm_block(w, tb)
                    K.cp("scalar", cqb[:, kc, blk(tb)], ps)
            ckvT = P.sbuf("ckvT", [128, Lk], BF16)
            krT = P.sbuf("krT", [128, Lk], BF16)
            w = load_group(l, "Dckv")

            def ckv_tile(src_ps, col0, raw_is_psum=True, out_to=None):
                cnb = wb.next()
                if out_to is None:
                    K.cp("vector", cnb[:, 0:128], src_ps)
                else:
                    r = rstd_of(src_ps, 128, 1.0 / 128)
                    cn = wf.next()
                    K.stt("vector", cn[:, 0:128], src_ps, r, mkvn[:, l, :], MUL, MUL)
                    if out_to is not False:
                        K.dma(out_to, cn[:, 0:128], out=True, eng="gpsimd")
                    K.cp("gpsimd", cnb[:, 0:128], cn[:, 0:128])
                pst = psr.next()
                K.mm(pst[:, 0:128], cnb[:, 0:128], ident_b)
                K.cp("scalar", ckvT[:, col0:col0 + 128], pst[:, 0:128])

            for i in range(nt):
                ps = tm_tile(w, i)
                if lat:
                    ckv_tile(ps[:, 0:128], i * 128, out_to=False)
                else:
                    sq, r0 = tile_seq_rows(i)
                    ckv_tile(ps[:, 0:128], i * 128, out_to=dout["o_ckv"][sq, l, r0:r0 + 128, :])
            if lat:
                for j in range(2):
                    s = wf.next()
                    K.dma(s[:, 0:128], din["cckv"][l][j])
                    ckv_tile(s[:, 0:128], L + j * 128)
            w = load_group(l, "Dkr")
            if lat:
                wp = load_group(l, "Dkrp")
            for tb in range(L // TB):
                ps = fm_block(w, tb, M=96)
                if lat:
                    psp = fm_block(wp, tb, M=96)
                    t1 = wf.next()
                    t2 = wf.next()
                    K.tt("vector", t1[64:96, :], ps[64:96, :], ropeC[64:96, blk(tb)], MUL)
                    K.tt("vector", t2[64:96, :], psp[64:96, :], ropeS[64:96, blk(tb)], MUL)
                    K.tt("gpsimd", krT[64:96, blk(tb)], t1[64:96, :], t2[64:96, :], ADD)
                else:
                    K.cp("scalar", krT[64:96, blk(tb)], ps[64:96, :])
            if lat:
                kpad = P.sbuf("kpadD", [128, 128], BF16)
                K.memset("vector", kpad, 0.0)
                for j in range(2):
                    s = wf.next()
                    K.dma(s[:, 0:32], din["ckr"][l][j])
                    K.cp("vector", kpad[:, 64:96], s[:, 0:32])
                    pst = psr.next()
                    K.mm(pst[0:96, 0:128], kpad[:, 0:96], ident_b)
                    K.cp("vector", krT[64:96, L + j * 128:L + (j + 1) * 128], pst[64:96, 0:128])
            else:
                w = load_group(l, "Dkrt")
                for i in range(nt):
                    ps = tm_tile(w, i, N=32)
                    o = wf.next()
                    K.cp("vector", o[:, 0:32], ps[:, 0:32])
                    sq, r0 = tile_seq_rows(i)
                    K.dma(dout["o_kr"][sq, l, r0:r0 + 128, :], o[:, 0:32], out=True, eng="gpsimd")
            qT = P.sbuf("qTD", [128, L], BF16)
            kT = P.sbuf("kTD", [128, Lk], BF16)
            Vh = P.sbuf("VhD", [128, nkt, 128], BF16)
            sgD = P.sbuf("sgD", [128, L], BF16)
            for h in range(NH):
                pb = 64 * (h % 2)
                if h % 2 == 0:
                    w = load_group(l, "Dg%d" % (h // 2))
                    for tb in range(L // TB):
                        ps = fm_block(w, tb)
                        K.act(sgD[:, blk(tb)], ps, AF.Silu)
                for tb in range(L // TB):
                    sq0 = wb.next()
                    sq1 = wb.next()
                    K.act(sq0, cqb[:, 0, blk(tb)], AF.Square)
                    K.act(sq1, cqb[:, 1, blk(tb)], AF.Square)
                    pss = psr.next()
                    K.mm(pss[0:96, :], ones_b[:, 0:96], sq0, start=True, stop=False)
                    K.mm(pss[0:96, :], ones_b[:, 0:96], sq1, start=False, stop=True)
                    rq = wf.next()
                    K.act(rq[0:96, :], pss[0:96, :], AF.Ln, bias=epsc[0:96, :], scale=1.0 / 192)
                    K.act(rq[0:96, :], rq[0:96, :], AF.Exp, scale=-0.5)
                    z = psr.next()
                    for kc in range(2):
                        K.mm(z[0:96, :], wuq_b[:, kc, h * 96:(h + 1) * 96], cqb[:, kc, blk(tb)], start=(kc == 0), stop=(kc == 1))
                    K.tt("vector", qT[0:64, blk(tb)], z[0:64, :], rq[0:64, :], MUL)
                    if lat:
                        zp = psr.next()
                        for kc in range(2):
                            K.mm(zp[0:96, :], wuq_b[:, kc, QPO + h * 96:QPO + (h + 1) * 96], cqb[:, kc, blk(tb)],
                                 start=(kc == 0), stop=(kc == 1))
                        t1 = wf.next()
                        t2 = wf.next()
                        K.tt("vector", t1[64:96, :], z[64:96, :], ropeC[64:96, blk(tb)], MUL)
                        K.tt("vector", t2[64:96, :], zp[64:96, :], ropeS[64:96, blk(tb)], MUL)
                        K.tt("vector", t1[64:96, :], t1[64:96, :], t2[64:96, :], ADD)
                        K.tt("vector", qT[64:96, blk(tb)], t1[64:96, :], rq[64:96, :], MUL)
                    else:
                        K.tt("vector", qT[64:96, blk(tb)], z[64:96, :], rq[64:96, :], MUL)
                for c0 in range(0, Lk, TB):
                    n = min(TB, Lk - c0)
                    ps = psr.next()
                    K.mm(ps[0:64, 0:n], wukv_b[:, h * 128:h * 128 + 64], ckvT[:, c0:c0 + n])
                    K.cp("scalar", kT[0:64, c0:c0 + n], ps[0:64, 0:n])
                K.cp("vector", kT[64:96, :], krT[64:96, :])
                K.memset("gpsimd", Vh[:, :, 64 - pb:128 - pb], 1.0)
                for kt in range(nkt):
                    ps = psr.next()
                    K.mm(ps[:, 0:64], ckvT[:, kt * 128:(kt + 1) * 128], wukv_b[:, h * 128 + 64:h * 128 + 128])
                    K.cp("vector" if kt % 2 else "scalar", Vh[:, kt, pb:pb + 64], ps[:, 0:64])

                def epi(ocs, q0, QB, pb, h=h):
                    dst = mixh[0:64, 2, q0:q0 + QB] if lat else mixed[pb:pb + 64, h // 2, q0:q0 + QB]
                    K.tt("vector", dst, ocs[0][pb:pb + 64, 0:QB], sgD[pb:pb + 64, q0:q0 + QB], MUL)

                attention(qT, kT, lambda kt: Vh[:, kt, :], 96, (0,), 96 ** -0.5, h, epi)
            P.pop_scope()

        def branch_C(l):
            P.push_scope()
            qb_ = P.sbuf("hq", [128, L], BF16)
            vtok = P.sbuf("hv", [128, nt, 128], BF16)
            obuf = P.sbuf("ho", [128, L], F32)
            if lat:
                K.memset("gpsimd", obuf, 0.0)
            Sfr = Ring([P.sbuf("hSf%d" % i, [128, 128], F32) for i in range(3)])
            sbring = Ring([P.sbuf("hSb%d" % i, [128, 128], BF16) for i in range(6)])
            hsg, hf, hg_, hkk, hb, heb = [P.sbuf("h_t%d" % i, [128, 512], F32) for i in range(6)]
            hqt, hkt, hkh = [P.sbuf("h_b%d" % i, [128, 512], BF16) for i in range(3)]
            HH = 1 if lat else 2
            for hp in range(1 if lat else 2):
                w = load_group(l, "Cq%d" % hp)
                for tb in range(L // TB):
                    ps = fm_block(w, tb)
                    K.cp("scalar", qb_[:, blk(tb)], ps)
                w = load_group(l, "Cv%d" % hp)
                for i in range(nt):
                    ps = tm_tile(w, i)
                    K.cp("vector", vtok[:, i, :], ps[:, 0:128])
                for d in range(2):
                    w = load_group(l, "Cf%d%d" % (d, hp))
                    ci = 2 * d + hp
                    if lat:
                        lbv, omv, nomv = lbL[:, l, d:d + 1], omlL[:, l, d:d + 1], nomlL[:, l, d:d + 1]
                    else:
                        lbv, omv, nomv = lb[:, l, ci:ci + 1], oml[:, l, ci:ci + 1], noml[:, l, ci:ci + 1]
                    for si, (s0, sl) in enumerate(seqs):
                        Sf = Sfr.next()
                        K.memset("vector", Sf, 0.0)
                        if lat:
                            K.dma(Sf[0:64, 0:64], din["chgL"][l][d])
                        cur = {"sb": sbring.next(), "sf": Sf}
                        K.cp("scalar", cur["sb"], Sf)
                        nb = sl // BS
                        for bi in (range(nb) if d == 0 else range(nb - 1, -1, -1)):
                            c0 = s0 + bi * BS
                            ncks = BS // HGC
                            if CSTOP < 1:
                                continue
                            ps = fm_cols(w, c0, BS)
                            sg = hsg
                            K.act(sg[:, 0:BS], ps[:, 0:BS], AF.Sigmoid)
                            f = hf
                            K.ts("vector", f[:, 0:BS], sg[:, 0:BS], omv, MUL, lbv, ADD)
                            g = hg_
                            K.act(g[:, 0:BS], f[:, 0:BS], AF.Ln)
                            kk = hkk
                            K.ts("vector", kk[:, 0:BS], sg[:, 0:BS], nomv, MUL, omv, ADD)
                            b = hb
                            if d == 0:
                                K.scan(b[:, 0:BS], m01[:, 0:BS], g[:, 0:BS], 0.0)
                            else:
                                K.scan(b[:, BS - 1::-1] if False else b[:, 0:BS][:, ::-1], m01[:, 0:BS], g[:, 0:BS][:, ::-1], 0.0)
                            if CSTOP < 2:
                                continue
                            eb = heb
                            K.act(eb[:, 0:BS], b[:, 0:BS], AF.Exp)
                            enb = g
                            K.act(enb[:, 0:BS], b[:, 0:BS], AF.Exp, scale=-1.0)
                            if CSTOP < 2.1:
                                continue
                            qt = hqt
                            K.tt("vector", qt[:, 0:BS], qb_[:, c0:c0 + BS], eb[:, 0:BS], MUL)
                            kt_ = hkt
                            K.tt("gpsimd", kt_[:, 0:BS], kk[:, 0:BS], enb[:, 0:BS], MUL)
                            if CSTOP < 2.2:
                                continue
                            b3 = b[:, 0:BS].re("p (c j) -> p c j", j=HGC)
                            e = HGC - 1 if d == 0 else 0
                            d2 = f
                            K.tt("vector", d2[:, 0:BS].re("p (c j) -> p c j", j=HGC), b3[:, :, e:e + 1].bc([128, ncks, HGC]), b3, SUB)
                            K.act(d2[:, 0:BS], d2[:, 0:BS], AF.Exp)
                            if CSTOP < 2.3:
                                continue
                            kh = hkh
                            K.tt("gpsimd", kh[:, 0:BS], kk[:, 0:BS], d2[:, 0:BS], MUL)
                            ntile = BS // 128
                            if CSTOP < 3:
                                continue
                            for ti in (range(ntile) if d == 0 else range(ntile - 1, -1, -1)):
                                lo = ti * 128
                                gi = (c0 + lo) // 128
                                pss2 = [psr.next(), psr.next()]
                                for hh in range(HH):
                                    K.mm(pss2[hh][:, 0:128], kt_[64 * hh:64 * hh + 64, lo:lo + 128],
                                         qt[64 * hh:64 * hh + 64, lo:lo + 128])
                                if CSTOP < 3.1:
                                    continue
                                sc = wb.next()
                                for hh in range(HH):
                                    K.tt("vector", sc[:, hh * 128:(hh + 1) * 128], pss2[hh][:, 0:128], hmask[d], MUL)
                                if CSTOP < 4:
                                    continue
                                pst = psr.next()
                                K.mm(pst[:, 0:128], kh[:, lo:lo + 128], ident_b)
                                kexp = wb.next()
                                K.tt("vector", kexp[:, :].re("p (c k) -> p c k", c=4),
                                     pst[:, 0:128].re("p (o k) -> p o k", o=1).bc([128, 4, 128]),
                                     m4[:, :].re("p (c o) -> p c o", o=1).bc([128, 4, 128]), MUL)
                                if CSTOP < 5:
                                    continue
                                psu = psr.next()
                                for c in range(4):
                                    K.mm(psu[:, c * 128:(c + 1) * 128], kexp[:, c * 128:(c + 1) * 128], vtok[:, gi, :])
                                if CSTOP < 6:
                                    continue
                                corder = list(range(4)) if d == 0 else [3, 2, 1, 0]
                                Sbs = []
                                for cn, c in enumerate(corder):
                                    Sbs.append(cur["sb"])
                                    ce = lo + c * HGC + e
                                    Sfn = Sfr.next()
                                    K.stt("vector", Sfn, cur["sf"], eb[:, ce:ce + 1], psu[:, c * 128:(c + 1) * 128], MUL, ADD)
                                    cur["sf"] = Sfn
                                    nsb = sbring.next()
                                    K.cp("scalar", nsb, Sfn)
                                    cur["sb"] = nsb
                                psos = [psr.next(), psr.next()]
                                for cn, c in enumerate(corder):
                                    for hh in range(HH):
                                        pso = psos[hh]
                                        K.mm(pso[:, c * HGC:(c + 1) * HGC], vtok[:, gi, :],
                                             sc[:, hh * 128 + c * HGC:hh * 128 + (c + 1) * HGC], start=True, stop=False)
                                        K.mm(pso[:, c * HGC:(c + 1) * HGC], Sbs[cn][64 * hh:64 * hh + 64, :],
                                             qt[64 * hh:64 * hh + 64, lo + c * HGC:lo + (c + 1) * HGC], start=False, stop=True)
                                t0 = c0 + lo
                                for hh in range(HH):
                                    pbh = 64 * hh
                                    pso = psos[hh]
                                    if d == 0:
                                        K.cp("vector", obuf[pbh:pbh + 64, t0:t0 + 128], pso[pbh:pbh + 64, 0:128])
                                    else:
                                        K.tt("vector", obuf[pbh:pbh + 64, t0:t0 + 128], pso[pbh:pbh + 64, 0:128],
                                             obuf[pbh:pbh + 64, t0:t0 + 128], ADD)
                        if not lat:
                            seqi = si
                            for hh in range(2):
                                K.dma(dout["o_hg"][seqi, l, d, 2 * hp + hh], cur["sf"][64 * hh:64 * hh + 64, 64 * hh:64 * hh + 64],
                                      out=True, eng="gpsimd")
                w = load_group(l, "Cg%d" % hp)
                for tb in range(L // TB):
                    ps = fm_block(w, tb)
                    sg = wb.next()
                    K.act(sg, ps, AF.Silu)
                    sq = wb.next()
                    K.tt("gpsimd", sq, obuf[:, blk(tb)], obuf[:, blk(tb)], MUL)
                    pss = psr.next()
                    K.mm(pss, bdones, sq)
                    rs = wf.next()
                    K.act(rs, pss, AF.Ln, bias=epsc, scale=1.0 / 64)
                    K.act(rs, rs, AF.Exp, scale=-0.5)
                    t = wf.next()
                    K.tt("vector", t, obuf[:, blk(tb)], rs, MUL)
                    if lat:
                        K.stt("vector", mixh[0:64, 1, blk(tb)], t[0:64, :], hgn[0:64, l:l + 1], sg[0:64, :], MUL, MUL)
                    else:
                        K.stt("vector", mixed[:, hp, blk(tb)], t, hgn[:, l:l + 1], sg, MUL, MUL)
            P.pop_scope()

        BP = {}

        def b_prep(l):
            P.push_scope()
            C1 = 6.28125
            C2 = 2.0 * PI - C1
            NST = 2 if lat else 8
            NCC = 1 if lat else 2
            KR = 64 if lat else 128
            sp = P.sbuf("s5sp", [128, 16, 3], F32)
            K.dma(sp[:, 0:2 * NST, :], din["s5pL" if lat else "s5p"][l])
            step = P.sbuf("s5step", [128, 16], F32)
            th = P.sbuf("s5th", [128, 16], F32)
            rmag = P.sbuf("s5mag", [128, 16], F32)
            N2 = 2 * NST
            K.act(step[:, 0:N2], sp[:, 0:N2, 2], AF.Exp)
            K.tt("vector", th[:, 0:N2], sp[:, 0:N2, 1], step[:, 0:N2], MUL)
            K.tt("vector", rmag[:, 0:N2], sp[:, 0:N2, 0], step[:, 0:N2], MUL)
            K.act(rmag[:, 0:N2], rmag[:, 0:N2], AF.Exp)
            EC2 = P.sbuf("s5EC", [128, 2, NST, S5T], F32)
            ES2 = P.sbuf("s5ES", [128, 2, NST, S5T], F32)
            Bm2 = P.sbuf("s5Bm", [128, 2, NST, 2, 128], BF16)
            Cm2 = P.sbuf("s5Cm", [128, 2, NST, 3, 128], BF16)
            fz2 = P.sbuf("s5f", [128, 2, 8, 4], F32)
            cst = P.sbuf("s5cst", [128, 8, 2], F32)
            h0 = P.sbuf("s5h0", [128, 16, 2], F32)
            if lat:
                K.dma(h0[:, 0:4, :], din["cs5L"][l])
            else:
                wglu_b = P.sbuf("wglu_b", [128, 2, 512], BF16)
                s = wst.next()
                sv = s.re("p a b -> p (a b)")
                K.dma(sv, din["wglu"][l].re("p a b -> p (a b)"))
                K.cp("vector", wglu_b[:, :, :].re("p a b -> p (a b)"), sv)
            for d in range(2):
                EC, ES, Bm, Cm, fz = EC2[:, d], ES2[:, d], Bm2[:, d], Cm2[:, d], fz2[:, d]
                P.push_scope()
                kint = P.sbuf("s5ki", [128, 512], I32)
                SH = min(512 // S5T, NST)
                for half in range(NST // SH):
                    ang = wf.next()
                    a3 = ang[:, 0:SH * S5T].re("p (s t) -> p s t", s=SH)
                    K.tt("vector", a3, iota[:, :].re("p (o t) -> p o t", o=1).bc([128, SH, S5T]),
                         th[:, d * NST + half * SH:d * NST + half * SH + SH].re("p (s o) -> p s o", o=1).bc([128, SH, S5T]), MUL)
                    W_ = SH * S5T
                    kf = wf.next()
                    K.ts("vector", kf[:, 0:W_], ang[:, 0:W_], 1.0 / (2 * PI), MUL)
                    K.cp("vector", kint[:, 0:W_], kf[:, 0:W_])
                    K.cp("vector", kf[:, 0:W_], kint[:, 0:W_])
                    xx = wf.next()
                    K.stt("vector", xx[:, 0:W_], kf[:, 0:W_], -C1, ang[:, 0:W_], MUL, ADD)
                    K.stt("vector", xx[:, 0:W_], kf[:, 0:W_], -C2, xx[:, 0:W_], MUL, ADD)
                    K.ts("vector", xx[:, 0:W_], xx[:, 0:W_], PI, MIN, -PI, MAX)
                    K.act(ES[:, half * SH:half * SH + SH, :].re("p s t -> p (s t)"), xx[:, 0:W_], AF.Sin)
                    K.act(kf[:, 0:W_], xx[:, 0:W_], AF.Sin, scale=0.5)
                    K.tt("vector", kf[:, 0:W_], kf[:, 0:W_], kf[:, 0:W_], MUL)
                    K.ts("vector", EC[:, half * SH:half * SH + SH, :].re("p s t -> p (s t)"), kf[:, 0:W_], -2.0, MUL, 1.0, ADD)
                P.pop_scope()
                are = sp[:, d * NST:d * NST + NST, 0]
                aim = sp[:, d * NST:d * NST + NST, 1]
                mg = rmag[:, d * NST:d * NST + NST]
                t = sm.next()
                abr, abi, den, t1, t2 = t[:, 0:NST], t[:, 8:8 + NST], None, None, None
                u = sm.next()
                den, t1 = u[:, 0:NST], u[:, 8:8 + NST]
                u2 = sm.next()
                t2, t3 = u2[:, 0:NST], u2[:, 8:8 + NST]
                fzv = fz[:, 0:NST, :]
                K.tt("vector", abr, mg, EC[:, 0:NST, 0], MUL)
                K.tt("vector", abi, mg, ES[:, 0:NST, 0], MUL)
                K.ts("vector", abr, abr, -1.0, ADD)
                K.tt("vector", den, are, are, MUL)
                K.tt("vector", t1, aim, aim, MUL)
                K.tt("vector", den, den, t1, ADD)
                K.recip(den, den)
                K.tt("vector", t1, abr, are, MUL)
                K.tt("vector", t2, abi, aim, MUL)
                K.tt("vector", t1, t1, t2, ADD)
                K.tt("vector", fzv[:, :, 0], t1, den, MUL)
                K.tt("vector", t1, abi, are, MUL)
                K.tt("vector", t2, abr, aim, MUL)
                K.tt("vector", t1, t1, t2, SUB)
                K.tt("vector", fzv[:, :, 1], t1, den, MUL)
                K.ts("vector", fzv[:, :, 2], fzv[:, :, 1], -1.0, MUL)
                for st in range(NST):
                    s = wst.next()
                    sv = s.re("p a b -> p (a b)")
                    if lat:
                        K.dma(sv[0:64, 0:256].re("p (r c) -> p r c", r=2), din["s5BL"][l][d][st].re("r p c -> p r c"))
                        K.cp("gpsimd", Bm[0:64, st, :, :], sv[0:64, 0:256].re("p (r c) -> p r c", r=2))
                        K.memset("vector", sv[:, 256:512], 0.0)
                        K.dma(sv[:, 256:512].re("p (r c) -> p r c", r=2)[:, :, 0:64], din["s5CL"][l][d][st].re("r p c -> p r c"))
                    else:
                        K.dma(sv[:, 0:256].re("p (r c) -> p r c", r=2), din["s5B"][l][d][st].re("r p c -> p r c"))
                        K.cp("gpsimd", Bm[:, st, :, :], sv[:, 0:256].re("p (r c) -> p r c", r=2))
                        K.dma(sv[:, 256:512].re("p (r c) -> p r c", r=2), din["s5C"][l][d][st].re("r p c -> p r c"))
                    cre, cim = sv[:, 256:384], sv[:, 384:512]
                    tw = wf.next()
                    K.ts("vector", tw[:, 0:128], cre, fz[:, st, 0:1], MUL)
                    K.stt("vector", tw[:, 0:128], cim, fz[:, st, 2:3], tw[:, 0:128], MUL, ADD)
                    K.cp("vector", Cm[:, st, 0, :], tw[:, 0:128])
                    K.ts("vector", Cm[:, st, 1, :], tw[:, 0:128], -1.0, MUL)
                    K.ts("vector", tw[:, 128:256], cre, fz[:, st, 1:2], MUL)
                    K.stt("vector", tw[:, 128:256], cim, fz[:, st, 0:1], tw[:, 128:256], MUL, ADD)
                    K.ts("vector", Cm[:, st, 2, :], tw[:, 128:256], -1.0, MUL)
            BP.update(dict(sp=sp, rmag=rmag, EC2=EC2, ES2=ES2, Bm2=Bm2, Cm2=Cm2, fz2=fz2, cst=cst, h0=h0, NST=NST, NCC=NCC, KR=KR))
            if not lat:
                BP["wglu_b"] = wglu_b

        def branch_B(l):
            P.push_scope()
            sp, rmag, EC2, ES2, Bm2, Cm2, fz2, cst, h0 = (BP[k_] for k_ in ("sp", "rmag", "EC2", "ES2", "Bm2", "Cm2", "fz2", "cst", "h0"))
            NST, NCC, KR = BP["NST"], BP["NCC"], BP["KR"]
            if not lat:
                wglu_b = BP["wglu_b"]
            ytot_t = [P.sbuf("s5yt%d" % i, [128, 512], F32) for i in range(NCC)]
            NG4 = 2 if lat else 4
            WGr = Ring([P.sbuf("s5wg%d" % i, [128, NG4, 4 * S5T], F32) for i in range(2)])
            uT = P.sbuf("s5u", [128, NCC, L], BF16)
            yf = mixed if not lat else P.sbuf("s5yfL", [128, 1, L], BF16)
            for cc in range(NCC):
                w = load_group(l, "Bu%d" % cc)
                for tb in range(L // TB):
                    ps = fm_block(w, tb)
                    K.cp("scalar", uT[:, cc, blk(tb)], ps)
            for d in range(2):
                EC, ES, Bm, Cm, fz = EC2[:, d], ES2[:, d], Bm2[:, d], Cm2[:, d], fz2[:, d]
                fzv = fz[:, 0:NST, :]
                for si, (s0, sl) in enumerate(seqs):
                    if lat:
                        hr, hi = h0[:, d * NST:d * NST + NST, 0], h0[:, d * NST:d * NST + NST, 1]
                        t = sm.next()
                        n2, ta = t[:, 0:NST], t[:, 8:8 + NST]
                        u = sm.next()
                        tb_, tc = u[:, 0:NST], u[:, 8:8 + NST]
                        cstv = cst[:, 0:NST, :]
                        K.tt("vector", n2, fzv[:, :, 0], fzv[:, :, 0], MUL)
                        K.tt("vector", ta, fzv[:, :, 1], fzv[:, :, 1], MUL)
                        K.tt("vector", n2, n2, ta, ADD)
                        K.recip(n2, n2)
                        K.tt("vector", ta, hr, fzv[:, :, 0], MUL)
                        K.tt("vector", tb_, hi, fzv[:, :, 1], MUL)
                        K.tt("vector", ta, ta, tb_, ADD)
                        K.tt("vector", cstv[:, :, 0], ta, n2, MUL)
                        K.tt("vector", ta, hi, fzv[:, :, 0], MUL)
                        K.tt("vector", tb_, hr, fzv[:, :, 1], MUL)
                        K.tt("vector", ta, ta, tb_, SUB)
                        K.tt("vector", cstv[:, :, 1], ta, n2, MUL)
                    else:
                        K.memset("vector", cst, 0.0)
                    nch = sl // S5T
                    ytot = None
                    for c in (range(nch) if d == 0 else range(nch - 1, -1, -1)):
                        cols = slice(s0 + c * S5T, s0 + (c + 1) * S5T)
                        yps = [accs.next() for _ in range(NCC)]
                        for grp in range(NCC):
                            cc = grp
                            sts = list(range(4 * grp, 4 * grp + 4)) if not lat else [0, 1]
                            psbs, p12s, wgs, qs = {}, {}, {}, {}
                            tabs = {}
                            T_ = S5T
                            for st in sts:
                                psb = psr.next()
                                K.mm(psb[:, 0:T_], Bm[0:KR, st, 0, :], uT[0:KR, cc, cols])
                                K.mm(psb[:, T_:2 * T_], Bm[0:KR, st, 1, :], uT[0:KR, cc, cols])
                                psbs[st] = psb
                                tabs[st] = (EC[:, st, :].re("p (o t) -> p o t", o=1).bc([128, 2, T_]),
                                            ES[:, st, :].re("p (o t) -> p o t", o=1).bc([128, 2, T_]))
                            for st in sts:
                                bu3 = psbs[st][:, 0:2 * T_].re("p (r t) -> p r t", r=2)
                                if d == 1:
                                    bu3 = bu3[:, :, ::-1]
                                ecb, esb = tabs[st]
                                p1t = wf.next()
                                p2t = wf.next()
                                K.tt("vector", p1t[:, 0:2 * T_].re("p (r t) -> p r t", r=2), bu3, ecb, MUL)
                                K.tt("vector", p2t[:, 0:2 * T_].re("p (r t) -> p r t", r=2), bu3[:, ::-1, :], esb, MUL)
                                p12s[st] = (p1t, p2t)
                            WG = WGr.next()
                            for si_, st in enumerate(sts):
                                p1t, p2t = p12s[st]
                                wg = WG[:, si_, :]
                                K.tt("gpsimd", wg[:, 0:T_], p1t[:, 0:T_], p2t[:, 0:T_], ADD)
                                K.tt("gpsimd", wg[:, T_:2 * T_], p1t[:, T_:2 * T_], p2t[:, T_:2 * T_], SUB)
                                wgs[st] = wg
                            for st in sts:
                                wg = wgs[st]
                                rb = rmag[:, d * NST + st:d * NST + st + 1].bc([128, T_])
                                K.scan(wg[:, 2 * T_:3 * T_], rb, wg[:, 0:T_], cst[:, st, 0:1])
                                K.scan(wg[:, 3 * T_:4 * T_], rb, wg[:, T_:2 * T_], cst[:, st, 1:2])
                            for st in sts:
                                ecb, esb = tabs[st]
                                gg3 = wgs[st][:, 2 * T_:4 * T_].re("p (r t) -> p r t", r=2)
                                q1t = wb.next()
                                q2t = wb.next()
                                K.tt("gpsimd", q1t[:, 0:2 * T_].re("p (r t) -> p r t", r=2), gg3, ecb, MUL)
                                K.tt("vector", q2t[:, 0:2 * T_].re("p (r t) -> p r t", r=2), gg3, esb, MUL)
                                qs[st] = (q1t, q2t)
                            ns_ = len(sts)
                            s0_, s1_ = sts[0], sts[-1] + 1
                            gl = WG[:, :, 2 * T_:4 * T_].re("p s (r t) -> p s r t", r=2)[:, :, :, T_ - 1]
                            a = sm.next()
                            a1 = a[:, 0:2 * ns_].re("p (s r) -> p s r", r=2)
                            a2 = a[:, 8:8 + 2 * ns_].re("p (s r) -> p s r", r=2)
                            K.tt("vector", a1, gl, EC[:, s0_:s1_, T_ - 1:T_].bc([128, ns_, 2]), MUL)
                            K.tt("vector", a2, gl, ES[:, s0_:s1_, T_ - 1:T_].bc([128, ns_, 2]), MUL)
                            K.tt("vector", cst[:, s0_:s1_, 0], a1[:, :, 0], a2[:, :, 1], SUB)
                            K.tt("vector", cst[:, s0_:s1_, 1], a2[:, :, 0], a1[:, :, 1], ADD)
                            for st in sts:
                                q1t, q2t = qs[st]
                                K.mm(yps[cc][:, 0:T_], Cm[:, st, 0, :], q1t[:, 0:T_], start=(st == sts[0]), stop=False)
                                K.mm(yps[cc][:, 0:T_], Cm[:, st, 1, :], q2t[:, T_:2 * T_], start=False, stop=False)
                                K.mm(yps[cc][:, 0:T_], Cm[:, st, 2, :], q2t[:, 0:T_], start=False, stop=False)
                                K.mm(yps[cc][:, 0:T_], Cm[:, st, 2, :], q1t[:, T_:2 * T_], start=False, stop=(st == sts[-1]))
                        if d == 0:
                            for cc in range(NCC):
                                dsc = s5dl[:, l:l + 1] if lat else s5d[:, l, cc:cc + 1]
                                K.stt("vector", yf[:, cc, cols], uT[:, cc, cols], dsc, yps[cc][:, 0:S5T], MUL, ADD)
                        else:
                            lc = ((s0 + c * S5T) % BS)
                            if ytot is None:
                                ytot = ytot_t
                            for cc in range(NCC):
                                K.tt("vector", ytot[cc][:, lc:lc + S5T], yps[cc][:, 0:S5T][:, ::-1], yf[:, cc, cols], ADD)
                            if lc == 0:
                                b0 = s0 + c * S5T
                                gy = []
                                for cc in range(NCC):
                                    xx = ytot[cc]
                                    t = wf.next()
                                    K.tt("gpsimd", t[:, 0:BS], xx[:, 0:BS], xx[:, 0:BS], MUL)
                                    K.ts("vector", t[:, 0:BS], t[:, 0:BS], 0.044715, MUL, 1.0, ADD)
                                    K.tt("gpsimd", t[:, 0:BS], t[:, 0:BS], xx[:, 0:BS], MUL)
                                    K.act(t[:, 0:BS], t[:, 0:BS], AF.Sigmoid, scale=1.5957691216057308)
                                    if lat:
                                        K.tt("vector", mixh[0:64, 3, b0:b0 + BS], xx[0:64, 0:BS], t[0:64, 0:BS], MUL)
                                        continue
                                    gb = wb.next()
                                    K.tt("vector", gb[:, 0:BS], xx[:, 0:BS], t[:, 0:BS], MUL)
                                    gy.append(gb)
                                ytot = None
                                if lat:
                                    continue
                                pg = []
                                for fo in range(4):
                                    ps = psr.next()
                                    for cc in range(2):
                                        K.mm(ps[:, 0:BS], wglu_b[:, cc, fo * 128:(fo + 1) * 128], gy[cc][:, 0:BS],
                                             start=(cc == 0), stop=(cc == 1))
                                    pg.append(ps)
                                for ch in range(2):
                                    sgm = wf.next()
                                    K.act(sgm[:, 0:BS], pg[2 + ch][:, 0:BS], AF.Sigmoid)
                                    t = wf.next()
                                    K.tt("vector", t[:, 0:BS], pg[ch][:, 0:BS], sgm[:, 0:BS], MUL)
                                    w = load_group(l, "Bg%d" % ch)
                                    psg = fm_cols(w, b0, BS)
                                    sgB = wb.next()
                                    K.act(sgB[:, 0:BS], psg[:, 0:BS], AF.Silu)
                                    K.tt("vector", mixed[:, ch, b0:b0 + BS], t[:, 0:BS], sgB[:, 0:BS], MUL)
                    if d == 0 and l == 0 and si == len(seqs) - 1:
                        dump("s5yf", yf[:, :, 0:512], [128, 2, 512])
                    if not lat:
                        o = sm.next()
                        o3 = o[:, :].re("p (s r) -> p s r", r=2)
                        t = sm.next()
                        K.tt("vector", t[:, 0:8], cst[:, :, 0], fz[:, :, 0], MUL)
                        K.tt("vector", t[:, 8:16], cst[:, :, 1], fz[:, :, 1], MUL)
                        K.tt("vector", o3[:, :, 0], t[:, 0:8], t[:, 8:16], SUB)
                        K.tt("vector", t[:, 0:8], cst[:, :, 0], fz[:, :, 1], MUL)
                        K.tt("vector", t[:, 8:16], cst[:, :, 1], fz[:, :, 0], MUL)
                        K.tt("vector", o3[:, :, 1], t[:, 0:8], t[:, 8:16], ADD)
                        P.dma(dout["o_s5"][si, l, d].ap.rearrange("(s g) p r -> (g p) s r", g=2), o3.ap,
                              reads=[o], writes=[], out=True, eng="gpsimd", allow_slow_non_contiguous=True)
            P.pop_scope()

        def post_gather(l):
            agin = DR(nc.dram_tensor("agin%d" % l, [256, LL], BF16).ap())
            agout = DR(nc.dram_tensor("agout%d" % l, [1024, LL], BF16).ap())
            agin_tok = Buf("agin_tok%d" % l, None)
            agout_tok = Buf("agout_tok%d" % l, None)
            P.dma(agin.ap.rearrange("(s p) t -> p s t", p=64), mixh.ap, reads=[mixh], writes=[agin_tok], eng="gpsimd")
            P.collective(lambda e: e.collective_compute("AllGather", ALU.bypass, replica_groups=[[0, 1, 2, 3], [4, 5, 6, 7]],
                                                        ins=[agin.ap.opt()], outs=[agout.ap.opt()]),
                         reads=[agin_tok], writes=[agout_tok])
            P.push_scope()
            woL = P.sbuf("woL", [128, 10, D], BF16)
            wgl = P.sbuf("wgl", [64, 4, 512], BF16)
            PB = 256
            gyb = P.sbuf("gyb", [64, 4, PB], BF16)
            mB = P.sbuf("mB", [128, 2, PB], BF16)
            mcb = P.sbuf("mcb", [128, 8, PB], BF16)
            mcr = Ring([P.sbuf("mcj%d" % i, [128, 8, PB], BF16) for i in range(2)])
            gyr = Ring([P.sbuf("gyj%d" % i, [64, 4, PB], BF16) for i in range(2)])
            for chunk in range(10):
                s = wst.next()
                sv = s.re("p a b -> p (a b)")
                K.dma(sv, din["woutL"][l][:, chunk, :])
                for n in range(2):
                    gb = wf.next()
                    P.dma(gb.ap, gsc[l][v:v + 1, n * 512:(n + 1) * 512].ap.partition_broadcast(128), reads=[gsc_tok], writes=[gb])
                    K.tt("vector" if n == 0 else "gpsimd", woL[:, chunk, n * 512:(n + 1) * 512], gb, sv[:, n * 512:(n + 1) * 512], MUL)
            for r in range(4):
                s = wf.next()
                K.dma(s[0:64, :], din["wgluL"][l][:, r, :])
                K.cp("vector", wgl[:, r, :], s[0:64, :])
            g4 = agout.ap.rearrange("(r s p) t -> p r s t", r=4, s=4)
            c8 = agout.ap.rearrange("(c p) t -> p c t", p=128)
            for tb in range(512 // PB):
                sgBs = []
                for ch in range(2):
                    w = load_group(l, "Bg%d" % ch)
                    psg = psr.next()
                    for k in range(8):
                        K.mm(psg[:, 0:PB], w[:, k, :], hTo[:, k, tb * PB:(tb + 1) * PB], start=(k == 0), stop=(k == 7))
                    sgB = wb.next()
                    K.act(sgB[:, 0:PB], psg[:, 0:PB], AF.Silu)
                    sgBs.append(sgB)
                for j_ in range(4):
                    cj = j_ * 512 + tb * PB
                    gj = gyr.next()
                    P.dma(gj.ap, g4[:, :, 3, cj:cj + PB], reads=[agout_tok], writes=[gj])
                    mj = mcr.next()
                    P.dma(mj.ap, c8[:, :, cj:cj + PB], reads=[agout_tok], writes=[mj])
                    if j_ == 0:
                        K.ts("vector", gyb, gj, ohs[0:64, 0:1], MUL)
                        K.ts("vector", mcb, mj, ohs[:, 0:1], MUL)
                    else:
                        K.stt("gpsimd" if False else "vector", gyb, gj, ohs[0:64, j_:j_ + 1], gyb, MUL, ADD)
                        K.stt("vector", mcb, mj, ohs[:, j_:j_ + 1], mcb, MUL, ADD)
                pg = []
                for fo in range(4):
                    ps = psr.next()
                    for r in range(4):
                        K.mm(ps[:, 0:PB], wgl[:, r, fo * 128:(fo + 1) * 128], gyb[:, r, :], start=(r == 0), stop=(r == 3))
                    pg.append(ps)
                for ch in range(2):
                    sgm = wf.next()
                    K.act(sgm[:, 0:PB], pg[2 + ch][:, 0:PB], AF.Sigmoid)
                    t = wf.next()
                    K.tt("vector", t[:, 0:PB], pg[ch][:, 0:PB], sgm[:, 0:PB], MUL)
                    K.tt("gpsimd", mB[:, ch, :], t[:, 0:PB], sgBs[ch][:, 0:PB], MUL)
                for ii in range(PB // 128):
                    i = tb * (PB // 128) + ii
                    for n in range(2):
                        ps = psr.next()
                        for chunk in range(10):
                            lhs = mcb[:, chunk, ii * 128:(ii + 1) * 128] if chunk < 8 else mB[:, chunk - 8, ii * 128:(ii + 1) * 128]
                            K.mm(ps, lhs, woL[:, chunk, n * 512:(n + 1) * 512], start=(chunk == 0), stop=(chunk == 9))
                        K.tt("vector", xv[i][:, n * 512:(n + 1) * 512], ps, xv[i][:, n * 512:(n + 1) * 512], ADD)
            P.pop_scope()

        fns = dict(A=(branch_A, 0), B=(branch_B, 1), C=(branch_C, 2), D=(branch_D, 3))
        for l in range(nlayers):
            norm_mod(l)
            dump("hT_%s_%d" % ("lat" if lat else "ctx", l), hT[:, :, 0:512], [128, 8, 512])
            if "B" in branches and lat:
                b_prep(l)
            if lat:
                hT_load()
            for bn in branches:
                fn, bi = fns[bn]
                if bn == "B" and not lat:
                    b_prep(l)
                fn(l)
                if bn == "B":
                    P.pop_scope()
                if lat:
                    continue
                dump("mix%s_%s_%d" % (bn, "lat" if lat else "ctx", l), mixed[:, :, 0:512], [128, 2, 512])
                out_proj(l, bi)
            if lat:
                dump("mixh_%d" % l, mixh[:, :, 0:512], [64, 4, 512])
                post_gather(l)
        for i in range(ntx):
            r = rstd_of(xv[i], D, 1.0 / D)
            for n in range(2):
                o = wf.next()
                fn_ = wf.next()
                K.dma(fn_, din["fnorm"][:, n * 512:(n + 1) * 512])
                K.stt("vector", o, xv[i][:, n * 512:(n + 1) * 512], r, fn_, MUL, MUL)
                K.dma(ydst[i][:, n * 512:(n + 1) * 512], o, out=True, eng="gpsimd")
        P.pop_scope()

    for j in jobs:
        run_job(j == "lat")
    P.emit()
    return nc, dbg_shapes


_NC_CACHE = {}


def kernel(**inputs):
    if "nc" not in _NC_CACHE:
        _NC_CACHE["nc"] = build()[0]
    nc = _NC_CACHE["nc"]
    sh = _shared_inputs(inputs)
    in_maps = [_core_inputs(inputs, sh, c) for c in range(8)]
    res = run_bass_kernel_spmd(nc, in_maps, core_ids=list(range(8)))
    return assemble([r for r in res.results])


def assemble(rs):
    B, SEQ = 16, 256
    y_prompt = np.concatenate([r["y_c"].reshape(2, SEQ, D) for r in rs], axis=0)
    y_sample = np.stack([np.concatenate([rs[4 * s_ + r_]["y_l"].reshape(512, D) for r_ in range(4)], axis=0) for s_ in range(2)], axis=0)
    dk = np.concatenate([r["o_dk"] for r in rs], axis=0).reshape(B, NL, SEQ, 4, 64)
    dv = np.concatenate([r["o_dv"] for r in rs], axis=0).reshape(B, NL, SEQ, 4, 64)
    s5 = np.concatenate([r["o_s5"] for r in rs], axis=0)
    hg = np.concatenate([r["o_hg"] for r in rs], axis=0)
    ckv = np.concatenate([r["o_ckv"] for r in rs], axis=0)
    kr = np.concatenate([r["o_kr"] for r in rs], axis=0)
    f = lambda a: np.ascontiguousarray(a, dtype=np.float32)
    return tuple(f(a) for a in (y_prompt, y_sample, dk, dv, s5, hg, ckv, kr))
```

```python
import numpy as np
import concourse.bass as bass
import concourse.mybir as mybir
from concourse.bass_utils import run_bass_kernel_spmd

F32 = mybir.dt.float32
BF16 = mybir.dt.bfloat16
I32 = mybir.dt.int32
AF = mybir.ActivationFunctionType
ALU = mybir.AluOpType
AX = mybir.AxisListType

SAME_ENGINE_SYNC = True
ATTACH_WAIT = True
CSTOP = 99
N_DMA_SEMS = 24


class Buf:
    def __init__(self, name, ap, parent=None):
        self.name = name
        self.ap = ap
        self.parent = parent
        self.children = []
        self.lastw = None
        self.readers = []
        if parent is not None:
            parent.children.append(self)

    def view(self, ap, name=None):
        return Buf(name or self.name + ".v", ap, parent=self)

    def __getitem__(self, idx):
        return Ref(self, self.ap[idx])

    @property
    def buf(self):
        return self

    def re(self, pat_, **kw):
        return Ref(self, self.ap.rearrange(pat_, **kw))

    def bc(self, shape):
        return Ref(self, self.ap.broadcast_to(list(shape)))

    def _up(self):
        b = self.parent
        while b is not None:
            yield b
            b = b.parent

    def _down(self):
        for c in self.children:
            yield c
            yield from c._down()


class Ref:
    __slots__ = ("buf", "ap")

    def __init__(self, buf, ap):
        self.buf = buf
        self.ap = ap

    def __getitem__(self, idx):
        return Ref(self.buf, self.ap[idx])

    def re(self, pat_, **kw):
        return Ref(self.buf, self.ap.rearrange(pat_, **kw))

    def bc(self, shape):
        return Ref(self.buf, self.ap.broadcast_to(list(shape)))


class Op:
    __slots__ = ("eng", "fn", "deps", "idx", "is_dma", "ticket", "waits", "signal", "out", "clk", "inc")

    def __init__(self, eng, fn, is_dma=False, out=False):
        self.inc = 16
        self.eng = eng
        self.fn = fn
        self.deps = set()
        self.is_dma = is_dma
        self.ticket = None
        self.waits = []
        self.signal = False
        self.out = out
        self.clk = None


class Prog:
    def __init__(self, nc):
        self.nc = nc
        self.ops = []
        self._ctx = []
        self.nsb = 0
        self.scopes = []
        self.scope_pending = []
        self.allbufs = []

    def sbuf(self, name, shape, dtype):
        self.nsb += 1
        cm = self.nc.sbuf_tensor("%s_%d" % (name, self.nsb), list(shape), dtype)
        t = cm.__enter__()
        self._ctx.append(cm)
        b = Buf(name, t.ap() if hasattr(t, "ap") and callable(getattr(t, "ap")) else t[:])
        b.init_deps = list(self.scope_pending)
        self.allbufs.append(b)
        return b

    def push_scope(self):
        self.scopes.append((len(self._ctx), len(self.allbufs)))

    def pop_scope(self):
        nctx, nb = self.scopes.pop()
        dead = self.allbufs[nb:]
        del self.allbufs[nb:]
        pend = set(self.scope_pending)
        for b in dead:
            for x in (b, *b._down()):
                if x.lastw is not None:
                    pend.add(x.lastw)
                pend.update(x.readers)
        last = {}
        keep = []
        for o in pend:
            if o.is_dma:
                keep.append(o)
            elif o.eng not in last or last[o.eng].idx < o.idx:
                last[o.eng] = o
        self.scope_pending = keep + list(last.values())
        while len(self._ctx) > nctx:
            self._ctx.pop().__exit__(None, None, None)

    def psum(self, name, shape, dtype):
        cm = self.nc.psum_tensor(name, list(shape), dtype)
        t = cm.__enter__()
        self._ctx.append(cm)
        return Buf(name, t.ap() if hasattr(t, "ap") and callable(getattr(t, "ap")) else t[:])

    def _track(self, op, reads, writes):
        for b in (*reads, *writes):
            r = b
            while r.parent is not None:
                r = r.parent
            idp = getattr(r, "init_deps", None)
            if idp:
                op.deps.update(idp)
        for b in reads:
            for x in (b, *b._up(), *b._down()):
                if x.lastw is not None:
                    op.deps.add(x.lastw)
        for b in writes:
            for x in (b, *b._up(), *b._down()):
                if x.lastw is not None:
                    op.deps.add(x.lastw)
                for r in x.readers:
                    op.deps.add(r)
        for b in reads:
            b.readers.append(op)
        for b in writes:
            b.lastw = op
            b.readers = []
            for x in b._down():
                x.lastw = None
                x.readers = []
        op.deps.discard(op)

    def op(self, eng, fn, reads=(), writes=()):
        o = Op(eng, fn)
        o.idx = len(self.ops)
        self._track(o, reads, writes)
        self.ops.append(o)
        return o

    def dma(self, out_ap, in_ap, reads=(), writes=(), eng="sync", out=False, **kw):
        o = Op(eng, lambda e: e.dma_start(out=out_ap, in_=in_ap, **kw), is_dma=True, out=out)
        o.idx = len(self.ops)
        self._track(o, reads, writes)
        self.ops.append(o)
        return o

    def collective(self, fn, reads=(), writes=()):
        o = Op("gpsimd", fn, is_dma=True)
        o.inc = 1
        o.idx = len(self.ops)
        self._track(o, reads, writes)
        self.ops.append(o)
        return o

    def emit(self):
        nc = self.nc
        engines = ["sync", "tensor", "vector", "scalar", "gpsimd"]
        for o in self.ops:
            for d in o.deps:
                if d.eng == o.eng and not d.is_dma and not o.is_dma:
                    if o.eng == "tensor" or not SAME_ENGINE_SYNC:
                        continue
                d.signal = True
        for o in self.ops:
            if o.is_dma:
                o.signal = True
        sems = {}
        ctxs = []

        def mksem(name):
            cm = nc.semaphore(name)
            s = cm.__enter__()
            ctxs.append(cm)
            return s

        for e in engines:
            sems[e] = mksem("s_" + e)
        dma_sems = {e: [mksem("d_%s_%d" % (e, i)) for i in range(N_DMA_SEMS)] for e in ("sync", "scalar", "gpsimd")}
        dma_cnt = {e: [0] * N_DMA_SEMS for e in dma_sems}
        dma_last = {e: [None] * N_DMA_SEMS for e in dma_sems}
        dma_rr = {e: 0 for e in dma_sems}
        cnt = {e: 0 for e in engines}
        cc_sems = []
        clock = {e: {} for e in engines}
        final_waits = []
        for o in self.ops:
            E = o.eng
            deps = set(o.deps)
            if o.is_dma and o.inc == 1:
                pass
            elif o.is_dma:
                k = dma_rr[E]
                dma_rr[E] = (k + 1) % N_DMA_SEMS
                if dma_last[E][k] is not None:
                    deps.add(dma_last[E][k])
                dma_last[E][k] = o
            ck = clock[E]
            for d in sorted(deps, key=lambda z: z.idx):
                if d.ticket is None:
                    continue
                if d.eng == E and not d.is_dma and not o.is_dma and (E == "tensor" or not SAME_ENGINE_SYNC):
                    continue
                skey, val = d.ticket
                if ck.get(skey, 0) >= val:
                    continue
                o.waits.append((skey, val))
                ck[skey] = val
                for k2, v2 in d.clk.items():
                    if ck.get(k2, 0) < v2:
                        ck[k2] = v2
            if o.is_dma and o.inc == 1:
                cc_sems.append(mksem("cc%d" % len(cc_sems)))
                o.ticket = (("c", len(cc_sems) - 1), 1)
            elif o.is_dma:
                dma_cnt[E][k] += o.inc
                o.ticket = (("d", E, k), dma_cnt[E][k])
                if o.out:
                    final_waits.append(o.ticket)
            elif o.signal:
                cnt[E] += 1
                o.ticket = (("e", E), cnt[E])
            o.clk = dict(ck)
            if o.ticket is not None and not o.is_dma:
                o.clk[o.ticket[0]] = o.ticket[1]

        def semof(skey):
            if skey[0] == "e":
                return sems[skey[1]]
            if skey[0] == "c":
                return cc_sems[skey[1]]
            return dma_sems[skey[1]][skey[2]]

        ops = self.ops
        with nc.Block() as block:
            def run(engname):
                def body(eng):
                    for o in ops:
                        if o.eng != engname:
                            continue
                        ws = list(o.waits)
                        att = None
                        if ATTACH_WAIT and ws and not (o.is_dma and o.inc == 1):
                            att = ws.pop()
                        for skey, val in ws:
                            eng.wait_ge(semof(skey), val)
                        ins = o.fn(eng)
                        if att is not None:
                            ins._wait_ge(semof(att[0]), att[1])
                        if o.ticket is not None:
                            if o.is_dma:
                                ins.then_inc(semof(o.ticket[0]), o.inc)
                            else:
                                ins.then_inc(semof(o.ticket[0]), 1)
                    if engname == "sync":
                        for skey, val in final_waits:
                            eng.wait_ge(semof(skey), val)
                return body
            block.sync(run("sync"))
            block.tensor(run("tensor"))
            block.vector(run("vector"))
            block.scalar(run("scalar"))
            block.gpsimd(run("gpsimd"))
        for cm in reversed(ctxs):
            cm.__exit__(None, None, None)
        for cm in reversed(self._ctx):
            cm.__exit__(None, None, None)


D = 1024
NL = 2
LC = 512
LL = 2048
PAST = 256
EPS = 1e-6
TB = 512
S5T = 256
HGC = 32
GRID_W = 64

OFF = dict(da_q=0, da_k=256, da_v=512, da_g=768, s5_u=1024, s5_g=1280, hg_q=1536, hg_ff=1792, hg_fb=2048,
           hg_i=2304, hg_g=2560, mla_cq=2816, mla_ckv=3008, mla_kr=3136, mla_g=3168)


def _rope_perm32():
    p = np.zeros(32, np.int64)
    for i in range(32):
        p[i] = i + 8 if (i % 16) < 8 else i - 8
    return p


def _groups():
    cols = []
    table = {}

    def add(name, src):
        src = list(src)
        assert len(src) <= 128
        src = src + [-1] * (128 - len(src))
        table[name] = len(cols)
        cols.extend(src)

    perm = _rope_perm32()
    for nm, off in (("Aq", OFF["da_q"]), ("Ak", OFF["da_k"])):
        for h in range(4):
            c1 = [off + h * 64 + d for d in range(32)]
            c2 = [off + h * 64 + 32 + d for d in range(32)]
            add("%s%d" % (nm, h), c1 + [-1] * 32 + c2 + [-1] * 32)
            p1 = [off + h * 64 + perm[d] for d in range(32)]
            p2 = [off + h * 64 + 32 + perm[d] for d in range(32)]
            add("%sp%d" % (nm, h), p1 + [-1] * 32 + p2 + [-1] * 32)
    for i in range(2):
        add("Av%d" % i, range(OFF["da_v"] + 128 * i, OFF["da_v"] + 128 * i + 128))
        add("Akt%d" % i, range(OFF["da_k"] + 128 * i, OFF["da_k"] + 128 * i + 128))
        add("Ag%d" % i, range(OFF["da_g"] + 128 * i, OFF["da_g"] + 128 * i + 128))
        add("Bu%d" % i, range(OFF["s5_u"] + 128 * i, OFF["s5_u"] + 128 * i + 128))
        add("Bg%d" % i, range(OFF["s5_g"] + 128 * i, OFF["s5_g"] + 128 * i + 128))
        add("Cq%d" % i, range(OFF["hg_q"] + 128 * i, OFF["hg_q"] + 128 * i + 128))
        add("Cf0%d" % i, range(OFF["hg_ff"] + 128 * i, OFF["hg_ff"] + 128 * i + 128))
        add("Cf1%d" % i, range(OFF["hg_fb"] + 128 * i, OFF["hg_fb"] + 128 * i + 128))
        add("Cv%d" % i, range(OFF["hg_i"] + 128 * i, OFF["hg_i"] + 128 * i + 128))
        add("Cg%d" % i, range(OFF["hg_g"] + 128 * i, OFF["hg_g"] + 128 * i + 128))
        add("Dg%d" % i, range(OFF["mla_g"] + 128 * i, OFF["mla_g"] + 128 * i + 128))
    add("Dcq0", range(OFF["mla_cq"], OFF["mla_cq"] + 128))
    add("Dcq1", range(OFF["mla_cq"] + 128, OFF["mla_cq"] + 192))
    add("Dckv", range(OFF["mla_ckv"], OFF["mla_ckv"] + 128))
    kr = [OFF["mla_kr"] + d for d in range(32)]
    add("Dkr", [-1] * 64 + kr)
    add("Dkrp", [-1] * 64 + [OFF["mla_kr"] + perm[d] for d in range(32)])
    add("Dkrt", kr)
    return table, np.array(cols, np.int64)


GT, GCOLS = _groups()
NCOLS = len(GCOLS)


def _groups_lat(r):
    cols = []
    table = {}

    def add(name, src):
        src = list(src)
        src = src + [-1] * (128 - len(src))
        table[name] = len(cols)
        cols.extend(src)

    perm = _rope_perm32()
    for nm, off in (("Aq", OFF["da_q"]), ("Ak", OFF["da_k"])):
        c1 = [off + r * 64 + d for d in range(32)]
        c2 = [off + r * 64 + 32 + d for d in range(32)]
        add(nm, c1 + [-1] * 32 + c2 + [-1] * 32)
        p1 = [off + r * 64 + perm[d] for d in range(32)]
        p2 = [off + r * 64 + 32 + perm[d] for d in range(32)]
        add(nm + "p", p1 + [-1] * 32 + p2 + [-1] * 32)
    for nm, key in (("Av", "da_v"), ("Ag", "da_g"), ("Bu", "s5_u"), ("Cq", "hg_q"), ("Cf0", "hg_ff"), ("Cf1", "hg_fb"),
                    ("Cv", "hg_i"), ("Cg", "hg_g"), ("Dg", "mla_g")):
        add(nm, range(OFF[key] + 64 * r, OFF[key] + 64 * r + 64))
    for i in range(2):
        add("Bg%d" % i, range(OFF["s5_g"] + 128 * i, OFF["s5_g"] + 128 * i + 128))
    add("Dcq0", range(OFF["mla_cq"], OFF["mla_cq"] + 128))
    add("Dcq1", range(OFF["mla_cq"] + 128, OFF["mla_cq"] + 192))
    add("Dckv", range(OFF["mla_ckv"], OFF["mla_ckv"] + 128))
    kr = [OFF["mla_kr"] + d for d in range(32)]
    add("Dkr", [-1] * 64 + kr)
    add("Dkrp", [-1] * 64 + [OFF["mla_kr"] + perm[d] for d in range(32)])
    return table, np.array(cols, np.int64)


LGT = _groups_lat(0)[0]
NCOLS_L = len(_groups_lat(0)[1])


def _rope_tables():
    t = np.arange(LL)
    row = (t // GRID_W).astype(np.float32)
    col = (t % GRID_W).astype(np.float32)
    inv = (10000.0 ** (-np.arange(8, dtype=np.float32) / 8)).astype(np.float32)
    C = np.ones((128, LL), np.float32)
    S = np.zeros((128, LL), np.float32)
    for base in (0, 64):
        for i in range(32):
            pos = row if i < 16 else col
            ang = pos * inv[i % 8]
            C[base + i] = np.cos(ang)
            S[base + i] = (-1.0 if (i % 16) < 8 else 1.0) * np.sin(ang)
    return C, S


def _consts():
    c = {}
    c["ident"] = np.eye(128, dtype=np.float32)
    s = np.arange(128)[:, None]
    t = np.arange(128)[None, :]
    same = (s // HGC) == (t // HGC)
    c["hmask0"] = (same & (s <= t)).astype(np.float32)
    c["hmask1"] = (same & (s >= t)).astype(np.float32)
    m4 = np.zeros((128, 4), np.float32)
    m4[np.arange(128), np.arange(128) // HGC] = 1.0
    c["m4"] = m4
    m01 = np.ones((128, TB), np.float32)
    m01[:, ::HGC] = 0.0
    c["m01"] = m01
    c["iota"] = np.broadcast_to(np.arange(1, S5T + 1, dtype=np.float32)[None, :], (128, S5T)).copy()
    bd = np.zeros((128, 128), np.float32)
    bd[:64, :64] = 1.0
    bd[64:, 64:] = 1.0
    c["bdones"] = bd
    C, S = _rope_tables()
    c["ropeC"] = C
    c["ropeS"] = S
    sel = np.zeros((2, 2, 128), np.float32)
    sel[0, 0] = 1.0
    sel[1, 1] = 1.0
    c["selc"] = sel
    return c


CONST_SHAPES = dict(ident=[128, 128], hmask0=[128, 128], hmask1=[128, 128], m4=[128, 4], m01=[128, TB],
                    iota=[128, S5T], bdones=[128, 128], ropeC=[128, LL], ropeS=[128, LL], selc=[2, 2, 128])


IN_SHAPES = dict(
    xc=[4, 128, D], xl=[4, 128, D], cvec=[128, 8, 2],
    wmod=[NL, 24, 128, 8, 128], bmodT=[NL, 128, 24], bmodg=[NL, 2, D],
    win=[NL, NCOLS // 128, 128, 8, 128], wout=[NL, 128, 8, D], wglu=[NL, 128, 2, 512],
    wuq=[NL, 128, 2, 768], mqn=[NL, 128, 2], wukv=[NL, 128, 512], mkvn=[NL, 128, 128], fnorm=[128, D],
    s5B=[NL, 2, 8, 2, 128, 128], s5C=[NL, 2, 8, 2, 128, 128], s5p=[NL, 128, 16, 3], s5d=[NL, 128, 2],
    hglb=[NL, 128, 4], hgn=[NL, 128, 1], dalam=[NL, 1, 128], dan=[NL, 128, 1],
    cdk=[NL, 2, 128, 256], cdv=[NL, 2, 128, 256], cs5=[NL, 128, 16, 2], chg=[NL, 2, 4, 64, 64],
    cckv=[NL, 2, 128, 128], ckr=[NL, 2, 128, 32],
)
IN_SHAPES.update(dict(
    winL=[NL, NCOLS_L // 128, 128, 8, 128], woutL=[NL, 128, 10, D], wgluL=[NL, 64, 4, 512], wuqL=[NL, 128, 2, 192], wukvL=[NL, 128, 128],
    s5BL=[NL, 2, 2, 2, 64, 128], s5CL=[NL, 2, 2, 2, 128, 64], s5pL=[NL, 128, 4, 3], s5dL=[NL, 64, 1], hglbL=[NL, 64, 2],
    cdkL=[NL, 2, 128, 64], cdvL=[NL, 2, 128, 64], cs5L=[NL, 128, 4, 2], chgL=[NL, 2, 64, 64], oh=[128, 4],
))
IN_SHAPES.update(CONST_SHAPES)
OUT_SHAPES = dict(
    y_c=[4, 128, D], y_l=[4, 128, D], o_dk=[2, NL, 256, 256], o_dv=[2, NL, 256, 256],
    o_s5=[2, NL, 2, 16, 64, 2], o_hg=[2, NL, 2, 4, 64, 64], o_ckv=[2, NL, 256, 128], o_kr=[2, NL, 256, 32],
)


def _shared_inputs(inp):
    f = lambda a: np.ascontiguousarray(np.asarray(a, dtype=np.float32))
    sh = {}
    w_in = f(inp["w_in"])
    wpad = np.concatenate([w_in, np.zeros((NL, D, 1), np.float32)], axis=2)
    win = wpad[:, :, GCOLS]
    sh["win"] = f(win.reshape(NL, 8, 128, NCOLS // 128, 128).transpose(0, 3, 2, 1, 4))
    sh["wmod"] = f(f(inp["w_mod"]).reshape(NL, 8, 128, 24, 128).transpose(0, 3, 2, 1, 4))
    bm = f(inp["b_mod"])
    sh["bmodT"] = f(bm.reshape(NL, 24, 128).transpose(0, 2, 1))
    sh["bmodg"] = f(np.broadcast_to(bm[:, None, 2 * D:3 * D], (NL, 2, D)))
    sh["wout"] = f(f(inp["w_out"]).reshape(NL, 8, 128, D).transpose(0, 2, 1, 3))
    sh["wglu"] = f(f(inp["s5_w_glu"]).reshape(NL, 2, 128, 512).transpose(0, 2, 1, 3))
    perm = _rope_perm32()
    wuq = f(inp["mla_w_uq"])
    cols_n = np.arange(384)
    cols_p = np.array([h * 96 + (j if j < 64 else 64 + perm[j - 64]) for h in range(4) for j in range(96)])
    wq = np.concatenate([wuq[:, :, cols_n], wuq[:, :, cols_p]], axis=2)
    wq = np.concatenate([wq, np.zeros((NL, 64, 768), np.float32)], axis=1)
    sh["wuq"] = f(wq.reshape(NL, 2, 128, 768).transpose(0, 2, 1, 3))
    qn = np.concatenate([f(inp["mla_q_norm"]), np.zeros((NL, 64), np.float32)], axis=1)
    sh["mqn"] = f(qn.reshape(NL, 2, 128).transpose(0, 2, 1))
    sh["wukv"] = f(inp["mla_w_ukv"])
    sh["mkvn"] = f(np.broadcast_to(f(inp["mla_kv_norm"])[:, None, :], (NL, 128, 128)))
    sh["fnorm"] = f(np.broadcast_to(f(inp["final_norm"])[None, :], (128, D)))
    bre, bim = f(inp["s5_b_re"]), f(inp["s5_b_im"])
    cre, cim = f(inp["s5_c_re"]), f(inp["s5_c_im"])
    sB = np.zeros((NL, 2, 8, 2, 128, 128), np.float32)
    sC = np.zeros((NL, 2, 8, 2, 128, 128), np.float32)
    for st in range(8):
        for gi in range(2):
            g = 2 * st + gi
            r0 = 16 * (g % 8)
            for ri, (bb, cc) in enumerate(((bre, cre), (bim, cim))):
                sB[:, :, st, ri, r0:r0 + 16, 64 * gi:64 * gi + 64] = bb[:, :, g].transpose(0, 1, 3, 2)
                sC[:, :, st, ri, 64 * gi:64 * gi + 64, r0:r0 + 16] = cc[:, :, g].transpose(0, 1, 3, 2)
    sh["s5B"], sh["s5C"] = sB, sC
    are, aim, ldt = f(inp["s5_a_re"]), f(inp["s5_a_im"]), f(inp["s5_log_dt"])
    sp = np.zeros((NL, 128, 16, 3), np.float32)
    for d in range(2):
        for st in range(8):
            for gi in range(2):
                g = 2 * st + gi
                sp[:, 64 * gi:64 * gi + 64, d * 8 + st, 0] = are[:, d, g]
                sp[:, 64 * gi:64 * gi + 64, d * 8 + st, 1] = aim[:, d, g]
                sp[:, 64 * gi:64 * gi + 64, d * 8 + st, 2] = ldt[:, d, g][:, None]
    sh["s5p"] = sp
    sh["s5d"] = f(f(inp["s5_d"]).reshape(NL, 2, 128).transpose(0, 2, 1))
    lb = f(inp["hg_lb"])
    sh["hglb"] = f(lb.reshape(NL, 2, 2, 128).transpose(0, 3, 1, 2).reshape(NL, 128, 4))
    sh["hgn"] = f(np.tile(f(inp["hg_norm"]), (1, 2))[:, :, None])
    sh["dalam"] = f(f(inp["da_lambda"]).reshape(NL, 1, 128))
    sh["dan"] = f(np.tile(f(inp["da_norm"]), (1, 2))[:, :, None])
    sh.update(_consts())
    return sh


def _core_inputs(inp, sh, core):
    f = lambda a: np.ascontiguousarray(np.asarray(a, dtype=np.float32))
    m = dict(sh)
    s = core // 4
    m["xc"] = f(np.asarray(inp["x_prompt"])[2 * core:2 * core + 2].reshape(4, 128, D))
    m["xl"] = f(np.asarray(inp["x_sample"])[s].reshape(4, 4, 128, D)[core % 4])
    cv = np.stack([f(inp["c_ctx"]), f(inp["c"])[s]], axis=-1)
    m["cvec"] = f(cv.reshape(8, 128, 2).transpose(1, 0, 2))
    m["cdk"] = f(np.asarray(inp["cache_diff_k"])[s].reshape(NL, 2, 128, 256))
    m["cdv"] = f(np.asarray(inp["cache_diff_v"])[s].reshape(NL, 2, 128, 256))
    st5 = f(np.asarray(inp["state_s5"])[s])
    m["cs5"] = f(st5.reshape(NL, 2, 8, 2, 64, 2).transpose(0, 3, 4, 1, 2, 5).reshape(NL, 128, 16, 2))
    m["chg"] = f(np.asarray(inp["state_hgrn"])[s])
    m["cckv"] = f(np.asarray(inp["cache_mla_ckv"])[s].reshape(NL, 2, 128, 128))
    m["ckr"] = f(np.asarray(inp["cache_mla_krope"])[s].reshape(NL, 2, 128, 32))
    r = core % 4
    w_in = f(inp["w_in"])
    wpad = np.concatenate([w_in, np.zeros((NL, D, 1), np.float32)], axis=2)
    lcols = _groups_lat(r)[1]
    m["winL"] = f(wpad[:, :, lcols].reshape(NL, 8, 128, NCOLS_L // 128, 128).transpose(0, 3, 2, 1, 4))
    wo = f(inp["w_out"])
    chunks = []
    z64 = np.zeros((NL, 64, D), np.float32)
    for rr in range(4):
        chunks.append(np.concatenate([wo[:, 64 * rr:64 * rr + 64], wo[:, 512 + 64 * rr:512 + 64 * rr + 64]], axis=1))
        chunks.append(np.concatenate([wo[:, 768 + 64 * rr:768 + 64 * rr + 64], z64], axis=1))
    chunks.append(wo[:, 256:384])
    chunks.append(wo[:, 384:512])
    m["woutL"] = f(np.stack(chunks, axis=2))
    m["wgluL"] = f(f(inp["s5_w_glu"]).reshape(NL, 4, 64, 512).transpose(0, 2, 1, 3))
    perm = _rope_perm32()
    wuq = f(inp["mla_w_uq"])
    cn = np.array([r * 96 + j for j in range(96)])
    cp_ = np.array([r * 96 + (j if j < 64 else 64 + perm[j - 64]) for j in range(96)])
    wq = np.concatenate([wuq[:, :, cn], wuq[:, :, cp_]], axis=2)
    wq = np.concatenate([wq, np.zeros((NL, 64, 192), np.float32)], axis=1)
    m["wuqL"] = f(wq.reshape(NL, 2, 128, 192).transpose(0, 2, 1, 3))
    m["wukvL"] = f(f(inp["mla_w_ukv"])[:, :, r * 128:(r + 1) * 128])
    bre, bim = f(inp["s5_b_re"]), f(inp["s5_b_im"])
    cre, cim = f(inp["s5_c_re"]), f(inp["s5_c_im"])
    sB = np.zeros((NL, 2, 2, 2, 64, 128), np.float32)
    sC = np.zeros((NL, 2, 2, 2, 128, 64), np.float32)
    are, aim, ldt = f(inp["s5_a_re"]), f(inp["s5_a_im"]), f(inp["s5_log_dt"])
    sp = np.zeros((NL, 128, 4, 3), np.float32)
    st5 = f(np.asarray(inp["state_s5"])[s])
    c5 = np.zeros((NL, 128, 4, 2), np.float32)
    for st in range(2):
        for gi in range(2):
            g = 4 * r + 2 * st + gi
            r0 = 16 * (g % 4)
            for ri, (bb, cc) in enumerate(((bre, cre), (bim, cim))):
                sB[:, :, st, ri, r0:r0 + 16, 64 * gi:64 * gi + 64] = bb[:, :, g].transpose(0, 1, 3, 2)
                sC[:, :, st, ri, 64 * gi:64 * gi + 64, r0:r0 + 16] = cc[:, :, g].transpose(0, 1, 3, 2)
            for d in range(2):
                sp[:, 64 * gi:64 * gi + 64, d * 2 + st, 0] = are[:, d, g]
                sp[:, 64 * gi:64 * gi + 64, d * 2 + st, 1] = aim[:, d, g]
                sp[:, 64 * gi:64 * gi + 64, d * 2 + st, 2] = ldt[:, d, g][:, None]
                c5[:, 64 * gi:64 * gi + 64, d * 2 + st, :] = st5[:, d, g]
    m["s5BL"], m["s5CL"], m["s5pL"], m["cs5L"] = sB, sC, sp, c5
    m["s5dL"] = f(f(inp["s5_d"])[:, 64 * r:64 * r + 64, None])
    m["hglbL"] = f(f(inp["hg_lb"])[:, :, 64 * r:64 * r + 64].transpose(0, 2, 1))
    m["cdkL"] = f(np.asarray(inp["cache_diff_k"])[s][:, :, r].reshape(NL, 2, 128, 64))
    m["cdvL"] = f(np.asarray(inp["cache_diff_v"])[s][:, :, r].reshape(NL, 2, 128, 64))
    m["chgL"] = f(np.asarray(inp["state_hgrn"])[s][:, :, r])
    oh = np.zeros((128, 4), np.float32)
    oh[:, r] = 1.0
    m["oh"] = oh
    for k, shp in IN_SHAPES.items():
        assert list(m[k].shape) == list(shp), (k, m[k].shape, shp)
    return {k: m[k] for k in IN_SHAPES}


class DR:
    buf = None

    def __init__(self, ap):
        self.ap = ap

    def __getitem__(self, idx):
        return DR(self.ap[idx])

    def re(self, pat_, **kw):
        return DR(self.ap.rearrange(pat_, **kw))


class Ring:
    def __init__(self, bufs):
        self.bufs = bufs
        self.i = 0

    def next(self):
        b = self.bufs[self.i % len(self.bufs)]
        self.i += 1
        return b


def _b(xs):
    return [x.buf for x in xs if x is not None and not isinstance(x, (int, float)) and x.buf is not None]


class KB:
    def __init__(self, P):
        self.P = P

    def tt(self, eng, o, a, b, op):
        self.P.op(eng, lambda e: e.tensor_tensor(o.ap, a.ap, b.ap, op=op), _b([a, b]), _b([o]))

    def ts(self, eng, o, a, s1, op0, s2=None, op1=None):
        v1 = s1.ap if hasattr(s1, "ap") else s1
        v2 = s2.ap if hasattr(s2, "ap") else s2
        if op1 is None:
            fn = lambda e: e.tensor_scalar(o.ap, a.ap, v1, None, op0=op0)
        else:
            fn = lambda e: e.tensor_scalar(o.ap, a.ap, v1, v2, op0=op0, op1=op1)
        self.P.op(eng, fn, _b([a, s1, s2]), _b([o]))

    def stt(self, eng, o, a, s, b, op0, op1):
        v = s.ap if hasattr(s, "ap") else s
        self.P.op(eng, lambda e: e.scalar_tensor_tensor(o.ap, a.ap, v, b.ap, op0=op0, op1=op1),
                  _b([a, s, b]), _b([o]))

    def act(self, o, a, func, bias=None, scale=None, accum=None):
        kw = {}
        if bias is not None:
            kw["bias"] = bias.ap if hasattr(bias, "ap") else bias
        if scale is not None:
            kw["scale"] = scale.ap if hasattr(scale, "ap") else scale
        if accum is not None:
            kw["accum_out"] = accum.ap
        self.P.op("scalar", lambda e: e.activation(o.ap, a.ap, func, **kw), _b([a, bias, scale]), _b([o, accum]))

    def cp(self, eng, o, a):
        if eng == "scalar":
            self.P.op(eng, lambda e: e.activation(o.ap, a.ap, AF.Copy), _b([a]), _b([o]))
        else:
            self.P.op(eng, lambda e: e.tensor_copy(o.ap, a.ap), _b([a]), _b([o]))

    def recip(self, o, a):
        self.P.op("vector", lambda e: e.reciprocal(o.ap, a.ap), _b([a]), _b([o]))

    def memset(self, eng, o, val):
        self.P.op(eng, lambda e: e.memset(o.ap, val), [], _b([o]))

    def mm(self, o, lhsT, rhs, start=True, stop=True):
        self.P.op("tensor", lambda e: e.matmul(o.ap, lhsT.ap, rhs.ap, start=start, stop=stop),
                  _b([lhsT, rhs]), _b([o]))

    def scan(self, o, d0, d1, init, op0=ALU.mult, op1=ALU.add):
        iv = init.ap if hasattr(init, "ap") else init
        self.P.op("vector", lambda e: e.tensor_tensor_scan(o.ap, d0.ap, d1.ap, iv, op0=op0, op1=op1),
                  _b([d0, d1, init]), _b([o]))

    def dma(self, o, a, out=False, eng="sync"):
        self.P.dma(o.ap, a.ap, reads=_b([a]), writes=_b([o]), out=out, eng=eng)


def lam_init(l):
    import math
    return 0.8 - 0.6 * math.exp(-0.3 * l)


def build(jobs=("ctx", "lat"), nlayers=NL, branches="ABCD", dbg=(), nwf=12):
    nc = bass.Bass("TRN2", target_bir_lowering=False)
    P = Prog(nc)
    K = KB(P)
    din = {k: DR(nc.dram_tensor(k, shp, F32, kind="ExternalInput").ap()) for k, shp in IN_SHAPES.items()}
    dout = {k: DR(nc.dram_tensor(k, shp, F32, kind="ExternalOutput").ap()) for k, shp in OUT_SHAPES.items()}
    dbg_shapes = {}

    def dump(name, ref, shape):
        if name in dbg:
            shape = list(shape)
            dbg_shapes[name] = shape
            dd = DR(nc.dram_tensor("dbg_" + name, shape, F32, kind="ExternalOutput").ap())
            t = P.sbuf("dbgt_" + name, shape, F32)
            K.cp("vector", t, ref)
            K.dma(dd, t, out=True, eng="gpsimd")

    PI = float(np.pi)
    ADD, SUB, MUL, MAX, MIN = ALU.add, ALU.subtract, ALU.mult, ALU.max, ALU.min
    psr = Ring([P.psum("ps%d" % i, [128, 512], F32) for i in range(6)])
    accs = Ring([P.psum("acc%d" % i, [128, 512], F32) for i in range(2)])
    wst = Ring([P.sbuf("wst%d" % i, [128, 8, 128], F32) for i in range(5)])
    wbf = Ring([P.sbuf("wbf%d" % i, [128, 8, 128], BF16) for i in range(5)])
    wf = Ring([P.sbuf("wf%d" % i, [128, 512], F32) for i in range(nwf)])
    wb = Ring([P.sbuf("wb%d" % i, [128, 512], BF16) for i in range(10)])
    xnr = Ring([P.sbuf("xn%d" % i, [128, D], BF16) for i in range(2)])
    sm = Ring([P.sbuf("sm%d" % i, [128, 16], F32) for i in range(16)])
    junk = P.sbuf("junk", [128, D], BF16)

    def cbf(name, shape, src):
        t = P.sbuf(name, shape, BF16)
        n = shape[1]
        for c0 in range(0, n, 512):
            w = min(512, n - c0)
            s = wf.next()
            K.dma(s[:, 0:w], src[:, c0:c0 + w])
            K.cp("vector", t[:, c0:c0 + w], s[:, 0:w])
        return t

    ident_b = cbf("ident_b", [128, 128], din["ident"])
    hmask = [cbf("hmask%d" % i, [128, 128], din["hmask%d" % i]) for i in range(2)]
    bdones = cbf("bdones", [128, 128], din["bdones"])
    ropeC = cbf("ropeC", [128, LL], din["ropeC"])
    ropeS = cbf("ropeS", [128, LL], din["ropeS"])
    ones_b = P.sbuf("ones_b", [128, 128], BF16)
    K.memset("vector", ones_b, 1.0)
    ones_f = P.sbuf("ones_f", [128, 128], F32)
    K.memset("vector", ones_f, 1.0)
    epsc = P.sbuf("epsc", [128, 1], F32)
    K.memset("vector", epsc, EPS)
    halfpi = P.sbuf("halfpi", [128, 1], F32)
    K.memset("vector", halfpi, float(np.pi / 2))
    zpad = P.sbuf("zpad", [128, 128], BF16)
    K.memset("vector", zpad, 0.0)
    m4 = P.sbuf("m4", [128, 4], F32)
    K.dma(m4, din["m4"])
    ohs = P.sbuf("ohs", [128, 4], F32)
    K.dma(ohs, din["oh"])
    m01 = P.sbuf("m01", [128, TB], F32)
    K.dma(m01, din["m01"])
    iota = P.sbuf("iota", [128, S5T], F32)
    K.dma(iota, din["iota"])
    modT = P.sbuf("modT", [128, NL, 2, 16], F32)
    gsc = DR(nc.dram_tensor("gsc", [NL, 2, D], F32).ap())
    gsc_tok = Buf("gsc_tok", None)
    bmT = P.sbuf("bmT", [128, NL, 24], F32)
    cs = P.sbuf("cs", [128, 8, 2], F32)
    for l in range(NL):
        K.dma(bmT[:, l, :], din["bmodT"][l])
    K.dma(cs, din["cvec"])
    K.act(cs, cs, AF.Silu)
    for l in range(nlayers):
        for j in range(16):
            s = wst.next()
            K.dma(s, din["wmod"][l][j])
            ps = psr.next()
            for k in range(8):
                K.mm(ps[:, 0:2], s[:, k, :], cs[:, k, :], start=(k == 0), stop=(k == 7))
            K.ts("vector", modT[:, l, :, j], ps[:, 0:2], bmT[:, l, j:j + 1], ADD, 1.0 if j >= 8 else 0.0, ADD)
        for n in range(8):
            s = wst.next()
            K.dma(s, din["wmod"][l][16 + n])
            ps = psr.next()
            for k in range(8):
                K.mm(ps[0:2, 0:128], cs[:, k, :], s[:, k, :], start=(k == 0), stop=(k == 7))
            bt = wf.next()
            K.dma(bt[0:2, 0:128], din["bmodg"][l][:, n * 128:(n + 1) * 128])
            K.tt("vector", bt[0:2, 128:256], ps[0:2, 0:128], bt[0:2, 0:128], ADD)
            P.dma(gsc[l][:, n * 128:(n + 1) * 128].ap, bt[0:2, 128:256].ap, reads=[bt], writes=[gsc_tok])
    lamr = P.sbuf("lamr", [1, NL, 128], F32)
    lamv = P.sbuf("lamv", [1, 8], F32)
    neglam = P.sbuf("neglam", [128, NL], F32)
    dan_s = P.sbuf("dan_s", [128, NL], F32)
    for l in range(NL):
        K.dma(lamr[:, l, :], din["dalam"][l])
        K.dma(dan_s[:, l:l + 1], din["dan"][l])
        K.ts("vector", dan_s[:, l:l + 1], dan_s[:, l:l + 1], 1.0 - lam_init(l), MUL)
        t = sm.next()
        e = sm.next()
        for c in range(2):
            K.tt("vector", lamr[:, l, 64 * c:64 * c + 32], lamr[:, l, 64 * c:64 * c + 32],
                 lamr[:, l, 64 * c + 32:64 * c + 64], MUL)
            P.op("vector", lambda e_, o=t[0:1, c:c + 1], a=lamr[:, l, 64 * c:64 * c + 32]: e_.reduce_sum(o.ap, a.ap, axis=AX.X),
                 _b([lamr]), _b([t]))
        K.act(e[0:1, 0:2], t[0:1, 0:2], AF.Exp)
        K.tt("vector", lamv[:, l:l + 1], e[0:1, 0:1], e[0:1, 1:2], SUB)
        K.ts("vector", lamv[:, l:l + 1], lamv[:, l:l + 1], -1.0, MUL, -lam_init(l), ADD)
    ps = psr.next()
    K.mm(ps[:, 0:NL], ones_f[0:1, :], lamv[0:1, 0:NL])
    K.cp("vector", neglam, ps[:, 0:NL])
    hgl = P.sbuf("hgl", [128, NL, 4], F32)
    lb = P.sbuf("lb", [128, NL, 4], F32)
    oml = P.sbuf("oml", [128, NL, 4], F32)
    noml = P.sbuf("noml", [128, NL, 4], F32)
    hgn = P.sbuf("hgn", [128, NL], F32)
    for l in range(NL):
        K.dma(hgl[:, l, :], din["hglb"][l])
        K.dma(hgn[:, l:l + 1], din["hgn"][l])
    K.memset("vector", lb, 0.0)
    K.tt("vector", lb[:, 1, :], hgl[:, 1, :], hgl[:, 0, :], SUB)
    K.act(lb[:, 1, :], lb[:, 1, :], AF.Sigmoid)
    K.ts("vector", oml, lb, -1.0, MUL, 1.0, ADD)
    K.ts("vector", noml, oml, -1.0, MUL)
    hglL = P.sbuf("hglL", [128, NL, 2], F32)
    lbL = P.sbuf("lbL", [128, NL, 2], F32)
    omlL = P.sbuf("omlL", [128, NL, 2], F32)
    nomlL = P.sbuf("nomlL", [128, NL, 2], F32)
    s5dl = P.sbuf("s5dl", [128, NL], F32)
    K.memset("vector", hglL, 0.0)
    K.memset("vector", s5dl, 0.0)
    for l in range(NL):
        K.dma(hglL[0:64, l, :], din["hglbL"][l])
        K.dma(s5dl[0:64, l:l + 1], din["s5dL"][l])
    K.memset("vector", lbL, 0.0)
    K.tt("vector", lbL[:, 1, :], hglL[:, 1, :], hglL[:, 0, :], SUB)
    K.act(lbL[:, 1, :], lbL[:, 1, :], AF.Sigmoid)
    K.ts("vector", omlL, lbL, -1.0, MUL, 1.0, ADD)
    K.ts("vector", nomlL, omlL, -1.0, MUL)
    mqn = P.sbuf("mqn", [128, NL, 2], F32)
    mkvn = P.sbuf("mkvn", [128, NL, 128], F32)
    s5d = P.sbuf("s5d", [128, NL, 2], F32)
    for l in range(NL):
        K.dma(mqn[:, l, :], din["mqn"][l])
        K.dma(mkvn[:, l, :], din["mkvn"][l])
        K.dma(s5d[:, l, :], din["s5d"][l])

    def run_job(lat):
        P.push_scope()
        nt = 16 if lat else 4
        L = nt * 128
        v = 1 if lat else 0
        seqs = [(0, L)] if lat else [(0, 256), (256, 256)]
        BS = 512 if lat else 256
        xsrc = din["xl"] if lat else din["xc"]
        ydst = dout["y_l"] if lat else dout["y_c"]
        ntx = 4 if lat else nt
        x = P.sbuf("x", [128, ntx, D], F32)
        xv = [x.view(x.ap[:, i, :], "x%d" % i) for i in range(ntx)]
        hT = P.sbuf("hT", [128, 8, L], BF16)
        hTo = P.sbuf("hTo", [128, 8, 512], BF16) if lat else hT
        mixed = P.sbuf("mixed", [128, 2, L], BF16) if not lat else None
        mixh = P.sbuf("mixh", [64, 4, L], BF16) if lat else None
        NH = 1 if lat else 4
        for i in range(ntx):
            K.dma(xv[i], xsrc[i])

        def rstd_of(xt, n, ss_scale):
            s = sm.next()
            K.act(junk[:, 0:n], xt, AF.Square, accum=s[:, 0:1])
            K.act(s[:, 1:2], s[:, 0:1], AF.Sqrt, bias=epsc, scale=ss_scale)
            K.recip(s[:, 2:3], s[:, 1:2])
            return s[:, 2:3]

        def load_group(l, name):
            if lat:
                if name.startswith("Cf"):
                    name = name[:3]
                elif name[:2] not in ("Bg", "Dc"):
                    name = name.rstrip("0123456789")
            off = (LGT if lat else GT)[name]
            s = wst.next()
            K.dma(s, din["winL" if lat else "win"][l][off // 128])
            w = wbf.next()
            K.cp("scalar", w, s)
            return w

        def fm_cols(w, c0, n, M=128):
            ps = psr.next()
            for k in range(8):
                K.mm(ps[0:M, 0:n], w[:, k, 0:M], hT[:, k, c0:c0 + n], start=(k == 0), stop=(k == 7))
            return ps

        def fm_block(w, tb, M=128):
            return fm_cols(w, tb * TB, TB, M)

        def tm_tile(w, i, N=128):
            ps = psr.next()
            for k in range(8):
                K.mm(ps[:, 0:N], hT[:, k, i * 128:(i + 1) * 128], w[:, k, 0:N], start=(k == 0), stop=(k == 7))
            return ps

        def blk(tb):
            return slice(tb * TB, (tb + 1) * TB)

        def tile_seq_rows(i):
            return i // 2, (i % 2) * 128

        def norm_mod(l):
            for i in range(ntx):
                r = rstd_of(xv[i], D, 1.0 / D)
                xn = xnr.next()
                K.ts("vector", xn, xv[i], r, MUL)
                for half in range(2):
                    ps = psr.next()
                    for kk in range(4):
                        k = half * 4 + kk
                        K.mm(ps[:, kk * 128:(kk + 1) * 128], xn[:, k * 128:(k + 1) * 128], ident_b)
                    for kk in range(4):
                        k = half * 4 + kk
                        eng = "vector" if kk % 2 == 0 else "gpsimd"
                        if eng == "gpsimd":
                            K.act(hTo[:, k, i * 128:(i + 1) * 128], ps[:, kk * 128:(kk + 1) * 128], AF.Identity,
                                  bias=modT[:, l, v, k:k + 1], scale=modT[:, l, v, 8 + k:9 + k])
                        else:
                            K.ts("vector", hTo[:, k, i * 128:(i + 1) * 128], ps[:, kk * 128:(kk + 1) * 128],
                                 modT[:, l, v, 8 + k:9 + k], MUL, modT[:, l, v, k:k + 1], ADD)
            if lat:
                ahin = DR(nc.dram_tensor("ahin%d" % l, [1024, 512], BF16).ap())
                ahout = DR(nc.dram_tensor("ahout%d" % l, [4096, 512], BF16).ap())
                ahin_tok = Buf("ahin_tok%d" % l, None)
                ahout_tok = Buf("ahout_tok%d" % l, None)
                P.dma(ahin.ap.rearrange("(k p) t -> p k t", p=128), hTo.ap, reads=[hTo], writes=[ahin_tok], eng="gpsimd")
                P.collective(lambda e: e.collective_compute("AllGather", ALU.bypass, replica_groups=[[0, 1, 2, 3], [4, 5, 6, 7]],
                                                            ins=[ahin.ap.opt()], outs=[ahout.ap.opt()]),
                             reads=[ahin_tok], writes=[ahout_tok])
                HTL["ahout"], HTL["tok"] = ahout, ahout_tok

        HTL = {}

        def hT_load():
            ahout, ahout_tok = HTL["ahout"], HTL["tok"]
            for r_ in range(4):
                P.dma(hT.ap[:, :, r_ * 512:(r_ + 1) * 512], ahout.ap[r_ * 1024:(r_ + 1) * 1024, :].rearrange("(k p) t -> p k t", p=128),
                      reads=[ahout_tok], writes=[hT])

        def out_proj(l, bi):
            wo = [wbf.next(), wbf.next()]
            for kc in range(2):
                s = wst.next()
                sv = s.re("p a b -> p (a b)")
                K.dma(sv, din["wout"][l][:, 2 * bi + kc, :])
                wov = wo[kc].re("p a b -> p (a b)")
                for n in range(2):
                    gb = wf.next()
                    P.dma(gb.ap, gsc[l][v:v + 1, n * 512:(n + 1) * 512].ap.partition_broadcast(128), reads=[gsc_tok], writes=[gb])
                    K.tt("vector", wov[:, n * 512:(n + 1) * 512], gb, sv[:, n * 512:(n + 1) * 512], MUL)
            for i in range(nt):
                for n in range(2):
                    ps = psr.next()
                    for kc in range(2):
                        K.mm(ps, mixed[:, kc, i * 128:(i + 1) * 128], wo[kc].re("p a b -> p (a b)")[:, n * 512:(n + 1) * 512],
                             start=(kc == 0), stop=(kc == 1))
                    K.tt("vector", xv[i][:, n * 512:(n + 1) * 512], ps, xv[i][:, n * 512:(n + 1) * 512], ADD)

        def attention(qT, kT, Vaug_of, Kdim, bases, scale, h, epilogue):
            pb = 64 * (h % 2)
            dn = 64 - pb
            QB = BS
            steps = []
            for (s0, sl) in seqs:
                if lat:
                    ktiles = list(range((L + PAST) // 128))
                else:
                    ktiles = list(range(s0 // 128, (s0 + sl) // 128))
                for qb in range(sl // QB):
                    q0 = s0 + qb * QB
                    for bi_, base in enumerate(bases):
                        for idx, kt in enumerate(ktiles):
                            steps.append((q0, bi_, base, idx, kt, len(ktiles)))
            pend = []
            state = {}

            def do_pv(item):
                (q0, bi_, base, idx, kt, nk), pt = item
                if idx == 0:
                    state["acc"] = accs.next()
                    if bi_ == 0:
                        state["ocs"] = []
                acc = state["acc"]
                K.mm(acc[:, 0:QB], Vaug_of(kt), pt[:, 0:QB], start=(idx == 0), stop=(idx == nk - 1))
                if idx == nk - 1:
                    rec = wf.next()
                    K.act(rec[dn:dn + 64, 0:QB], acc[dn:dn + 64, 0:QB], AF.Ln)
                    K.act(rec[dn:dn + 64, 0:QB], rec[dn:dn + 64, 0:QB], AF.Exp, scale=-1.0)
                    oc = wf.next()
                    K.tt("vector", oc[pb:pb + 64, 0:QB], acc[pb:pb + 64, 0:QB], rec[dn:dn + 64, 0:QB], MUL)
                    state["ocs"].append(oc)
                    if bi_ == len(bases) - 1:
                        epilogue(state["ocs"], q0, QB, pb)

            for stp in steps:
                (q0, bi_, base, idx, kt, nk) = stp
                pss = psr.next()
                K.mm(pss[:, 0:QB], kT[base:base + Kdim, kt * 128:(kt + 1) * 128], qT[base:base + Kdim, q0:q0 + QB])
                pt = wb.next()
                K.act(pt[:, 0:QB], pss[:, 0:QB], AF.Exp, scale=scale)
                pend.append((stp, pt))
                if len(pend) > 3:
                    do_pv(pend.pop(0))
            while pend:
                do_pv(pend.pop(0))

        def branch_A(l):
            P.push_scope()
            Lk = L + PAST if lat else L
            nkt = Lk // 128
            if not lat:
                for i2 in range(2):
                    for nm, dst in (("Av", "o_dv"), ("Akt", "o_dk")):
                        w = load_group(l, "%s%d" % (nm, i2))
                        for i in range(nt):
                            ps = tm_tile(w, i)
                            o = wf.next()
                            K.cp("scalar" if i % 2 else "vector", o[:, 0:128], ps[:, 0:128])
                            sq, r0 = tile_seq_rows(i)
                            K.dma(dout[dst][sq, l, r0:r0 + 128, i2 * 128:(i2 + 1) * 128], o[:, 0:128], out=True, eng="gpsimd")
            else:
                ckf = P.sbuf("ckf", [128, 2, 64], F32)
                cvf = P.sbuf("cvf", [128, 2, 64], F32)
                for j in range(2):
                    K.dma(cvf[:, j, :], din["cdvL"][l][j])
                    K.dma(ckf[:, j, :], din["cdkL"][l][j])
            sgA = P.sbuf("sgA", [128, L], BF16)
            Vh = P.sbuf("VhA", [128, nkt, 128], BF16)
            qT = P.sbuf("qT", [128, L], BF16)
            kT = P.sbuf("kT", [128, Lk], BF16)
            kpad = P.sbuf("kpad", [128, 128], BF16)
            K.memset("vector", kpad, 0.0)
            for h in range(NH):
                pbh = 64 * (h % 2)
                if h % 2 == 0:
                    w = load_group(l, "Ag%d" % (h // 2))
                    for tb in range(L // TB):
                        ps = fm_block(w, tb)
                        K.act(sgA[:, blk(tb)], ps, AF.Silu)
                w = load_group(l, "Av%d" % (h // 2))
                K.memset("gpsimd", Vh[:, :, 64 - pbh:128 - pbh], 1.0)
                for i in range(nt):
                    ps = psr.next()
                    for k in range(8):
                        K.mm(ps[:, 0:64], hT[:, k, i * 128:(i + 1) * 128], w[:, k, pbh:pbh + 64], start=(k == 0), stop=(k == 7))
                    K.cp("scalar" if i % 2 else "vector", Vh[:, i, pbh:pbh + 64], ps[:, 0:64])
                if lat:
                    for j in range(2):
                        K.cp("gpsimd", Vh[:, nt + j, pbh:pbh + 64], cvf[:, j, h * 64:(h + 1) * 64])
                for nm, dst in (("Aq", qT), ("Ak", kT)):
                    w = load_group(l, "%s%d" % (nm, h))
                    if lat:
                        wp = load_group(l, "%sp%d" % (nm, h))
                    for tb in range(L // TB):
                        ps = fm_block(w, tb)
                        if lat:
                            psp = fm_block(wp, tb)
                            t1 = wf.next()
                            t2 = wf.next()
                            K.tt("vector", t1, ps, ropeC[:, blk(tb)], MUL)
                            K.tt("vector", t2, psp, ropeS[:, blk(tb)], MUL)
                            K.tt("vector", dst[:, blk(tb)], t1, t2, ADD)
                        else:
                            K.cp("scalar", dst[:, blk(tb)], ps)
                if lat:
                    for j in range(2):
                        K.cp("vector", kpad[:, 0:32], ckf[:, j, h * 64:h * 64 + 32])
                        K.cp("vector", kpad[:, 64:96], ckf[:, j, h * 64 + 32:h * 64 + 64])
                        ps = psr.next()
                        K.mm(ps[:, 0:128], kpad, ident_b)
                        K.cp("vector", kT[:, L + j * 128:L + (j + 1) * 128], ps[:, 0:128])

                def epi(ocs, q0, QB, pb, h=h):
                    o = wf.next()
                    K.stt("vector", o[pb:pb + 64, 0:QB], ocs[1][pb:pb + 64, 0:QB], neglam[pb:pb + 64, l:l + 1],
                          ocs[0][pb:pb + 64, 0:QB], MUL, ADD)
                    sq = wb.next()
                    K.tt("vector", sq[pb:pb + 64, 0:QB], o[pb:pb + 64, 0:QB], o[pb:pb + 64, 0:QB], MUL)
                    pss = psr.next()
                    K.mm(pss[:, 0:QB], ones_b[pb:pb + 64, :], sq[pb:pb + 64, 0:QB])
                    rs = wf.next()
                    K.act(rs[pb:pb + 64, 0:QB], pss[pb:pb + 64, 0:QB], AF.Ln, bias=epsc[pb:pb + 64, :], scale=1.0 / 64)
                    K.act(rs[pb:pb + 64, 0:QB], rs[pb:pb + 64, 0:QB], AF.Exp, scale=-0.5)
                    K.tt("vector", o[pb:pb + 64, 0:QB], o[pb:pb + 64, 0:QB], rs[pb:pb + 64, 0:QB], MUL)
                    dst = mixh[0:64, 0, q0:q0 + QB] if lat else mixed[pb:pb + 64, h // 2, q0:q0 + QB]
                    K.stt("vector", dst, o[pb:pb + 64, 0:QB], dan_s[pb:pb + 64, l:l + 1],
                          sgA[pb:pb + 64, q0:q0 + QB], MUL, MUL)

                attention(qT, kT, lambda kt: Vh[:, kt, :], 64, (0, 64), 32 ** -0.5, h, epi)
            P.pop_scope()

        def branch_D(l):
            P.push_scope()
            Lk = L + PAST if lat else L
            nkt = Lk // 128
            wuq_b = P.sbuf("wuq_b", [128, 2, 192 if lat else 768], BF16)
            wukv_b = P.sbuf("wukv_b", [128, 128 if lat else 512], BF16)
            P.push_scope()
            wuq_f = P.sbuf("wuq_f", [128, 2, 768], F32)
            NQ = 192 if lat else 768
            K.dma(wuq_f[:, :, 0:NQ], din["wuqL" if lat else "wuq"][l])
            for kc in range(2):
                K.ts("vector", wuq_b[:, kc, 0:NQ], wuq_f[:, kc, 0:NQ], mqn[:, l, kc:kc + 1], MUL)
            s = wf.next()
            NKV = 128 if lat else 512
            K.dma(s[:, 0:NKV], din["wukvL" if lat else "wukv"][l])
            K.cp("vector", wukv_b[:, 0:NKV], s[:, 0:NKV])
            QPO = 96 if lat else 384
            P.pop_scope()
            cqb = P.sbuf("cqb", [128, 2, L], BF16)
            for kc in range(2):
                w = load_group(l, "Dcq%d" % kc)
                for tb in range(L // TB):
                    ps = fm_block(w, tb)
                    K.cp("scalar", cqb[:, kc, blk(tb)], ps)
            ckvT = P.sbuf("ckvT", [128, Lk], BF16)
            krT = P.sbuf("krT", [128, Lk], BF16)
            w = load_group(l, "Dckv")

            def ckv_tile(src_ps, col0, raw_is_psum=True, out_to=None):
                cnb = wb.next()
                if out_to is None:
                    K.cp("vector", cnb[:, 0:128], src_ps)
                else:
                    r = rstd_of(src_ps, 128, 1.0 / 128)
                    cn = wf.next()
                    K.stt("vector", cn[:, 0:128], src_ps, r, mkvn[:, l, :], MUL, MUL)
                    if out_to is not False:
                        K.dma(out_to, cn[:, 0:128], out=True, eng="gpsimd")
                    K.cp("gpsimd", cnb[:, 0:128], cn[:, 0:128])
                pst = psr.next()
                K.mm(pst[:, 0:128], cnb[:, 0:128], ident_b)
                K.cp("scalar", ckvT[:, col0:col0 + 128], pst[:, 0:128])

            for i in range(nt):
                ps = tm_tile(w, i)
                if lat:
                    ckv_tile(ps[:, 0:128], i * 128, out_to=False)
                else:
                    sq, r0 = tile_seq_rows(i)
                    ckv_tile(ps[:, 0:128], i * 128, out_to=dout["o_ckv"][sq, l, r0:r0 + 128, :])
            if lat:
                for j in range(2):
                    s = wf.next()
                    K.dma(s[:, 0:128], din["cckv"][l][j])
                    ckv_tile(s[:, 0:128], L + j * 128)
            w = load_group(l, "Dkr")
            if lat:
                wp = load_group(l, "Dkrp")
            for tb in range(L // TB):
                ps = fm_block(w, tb, M=96)
                if lat:
                    psp = fm_block(wp, tb, M=96)
                    t1 = wf.next()
                    t2 = wf.next()
                    K.tt("vector", t1[64:96, :], ps[64:96, :], ropeC[64:96, blk(tb)], MUL)
                    K.tt("vector", t2[64:96, :], psp[64:96, :], ropeS[64:96, blk(tb)], MUL)
                    K.tt("gpsimd", krT[64:96, blk(tb)], t1[64:96, :], t2[64:96, :], ADD)
                else:
                    K.cp("scalar", krT[64:96, blk(tb)], ps[64:96, :])
            if lat:
                kpad = P.sbuf("kpadD", [128, 128], BF16)
                K.memset("vector", kpad, 0.0)
                for j in range(2):
                    s = wf.next()
                    K.dma(s[:, 0:32], din["ckr"][l][j])
                    K.cp("vector", kpad[:, 64:96], s[:, 0:32])
                    pst = psr.next()
                    K.mm(pst[0:96, 0:128], kpad[:, 0:96], ident_b)
                    K.cp("vector", krT[64:96, L + j * 128:L + (j + 1) * 128], pst[64:96, 0:128])
            else:
                w = load_group(l, "Dkrt")
                for i in range(nt):
                    ps = tm_tile(w, i, N=32)
                    o = wf.next()
                    K.cp("vector", o[:, 0:32], ps[:, 0:32])
                    sq, r0 = tile_seq_rows(i)
                    K.dma(dout["o_kr"][sq, l, r0:r0 + 128, :], o[:, 0:32], out=True, eng="gpsimd")
            qT = P.sbuf("qTD", [128, L], BF16)
            kT = P.sbuf("kTD", [128, Lk], BF16)
            Vh = P.sbuf("VhD", [128, nkt, 128], BF16)
            sgD = P.sbuf("sgD", [128, L], BF16)
            for h in range(NH):
                pb = 64 * (h % 2)
                if h % 2 == 0:
                    w = load_group(l, "Dg%d" % (h // 2))
                    for tb in range(L // TB):
                        ps = fm_block(w, tb)
                        K.act(sgD[:, blk(tb)], ps, AF.Silu)
                for tb in range(L // TB):
                    sq0 = wb.next()
                    sq1 = wb.next()
                    K.act(sq0, cqb[:, 0, blk(tb)], AF.Square)
                    K.act(sq1, cqb[:, 1, blk(tb)], AF.Square)
                    pss = psr.next()
                    K.mm(pss[0:96, :], ones_b[:, 0:96], sq0, start=True, stop=False)
                    K.mm(pss[0:96, :], ones_b[:, 0:96], sq1, start=False, stop=True)
                    rq = wf.next()
                    K.act(rq[0:96, :], pss[0:96, :], AF.Ln, bias=epsc[0:96, :], scale=1.0 / 192)
                    K.act(rq[0:96, :], rq[0:96, :], AF.Exp, scale=-0.5)
                    z = psr.next()
                    for kc in range(2):
                        K.mm(z[0:96, :], wuq_b[:, kc, h * 96:(h + 1) * 96], cqb[:, kc, blk(tb)], start=(kc == 0), stop=(kc == 1))
                    K.tt("vector", qT[0:64, blk(tb)], z[0:64, :], rq[0:64, :], MUL)
                    if lat:
                        zp = psr.next()
                        for kc in range(2):
                            K.mm(zp[0:96, :], wuq_b[:, kc, QPO + h * 96:QPO + (h + 1) * 96], cqb[:, kc, blk(tb)],
                                 start=(kc == 0), stop=(kc == 1))
                        t1 = wf.next()
                        t2 = wf.next()
                        K.tt("vector", t1[64:96, :], z[64:96, :], ropeC[64:96, blk(tb)], MUL)
                        K.tt("vector", t2[64:96, :], zp[64:96, :], ropeS[64:96, blk(tb)], MUL)
                        K.tt("vector", t1[64:96, :], t1[64:96, :], t2[64:96, :], ADD)
                        K.tt("vector", qT[64:96, blk(tb)], t1[64:96, :], rq[64:96, :], MUL)
                    else:
                        K.tt("vector", qT[64:96, blk(tb)], z[64:96, :], rq[64:96, :], MUL)
                for c0 in range(0, Lk, TB):
                    n = min(TB, Lk - c0)
                    ps = psr.next()
                    K.mm(ps[0:64, 0:n], wukv_b[:, h * 128:h * 128 + 64], ckvT[:, c0:c0 + n])
                    K.cp("scalar", kT[0:64, c0:c0 + n], ps[0:64, 0:n])
                K.cp("vector", kT[64:96, :], krT[64:96, :])
                K.memset("gpsimd", Vh[:, :, 64 - pb:128 - pb], 1.0)
                for kt in range(nkt):
                    ps = psr.next()
                    K.mm(ps[:, 0:64], ckvT[:, kt * 128:(kt + 1) * 128], wukv_b[:, h * 128 + 64:h * 128 + 128])
                    K.cp("vector" if kt % 2 else "scalar", Vh[:, kt, pb:pb + 64], ps[:, 0:64])

                def epi(ocs, q0, QB, pb, h=h):
                    dst = mixh[0:64, 2, q0:q0 + QB] if lat else mixed[pb:pb + 64, h // 2, q0:q0 + QB]
                    K.tt("vector", dst, ocs[0][pb:pb + 64, 0:QB], sgD[pb:pb + 64, q0:q0 + QB], MUL)

                attention(qT, kT, lambda kt: Vh[:, kt, :], 96, (0,), 96 ** -0.5, h, epi)
            P.pop_scope()

        def branch_C(l):
            P.push_scope()
            qb_ = P.sbuf("hq", [128, L], BF16)
            vtok = P.sbuf("hv", [128, nt, 128], BF16)
            obuf = P.sbuf("ho", [128, L], F32)
            if lat:
                K.memset("gpsimd", obuf, 0.0)
            Sfr = Ring([P.sbuf("hSf%d" % i, [128, 128], F32) for i in range(3)])
            sbring = Ring([P.sbuf("hSb%d" % i, [128, 128], BF16) for i in range(6)])
            hsg, hf, hg_, hkk, hb, heb = [P.sbuf("h_t%d" % i, [128, 512], F32) for i in range(6)]
            hqt, hkt, hkh = [P.sbuf("h_b%d" % i, [128, 512], BF16) for i in range(3)]
            HH = 1 if lat else 2
            for hp in range(1 if lat else 2):
                w = load_group(l, "Cq%d" % hp)
                for tb in range(L // TB):
                    ps = fm_block(w, tb)
                    K.cp("scalar", qb_[:, blk(tb)], ps)
                w = load_group(l, "Cv%d" % hp)
                for i in range(nt):
                    ps = tm_tile(w, i)
                    K.cp("vector", vtok[:, i, :], ps[:, 0:128])
                for d in range(2):
                    w = load_group(l, "Cf%d%d" % (d, hp))
                    ci = 2 * d + hp
                    if lat:
                        lbv, omv, nomv = lbL[:, l, d:d + 1], omlL[:, l, d:d + 1], nomlL[:, l, d:d + 1]
                    else:
                        lbv, omv, nomv = lb[:, l, ci:ci + 1], oml[:, l, ci:ci + 1], noml[:, l, ci:ci + 1]
                    for si, (s0, sl) in enumerate(seqs):
                        Sf = Sfr.next()
                        K.memset("vector", Sf, 0.0)
                        if lat:
                            K.dma(Sf[0:64, 0:64], din["chgL"][l][d])
                        cur = {"sb": sbring.next(), "sf": Sf}
                        K.cp("scalar", cur["sb"], Sf)
                        nb = sl // BS
                        for bi in (range(nb) if d == 0 else range(nb - 1, -1, -1)):
                            c0 = s0 + bi * BS
                            ncks = BS // HGC
                            if CSTOP < 1:
                                continue
                            ps = fm_cols(w, c0, BS)
                            sg = hsg
                            K.act(sg[:, 0:BS], ps[:, 0:BS], AF.Sigmoid)
                            f = hf
                            K.ts("vector", f[:, 0:BS], sg[:, 0:BS], omv, MUL, lbv, ADD)
                            g = hg_
                            K.act(g[:, 0:BS], f[:, 0:BS], AF.Ln)
                            kk = hkk
                            K.ts("vector", kk[:, 0:BS], sg[:, 0:BS], nomv, MUL, omv, ADD)
                            b = hb
                            if d == 0:
                                K.scan(b[:, 0:BS], m01[:, 0:BS], g[:, 0:BS], 0.0)
                            else:
                                K.scan(b[:, BS - 1::-1] if False else b[:, 0:BS][:, ::-1], m01[:, 0:BS], g[:, 0:BS][:, ::-1], 0.0)
                            if CSTOP < 2:
                                continue
                            eb = heb
                            K.act(eb[:, 0:BS], b[:, 0:BS], AF.Exp)
                            enb = g
                            K.act(enb[:, 0:BS], b[:, 0:BS], AF.Exp, scale=-1.0)
                            if CSTOP < 2.1:
                                continue
                            qt = hqt
                            K.tt("vector", qt[:, 0:BS], qb_[:, c0:c0 + BS], eb[:, 0:BS], MUL)
                            kt_ = hkt
                            K.tt("gpsimd", kt_[:, 0:BS], kk[:, 0:BS], enb[:, 0:BS], MUL)
                            if CSTOP < 2.2:
                                continue
                            b3 = b[:, 0:BS].re("p (c j) -> p c j", j=HGC)
                            e = HGC - 1 if d == 0 else 0
                            d2 = f
                            K.tt("vector", d2[:, 0:BS].re("p (c j) -> p c j", j=HGC), b3[:, :, e:e + 1].bc([128, ncks, HGC]), b3, SUB)
                            K.act(d2[:, 0:BS], d2[:, 0:BS], AF.Exp)
                            if CSTOP < 2.3:
                                continue
                            kh = hkh
                            K.tt("gpsimd", kh[:, 0:BS], kk[:, 0:BS], d2[:, 0:BS], MUL)
                            ntile = BS // 128
                            if CSTOP < 3:
                                continue
                            for ti in (range(ntile) if d == 0 else range(ntile - 1, -1, -1)):
                                lo = ti * 128
                                gi = (c0 + lo) // 128
                                pss2 = [psr.next(), psr.next()]
                                for hh in range(HH):
                                    K.mm(pss2[hh][:, 0:128], kt_[64 * hh:64 * hh + 64, lo:lo + 128],
                                         qt[64 * hh:64 * hh + 64, lo:lo + 128])
                                if CSTOP < 3.1:
                                    continue
                                sc = wb.next()
                                for hh in range(HH):
                                    K.tt("vector", sc[:, hh * 128:(hh + 1) * 128], pss2[hh][:, 0:128], hmask[d], MUL)
                                if CSTOP < 4:
                                    continue
                                pst = psr.next()
                                K.mm(pst[:, 0:128], kh[:, lo:lo + 128], ident_b)
                                kexp = wb.next()
                                K.tt("vector", kexp[:, :].re("p (c k) -> p c k", c=4),
                                     pst[:, 0:128].re("p (o k) -> p o k", o=1).bc([128, 4, 128]),
                                     m4[:, :].re("p (c o) -> p c o", o=1).bc([128, 4, 128]), MUL)
                                if CSTOP < 5:
                                    continue
                                psu = psr.next()
                                for c in range(4):
                                    K.mm(psu[:, c * 128:(c + 1) * 128], kexp[:, c * 128:(c + 1) * 128], vtok[:, gi, :])
                                if CSTOP < 6:
                                    continue
                                corder = list(range(4)) if d == 0 else [3, 2, 1, 0]
                                Sbs = []
                                for cn, c in enumerate(corder):
                                    Sbs.append(cur["sb"])
                                    ce = lo + c * HGC + e
                                    Sfn = Sfr.next()
                                    K.stt("vector", Sfn, cur["sf"], eb[:, ce:ce + 1], psu[:, c * 128:(c + 1) * 128], MUL, ADD)
                                    cur["sf"] = Sfn
                                    nsb = sbring.next()
                                    K.cp("scalar", nsb, Sfn)
                                    cur["sb"] = nsb
                                psos = [psr.next(), psr.next()]
                                for cn, c in enumerate(corder):
                                    for hh in range(HH):
                                        pso = psos[hh]
                                        K.mm(pso[:, c * HGC:(c + 1) * HGC], vtok[:, gi, :],
                                             sc[:, hh * 128 + c * HGC:hh * 128 + (c + 1) * HGC], start=True, stop=False)
                                        K.mm(pso[:, c * HGC:(c + 1) * HGC], Sbs[cn][64 * hh:64 * hh + 64, :],
                                             qt[64 * hh:64 * hh + 64, lo + c * HGC:lo + (c + 1) * HGC], start=False, stop=True)
                                t0 = c0 + lo
                                for hh in range(HH):
                                    pbh = 64 * hh
                                    pso = psos[hh]
                                    if d == 0:
                                        K.cp("vector", obuf[pbh:pbh + 64, t0:t0 + 128], pso[pbh:pbh + 64, 0:128])
                                    else:
                                        K.tt("vector", obuf[pbh:pbh + 64, t0:t0 + 128], pso[pbh:pbh + 64, 0:128],
                                             obuf[pbh:pbh + 64, t0:t0 + 128], ADD)
                        if not lat:
                            seqi = si
                            for hh in range(2):
                                K.dma(dout["o_hg"][seqi, l, d, 2 * hp + hh], cur["sf"][64 * hh:64 * hh + 64, 64 * hh:64 * hh + 64],
                                      out=True, eng="gpsimd")
                w = load_group(l, "Cg%d" % hp)
                for tb in range(L // TB):
                    ps = fm_block(w, tb)
                    sg = wb.next()
                    K.act(sg, ps, AF.Silu)
                    sq = wb.next()
                    K.tt("gpsimd", sq, obuf[:, blk(tb)], obuf[:, blk(tb)], MUL)
                    pss = psr.next()
                    K.mm(pss, bdones, sq)
                    rs = wf.next()
                    K.act(rs, pss, AF.Ln, bias=epsc, scale=1.0 / 64)
                    K.act(rs, rs, AF.Exp, scale=-0.5)
                    t = wf.next()
                    K.tt("vector", t, obuf[:, blk(tb)], rs, MUL)
                    if lat:
                        K.stt("vector", mixh[0:64, 1, blk(tb)], t[0:64, :], hgn[0:64, l:l + 1], sg[0:64, :], MUL, MUL)
                    else:
                        K.stt("vector", mixed[:, hp, blk(tb)], t, hgn[:, l:l + 1], sg, MUL, MUL)
            P.pop_scope()

        BP = {}

        def b_prep(l):
            P.push_scope()
            C1 = 6.28125
            C2 = 2.0 * PI - C1
            NST = 2 if lat else 8
            NCC = 1 if lat else 2
            KR = 64 if lat else 128
            sp = P.sbuf("s5sp", [128, 16, 3], F32)
            K.dma(sp[:, 0:2 * NST, :], din["s5pL" if lat else "s5p"][l])
            step = P.sbuf("s5step", [128, 16], F32)
            th = P.sbuf("s5th", [128, 16], F32)
            rmag = P.sbuf("s5mag", [128, 16], F32)
            N2 = 2 * NST
            K.act(step[:, 0:N2], sp[:, 0:N2, 2], AF.Exp)
            K.tt("vector", th[:, 0:N2], sp[:, 0:N2, 1], step[:, 0:N2], MUL)
            K.tt("vector", rmag[:, 0:N2], sp[:, 0:N2, 0], step[:, 0:N2], MUL)
            K.act(rmag[:, 0:N2], rmag[:, 0:N2], AF.Exp)
            EC2 = P.sbuf("s5EC", [128, 2, NST, S5T], F32)
            ES2 = P.sbuf("s5ES", [128, 2, NST, S5T], F32)
            Bm2 = P.sbuf("s5Bm", [128, 2, NST, 2, 128], BF16)
            Cm2 = P.sbuf("s5Cm", [128, 2, NST, 3, 128], BF16)
            fz2 = P.sbuf("s5f", [128, 2, 8, 4], F32)
            cst = P.sbuf("s5cst", [128, 8, 2], F32)
            h0 = P.sbuf("s5h0", [128, 16, 2], F32)
            if lat:
                K.dma(h0[:, 0:4, :], din["cs5L"][l])
            else:
                wglu_b = P.sbuf("wglu_b", [128, 2, 512], BF16)
                s = wst.next()
                sv = s.re("p a b -> p (a b)")
                K.dma(sv, din["wglu"][l].re("p a b -> p (a b)"))
                K.cp("vector", wglu_b[:, :, :].re("p a b -> p (a b)"), sv)
            for d in range(2):
                EC, ES, Bm, Cm, fz = EC2[:, d], ES2[:, d], Bm2[:, d], Cm2[:, d], fz2[:, d]
                P.push_scope()
                kint = P.sbuf("s5ki", [128, 512], I32)
                SH = min(512 // S5T, NST)
                for half in range(NST // SH):
                    ang = wf.next()
                    a3 = ang[:, 0:SH * S5T].re("p (s t) -> p s t", s=SH)
                    K.tt("vector", a3, iota[:, :].re("p (o t) -> p o t", o=1).bc([128, SH, S5T]),
                         th[:, d * NST + half * SH:d * NST + half * SH + SH].re("p (s o) -> p s o", o=1).bc([128, SH, S5T]), MUL)
                    W_ = SH * S5T
                    kf = wf.next()
                    K.ts("vector", kf[:, 0:W_], ang[:, 0:W_], 1.0 / (2 * PI), MUL)
                    K.cp("vector", kint[:, 0:W_], kf[:, 0:W_])
                    K.cp("vector", kf[:, 0:W_], kint[:, 0:W_])
                    xx = wf.next()
                    K.stt("vector", xx[:, 0:W_], kf[:, 0:W_], -C1, ang[:, 0:W_], MUL, ADD)
                    K.stt("vector", xx[:, 0:W_], kf[:, 0:W_], -C2, xx[:, 0:W_], MUL, ADD)
                    K.ts("vector", xx[:, 0:W_], xx[:, 0:W_], PI, MIN, -PI, MAX)
                    K.act(ES[:, half * SH:half * SH + SH, :].re("p s t -> p (s t)"), xx[:, 0:W_], AF.Sin)
                    K.act(kf[:, 0:W_], xx[:, 0:W_], AF.Sin, scale=0.5)
                    K.tt("vector", kf[:, 0:W_], kf[:, 0:W_], kf[:, 0:W_], MUL)
                    K.ts("vector", EC[:, half * SH:half * SH + SH, :].re("p s t -> p (s t)"), kf[:, 0:W_], -2.0, MUL, 1.0, ADD)
                P.pop_scope()
                are = sp[:, d * NST:d * NST + NST, 0]
                aim = sp[:, d * NST:d * NST + NST, 1]
                mg = rmag[:, d * NST:d * NST + NST]
                t = sm.next()
                abr, abi, den, t1, t2 = t[:, 0:NST], t[:, 8:8 + NST], None, None, None
                u = sm.next()
                den, t1 = u[:, 0:NST], u[:, 8:8 + NST]
                u2 = sm.next()
                t2, t3 = u2[:, 0:NST], u2[:, 8:8 + NST]
                fzv = fz[:, 0:NST, :]
                K.tt("vector", abr, mg, EC[:, 0:NST, 0], MUL)
                K.tt("vector", abi, mg, ES[:, 0:NST, 0], MUL)
                K.ts("vector", abr, abr, -1.0, ADD)
                K.tt("vector", den, are, are, MUL)
                K.tt("vector", t1, aim, aim, MUL)
                K.tt("vector", den, den, t1, ADD)
                K.recip(den, den)
                K.tt("vector", t1, abr, are, MUL)
                K.tt("vector", t2, abi, aim, MUL)
                K.tt("vector", t1, t1, t2, ADD)
                K.tt("vector", fzv[:, :, 0], t1, den, MUL)
                K.tt("vector", t1, abi, are, MUL)
                K.tt("vector", t2, abr, aim, MUL)
                K.tt("vector", t1, t1, t2, SUB)
                K.tt("vector", fzv[:, :, 1], t1, den, MUL)
                K.ts("vector", fzv[:, :, 2], fzv[:, :, 1], -1.0, MUL)
                for st in range(NST):
                    s = wst.next()
                    sv = s.re("p a b -> p (a b)")
                    if lat:
                        K.dma(sv[0:64, 0:256].re("p (r c) -> p r c", r=2), din["s5BL"][l][d][st].re("r p c -> p r c"))
                        K.cp("gpsimd", Bm[0:64, st, :, :], sv[0:64, 0:256].re("p (r c) -> p r c", r=2))
                        K.memset("vector", sv[:, 256:512], 0.0)
                        K.dma(sv[:, 256:512].re("p (r c) -> p r c", r=2)[:, :, 0:64], din["s5CL"][l][d][st].re("r p c -> p r c"))
                    else:
                        K.dma(sv[:, 0:256].re("p (r c) -> p r c", r=2), din["s5B"][l][d][st].re("r p c -> p r c"))
                        K.cp("gpsimd", Bm[:, st, :, :], sv[:, 0:256].re("p (r c) -> p r c", r=2))
                        K.dma(sv[:, 256:512].re("p (r c) -> p r c", r=2), din["s5C"][l][d][st].re("r p c -> p r c"))
                    cre, cim = sv[:, 256:384], sv[:, 384:512]
                    tw = wf.next()
                    K.ts("vector", tw[:, 0:128], cre, fz[:, st, 0:1], MUL)
                    K.stt("vector", tw[:, 0:128], cim, fz[:, st, 2:3], tw[:, 0:128], MUL, ADD)
                    K.cp("vector", Cm[:, st, 0, :], tw[:, 0:128])
                    K.ts("vector", Cm[:, st, 1, :], tw[:, 0:128], -1.0, MUL)
                    K.ts("vector", tw[:, 128:256], cre, fz[:, st, 1:2], MUL)
                    K.stt("vector", tw[:, 128:256], cim, fz[:, st, 0:1], tw[:, 128:256], MUL, ADD)
                    K.ts("vector", Cm[:, st, 2, :], tw[:, 128:256], -1.0, MUL)
            BP.update(dict(sp=sp, rmag=rmag, EC2=EC2, ES2=ES2, Bm2=Bm2, Cm2=Cm2, fz2=fz2, cst=cst, h0=h0, NST=NST, NCC=NCC, KR=KR))
            if not lat:
                BP["wglu_b"] = wglu_b

        def branch_B(l):
            P.push_scope()
            sp, rmag, EC2, ES2, Bm2, Cm2, fz2, cst, h0 = (BP[k_] for k_ in ("sp", "rmag", "EC2", "ES2", "Bm2", "Cm2", "fz2", "cst", "h0"))
            NST, NCC, KR = BP["NST"], BP["NCC"], BP["KR"]
            if not lat:
                wglu_b = BP["wglu_b"]
            ytot_t = [P.sbuf("s5yt%d" % i, [128, 512], F32) for i in range(NCC)]
            NG4 = 2 if lat else 4
            WGr = Ring([P.sbuf("s5wg%d" % i, [128, NG4, 4 * S5T], F32) for i in range(2)])
            uT = P.sbuf("s5u", [128, NCC, L], BF16)
            yf = mixed if not lat else P.sbuf("s5yfL", [128, 1, L], BF16)
            for cc in range(NCC):
                w = load_group(l, "Bu%d" % cc)
                for tb in range(L // TB):
                    ps = fm_block(w, tb)
                    K.cp("scalar", uT[:, cc, blk(tb)], ps)
            for d in range(2):
                EC, ES, Bm, Cm, fz = EC2[:, d], ES2[:, d], Bm2[:, d], Cm2[:, d], fz2[:, d]
                fzv = fz[:, 0:NST, :]
                for si, (s0, sl) in enumerate(seqs):
                    if lat:
                        hr, hi = h0[:, d * NST:d * NST + NST, 0], h0[:, d * NST:d * NST + NST, 1]
                        t = sm.next()
                        n2, ta = t[:, 0:NST], t[:, 8:8 + NST]
                        u = sm.next()
                        tb_, tc = u[:, 0:NST], u[:, 8:8 + NST]
                        cstv = cst[:, 0:NST, :]
                        K.tt("vector", n2, fzv[:, :, 0], fzv[:, :, 0], MUL)
                        K.tt("vector", ta, fzv[:, :, 1], fzv[:, :, 1], MUL)
                        K.tt("vector", n2, n2, ta, ADD)
                        K.recip(n2, n2)
                        K.tt("vector", ta, hr, fzv[:, :, 0], MUL)
                        K.tt("vector", tb_, hi, fzv[:, :, 1], MUL)
                        K.tt("vector", ta, ta, tb_, ADD)
                        K.tt("vector", cstv[:, :, 0], ta, n2, MUL)
                        K.tt("vector", ta, hi, fzv[:, :, 0], MUL)
                        K.tt("vector", tb_, hr, fzv[:, :, 1], MUL)
                        K.tt("vector", ta, ta, tb_, SUB)
                        K.tt("vector", cstv[:, :, 1], ta, n2, MUL)
                    else:
                        K.memset("vector", cst, 0.0)
                    nch = sl // S5T
                    ytot = None
                    for c in (range(nch) if d == 0 else range(nch - 1, -1, -1)):
                        cols = slice(s0 + c * S5T, s0 + (c + 1) * S5T)
                        yps = [accs.next() for _ in range(NCC)]
                        for grp in range(NCC):
                            cc = grp
                            sts = list(range(4 * grp, 4 * grp + 4)) if not lat else [0, 1]
                            psbs, p12s, wgs, qs = {}, {}, {}, {}
                            tabs = {}
                            T_ = S5T
                            for st in sts:
                                psb = psr.next()
                                K.mm(psb[:, 0:T_], Bm[0:KR, st, 0, :], uT[0:KR, cc, cols])
                                K.mm(psb[:, T_:2 * T_], Bm[0:KR, st, 1, :], uT[0:KR, cc, cols])
                                psbs[st] = psb
                                tabs[st] = (EC[:, st, :].re("p (o t) -> p o t", o=1).bc([128, 2, T_]),
                                            ES[:, st, :].re("p (o t) -> p o t", o=1).bc([128, 2, T_]))
                            for st in sts:
                                bu3 = psbs[st][:, 0:2 * T_].re("p (r t) -> p r t", r=2)
                                if d == 1:
                                    bu3 = bu3[:, :, ::-1]
                                ecb, esb = tabs[st]
                                p1t = wf.next()
                                p2t = wf.next()
                                K.tt("vector", p1t[:, 0:2 * T_].re("p (r t) -> p r t", r=2), bu3, ecb, MUL)
                                K.tt("vector", p2t[:, 0:2 * T_].re("p (r t) -> p r t", r=2), bu3[:, ::-1, :], esb, MUL)
                                p12s[st] = (p1t, p2t)
                            WG = WGr.next()
                            for si_, st in enumerate(sts):
                                p1t, p2t = p12s[st]
                                wg = WG[:, si_, :]
                                K.tt("gpsimd", wg[:, 0:T_], p1t[:, 0:T_], p2t[:, 0:T_], ADD)
                                K.tt("gpsimd", wg[:, T_:2 * T_], p1t[:, T_:2 * T_], p2t[:, T_:2 * T_], SUB)
                                wgs[st] = wg
                            for st in sts:
                                wg = wgs[st]
                                rb = rmag[:, d * NST + st:d * NST + st + 1].bc([128, T_])
                                K.scan(wg[:, 2 * T_:3 * T_], rb, wg[:, 0:T_], cst[:, st, 0:1])
                                K.scan(wg[:, 3 * T_:4 * T_], rb, wg[:, T_:2 * T_], cst[:, st, 1:2])
                            for st in sts:
                                ecb, esb = tabs[st]
                                gg3 = wgs[st][:, 2 * T_:4 * T_].re("p (r t) -> p r t", r=2)
                                q1t = wb.next()
                                q2t = wb.next()
                                K.tt("gpsimd", q1t[:, 0:2 * T_].re("p (r t) -> p r t", r=2), gg3, ecb, MUL)
                                K.tt("vector", q2t[:, 0:2 * T_].re("p (r t) -> p r t", r=2), gg3, esb, MUL)
                                qs[st] = (q1t, q2t)
                            ns_ = len(sts)
                            s0_, s1_ = sts[0], sts[-1] + 1
                            gl = WG[:, :, 2 * T_:4 * T_].re("p s (r t) -> p s r t", r=2)[:, :, :, T_ - 1]
                            a = sm.next()
                            a1 = a[:, 0:2 * ns_].re("p (s r) -> p s r", r=2)
                            a2 = a[:, 8:8 + 2 * ns_].re("p (s r) -> p s r", r=2)
                            K.tt("vector", a1, gl, EC[:, s0_:s1_, T_ - 1:T_].bc([128, ns_, 2]), MUL)
                            K.tt("vector", a2, gl, ES[:, s0_:s1_, T_ - 1:T_].bc([128, ns_, 2]), MUL)
                            K.tt("vector", cst[:, s0_:s1_, 0], a1[:, :, 0], a2[:, :, 1], SUB)
                            K.tt("vector", cst[:, s0_:s1_, 1], a2[:, :, 0], a1[:, :, 1], ADD)
                            for st in sts:
                                q1t, q2t = qs[st]
                                K.mm(yps[cc][:, 0:T_], Cm[:, st, 0, :], q1t[:, 0:T_], start=(st == sts[0]), stop=False)
                                K.mm(yps[cc][:, 0:T_], Cm[:, st, 1, :], q2t[:, T_:2 * T_], start=False, stop=False)
                                K.mm(yps[cc][:, 0:T_], Cm[:, st, 2, :], q2t[:, 0:T_], start=False, stop=False)
                                K.mm(yps[cc][:, 0:T_], Cm[:, st, 2, :], q1t[:, T_:2 * T_], start=False, stop=(st == sts[-1]))
                        if d == 0:
                            for cc in range(NCC):
                                dsc = s5dl[:, l:l + 1] if lat else s5d[:, l, cc:cc + 1]
                                K.stt("vector", yf[:, cc, cols], uT[:, cc, cols], dsc, yps[cc][:, 0:S5T], MUL, ADD)
                        else:
                            lc = ((s0 + c * S5T) % BS)
                            if ytot is None:
                                ytot = ytot_t
                            for cc in range(NCC):
                                K.tt("vector", ytot[cc][:, lc:lc + S5T], yps[cc][:, 0:S5T][:, ::-1], yf[:, cc, cols], ADD)
                            if lc == 0:
                                b0 = s0 + c * S5T
                                gy = []
                                for cc in range(NCC):
                                    xx = ytot[cc]
                                    t = wf.next()
                                    K.tt("gpsimd", t[:, 0:BS], xx[:, 0:BS], xx[:, 0:BS], MUL)
                                    K.ts("vector", t[:, 0:BS], t[:, 0:BS], 0.044715, MUL, 1.0, ADD)
                                    K.tt("gpsimd", t[:, 0:BS], t[:, 0:BS], xx[:, 0:BS], MUL)
                                    K.act(t[:, 0:BS], t[:, 0:BS], AF.Sigmoid, scale=1.5957691216057308)
                                    if lat:
                                        K.tt("vector", mixh[0:64, 3, b0:b0 + BS], xx[0:64, 0:BS], t[0:64, 0:BS], MUL)
                                        continue
                                    gb = wb.next()
                                    K.tt("vector", gb[:, 0:BS], xx[:, 0:BS], t[:, 0:BS], MUL)
                                    gy.append(gb)
                                ytot = None
                                if lat:
                                    continue
                                pg = []
                                for fo in range(4):
                                    ps = psr.next()
                                    for cc in range(2):
                                        K.mm(ps[:, 0:BS], wglu_b[:, cc, fo * 128:(fo + 1) * 128], gy[cc][:, 0:BS],
                                             start=(cc == 0), stop=(cc == 1))
                                    pg.append(ps)
                                for ch in range(2):
                                    sgm = wf.next()
                                    K.act(sgm[:, 0:BS], pg[2 + ch][:, 0:BS], AF.Sigmoid)
                                    t = wf.next()
                                    K.tt("vector", t[:, 0:BS], pg[ch][:, 0:BS], sgm[:, 0:BS], MUL)
                                    w = load_group(l, "Bg%d" % ch)
                                    psg = fm_cols(w, b0, BS)
                                    sgB = wb.next()
                                    K.act(sgB[:, 0:BS], psg[:, 0:BS], AF.Silu)
                                    K.tt("vector", mixed[:, ch, b0:b0 + BS], t[:, 0:BS], sgB[:, 0:BS], MUL)
                    if d == 0 and l == 0 and si == len(seqs) - 1:
                        dump("s5yf", yf[:, :, 0:512], [128, 2, 512])
                    if not lat:
                        o = sm.next()
                        o3 = o[:, :].re("p (s r) -> p s r", r=2)
                        t = sm.next()
                        K.tt("vector", t[:, 0:8], cst[:, :, 0], fz[:, :, 0], MUL)
                        K.tt("vector", t[:, 8:16], cst[:, :, 1], fz[:, :, 1], MUL)
                        K.tt("vector", o3[:, :, 0], t[:, 0:8], t[:, 8:16], SUB)
                        K.tt("vector", t[:, 0:8], cst[:, :, 0], fz[:, :, 1], MUL)
                        K.tt("vector", t[:, 8:16], cst[:, :, 1], fz[:, :, 0], MUL)
                        K.tt("vector", o3[:, :, 1], t[:, 0:8], t[:, 8:16], ADD)
                        P.dma(dout["o_s5"][si, l, d].ap.rearrange("(s g) p r -> (g p) s r", g=2), o3.ap,
                              reads=[o], writes=[], out=True, eng="gpsimd", allow_slow_non_contiguous=True)
            P.pop_scope()

        def post_gather(l):
            agin = DR(nc.dram_tensor("agin%d" % l, [256, LL], BF16).ap())
            agout = DR(nc.dram_tensor("agout%d" % l, [1024, LL], BF16).ap())
            agin_tok = Buf("agin_tok%d" % l, None)
            agout_tok = Buf("agout_tok%d" % l, None)
            P.dma(agin.ap.rearrange("(s p) t -> p s t", p=64), mixh.ap, reads=[mixh], writes=[agin_tok], eng="gpsimd")
            P.collective(lambda e: e.collective_compute("AllGather", ALU.bypass, replica_groups=[[0, 1, 2, 3], [4, 5, 6, 7]],
                                                        ins=[agin.ap.opt()], outs=[agout.ap.opt()]),
                         reads=[agin_tok], writes=[agout_tok])
            P.push_scope()
            woL = P.sbuf("woL", [128, 10, D], BF16)
            wgl = P.sbuf("wgl", [64, 4, 512], BF16)
            PB = 256
            gyb = P.sbuf("gyb", [64, 4, PB], BF16)
            mB = P.sbuf("mB", [128, 2, PB], BF16)
            mcb = P.sbuf("mcb", [128, 8, PB], BF16)
            mcr = Ring([P.sbuf("mcj%d" % i, [128, 8, PB], BF16) for i in range(2)])
            gyr = Ring([P.sbuf("gyj%d" % i, [64, 4, PB], BF16) for i in range(2)])
            for chunk in range(10):
                s = wst.next()
                sv = s.re("p a b -> p (a b)")
                K.dma(sv, din["woutL"][l][:, chunk, :])
                for n in range(2):
                    gb = wf.next()
                    P.dma(gb.ap, gsc[l][v:v + 1, n * 512:(n + 1) * 512].ap.partition_broadcast(128), reads=[gsc_tok], writes=[gb])
                    K.tt("vector" if n == 0 else "gpsimd", woL[:, chunk, n * 512:(n + 1) * 512], gb, sv[:, n * 512:(n + 1) * 512], MUL)
            for r in range(4):
                s = wf.next()
                K.dma(s[0:64, :], din["wgluL"][l][:, r, :])
                K.cp("vector", wgl[:, r, :], s[0:64, :])
            g4 = agout.ap.rearrange("(r s p) t -> p r s t", r=4, s=4)
            c8 = agout.ap.rearrange("(c p) t -> p c t", p=128)
            for tb in range(512 // PB):
                sgBs = []
                for ch in range(2):
                    w = load_group(l, "Bg%d" % ch)
                    psg = psr.next()
                    for k in range(8):
                        K.mm(psg[:, 0:PB], w[:, k, :], hTo[:, k, tb * PB:(tb + 1) * PB], start=(k == 0), stop=(k == 7))
                    sgB = wb.next()
                    K.act(sgB[:, 0:PB], psg[:, 0:PB], AF.Silu)
                    sgBs.append(sgB)
                for j_ in range(4):
                    cj = j_ * 512 + tb * PB
                    gj = gyr.next()
                    P.dma(gj.ap, g4[:, :, 3, cj:cj + PB], reads=[agout_tok], writes=[gj])
                    mj = mcr.next()
                    P.dma(mj.ap, c8[:, :, cj:cj + PB], reads=[agout_tok], writes=[mj])
                    if j_ == 0:
                        K.ts("vector", gyb, gj, ohs[0:64, 0:1], MUL)
                        K.ts("vector", mcb, mj, ohs[:, 0:1], MUL)
                    else:
                        K.stt("gpsimd" if False else "vector", gyb, gj, ohs[0:64, j_:j_ + 1], gyb, MUL, ADD)
                        K.stt("vector", mcb, mj, ohs[:, j_:j_ + 1], mcb, MUL, ADD)
                pg = []
                for fo in range(4):
                    ps = psr.next()
                    for r in range(4):
                        K.mm(ps[:, 0:PB], wgl[:, r, fo * 128:(fo + 1) * 128], gyb[:, r, :], start=(r == 0), stop=(r == 3))
                    pg.append(ps)
                for ch in range(2):
                    sgm = wf.next()
                    K.act(sgm[:, 0:PB], pg[2 + ch][:, 0:PB], AF.Sigmoid)
                    t = wf.next()
                    K.tt("vector", t[:, 0:PB], pg[ch][:, 0:PB], sgm[:, 0:PB], MUL)
                    K.tt("gpsimd", mB[:, ch, :], t[:, 0:PB], sgBs[ch][:, 0:PB], MUL)
                for ii in range(PB // 128):
                    i = tb * (PB // 128) + ii
                    for n in range(2):
                        ps = psr.next()
                        for chunk in range(10):
                            lhs = mcb[:, chunk, ii * 128:(ii + 1) * 128] if chunk < 8 else mB[:, chunk - 8, ii * 128:(ii + 1) * 128]
                            K.mm(ps, lhs, woL[:, chunk, n * 512:(n + 1) * 512], start=(chunk == 0), stop=(chunk == 9))
                        K.tt("vector", xv[i][:, n * 512:(n + 1) * 512], ps, xv[i][:, n * 512:(n + 1) * 512], ADD)
            P.pop_scope()

        fns = dict(A=(branch_A, 0), B=(branch_B, 1), C=(branch_C, 2), D=(branch_D, 3))
        for l in range(nlayers):
            norm_mod(l)
            dump("hT_%s_%d" % ("lat" if lat else "ctx", l), hT[:, :, 0:512], [128, 8, 512])
            if "B" in branches and lat:
                b_prep(l)
            if lat:
                hT_load()
            for bn in branches:
                fn, bi = fns[bn]
                if bn == "B" and not lat:
                    b_prep(l)
                fn(l)
                if bn == "B":
                    P.pop_scope()
                if lat:
                    continue
                dump("mix%s_%s_%d" % (bn, "lat" if lat else "ctx", l), mixed[:, :, 0:512], [128, 2, 512])
                out_proj(l, bi)
            if lat:
                dump("mixh_%d" % l, mixh[:, :, 0:512], [64, 4, 512])
                post_gather(l)
        for i in range(ntx):
            r = rstd_of(xv[i], D, 1.0 / D)
            for n in range(2):
                o = wf.next()
                fn_ = wf.next()
                K.dma(fn_, din["fnorm"][:, n * 512:(n + 1) * 512])
                K.stt("vector", o, xv[i][:, n * 512:(n + 1) * 512], r, fn_, MUL, MUL)
                K.dma(ydst[i][:, n * 512:(n + 1) * 512], o, out=True, eng="gpsimd")
        P.pop_scope()

    for j in jobs:
        run_job(j == "lat")
    P.emit()
    return nc, dbg_shapes


_NC_CACHE = {}


def kernel(**inputs):
    if "nc" not in _NC_CACHE:
        _NC_CACHE["nc"] = build()[0]
    nc = _NC_CACHE["nc"]
    sh = _shared_inputs(inputs)
    in_maps = [_core_inputs(inputs, sh, c) for c in range(8)]
    res = run_bass_kernel_spmd(nc, in_maps, core_ids=list(range(8)))
    return assemble([r for r in res.results])


def assemble(rs):
    B, SEQ = 16, 256
    y_prompt = np.concatenate([r["y_c"].reshape(2, SEQ, D) for r in rs], axis=0)
    y_sample = np.stack([np.concatenate([rs[4 * s_ + r_]["y_l"].reshape(512, D) for r_ in range(4)], axis=0) for s_ in range(2)], axis=0)
    dk = np.concatenate([r["o_dk"] for r in rs], axis=0).reshape(B, NL, SEQ, 4, 64)
    dv = np.concatenate([r["o_dv"] for r in rs], axis=0).reshape(B, NL, SEQ, 4, 64)
    s5 = np.concatenate([r["o_s5"] for r in rs], axis=0)
    hg = np.concatenate([r["o_hg"] for r in rs], axis=0)
    ckv = np.concatenate([r["o_ckv"] for r in rs], axis=0)
    kr = np.concatenate([r["o_kr"] for r in rs], axis=0)
    f = lambda a: np.ascontiguousarray(a, dtype=np.float32)
    return tuple(f(a) for a in (y_prompt, y_sample, dk, dv, s5, hg, ckv, kr))
```
